# Optimizing a Trainium2 kernel written in Bass

```python
import math
import jax, jax.numpy as jnp
from jax import lax
import numpy as np

D_MODEL = 1024
BATCH = 16
SEQ = 256
DEPTH = 2
DEC_BATCH = 4
DEC_SEQ = 2048
PAST_LEN = 256

GRID_W = 64
HEAD_DIM = 64
GQA_Q_HEADS = 8
GQA_KV_HEADS = 2
MLA_HEADS = 8
MLA_Q_RANK = 384
MLA_KV_RANK = 256
MLA_NOPE = 64
MLA_ROPE = 32
MLA_V = 64
HY_CH = 512
HY_ORDER = 2
HY_BANDS = 8
HY_EMB = 1 + 2 * HY_BANDS
HY_FFN = 64
HY_TARGET = 1e-2
HY_FAST_DECAY_PCT = 0.3
HY_SLOW_DECAY_PCT = 1.5
N_BRANCH = 3
BRANCH_W = 512
FFN_DIM = 2816
ROPE_THETA = 10000.0
Q_BLOCK = 128
EPS = 1e-6
MOD_CHUNKS = 6
SPLIT_SIZES = (GQA_Q_HEADS * HEAD_DIM, GQA_KV_HEADS * HEAD_DIM, GQA_KV_HEADS * HEAD_DIM,
               MLA_Q_RANK, MLA_KV_RANK, MLA_ROPE, 3 * HY_CH, N_BRANCH * D_MODEL)
IN_COLS = sum(SPLIT_SIZES)

kernel_name = "hybrid_gqa_mla_hyena_prefix_diffusion_step"


def _split_points():
    pts, acc = [], 0
    for s in SPLIT_SIZES[:-1]:
        acc += s
        pts.append(acc)
    return pts


def rms_norm(x, g):
    xf = x.astype(jnp.float32)
    y = xf * lax.rsqrt(jnp.mean(xf * xf, axis=-1, keepdims=True) + EPS)
    return (y * g.astype(jnp.float32)).astype(x.dtype)


def grid_rope(L, rot_dim):
    rows = L // GRID_W
    row = jnp.repeat(jnp.arange(rows, dtype=jnp.float32), GRID_W)
    col = jnp.tile(jnp.arange(GRID_W, dtype=jnp.float32), rows)
    axis_dim = rot_dim // 2
    inv = ROPE_THETA ** (-jnp.arange(0, axis_dim, 2, dtype=jnp.float32) / axis_dim)
    ang = jnp.concatenate([row[:, None] * inv, col[:, None] * inv], axis=-1)
    return jnp.cos(ang), jnp.sin(ang)


def apply_rope(x, rope):
    cos, sin = rope
    shp = x.shape
    xr = x.astype(jnp.float32).reshape(shp[:-1] + (shp[-1] // 2, 2))
    x0, x1 = xr[..., 0], xr[..., 1]
    c = cos[:, None, :]
    s = sin[:, None, :]
    out = jnp.stack([x0 * c - x1 * s, x0 * s + x1 * c], axis=-1).reshape(shp)
    return out.astype(x.dtype)


def blocked_attention(q, k, v, scale):
    B, Lq, H, Dk = q.shape
    Hkv = k.shape[2]
    G = H // Hkv
    Dv = v.shape[-1]
    nb = Lq // Q_BLOCK
    qb = jnp.moveaxis(q.reshape(B, nb, Q_BLOCK, Hkv, G, Dk), 1, 0)

    def one_block(qblk):
        s = jnp.einsum("bqhgd,bkhd->bhgqk", qblk, k, preferred_element_type=jnp.float32) * scale
        p = jax.nn.softmax(s, axis=-1).astype(v.dtype)
        return jnp.einsum("bhgqk,bkhd->bqhgd", p, v)

    o = lax.map(one_block, qb)
    return jnp.moveaxis(o, 0, 1).reshape(B, Lq, H, Dv)


def dwconv3(x, w, b):
    xp = jnp.pad(x, ((0, 0), (1, 1), (0, 0)))
    return xp[:, :-2] * w[0] + xp[:, 1:-1] * w[1] + xp[:, 2:] * w[2] + b


def hyena_filters(L, w1, b1, w2, b2, w3, freq):
    f32 = jnp.float32
    t = jnp.arange(L, dtype=f32)
    tn = t / max(L - 1, 1)
    bands = jnp.linspace(1e-4, HY_BANDS - 1, HY_BANDS, dtype=f32)
    ang = (2.0 * math.pi / L) * t[:, None] * bands[None, :]
    z = jnp.concatenate([tn[:, None], jnp.cos(ang), -jnp.sin(ang)], axis=-1)
    freq = freq.astype(f32)
    h = jnp.sin(freq[0] * (z @ w1.astype(f32) + b1.astype(f32)))
    h = jnp.sin(freq[1] * (h @ w2.astype(f32) + b2.astype(f32)))
    h = (h @ w3.astype(f32)).reshape(L, 2, HY_ORDER, HY_CH)
    min_decay = math.log(HY_TARGET) / HY_FAST_DECAY_PCT
    max_decay = math.log(HY_TARGET) / HY_SLOW_DECAY_PCT
    deltas = jnp.abs(jnp.linspace(min_decay, max_decay, HY_CH, dtype=f32))
    window = jnp.exp(-tn[:, None] * deltas[None, :])
    h = h * window[:, None, None, :]
    fwd = h[:, 0]
    bwd = h[1:, 1]
    l1 = jnp.sum(jnp.abs(fwd), axis=0) + jnp.sum(jnp.abs(bwd), axis=0) + EPS
    fwd = fwd / l1
    bwd = bwd / l1
    kern = jnp.concatenate([fwd, jnp.zeros((1, HY_ORDER, HY_CH), f32), bwd[::-1]], axis=0)
    return jnp.fft.rfft(kern, axis=0)


def long_conv(u, kf):
    L = u.shape[1]
    uf = jnp.fft.rfft(u.astype(jnp.float32), n=2 * L, axis=1)
    return jnp.fft.irfft(uf * kf[None], n=2 * L, axis=1)[:, :L].astype(u.dtype)


def hyena_branch(hy_in, lp):
    L = hy_in.shape[1]
    u = dwconv3(hy_in, lp["hy_short_w"], lp["hy_short_b"])
    v, x1, x2 = jnp.split(u, 3, axis=-1)
    kf = hyena_filters(L, lp["hy_w1"], lp["hy_b1"], lp["hy_w2"], lp["hy_b2"], lp["hy_w3"], lp["hy_freq"])
    z = v
    for n, gate in enumerate((x1, x2)):
        z = gate * (long_conv(z, kf[:, n]) + lp["hy_bias"][n] * z)
    return z


def token_mixers(h, lp, ctx_cache, rope_a, rope_b):
    B, L, _ = h.shape
    proj = h @ lp["w_in"]
    q_a, k_a, v_a, cq, ckv, kpe, hy_in, gates = jnp.split(proj, _split_points(), axis=-1)
    q_a = rms_norm(q_a.reshape(B, L, GQA_Q_HEADS, HEAD_DIM), lp["gqa_q_norm"])
    k_a = rms_norm(k_a.reshape(B, L, GQA_KV_HEADS, HEAD_DIM), lp["gqa_k_norm"])
    v_a = v_a.reshape(B, L, GQA_KV_HEADS, HEAD_DIM)
    cq = rms_norm(cq, lp["mla_q_norm"])
    q_b = (cq @ lp["mla_w_uq"]).reshape(B, L, MLA_HEADS, MLA_NOPE + MLA_ROPE)
    q_nope, q_pe = q_b[..., :MLA_NOPE], q_b[..., MLA_NOPE:]
    ckv = rms_norm(ckv, lp["mla_kv_norm"])
    kpe = kpe[:, :, None, :]
    if ctx_cache is None:
        new_ctx = (k_a, v_a, ckv, kpe[:, :, 0])
        keys_a, vals_a, ckv_all, kpe_all = k_a, v_a, ckv, kpe
    else:
        new_ctx = None
        q_a = apply_rope(q_a, rope_a)
        q_pe = apply_rope(q_pe, rope_b)
        c_k, c_v, c_ckv, c_kpe = ctx_cache
        keys_a = jnp.concatenate([c_k, apply_rope(k_a, rope_a)], axis=1)
        vals_a = jnp.concatenate([c_v, v_a], axis=1)
        ckv_all = jnp.concatenate([c_ckv, ckv], axis=1)
        kpe_all = jnp.concatenate([c_kpe[:, :, None, :], apply_rope(kpe, rope_b)], axis=1)
    o_a = blocked_attention(q_a, keys_a, vals_a, HEAD_DIM ** -0.5).reshape(B, L, BRANCH_W)
    kv = (ckv_all @ lp["mla_w_ukv"]).reshape(B, -1, MLA_HEADS, MLA_NOPE + MLA_V)
    k_b = jnp.concatenate([kv[..., :MLA_NOPE], jnp.broadcast_to(kpe_all, kv.shape[:-1] + (MLA_ROPE,))], axis=-1)
    q_b = jnp.concatenate([q_nope, q_pe], axis=-1)
    o_b = blocked_attention(q_b, k_b, kv[..., MLA_NOPE:], (MLA_NOPE + MLA_ROPE) ** -0.5).reshape(B, L, BRANCH_W)
    o_c = hyena_branch(hy_in, lp)
    branches = jnp.einsum("nblc,ncd->nbld", jnp.stack([o_a, o_b, o_c]), lp["w_branch"])
    g = jax.nn.sigmoid(gates.reshape(B, L, N_BRANCH, D_MODEL))
    merged = jnp.einsum("blnd,nbld->bld", g, branches)
    return merged @ lp["w_out"], new_ctx


def conv_ffn(h, lp):
    u = dwconv3(h @ lp["ffn_up"], lp["ffn_conv_w"], lp["ffn_conv_b"])
    a, g = jnp.split(u, 2, axis=-1)
    return (jax.nn.silu(g) * a) @ lp["ffn_down"]


def trunk_layer(x, cond, lp, ctx_cache, rope_a, rope_b):
    mod = (jax.nn.silu(cond) @ lp["w_mod"] + lp["b_mod"])[:, None, :]
    sh1, sc1, g1, sh2, sc2, g2 = jnp.split(mod, MOD_CHUNKS, axis=-1)
    h = rms_norm(x, lp["norm1"]) * (1 + sc1) + sh1
    o, new_ctx = token_mixers(h, lp, ctx_cache, rope_a, rope_b)
    x = x + g1 * o
    h = rms_norm(x, lp["norm2"]) * (1 + sc2) + sh2
    x = x + g2 * conv_ffn(h, lp)
    return x, new_ctx


def setup_inputs(seed: int = 0) -> dict:
    key = jax.random.key(seed)
    keys = iter(jax.random.split(key, 48))
    f32 = jnp.float32
    D = D_MODEL

    def nrm(shape, scale):
        return jax.random.normal(next(keys), shape, f32) * scale

    def gain(shape):
        return 1.0 + nrm(shape, 0.05)

    return {
        "x_prompt": nrm((BATCH, SEQ, D), 1.0),
        "x_sample": nrm((DEC_BATCH, DEC_SEQ, D), 1.0),
        "cache_gqa_k": nrm((DEC_BATCH, DEPTH, PAST_LEN, GQA_KV_HEADS, HEAD_DIM), 1.0),
        "cache_gqa_v": nrm((DEC_BATCH, DEPTH, PAST_LEN, GQA_KV_HEADS, HEAD_DIM), 1.0),
        "cache_mla_ckv": nrm((DEC_BATCH, DEPTH, PAST_LEN, MLA_KV_RANK), 1.0),
        "cache_mla_kpe": nrm((DEC_BATCH, DEPTH, PAST_LEN, MLA_ROPE), 1.0),
        "c": nrm((DEC_BATCH, D), 1.0),
        "c_ctx": nrm((D,), 1.0),
        "w_mod": nrm((DEPTH, D, MOD_CHUNKS * D), 0.5 * D ** -0.5),
        "b_mod": nrm((DEPTH, MOD_CHUNKS * D), 0.01),
        "norm1": gain((DEPTH, D)),
        "norm2": gain((DEPTH, D)),
        "w_in": nrm((DEPTH, D, IN_COLS), D ** -0.5),
        "gqa_q_norm": gain((DEPTH, HEAD_DIM)),
        "gqa_k_norm": gain((DEPTH, HEAD_DIM)),
        "mla_q_norm": gain((DEPTH, MLA_Q_RANK)),
        "mla_kv_norm": gain((DEPTH, MLA_KV_RANK)),
        "mla_w_uq": nrm((DEPTH, MLA_Q_RANK, MLA_HEADS * (MLA_NOPE + MLA_ROPE)), MLA_Q_RANK ** -0.5),
        "mla_w_ukv": nrm((DEPTH, MLA_KV_RANK, MLA_HEADS * (MLA_NOPE + MLA_V)), MLA_KV_RANK ** -0.5),
        "hy_short_w": nrm((DEPTH, 3, 3 * HY_CH), 3 ** -0.5),
        "hy_short_b": nrm((DEPTH, 3 * HY_CH), 0.02),
        "hy_w1": nrm((DEPTH, HY_EMB, HY_FFN), HY_EMB ** -0.5),
        "hy_b1": nrm((DEPTH, HY_FFN), 0.1),
        "hy_w2": nrm((DEPTH, HY_FFN, HY_FFN), HY_FFN ** -0.5),
        "hy_b2": nrm((DEPTH, HY_FFN), 0.1),
        "hy_w3": nrm((DEPTH, HY_FFN, 2 * HY_ORDER * HY_CH), HY_FFN ** -0.5),
        "hy_freq": gain((DEPTH, 2, HY_FFN)),
        "hy_bias": nrm((DEPTH, HY_ORDER, HY_CH), 0.5),
        "w_branch": nrm((DEPTH, N_BRANCH, BRANCH_W, D), BRANCH_W ** -0.5),
        "w_out": nrm((DEPTH, D, D), D ** -0.5),
        "ffn_up": nrm((DEPTH, D, 2 * FFN_DIM), D ** -0.5),
        "ffn_conv_w": nrm((DEPTH, 3, 2 * FFN_DIM), 3 ** -0.5),
        "ffn_conv_b": nrm((DEPTH, 2 * FFN_DIM), 0.02),
        "ffn_down": nrm((DEPTH, FFN_DIM, D), FFN_DIM ** -0.5),
        "final_norm": gain((D,)),
    }


def reference(x_prompt, x_sample, cache_gqa_k, cache_gqa_v, cache_mla_ckv, cache_mla_kpe, c, c_ctx,
              w_mod, b_mod, norm1, norm2, w_in, gqa_q_norm, gqa_k_norm, mla_q_norm, mla_kv_norm,
              mla_w_uq, mla_w_ukv, hy_short_w, hy_short_b, hy_w1, hy_b1, hy_w2, hy_b2, hy_w3, hy_freq,
              hy_bias, w_branch, w_out, ffn_up, ffn_conv_w, ffn_conv_b, ffn_down, final_norm):
    L_lat = x_sample.shape[1]
    rope_a = grid_rope(L_lat, HEAD_DIM)
    rope_b = grid_rope(L_lat, MLA_ROPE)
    xp = x_prompt
    xs = x_sample
    ctx_cond = c_ctx[None, :]
    ks, vs, ckvs, kpes = [], [], [], []
    for l in range(DEPTH):
        lp = dict(w_mod=w_mod[l], b_mod=b_mod[l], norm1=norm1[l], norm2=norm2[l], w_in=w_in[l],
                  gqa_q_norm=gqa_q_norm[l], gqa_k_norm=gqa_k_norm[l], mla_q_norm=mla_q_norm[l],
                  mla_kv_norm=mla_kv_norm[l], mla_w_uq=mla_w_uq[l], mla_w_ukv=mla_w_ukv[l],
                  hy_short_w=hy_short_w[l], hy_short_b=hy_short_b[l], hy_w1=hy_w1[l], hy_b1=hy_b1[l],
                  hy_w2=hy_w2[l], hy_b2=hy_b2[l], hy_w3=hy_w3[l], hy_freq=hy_freq[l], hy_bias=hy_bias[l],
                  w_branch=w_branch[l], w_out=w_out[l], ffn_up=ffn_up[l], ffn_conv_w=ffn_conv_w[l],
                  ffn_conv_b=ffn_conv_b[l], ffn_down=ffn_down[l])
        xp, (k_l, v_l, ckv_l, kpe_l) = trunk_layer(xp, ctx_cond, lp, None, None, None)
        ks.append(k_l)
        vs.append(v_l)
        ckvs.append(ckv_l)
        kpes.append(kpe_l)
        cache_l = (cache_gqa_k[:, l], cache_gqa_v[:, l], cache_mla_ckv[:, l], cache_mla_kpe[:, l])
        xs, _ = trunk_layer(xs, c, lp, cache_l, rope_a, rope_b)
    y_prompt = rms_norm(xp, final_norm)
    y_sample = rms_norm(xs, final_norm)
    new_gqa_k = jnp.stack(ks, axis=1)
    new_gqa_v = jnp.stack(vs, axis=1)
    new_mla_ckv = jnp.stack(ckvs, axis=1)
    new_mla_kpe = jnp.stack(kpes, axis=1)
    return (y_prompt, y_sample, new_gqa_k, new_gqa_v, new_mla_ckv, new_mla_kpe)
```

```python
import math
from contextlib import ExitStack
import numpy as np
import ml_dtypes
import concourse.bass as bass
import concourse.mybir as mybir
from concourse.bass_utils import run_bass_kernel_spmd

F32 = mybir.dt.float32
BF16 = mybir.dt.bfloat16
AF = mybir.ActivationFunctionType
ALU = mybir.AluOpType

D = 1024
EPS = 1e-6
NCORES = 8


class Sched:
    ENG = ("pe", "act", "dve", "pool", "sp")
    NDS = 8
    NOSELF = ("pe",)

    def __init__(self, nc, stack):
        self.nc = nc
        self.stack = stack
        self.streams = {e: [] for e in self.ENG}
        self.cnt = {e: 0 for e in self.ENG}
        self.sem = {e: stack.enter_context(nc.semaphore("s_" + e)) for e in self.ENG}
        self.skey = {e: "s_" + e for e in self.ENG}
        self.epoch = 0
        self.dq = ("sp", "pool", "act")
        self.dsem = {q: [stack.enter_context(nc.semaphore("d_%s%d" % (q, i))) for i in range(self.NDS)]
                     for q in self.dq}
        self.dcnt = {q: [0] * self.NDS for q in self.dq}
        self.dnext = {q: 0 for q in self.dq}
        self.seen = {e: {} for e in self.ENG}
        self.lastw = {}
        self.readers = {}
        self.nops = 0

    def _wait(self, eng, tok):
        key, sem, val, src = tok
        if src == eng and eng in self.NOSELF:
            return
        if self.seen[eng].get(key, 0) >= val:
            return
        self.seen[eng][key] = val
        self.streams[eng].append(("wait", sem, val))

    def op(self, eng, fn, reads=(), writes=(), dma=False):
        toks = []
        for r in reads:
            t = self.lastw.get(r)
            if t is not None:
                toks.append(t)
            if isinstance(r, tuple) and r and r[0] == "ps":
                for t2 in self.readers.get(r, {}).values():
                    if t2[3] != eng:
                        toks.append(t2)
        for w in writes:
            t = self.lastw.get(w)
            if t is not None:
                toks.append(t)
            toks.extend(self.readers.get(w, {}).values())
        for t in toks:
            self._wait(eng, t)
        if dma:
            q = eng
            i = self.dnext[q]
            self.dnext[q] = (i + 1) % self.NDS
            key = "d_%s%d" % (q, i)
            if self.dcnt[q][i] > 0:
                self._wait(eng, (key, self.dsem[q][i], self.dcnt[q][i], None))
            self.dcnt[q][i] += 16
            tok = (key, self.dsem[q][i], self.dcnt[q][i], None)
            self.streams[eng].append(("op", fn, self.dsem[q][i], 16))
        else:
            self.cnt[eng] += 1
            tok = (self.skey[eng], self.sem[eng], self.cnt[eng], eng)
            self.streams[eng].append(("op", fn, self.sem[eng], 1))
        self.nops += 1
        for w in writes:
            self.lastw[w] = tok
            self.readers[w] = {}
        for r in reads:
            d = self.readers.setdefault(r, {})
            old = d.get(tok[0])
            if old is None or old[2] < tok[2]:
                d[tok[0]] = tok
        return tok

    def barrier(self):
        for e in self.ENG:
            for e2 in self.ENG:
                if self.cnt[e2] > 0 and e2 != e:
                    self._wait(e, (self.skey[e2], self.sem[e2], self.cnt[e2], e2))
            for q in self.dq:
                for i in range(self.NDS):
                    if self.dcnt[q][i] > 0:
                        self._wait(e, ("d_%s%d" % (q, i), self.dsem[q][i], self.dcnt[q][i], None))
        self.lastw = {}
        self.readers = {}
        for e in self.ENG:
            if self.cnt[e] > 8000:
                self.epoch += 1
                self.skey[e] = "s_%s_%d" % (e, self.epoch)
                self.sem[e] = self.stack.enter_context(self.nc.semaphore(self.skey[e]))
                self.cnt[e] = 0

    def finish(self):
        for e2 in self.ENG:
            if e2 != "sp" and self.cnt[e2] > 0:
                self._wait("sp", (self.skey[e2], self.sem[e2], self.cnt[e2], e2))
        for q in self.dq:
            for i in range(self.NDS):
                if self.dcnt[q][i] > 0:
                    self._wait("sp", ("d_%s%d" % (q, i), self.dsem[q][i], self.dcnt[q][i], None))

    def emit(self):
        nc = self.nc

        def run(e, eng):
            for it in self.streams[e]:
                if it[0] == "wait":
                    eng.wait_ge(it[1], it[2])
                else:
                    ins = it[1](eng)
                    ins.then_inc(it[2], it[3])

        with nc.Block() as block:
            @block.tensor
            def _(eng):
                run("pe", eng)

            @block.scalar
            def _(eng):
                run("act", eng)

            @block.vector
            def _(eng):
                run("dve", eng)

            @block.gpsimd
            def _(eng):
                run("pool", eng)

            @block.sync
            def _(eng):
                run("sp", eng)


VOFF = {}
_o = 0
for _n, _w in [("norm1", 8), ("norm2", 8), ("qg", 1), ("qgs", 1), ("kg", 1), ("kgs", 1), ("mqn", 3), ("mkvn", 2),
               ("hsw", 36), ("hsb", 12), ("fcw", 132), ("fcb", 44), ("fin", 8), ("bmod", 48)]:
    VOFF[_n] = _o
    _o += _w
NV = _o

WIN_Q, WIN_K, WIN_V, WIN_CQ, WIN_CKV, WIN_KPE, WIN_HY, WIN_G = 0, 512, 640, 768, 1152, 1408, 1440, 2976


def _fm(w, kc):
    return np.ascontiguousarray(w.reshape(kc, 128, w.shape[1]).transpose(1, 0, 2))


def _swap_pairs(w):
    o = np.empty_like(w)
    o[..., 0::2] = w[..., 1::2]
    o[..., 1::2] = w[..., 0::2]
    return o


def _rope_tables(L, rot_dim, grid_w=64, theta=10000.0):
    rows = L // grid_w
    row = np.repeat(np.arange(rows, dtype=np.float32), grid_w)
    col = np.tile(np.arange(grid_w, dtype=np.float32), rows)
    axis_dim = rot_dim // 2
    inv = (theta ** (-np.arange(0, axis_dim, 2, dtype=np.float32) / axis_dim)).astype(np.float32)
    ang = np.concatenate([row[:, None] * inv, col[:, None] * inv], axis=-1).astype(np.float32)
    c = np.cos(ang).astype(np.float32)
    s = np.sin(ang).astype(np.float32)
    cf = np.repeat(c, 2, axis=1).T
    sf = np.repeat(s, 2, axis=1).T.copy()
    sf[0::2] *= -1.0
    return np.ascontiguousarray(cf), np.ascontiguousarray(sf)


def _hy_consts(L):
    t = np.arange(L, dtype=np.float32)
    tn = t / max(L - 1, 1)
    bands = np.linspace(1e-4, 7, 8, dtype=np.float32)
    ang = (np.float32(2.0 * math.pi / L) * t[:, None] * bands[None, :]).astype(np.float32)
    z = np.concatenate([tn[:, None], np.cos(ang), -np.sin(ang)], axis=-1).astype(np.float32)
    min_decay = math.log(1e-2) / 0.3
    max_decay = math.log(1e-2) / 1.5
    deltas = np.abs(np.linspace(min_decay, max_decay, 512, dtype=np.float32))
    window = np.exp(-tn[:, None] * deltas[None, :]).astype(np.float32)
    idx = np.arange(L, dtype=np.float64) + 0.5
    phi = np.pi * np.outer(idx, idx) / L
    C = np.cos(phi)
    Sn = np.sin(phi)
    nt = L // 128

    def slabs(M):
        a = M.reshape(nt, 128, nt, 128).transpose(2, 1, 0, 3)
        return np.ascontiguousarray(a).astype(ml_dtypes.bfloat16)

    alpha = np.pi * idx / (2 * L)
    ca = np.cos(alpha).reshape(nt, 128).T.astype(np.float32)
    sa = np.sin(alpha).reshape(nt, 128).T.astype(np.float32)
    def rslabs(M):
        return np.ascontiguousarray(M.reshape(nt, 128, L)).astype(ml_dtypes.bfloat16)

    return dict(zT=np.ascontiguousarray(z.T), win=window, dc=slabs(C), ds=slabs(Sn), rc=rslabs(C), rs=rslabs(Sn),
                ca=np.ascontiguousarray(ca), sa=np.ascontiguousarray(sa))


def prep_shared(I):
    sh = {}
    sh["wmod"] = np.stack([_fm(I["w_mod"][l], 8) for l in range(2)])
    sh["win"] = np.stack([_fm(I["w_in"][l], 8) for l in range(2)])
    wx = []
    for l in range(2):
        w = I["w_in"][l]
        q = w[:, WIN_Q:WIN_Q + 512]
        k = w[:, WIN_K:WIN_K + 128]
        kd = np.concatenate([k[:, 0:64], k[:, 0:64], k[:, 64:128], k[:, 64:128]], axis=1)
        kpe = w[:, WIN_KPE:WIN_KPE + 32]
        wx.append(_fm(np.concatenate([_swap_pairs(q), kd, _swap_pairs(kd), _swap_pairs(kpe)], axis=1), 8))
    sh["winx"] = np.stack(wx)
    sh["wuq"] = np.stack([_fm(I["mla_w_uq"][l], 3) for l in range(2)])
    ux = []
    for l in range(2):
        w = I["mla_w_uq"][l].reshape(384, 8, 96)[:, :, 64:96].reshape(384, 256)
        ux.append(_fm(_swap_pairs(w), 3))
    sh["wuqx"] = np.stack(ux)
    sh["wukv"] = np.stack([_fm(I["mla_w_ukv"][l], 2) for l in range(2)])
    sh["wbr"] = np.stack([np.stack([_fm(I["w_branch"][l, n], 4) for n in range(3)]) for l in range(2)])
    sh["wout"] = np.stack([_fm(I["w_out"][l], 8) for l in range(2)])
    sh["fup"] = np.stack([_fm(I["ffn_up"][l], 8) for l in range(2)])
    sh["fdn"] = np.stack([_fm(I["ffn_down"][l], 22) for l in range(2)])
    vec = np.zeros((2, 128, NV), np.float32)
    for l in range(2):
        def put(name, arr):
            arr = np.asarray(arr, np.float32)
            vec[l, :, VOFF[name]:VOFF[name] + arr.shape[1]] = arr
        put("norm1", I["norm1"][l].reshape(8, 128).T)
        put("norm2", I["norm2"][l].reshape(8, 128).T)
        qg = I["gqa_q_norm"][l]
        kg = I["gqa_k_norm"][l]
        put("qg", np.tile(qg, 2)[:, None])
        put("qgs", np.tile(_swap_pairs(qg), 2)[:, None])
        put("kg", np.tile(kg, 2)[:, None])
        put("kgs", np.tile(_swap_pairs(kg), 2)[:, None])
        put("mqn", I["mla_q_norm"][l].reshape(3, 128).T)
        put("mkvn", I["mla_kv_norm"][l].reshape(2, 128).T)
        put("hsw", I["hy_short_w"][l].reshape(3, 12, 128).transpose(2, 1, 0).reshape(128, 36))
        put("hsb", I["hy_short_b"][l].reshape(12, 128).T)
        put("fcw", I["ffn_conv_w"][l].reshape(3, 44, 128).transpose(2, 1, 0).reshape(128, 132))
        put("fcb", I["ffn_conv_b"][l].reshape(44, 128).T)
        put("fin", I["final_norm"].reshape(8, 128).T)
        put("bmod", I["b_mod"][l].reshape(48, 128).T)
    sh["vec"] = vec
    sh["hyw1"] = np.ascontiguousarray(I["hy_w1"])
    sh["hyw2"] = np.ascontiguousarray(I["hy_w2"])
    sh["hyw3"] = np.ascontiguousarray(I["hy_w3"])
    hv = np.zeros((2, 64, 4), np.float32)
    for l in range(2):
        hv[l, :, 0] = I["hy_b1"][l]
        hv[l, :, 1] = I["hy_b2"][l]
        hv[l, :, 2] = I["hy_freq"][l, 0]
        hv[l, :, 3] = I["hy_freq"][l, 1]
    sh["hyv"] = hv
    sh["hybias"] = np.ascontiguousarray(I["hy_bias"].reshape(2, 2, 512))
    sh["hybiasT"] = np.ascontiguousarray(I["hy_bias"].reshape(2, 2, 4, 128).transpose(0, 2, 3, 1))
    ident = np.eye(128, dtype=np.float32)
    sh["ident"] = ident
    bd = np.zeros((128, 128), np.float32)
    bd[:64, :64] = 1.0
    bd[64:, 64:] = 1.0
    sh["bd64"] = bd
    ca, sa = _rope_tables(2048, 64)
    sh["ropeAc"] = np.concatenate([ca, ca], 0)
    sh["ropeAs"] = np.concatenate([sa, sa], 0)
    cb, sb_ = _rope_tables(2048, 32)
    rb = np.zeros((128, 2048), np.float32)
    rb[64:96] = cb
    rb[0:32] = cb
    rb[32:64] = cb
    sh["ropeBc"] = rb
    rb2 = np.zeros((128, 2048), np.float32)
    rb2[64:96] = sb_
    rb2[0:32] = sb_
    rb2[32:64] = sb_
    sh["ropeBs"] = rb2
    for L in (256, 2048):
        hc = _hy_consts(L)
        for k, v in hc.items():
            sh["hy%d_%s" % (L, k)] = v
    return sh


def prep_core(I, c):
    b = c % 4
    m = {}
    m["xp"] = np.ascontiguousarray(I["x_prompt"][2 * c:2 * c + 2].reshape(512, 1024))
    m["xs"] = np.ascontiguousarray(I["x_sample"][b])
    cond = np.stack([I["c_ctx"], I["c"][b]], axis=-1)
    m["cond"] = np.ascontiguousarray(cond.reshape(8, 128, 2).transpose(1, 0, 2))
    ck = I["cache_gqa_k"][b]
    m["ckd"] = np.ascontiguousarray(np.stack([ck, ck], axis=3).reshape(2, 256, 256))
    m["cv"] = np.ascontiguousarray(I["cache_gqa_v"][b].reshape(2, 256, 128))
    m["cckv"] = np.ascontiguousarray(I["cache_mla_ckv"][b])
    m["ckpe"] = np.ascontiguousarray(I["cache_mla_kpe"][b])
    return m


class Path:
    def __init__(self, name, T, seqs, sample, xin, yout, ccol):
        self.name, self.T, self.seqs, self.sample = name, T, seqs, sample
        self.xin, self.yout, self.ccol = xin, yout, ccol
        self.koff = 256 if sample else 0
        self.L = seqs[0][1]
        self.blocks = []
        for si, (t0, L) in enumerate(seqs):
            for s in range(0, L, 512):
                self.blocks.append((t0 + s, min(512, L - s), si))

    def hcol(self, t, si):
        return t + 1 + 2 * si


def build(shared_shapes, core_shapes, cfg):
    nc = bass.Bass("TRN2", target_bir_lowering=False)
    Din = {}
    for k, (shp, dt) in list(shared_shapes.items()) + list(core_shapes.items()):
        Din[k] = nc.dram_tensor(k, list(shp), BF16 if dt == "bf16" else F32, kind="ExternalInput").ap()

    def dout(name, shape):
        return nc.dram_tensor(name, list(shape), F32, kind="ExternalOutput").ap()

    O = dict(yp=dout("yp", [512, 1024]), ys=dout("ys", [2048, 1024]),
             nk=dout("nk", [2, 2, 256, 128]), nv=dout("nv", [2, 2, 256, 128]),
             nckv=dout("nckv", [2, 2, 256, 256]), nkpe=dout("nkpe", [2, 2, 256, 32]))

    fupb = nc.dram_tensor("fupb", [2, 128, 8, 5632], BF16, kind="Internal").ap()
    fdnb = nc.dram_tensor("fdnb", [2, 128, 22, 1024], BF16, kind="Internal").ap()

    with ExitStack() as st:
        S = Sched(nc, st)

        _un = [0]

        def sbt(stack, name, shape, dt):
            _un[0] += 1
            return stack.enter_context(nc.sbuf_tensor("%s_%d" % (name, _un[0]), list(shape), dt))

        ps = [st.enter_context(nc.psum_tensor("ps%d" % i, [128, 512], F32)) for i in range(8)]
        pctr = [0]

        def nb(excl=()):
            while True:
                i = pctr[0]
                pctr[0] = (i + 1) % 8
                if i not in excl:
                    return i

        def PS(i):
            return ("ps", i)

        class Pool:
            def __init__(self, stack, name, n, shape, dt):
                self.t = [sbt(stack, "%s%d" % (name, i), shape, dt) for i in range(n)]
                self.name, self.n, self.i = name, n, 0

            def get(self):
                i = self.i
                self.i = (i + 1) % self.n
                return self.t[i], (self.name, i)

        def MM(out, lhsT, rhs, start, stop):
            return lambda e: e.matmul(out, lhsT=lhsT, rhs=rhs, start=start, stop=stop)

        def ACT(out, in_, func, **kw):
            return lambda e: e.activation(out=out, in_=in_, func=func, **kw)

        def TT(out, in0, in1, op):
            return lambda e: e.tensor_tensor(out=out, in0=in0, in1=in1, op=op)

        def STT(out, in0, scalar, in1, op0, op1):
            return lambda e: e.scalar_tensor_tensor(out=out, in0=in0, scalar=scalar, in1=in1, op0=op0, op1=op1)

        def TS(out, in0, s1, s2, op0, op1=None):
            if op1 is None:
                return lambda e: e.tensor_scalar(out=out, in0=in0, scalar1=s1, scalar2=None, op0=op0)
            return lambda e: e.tensor_scalar(out=out, in0=in0, scalar1=s1, scalar2=s2, op0=op0, op1=op1)

        def CP(out, in_):
            return lambda e: e.tensor_copy(out=out, in_=in_)

        def DMA(out, in_):
            return lambda e: e.dma_start(out=out, in_=in_)

        def MS(ap, v):
            return lambda e: e.memset(ap, v)

        xT = sbt(st, "xT", [128, 8, 2048], F32)
        hT = sbt(st, "hT", [128, 8, 2052], BF16)
        oT = sbt(st, "oT", [128, 4, 2048], BF16)
        identf = sbt(st, "identf", [128, 128], F32)
        identb = sbt(st, "identb", [128, 128], BF16)
        bd64 = sbt(st, "bd64", [128, 128], F32)
        onesf = sbt(st, "onesf", [128, 128], F32)
        epsb = sbt(st, "epsb", [128, 1], F32)
        vec = sbt(st, "vec", [128, 2, NV], F32)
        modT = sbt(st, "modT", [128, 2, 48, 2], F32)
        gsh = sbt(st, "gsh", [128, 4, 8], F32)
        class _PP:
            pass
        PP = _PP()
        _pn = [0]

        def mkpools(sc):
            _pn[0] += 1
            k = _pn[0]
            PP.sq = Pool(sc, "sq%d_" % k, 2, [128, 512], F32)
            PP.ln = Pool(sc, "ln%d_" % k, 1, [128, 512], F32)
            PP.rs = Pool(sc, "rs%d_" % k, 2, [128, 512], F32)
            PP.tm = Pool(sc, "tm%d_" % k, 4, [128, 512], F32)

        S.op("sp", DMA(identf[:], Din["ident"]), writes=["c0"], dma=True)
        S.op("pool", DMA(identb[:], Din["ident"]), writes=["c1"], dma=True)
        S.op("sp", DMA(bd64[:], Din["bd64"]), writes=["c2"], dma=True)
        S.op("dve", MS(onesf[:], 1.0), writes=["c3"])
        S.op("dve", MS(epsb[:], EPS), writes=["c4"])
        S.op("dve", MS(hT[:], 0.0), writes=["c5"])
        for l in range(2):
            S.op("sp", DMA(vec[:, l, :], Din["vec"][l]), writes=["c6%d" % l], dma=True)

        def V(l, name, j=0, n=1):
            o = VOFF[name] + j
            return vec[:, l, o:o + n]

        with ExitStack() as sc:
            condt = sbt(sc, "condt", [128, 8, 2], F32)
            scond = sbt(sc, "scond", [128, 8, 64], F32)
            modrow = sbt(sc, "modrow", [64, 6144], F32)
            wmp = Pool(sc, "wm", 2, [128, 8, 512], F32)
            S.op("sp", DMA(condt[:], Din["cond"]), writes=["condt"], dma=True)
            S.op("dve", MS(scond[:], 0.0), writes=["scond"])
            S.op("act", ACT(scond[:, :, 0:2], condt[:], AF.Silu), reads=["condt", "scond"], writes=["scond"])
            for l in range(2):
                for sc12 in range(12):
                    wt, wr = wmp.get()
                    S.op("sp", DMA(wt[:], Din["wmod"][l][:, :, sc12 * 512:(sc12 + 1) * 512]), writes=[wr], dma=True)
                    b = nb()
                    for kc in range(8):
                        S.op("pe", MM(ps[b][0:64, :], scond[:, kc, :], wt[:, kc, :], kc == 0, kc == 7),
                             reads=[wr, "scond"], writes=[PS(b)])
                    S.op("dve", CP(modrow[:, sc12 * 512:(sc12 + 1) * 512], ps[b][0:64, :]), reads=[PS(b), "modrow"], writes=["modrow"])
                b = nb()
                for ch in range(48):
                    S.op("pe", MM(ps[b][:, 2 * ch:2 * ch + 2], modrow[:, ch * 128:(ch + 1) * 128], identf[0:64, 0:2], True, True),
                         reads=["modrow", "c0"], writes=[PS(b)])
                pv_ = ps[b][:, 0:96].rearrange("p (ch c) -> p ch c", c=2)
                for c in range(2):
                    S.op("dve", TT(modT[:, l, :, c], pv_[:, :, c], V(l, "bmod", 0, 48), ALU.add),
                         reads=[PS(b), "c6%d" % l, "modT"], writes=["modT"])
            S.barrier()

        def MOD(l, which, kc, ccol):
            return modT[:, l, which * 8 + kc, ccol:ccol + 1]

        def load_x(path):
            with ExitStack() as sc:
                xl = Pool(sc, "xl", 2, [128, 1024], F32)
                for tt in range(path.T // 128):
                    t_, r_ = xl.get()
                    S.op("sp", DMA(t_[:], path.xin[tt * 128:(tt + 1) * 128, :]), writes=[r_], dma=True)
                    for kc2 in range(2):
                        b = nb()
                        for j in range(4):
                            kc = kc2 * 4 + j
                            S.op("pe", MM(ps[b][:, j * 128:(j + 1) * 128], t_[:, kc * 128:(kc + 1) * 128], identf[:],
                                          j == 0, True), reads=[r_], writes=[PS(b)])
                        for j in range(4):
                            kc = kc2 * 4 + j
                            S.op("act" if j % 2 else "dve",
                                 (ACT(xT[:, kc, tt * 128:(tt + 1) * 128], ps[b][:, j * 128:(j + 1) * 128], AF.Copy) if j % 2
                                  else CP(xT[:, kc, tt * 128:(tt + 1) * 128], ps[b][:, j * 128:(j + 1) * 128])),
                                 reads=[PS(b)], writes=["xT"])
                S.barrier()

        def norm_mod(path, gcols, shfn, out_dram=None):
            for (t0, n, si) in path.blocks:
                b = nb()
                for kc in range(8):
                    sq, sqr = PP.sq.get()
                    S.op("dve", TT(sq[:, :n], xT[:, kc, t0:t0 + n], xT[:, kc, t0:t0 + n], ALU.mult), reads=["xT"], writes=[sqr])
                    S.op("pe", MM(ps[b][:, :n], onesf[:], sq[:, :n], kc == 0, kc == 7), reads=[sqr], writes=[PS(b)])
                ln_, lr = PP.ln.get()
                S.op("act", ACT(ln_[:, :n], ps[b][:, :n], AF.Ln, bias=epsb[:, 0:1], scale=1.0 / D), reads=[PS(b)], writes=[lr])
                rs_, rr = PP.rs.get()
                S.op("act", ACT(rs_[:, :n], ln_[:, :n], AF.Exp, scale=-0.5), reads=[lr], writes=[rr])
                hc = path.hcol(t0, si)
                for kc in range(8):
                    if out_dram is None:
                        tm_, tr = PP.tm.get()
                        S.op("dve", STT(tm_[:, :n], xT[:, kc, t0:t0 + n], gcols[:, kc:kc + 1], rs_[:, :n], ALU.mult, ALU.mult),
                             reads=["xT", rr, "gsh"], writes=[tr])
                        S.op("dve", TS(hT[:, kc, hc:hc + n], tm_[:, :n], shfn(kc), None, ALU.add),
                             reads=[tr, "modT"], writes=["hT"])
                    else:
                        S.op("dve", STT(xT[:, kc, t0:t0 + n], xT[:, kc, t0:t0 + n], gcols[:, kc:kc + 1], rs_[:, :n],
                                        ALU.mult, ALU.mult), reads=["xT", rr], writes=["xT"])

        def headnorm(psr, pss, n, g, gs, ones_mat, nfeat, ropeC, ropeS, outs, roperes=None, prow=slice(0, 128)):
            sq, sqr = PP.sq.get()
            S.op("act", ACT(sq[prow, :n], ps[psr][prow, :n], AF.Square), reads=[PS(psr)], writes=[sqr])
            b3 = nb()
            S.op("pe", MM(ps[b3][prow, :n], ones_mat, sq[prow, :n], True, True), reads=[sqr], writes=[PS(b3)])
            ln_, lr = PP.ln.get()
            S.op("act", ACT(ln_[prow, :n], ps[b3][prow, :n], AF.Ln, bias=epsb[prow, 0:1], scale=1.0 / nfeat),
                 reads=[PS(b3)], writes=[lr])
            rs_, rr = PP.rs.get()
            S.op("act", ACT(rs_[prow, :n], ln_[prow, :n], AF.Exp, scale=-0.5), reads=[lr], writes=[rr])
            t1, r1 = PP.tm.get()
            S.op("dve", STT(t1[prow, :n], ps[psr][prow, :n], g, rs_[prow, :n], ALU.mult, ALU.mult),
                 reads=[PS(psr), rr], writes=[r1])
            if pss is not None:
                t2, r2 = PP.tm.get()
                S.op("dve", STT(t2[prow, :n], ps[pss][prow, :n], gs, rs_[prow, :n], ALU.mult, ALU.mult),
                     reads=[PS(pss), rr], writes=[r2])
                S.op("dve", TT(t1[prow, :n], t1[prow, :n], ropeC, ALU.mult), reads=[r1, roperes], writes=[r1])
                S.op("dve", TT(t2[prow, :n], t2[prow, :n], ropeS, ALU.mult), reads=[r2, roperes], writes=[r2])
                for (ap, res) in outs:
                    S.op("dve", TT(ap, t1[prow, :n], t2[prow, :n], ALU.add), reads=[r1, r2], writes=[res])
            else:
                for (ap, res) in outs:
                    S.op("act", ACT(ap, t1[prow, :n], AF.Copy), reads=[r1], writes=[res])
            return t1, r1

        def attend(sc_pools, qap_fn, kap_fn, vap_fn, nkt, kt0, scale, par, chunk, qs, qn, qres, kres, vres):
            if cfg.get("noattn"):
                return
            PTp, rsm, rs0, otm = sc_pools
            bo = nb()
            pend = []

            def pv(item):
                kt, pt, pr = item
                S.op("pe", MM(ps[bo][:, :qn], vap_fn(kt0 + kt), pt[:, :qn], kt == 0, kt == nkt - 1),
                     reads=[pr] + vres, writes=[PS(bo)])
            for kt in range(nkt):
                bs = nb((bo,))
                S.op("pe", MM(ps[bs][:, :qn], kap_fn(kt0 + kt), qap_fn(), True, True), reads=qres + kres, writes=[PS(bs)])
                pt, pr = PTp.get()
                S.op("act", ACT(pt[:, :qn], ps[bs][:, :qn], AF.Exp, scale=scale), reads=[PS(bs)], writes=[pr])
                pend.append((kt, pt, pr))
                if len(pend) > 3:
                    pv(pend.pop(0))
            while pend:
                pv(pend.pop(0))
            r_, rr = rsm.get()
            S.op("dve", lambda e: e.reciprocal(out=r_[64:128, :qn], in_=ps[bo][64:128, :qn]), reads=[PS(bo)], writes=[rr])
            r0, r0r = rs0.get()
            S.op("act", ACT(r0[0:64, :qn], r_[64:128, :qn], AF.Copy), reads=[rr], writes=[r0r])
            if par == 0:
                S.op("dve", TT(oT[0:64, chunk, qs:qs + qn], ps[bo][0:64, :qn], r0[0:64, :qn], ALU.mult),
                     reads=[PS(bo), r0r], writes=["oT"])
            else:
                ot, otr = otm.get()
                S.op("dve", TT(ot[0:64, :qn], ps[bo][0:64, :qn], r0[0:64, :qn], ALU.mult), reads=[PS(bo), r0r], writes=[otr])
                S.op("act", ACT(oT[64:128, chunk, qs:qs + qn], ot[0:64, :qn], AF.Copy), reads=[otr], writes=["oT"])

        def rope_load(sc_pool, tabc, tabs, t0, n, prow=slice(0, 128)):
            rc, rcr = sc_pool.get()
            S.op("sp", DMA(rc[prow, 0, :n], Din[tabc][prow, t0:t0 + n]), writes=[rcr], dma=True)
            S.op("sp", DMA(rc[prow, 1, :n], Din[tabs][prow, t0:t0 + n]), writes=[rcr], dma=True)
            return rc, rcr

        def out_T(src_ap_fn, nrow, prow0, t0, n, dst_fn, res, pool32):
            for j in range(n // 128):
                b = nb()
                S.op("pe", MM(ps[b][:, 0:nrow], src_ap_fn(j), identf[prow0:prow0 + nrow, prow0:prow0 + nrow], True, True),
                     reads=res, writes=[PS(b)])
                o_, orr = pool32.get()
                S.op("dve", CP(o_[:, 0:nrow], ps[b][:, 0:nrow]), reads=[PS(b)], writes=[orr])
                S.op("sp", DMA(dst_fn(j), o_[:, 0:nrow]), reads=[orr], dma=True)

        def gqa(path, l):
            T, koff, smp = path.T, path.koff, path.sample
            nkt_all = (koff + T) // 128
            with ExitStack() as sc:
                mkpools(sc)
                qT = sbt(sc, "qT", [128, 4, T], BF16)
                kT = sbt(sc, "kT", [128, 2, koff + T], BF16)
                Va = sbt(sc, "Va", [128, nkt_all, 2, 128], BF16)
                wch = Pool(sc, "wch", 4, [128, 8, 128], BF16)
                wv = sbt(sc, "wv", [128, 8, 128], BF16)
                ropep = Pool(sc, "rp", 2, [128, 2, 512], F32)
                PTp = Pool(sc, "PT", 6, [128, 512], BF16)
                rsm = Pool(sc, "rsm", 1, [128, 512], F32)
                rs0 = Pool(sc, "rs0", 1, [128, 512], F32)
                otm = Pool(sc, "otm", 1, [128, 512], BF16)
                o32 = Pool(sc, "o32", 2, [128, 128], F32)
                k32 = Pool(sc, "k32", 2, [128, 512], F32)
                S.op("dve", MS(Va[:], 1.0), writes=["Va"])
                S.op("pool", DMA(wv[:], Din["win"][l][:, :, WIN_V:WIN_V + 128]), writes=["wv"], dma=True)
                if smp:
                    ckt = sbt(sc, "ckt", [128, 2, 256], BF16)
                    for tl in range(2):
                        S.op("pool", DMA(ckt[:, tl, :], Din["ckd"][l][tl * 128:(tl + 1) * 128, :]), writes=["ckt"], dma=True)
                        S.op("pool", DMA(Va[:, tl, :, 0:64],
                                         Din["cv"][l][tl * 128:(tl + 1) * 128, :].rearrange("p (g d) -> p g d", g=2)),
                             reads=["Va"], writes=["Va"], dma=True)
                    for tl in range(2):
                        for g in range(2):
                            b = nb()
                            S.op("pe", MM(ps[b][:, 0:128], ckt[:, tl, g * 128:(g + 1) * 128], identb[:], True, True),
                                 reads=["ckt"], writes=[PS(b)])
                            S.op("act", ACT(kT[:, g, tl * 128:(tl + 1) * 128], ps[b][:, 0:128], AF.Copy), reads=[PS(b)],
                                 writes=["kT"])

                def proj_chunk(src, c0, srcs, c0s, gname, gsname, t0, n, si, outs):
                    hc = path.hcol(t0, si)
                    w1_, w1r = src
                    b1 = nb()
                    for kc in range(8):
                        S.op("pe", MM(ps[b1][:, :n], w1_[:, kc, :], hT[:, kc, hc:hc + n], kc == 0, kc == 7),
                             reads=[w1r, "hT"], writes=[PS(b1)])
                    b2 = None
                    rc = rcr = None
                    if smp:
                        w2_, w2r = srcs
                        b2 = nb()
                        for kc in range(8):
                            S.op("pe", MM(ps[b2][:, :n], w2_[:, kc, :], hT[:, kc, hc:hc + n], kc == 0, kc == 7),
                                 reads=[w2r, "hT"], writes=[PS(b2)])
                        rc, rcr = rope_load(ropep, "ropeAc", "ropeAs", t0, n)
                    headnorm(b1, b2, n, V(l, gname), V(l, gsname), bd64[:], 64,
                             rc[:, 0, :n] if smp else None, rc[:, 1, :n] if smp else None, outs, roperes=rcr)

                def wload(name, c0):
                    w_, wr_ = wch.get()
                    S.op("pool", DMA(w_[:], Din[name][l][:, :, c0:c0 + 128]), writes=[wr_], dma=True)
                    return (w_, wr_)

                for mi in range(4):
                    w1 = wload("win", WIN_Q + mi * 128)
                    w2 = wload("winx", mi * 128) if smp else None
                    for (t0, n, si) in path.blocks:
                        proj_chunk(w1, 0, w2, 0, "qg", "qgs", t0, n, si, [(qT[:, mi, t0:t0 + n], "qT")])
                kfs = {}
                for g in range(2):
                    w1 = wload("winx", 512 + g * 128)
                    w2 = wload("winx", 768 + g * 128) if smp else None
                    for bi_, (t0, n, si) in enumerate(path.blocks):
                        outs = [(kT[:, g, koff + t0:koff + t0 + n], "kT")]
                        if not smp:
                            k3, k3r = k32.get()
                            outs.append((k3[:, :n], k3r))
                        proj_chunk(w1, 0, w2, 0, "kg", "kgs", t0, n, si, outs)
                        if not smp:
                            for j in range(n // 128):
                                b = nb()
                                S.op("pe", MM(ps[b][:, 0:64], k3[0:64, j * 128:(j + 1) * 128], identf[0:64, 0:64], True, True),
                                     reads=[k3r], writes=[PS(b)])
                                o_, orr = o32.get()
                                S.op("dve", CP(o_[:, 0:64], ps[b][:, 0:64]), reads=[PS(b)], writes=[orr])
                                tl0 = t0 - path.seqs[si][0] + j * 128
                                S.op("sp", DMA(O["nk"][si, l, tl0:tl0 + 128, g * 64:(g + 1) * 64], o_[:, 0:64]), reads=[orr], dma=True)
                for (t0, n, si) in path.blocks:
                    hc = path.hcol(t0, si)
                    for j in range(n // 128):
                        b = nb()
                        for kc in range(8):
                            S.op("pe", MM(ps[b][:, 0:128], hT[:, kc, hc + j * 128:hc + (j + 1) * 128], wv[:, kc, :], kc == 0, kc == 7),
                                 reads=["wv", "hT"], writes=[PS(b)])
                        kt = (koff + t0) // 128 + j
                        for g in range(2):
                            S.op("act" if g else "dve",
                                 ACT(Va[:, kt, g, 0:64], ps[b][:, g * 64:(g + 1) * 64], AF.Copy) if g
                                 else CP(Va[:, kt, g, 0:64], ps[b][:, g * 64:(g + 1) * 64]),
                                 reads=[PS(b), "Va"], writes=["Va"])
                        if not smp:
                            o_, orr = o32.get()
                            S.op("dve", CP(o_[:], ps[b][:, 0:128]), reads=[PS(b)], writes=[orr])
                            tl0 = t0 - path.seqs[si][0] + j * 128
                            S.op("sp", DMA(O["nv"][si, l, tl0:tl0 + 128, :], o_[:]), reads=[orr], dma=True)
                for si, (s0, L) in enumerate(path.seqs):
                    if smp:
                        kt0, nkt = 0, (koff + L) // 128
                    else:
                        kt0, nkt = s0 // 128, L // 128
                    for h in range(8):
                        g, par, chunk = h // 4, h % 2, h // 2
                        pr = slice(par * 64, par * 64 + 64)
                        for qs in range(s0, s0 + L, 512):
                            qn = min(512, s0 + L - qs)
                            attend((PTp, rsm, rs0, otm),
                                   lambda: qT[pr, chunk, qs:qs + qn],
                                   lambda kt: kT[pr, g, kt * 128:(kt + 1) * 128],
                                   lambda kt: Va[:, kt, g, :],
                                   nkt, kt0, 64 ** -0.5, par, chunk, qs, qn, ["qT"], ["kT"], ["Va"])
                S.barrier()

        def mla(path, l):
            T, koff, smp = path.T, path.koff, path.sample
            nkt_all = (koff + T) // 128
            with ExitStack() as sc:
                mkpools(sc)
                cqT = sbt(sc, "cqT", [128, 3, T], BF16)
                ckvT = sbt(sc, "ckvT", [128, 2, koff + T], BF16)
                KhT = sbt(sc, "KhT", [128, koff + T], BF16)
                Vh = sbt(sc, "Vh", [128, nkt_all, 128], BF16)
                QhT = Pool(sc, "QhT", 2, [128, 512], BF16)
                wcq = sbt(sc, "wcq", [128, 8, 384], BF16)
                wckv = sbt(sc, "wckv", [128, 8, 256], BF16)
                wkpe = sbt(sc, "wkpe", [128, 8, 64], BF16)
                wuq = sbt(sc, "wuq", [128, 3, 768], BF16)
                wuqx = sbt(sc, "wuqx", [128, 3, 288], BF16)
                S.op("dve", MS(wuqx[:], 0.0), writes=["wuqx"])
                wukv = sbt(sc, "wukv", [128, 2, 1024], BF16)
                ropep = Pool(sc, "rpb", 1, [128, 2, 512], F32)
                PTp = Pool(sc, "PTb", 6, [128, 512], BF16)
                rsm = Pool(sc, "rsmb", 1, [128, 512], F32)
                rs0 = Pool(sc, "rs0b", 1, [128, 512], F32)
                otm = Pool(sc, "otmb", 1, [128, 512], BF16)
                o32 = Pool(sc, "o32b", 2, [128, 256], F32)
                c32 = Pool(sc, "c32", 3, [128, 512], F32) if not smp else None
                S.op("dve", MS(Vh[:], 1.0), writes=["Vh"])
                S.op("pool", DMA(wcq[:], Din["win"][l][:, :, WIN_CQ:WIN_CQ + 384]), writes=["wcq"], dma=True)
                S.op("pool", DMA(wckv[:], Din["win"][l][:, :, WIN_CKV:WIN_CKV + 256]), writes=["wckv"], dma=True)
                S.op("pool", DMA(wkpe[:, :, 0:32], Din["win"][l][:, :, WIN_KPE:WIN_KPE + 32]), writes=["wkpe"], dma=True)
                S.op("pool", DMA(wkpe[:, :, 32:64], Din["winx"][l][:, :, 1024:1056]), writes=["wkpe"], dma=True)
                if cfg.get("mla_stage", 3) >= 3:
                    for kc in range(3):
                        S.op("pool", DMA(wuq[:, kc, :], Din["wuq"][l][:, kc, :]), writes=["wuq"], dma=True)
                        S.op("pool", DMA(wuqx[:, kc, 0:256], Din["wuqx"][l][:, kc, :]), reads=["wuqx"], writes=["wuqx"], dma=True)
                    for kc in range(2):
                        S.op("pool", DMA(wukv[:, kc, :], Din["wukv"][l][:, kc, :]), writes=["wukv"], dma=True)
                if smp:
                    cct = sbt(sc, "cct", [128, 2, 256], BF16)
                    cpt = sbt(sc, "cpt", [128, 2, 64], BF16)
                    S.op("dve", MS(cpt[:], 0.0), writes=["cpt"])
                    for tl in range(2):
                        S.op("pool", DMA(cct[:, tl, :], Din["cckv"][l][tl * 128:(tl + 1) * 128, :]), writes=["cct"], dma=True)
                        S.op("pool", DMA(cpt[:, tl, 0:32], Din["ckpe"][l][tl * 128:(tl + 1) * 128, :]), reads=["cpt"], writes=["cpt"], dma=True)
                    for tl in range(2):
                        for j in range(2):
                            b = nb()
                            S.op("pe", MM(ps[b][:, 0:128], cct[:, tl, j * 128:(j + 1) * 128], identb[:], True, True),
                                 reads=["cct"], writes=[PS(b)])
                            S.op("act", ACT(ckvT[:, j, tl * 128:(tl + 1) * 128], ps[b][:, 0:128], AF.Copy), reads=[PS(b)],
                                 writes=["ckvT"])
                        b = nb()
                        S.op("pe", MM(ps[b][0:64, 0:128], cpt[:, tl, :], identb[:], True, True), reads=["cpt"], writes=[PS(b)])
                        tq, tqr = PP.tm.get()
                        S.op("dve", CP(tq[0:32, 0:128], ps[b][0:32, 0:128]), reads=[PS(b)], writes=[tqr])
                        S.op("act", ACT(KhT[64:96, tl * 128:(tl + 1) * 128], tq[0:32, 0:128], AF.Copy), reads=[tqr],
                             writes=["KhTpe"])
                for (t0, n, si) in path.blocks:
                    hc = path.hcol(t0, si)
                    bs = []
                    for j in range(3):
                        b = nb()
                        bs.append(b)
                        for kc in range(8):
                            S.op("pe", MM(ps[b][:, :n], wcq[:, kc, j * 128:(j + 1) * 128], hT[:, kc, hc:hc + n], kc == 0, kc == 7),
                                 reads=["wcq", "hT"], writes=[PS(b)])
                    b3 = nb()
                    for j in range(3):
                        sq, sqr = PP.sq.get()
                        S.op("act", ACT(sq[:, :n], ps[bs[j]][:, :n], AF.Square), reads=[PS(bs[j])], writes=[sqr])
                        S.op("pe", MM(ps[b3][:, :n], onesf[:], sq[:, :n], j == 0, j == 2), reads=[sqr], writes=[PS(b3)])
                    ln_, lr = PP.ln.get()
                    S.op("act", ACT(ln_[:, :n], ps[b3][:, :n], AF.Ln, bias=epsb[:, 0:1], scale=1.0 / 384), reads=[PS(b3)], writes=[lr])
                    rs_, rr = PP.rs.get()
                    S.op("act", ACT(rs_[:, :n], ln_[:, :n], AF.Exp, scale=-0.5), reads=[lr], writes=[rr])
                    for j in range(3):
                        S.op("dve", STT(cqT[:, j, t0:t0 + n], ps[bs[j]][:, :n], V(l, "mqn", j), rs_[:, :n], ALU.mult, ALU.mult),
                             reads=[PS(bs[j]), rr], writes=["cqT"])
                    bs = []
                    for j in range(2):
                        b = nb()
                        bs.append(b)
                        for kc in range(8):
                            S.op("pe", MM(ps[b][:, :n], wckv[:, kc, j * 128:(j + 1) * 128], hT[:, kc, hc:hc + n], kc == 0, kc == 7),
                                 reads=["wckv", "hT"], writes=[PS(b)])
                    b3 = nb()
                    for j in range(2):
                        sq, sqr = PP.sq.get()
                        S.op("act", ACT(sq[:, :n], ps[bs[j]][:, :n], AF.Square), reads=[PS(bs[j])], writes=[sqr])
                        S.op("pe", MM(ps[b3][:, :n], onesf[:], sq[:, :n], j == 0, j == 1), reads=[sqr], writes=[PS(b3)])
                    ln_, lr = PP.ln.get()
                    S.op("act", ACT(ln_[:, :n], ps[b3][:, :n], AF.Ln, bias=epsb[:, 0:1], scale=1.0 / 256), reads=[PS(b3)], writes=[lr])
                    rs_, rr = PP.rs.get()
                    S.op("act", ACT(rs_[:, :n], ln_[:, :n], AF.Exp, scale=-0.5), reads=[lr], writes=[rr])
                    cf = []
                    for j in range(2):
                        if smp:
                            S.op("dve", STT(ckvT[:, j, koff + t0:koff + t0 + n], ps[bs[j]][:, :n], V(l, "mkvn", j), rs_[:, :n],
                                            ALU.mult, ALU.mult), reads=[PS(bs[j]), rr], writes=["ckvT"])
                        else:
                            c3, c3r = c32.get()
                            S.op("dve", STT(c3[:, :n], ps[bs[j]][:, :n], V(l, "mkvn", j), rs_[:, :n], ALU.mult, ALU.mult),
                                 reads=[PS(bs[j]), rr], writes=[c3r])
                            S.op("act", ACT(ckvT[:, j, t0:t0 + n], c3[:, :n], AF.Copy), reads=[c3r], writes=["ckvT"])
                            cf.append((c3, c3r))
                    if not smp:
                        for jj in range(n // 128):
                            b = nb()
                            for j in range(2):
                                c3, c3r = cf[j]
                                S.op("pe", MM(ps[b][:, j * 128:(j + 1) * 128], c3[:, jj * 128:(jj + 1) * 128], identf[:], j == 0, True),
                                     reads=[c3r], writes=[PS(b)])
                            o_, orr = o32.get()
                            S.op("dve", CP(o_[:], ps[b][:, 0:256]), reads=[PS(b)], writes=[orr])
                            tl0 = t0 - path.seqs[si][0] + jj * 128
                            S.op("sp", DMA(O["nckv"][si, l, tl0:tl0 + 128, :], o_[:]), reads=[orr], dma=True)
                    if cfg.get("mla_stage", 3) < 2:
                        continue
                    b = nb()
                    for kc in range(8):
                        S.op("pe", MM(ps[b][0:64, :n], wkpe[:, kc, 0:64], hT[:, kc, hc:hc + n], kc == 0, kc == 7),
                             reads=["wkpe", "hT"], writes=[PS(b)])
                    if smp:
                        rc, rcr = rope_load(ropep, "ropeBc", "ropeBs", t0, n, slice(0, 64))
                        t1, r1 = PP.tm.get()
                        t2, r2 = PP.tm.get()
                        t3, r3 = PP.tm.get()
                        S.op("dve", TT(t1[0:32, :n], ps[b][0:32, :n], rc[0:32, 0, :n], ALU.mult), reads=[PS(b), rcr], writes=[r1])
                        S.op("dve", TT(t2[32:64, :n], ps[b][32:64, :n], rc[32:64, 1, :n], ALU.mult), reads=[PS(b), rcr], writes=[r2])
                        S.op("act", ACT(t3[0:32, :n], t2[32:64, :n], AF.Copy), reads=[r2], writes=[r3])
                        S.op("dve", TT(t1[0:32, :n], t1[0:32, :n], t3[0:32, :n], ALU.add), reads=[r1, r3], writes=[r1])
                        S.op("act", ACT(KhT[64:96, koff + t0:koff + t0 + n], t1[0:32, :n], AF.Copy), reads=[r1], writes=["KhTpe"])
                    else:
                        c3, c3r = c32.get()
                        S.op("dve", CP(c3[0:64, :n], ps[b][0:64, :n]), reads=[PS(b)], writes=[c3r])
                        S.op("act", ACT(KhT[64:96, t0:t0 + n], c3[0:32, :n], AF.Copy), reads=[c3r], writes=["KhTpe"])
                        for jj in range(n // 128 if cfg.get("kpe_out", True) else 0):
                            b4 = nb()
                            S.op("pe", MM(ps[b4][:, 0:32], c3[0:64, jj * 128:(jj + 1) * 128], identf[0:64, 0:32], True, True),
                                 reads=[c3r], writes=[PS(b4)])
                            o_, orr = o32.get()
                            S.op("dve", CP(o_[:, 0:32], ps[b4][:, 0:32]), reads=[PS(b4)], writes=[orr])
                            tl0 = t0 - path.seqs[si][0] + jj * 128
                            S.op("sp", DMA(O["nkpe"][si, l, tl0:tl0 + 128, :], o_[:, 0:32]), reads=[orr], dma=True)
                ktot = koff + T
                for h in range(8 if cfg.get("mla_stage", 3) >= 3 else 0):
                    par, chunk = h % 2, h // 2
                    for k0 in range(0, ktot, 512):
                        kn = min(512, ktot - k0)
                        b = nb()
                        for kc in range(2):
                            S.op("pe", MM(ps[b][0:64, :kn], wukv[:, kc, h * 128:h * 128 + 64], ckvT[:, kc, k0:k0 + kn], kc == 0, kc == 1),
                                 reads=["wukv", "ckvT"], writes=[PS(b)])
                        S.op("act", ACT(KhT[0:64, k0:k0 + kn], ps[b][0:64, :kn], AF.Copy), reads=[PS(b)], writes=["KhTn"])
                    for kt in range(ktot // 128):
                        b = nb()
                        for kc in range(2):
                            S.op("pe", MM(ps[b][:, 0:64], ckvT[:, kc, kt * 128:(kt + 1) * 128], wukv[:, kc, h * 128 + 64:h * 128 + 128],
                                          kc == 0, kc == 1), reads=["wukv", "ckvT"], writes=[PS(b)])
                        S.op("dve", CP(Vh[:, kt, 0:64], ps[b][:, 0:64]), reads=[PS(b), "Vh"], writes=["Vh"])
                    for si, (s0, L) in enumerate(path.seqs):
                        if smp:
                            kt0, nkt = 0, (koff + L) // 128
                        else:
                            kt0, nkt = s0 // 128, L // 128
                        for qs in range(s0, s0 + L, 512):
                            qn = min(512, s0 + L - qs)
                            b = nb()
                            for kc in range(3):
                                S.op("pe", MM(ps[b][0:96, :qn], wuq[:, kc, h * 96:(h + 1) * 96], cqT[:, kc, qs:qs + qn], kc == 0, kc == 2),
                                     reads=["wuq", "cqT"], writes=[PS(b)])
                            qh, qhr = QhT.get()
                            S.op("act", ACT(qh[0:64, :qn], ps[b][0:64, :qn], AF.Copy), reads=[PS(b)], writes=[(qhr, 0)])
                            if smp:
                                b2 = nb()
                                for kc in range(3):
                                    S.op("pe", MM(ps[b2][0:64, :qn], wuqx[:, kc, h * 32:h * 32 + 64], cqT[:, kc, qs:qs + qn],
                                                  kc == 0, kc == 2), reads=["wuqx", "cqT"], writes=[PS(b2)])
                                rc, rcr = rope_load(ropep, "ropeBc", "ropeBs", qs, qn, slice(0, 96))
                                t1, r1 = PP.tm.get()
                                t2, r2 = PP.tm.get()
                                t3, r3 = PP.tm.get()
                                S.op("dve", TT(t1[64:96, :qn], ps[b][64:96, :qn], rc[64:96, 0, :qn], ALU.mult), reads=[PS(b), rcr], writes=[r1])
                                S.op("dve", TT(t2[0:32, :qn], ps[b2][0:32, :qn], rc[0:32, 1, :qn], ALU.mult), reads=[PS(b2), rcr], writes=[r2])
                                S.op("act", ACT(t3[64:96, :qn], t2[0:32, :qn], AF.Copy), reads=[r2], writes=[r3])
                                S.op("dve", TT(qh[64:96, :qn], t1[64:96, :qn], t3[64:96, :qn], ALU.add), reads=[r1, r3], writes=[(qhr, 1)])
                            else:
                                S.op("dve", CP(qh[64:96, :qn], ps[b][64:96, :qn]), reads=[PS(b)], writes=[(qhr, 1)])
                            attend((PTp, rsm, rs0, otm),
                                   lambda: qh[0:96, :qn],
                                   lambda kt: KhT[0:96, kt * 128:(kt + 1) * 128],
                                   lambda kt: Vh[:, kt, :],
                                   nkt, kt0, 96 ** -0.5, par, chunk, qs, qn, [(qhr, 0), (qhr, 1)], ["KhTn", "KhTpe"], ["Vh"])
                S.barrier()

        def merge(path, l, n_br):
            if cfg.get("nomerge"):
                return
            with ExitStack() as sc:
                wb = sbt(sc, "wb", [128, 4, 1024], BF16)
                wg = sbt(sc, "wg", [128, 8, 1024], BF16)
                wo = sbt(sc, "wo", [128, 8, 1024], BF16)
                mgp = Pool(sc, "mg", 2, [128, 8, 512], BF16)
                sgp = Pool(sc, "sg", 2, [128, 512], F32)
                for mc in range(8):
                    cs_ = slice(mc * 128, (mc + 1) * 128)
                    S.op("pool", DMA(wb[:, :, cs_], Din["wbr"][l, n_br][:, :, cs_]), writes=[("wb", mc)], dma=True)
                    g0 = WIN_G + n_br * 1024 + mc * 128
                    S.op("pool", DMA(wg[:, :, cs_], Din["win"][l][:, :, g0:g0 + 128]), writes=[("wg", mc)], dma=True)
                for mc in range(8):
                    cs_ = slice(mc * 128, (mc + 1) * 128)
                    S.op("pool", DMA(wo[:, :, cs_], Din["wout"][l][:, :, cs_]), writes=[("wo", mc)], dma=True)
                for (t0, n, si) in path.blocks:
                    hc = path.hcol(t0, si)
                    mg, mgr = mgp.get()
                    for mc in range(8):
                        bB = nb()
                        for kc in range(4):
                            S.op("pe", MM(ps[bB][:, :n], wb[:, kc, mc * 128:(mc + 1) * 128], oT[:, kc, t0:t0 + n], kc == 0, kc == 3),
                                 reads=[("wb", mc), "oT"], writes=[PS(bB)])
                        bG = nb()
                        for kc in range(8):
                            S.op("pe", MM(ps[bG][:, :n], wg[:, kc, mc * 128:(mc + 1) * 128], hT[:, kc, hc:hc + n], kc == 0, kc == 7),
                                 reads=[("wg", mc), "hT"], writes=[PS(bG)])
                        sg, sgr = sgp.get()
                        S.op("act", ACT(sg[:, :n], ps[bG][:, :n], AF.Sigmoid), reads=[PS(bG)], writes=[sgr])
                        S.op("dve", TT(mg[:, mc, :n], ps[bB][:, :n], sg[:, :n], ALU.mult), reads=[PS(bB), sgr], writes=[mgr])
                    for mo in range(8):
                        b = nb()
                        for kc in range(8):
                            S.op("pe", MM(ps[b][:, :n], wo[:, kc, mo * 128:(mo + 1) * 128], mg[:, kc, :n], kc == 0, kc == 7),
                                 reads=[("wo", mo), mgr], writes=[PS(b)])
                        S.op("dve", STT(xT[:, mo, t0:t0 + n], ps[b][:, :n], MOD(l, 2, mo, path.ccol), xT[:, mo, t0:t0 + n],
                                        ALU.mult, ALU.add), reads=[PS(b), "xT"], writes=["xT"])
                S.barrier()

        def sin_quarter(pool4, psb, n, sc_ap, b_ap, bc_ap, out_ap, out_res):
            s4, s4r = pool4.get()
            c4, c4r = pool4.get()
            S.op("act", ACT(s4[0:64, :n], ps[psb][0:64, :n], AF.Sin, bias=b_ap, scale=sc_ap), reads=[PS(psb), "hyd"], writes=[s4r])
            S.op("act", ACT(c4[0:64, :n], ps[psb][0:64, :n], AF.Sin, bias=bc_ap, scale=sc_ap), reads=[PS(psb), "hyd"], writes=[c4r])
            S.op("dve", TT(c4[0:64, :n], s4[0:64, :n], c4[0:64, :n], ALU.mult), reads=[s4r, c4r], writes=[c4r])
            S.op("dve", TT(s4[0:64, :n], s4[0:64, :n], s4[0:64, :n], ALU.mult), reads=[s4r], writes=[s4r])
            S.op("dve", TS(s4[0:64, :n], s4[0:64, :n], -2.0, 1.0, ALU.mult, ALU.add), reads=[s4r], writes=[s4r])
            S.op("dve", STT(out_ap, c4[0:64, :n], 4.0, s4[0:64, :n], ALU.mult, ALU.mult), reads=[s4r, c4r], writes=[out_res])

        def hyena(path, l):
            T, L = path.T, path.L
            NT = L // 128
            pfx = "hy%d_" % L
            with ExitStack() as sc:
                h2T = sbt(sc, "h2T", [64, L], F32)
                with ExitStack() as sc2:
                    h1p = Pool(sc2, "h1p", 2, [64, 512], F32)
                    p4 = Pool(sc2, "p4", 4, [64, 512], F32)
                    zTt = sbt(sc2, "zTt", [64, L], F32)
                    w1t = sbt(sc2, "w1t", [64, 64], F32)
                    w2t = sbt(sc2, "w2t", [64, 64], F32)
                    hyv = sbt(sc2, "hyv", [64, 4], F32)
                    hyd = sbt(sc2, "hyd", [64, 6], F32)
                    S.op("dve", MS(zTt[:], 0.0), writes=["zTt"])
                    S.op("dve", MS(w1t[:], 0.0), writes=["w1t"])
                    S.op("sp", DMA(zTt[0:17, :], Din[pfx + "zT"]), reads=["zTt"], writes=["zTt"], dma=True)
                    S.op("sp", DMA(w1t[0:17, :], Din["hyw1"][l]), reads=["w1t"], writes=["w1t"], dma=True)
                    S.op("sp", DMA(w2t[:], Din["hyw2"][l]), writes=["w2t"], dma=True)
                    S.op("sp", DMA(hyv[:], Din["hyv"][l]), writes=["hyv"], dma=True)
                    for i in range(2):
                        S.op("dve", TS(hyd[:, 3 * i:3 * i + 1], hyv[:, 2 + i:3 + i], 0.25, None, ALU.mult), reads=["hyv"], writes=["hyd"])
                        S.op("dve", TT(hyd[:, 3 * i + 1:3 * i + 2], hyd[:, 3 * i:3 * i + 1], hyv[:, i:i + 1], ALU.mult), reads=["hyd", "hyv"],
                             writes=["hyd"])
                        S.op("dve", TS(hyd[:, 3 * i + 2:3 * i + 3], hyd[:, 3 * i + 1:3 * i + 2], math.pi / 2, None, ALU.add), reads=["hyd"],
                             writes=["hyd"])
                    for c0 in range(0, L, 512):
                        n = min(512, L - c0)
                        b = nb()
                        S.op("pe", MM(ps[b][0:64, :n], w1t[:, :], zTt[:, c0:c0 + n], True, True), reads=["w1t", "zTt"], writes=[PS(b)])
                        h1, h1r = h1p.get()
                        sin_quarter(p4, b, n, hyd[:, 0:1], hyd[:, 1:2], hyd[:, 2:3], h1[:, :n], h1r)
                        b = nb()
                        S.op("pe", MM(ps[b][0:64, :n], w2t[:, :], h1[:, :n], True, True), reads=["w2t", h1r], writes=[PS(b)])
                        sin_quarter(p4, b, n, hyd[:, 3:4], hyd[:, 4:5], hyd[:, 5:6], h2T[:, c0:c0 + n], "h2T")
                    S.barrier()
                wh = sbt(sc, "wh", [128, 8, 384], BF16)
                vfm = sbt(sc, "vfm", [128, 3, T], BF16)
                vtm = sbt(sc, "vtm", [128, T // 128, 128], BF16)
                zA = sbt(sc, "zA", [128, T // 128, 128], BF16)
                z1T = sbt(sc, "z1T", [128, T], BF16)
                sd = sbt(sc, "sd", [128, NT, 3, 128], BF16)
                Y = sbt(sc, "Y", [128, NT, 2, 128], BF16)
                slc = Pool(sc, "slc", 2, [128, NT * 128], BF16)
                sls = Pool(sc, "sls", 2, [128, NT * 128], BF16)
                hbias = sbt(sc, "hbias", [128, 2], F32)
                gtp = Pool(sc, "gtp", 1, [128, 512], F32)
                w3t = sbt(sc, "w3t", [64, 256], F32)
                cat = sbt(sc, "cat", [128, NT], F32)
                sat = sbt(sc, "sat", [128, NT], F32)
                winp = Pool(sc, "winp", 2, [128, 128], F32)
                hwp = Pool(sc, "hwp", 2, [128, 256], F32)
                abp = Pool(sc, "abp", 2, [128, 256], F32)
                rl1 = sbt(sc, "rl1", [128, 128], F32)
                l1t = sbt(sc, "l1t", [128, 128], F32)
                ut = Pool(sc, "ut", 1, [128, 512], F32)
                kk = Pool(sc, "kk", 8, [128, 128], F32)
                S.op("sp", DMA(cat[:], Din[pfx + "ca"]), writes=["cat"], dma=True)
                S.op("sp", DMA(sat[:], Din[pfx + "sa"]), writes=["sat"], dma=True)
                for q4 in range(4):
                    for w in range(3):
                        c0 = WIN_HY + w * 512 + q4 * 128
                        S.op("pool", DMA(wh[:, :, w * 128:(w + 1) * 128], Din["win"][l][:, :, c0:c0 + 128]), reads=["wh"], writes=["wh"], dma=True)
                    for si, (s0, Ls) in enumerate(path.seqs):
                        for o0 in range(0, Ls, 384):
                            on = min(384, Ls - o0)
                            hc = path.hcol(s0 + o0, si) - 1
                            for w in range(3):
                                ch = w * 4 + q4
                                b = nb()
                                for kc in range(8):
                                    S.op("pe", MM(ps[b][:, :on + 2], wh[:, kc, w * 128:(w + 1) * 128], hT[:, kc, hc:hc + on + 2], kc == 0, kc == 7),
                                         reads=["wh", "hT"], writes=[PS(b)])
                                u, ur = ut.get()
                                S.op("dve", TS(u[:, :on], ps[b][:, 0:on], V(l, "hsw", ch * 3 + 0), V(l, "hsb", ch), ALU.mult, ALU.add),
                                     reads=[PS(b)], writes=[ur])
                                S.op("dve", STT(u[:, :on], ps[b][:, 1:on + 1], V(l, "hsw", ch * 3 + 1), u[:, :on], ALU.mult, ALU.add),
                                     reads=[PS(b), ur], writes=[ur])
                                S.op("dve", STT(u[:, :on], ps[b][:, 2:on + 2], V(l, "hsw", ch * 3 + 2), u[:, :on], ALU.mult, ALU.add),
                                     reads=[PS(b), ur], writes=[ur])
                                S.op("act", ACT(vfm[:, w, s0 + o0:s0 + o0 + on], u[:, :on], AF.Copy), reads=[ur, "vfm"], writes=["vfm"])
                                for j in range(on // 128 if w == 0 else 0):
                                    b2 = nb()
                                    S.op("pe", MM(ps[b2][:, 0:128], u[:, j * 128:(j + 1) * 128], identf[:], True, True), reads=[ur], writes=[PS(b2)])
                                    tt = (s0 + o0) // 128 + j
                                    S.op("dve", CP(vtm[:, tt, :], ps[b2][:, 0:128]), reads=[PS(b2), "vtm"], writes=["vtm"])
                    for o in range(2):
                        cf = o * 512 + q4 * 128
                        S.op("sp", DMA(w3t[:, 0:128], Din["hyw3"][l][:, cf:cf + 128]), reads=["w3t"], writes=["w3t"], dma=True)
                        S.op("sp", DMA(w3t[:, 128:256], Din["hyw3"][l][:, 1024 + cf:1024 + cf + 128]), reads=["w3t"], writes=["w3t"], dma=True)
                        if o == 0:
                            S.op("sp", DMA(hbias[:], Din["hybiasT"][l, q4]), reads=["hbias"], writes=["hbias"], dma=True)
                        bl = nb()
                        for tt in range(NT):
                            b = nb((bl,))
                            S.op("pe", MM(ps[b][:, 0:256], h2T[:, tt * 128:(tt + 1) * 128], w3t[:, :], True, True), reads=["h2T", "w3t"], writes=[PS(b)])
                            wt_, wr_ = winp.get()
                            S.op("pool", DMA(wt_[:], Din[pfx + "win"][tt * 128:(tt + 1) * 128, q4 * 128:(q4 + 1) * 128]), writes=[wr_], dma=True)
                            hw, hwr = hwp.get()
                            S.op("dve", TT(hw[:, 0:128], ps[b][:, 0:128], wt_[:], ALU.mult), reads=[PS(b), wr_], writes=[hwr])
                            S.op("dve", TT(hw[:, 128:256], ps[b][:, 128:256], wt_[:], ALU.mult), reads=[PS(b), wr_, hwr], writes=[hwr])
                            if tt == 0:
                                S.op("dve", MS(hw[0:1, 128:256], 0.0), reads=[hwr], writes=[hwr])
                            ab, abr = abp.get()
                            S.op("act", ACT(ab[:], hw[:], AF.Abs), reads=[hwr], writes=[abr])
                            S.op("pe", MM(ps[bl][:, 0:256], onesf[:], ab[:], tt == 0, tt == NT - 1), reads=[abr], writes=[PS(bl)])
                            S.op("dve", TT(sd[:, tt, 0, :], hw[:, 0:128], hw[:, 128:256], ALU.add), reads=[hwr, "sd"], writes=["sd"])
                            S.op("dve", TT(sd[:, tt, 1, :], hw[:, 0:128], hw[:, 128:256], ALU.subtract), reads=[hwr, "sd"], writes=["sd"])
                        S.op("act", ACT(l1t[:], ps[bl][:, 128:256], AF.Copy), reads=[PS(bl)], writes=["l1t"])
                        S.op("dve", STT(l1t[:], ps[bl][:, 0:128], EPS, l1t[:], ALU.add, ALU.add), reads=[PS(bl), "l1t"], writes=["l1t"])
                        S.op("dve", lambda e: e.reciprocal(out=rl1[:], in_=l1t[:]), reads=["l1t"], writes=["rl1"])
                        S.op("dve", TS(rl1[:], rl1[:], 1.0 / L, None, ALU.mult), reads=["rl1"], writes=["rl1"])
                        for si, (s0, Ls) in enumerate(path.seqs):
                            tb = s0 // 128

                            def zin(tt):
                                return vtm[:, tb + tt, :] if o == 0 else zA[:, tb + tt, :]
                            zres = "vtm" if o == 0 else "zA"
                            for tt in range(NT):
                                S.op("pool", CP(sd[:, tt, 2, :], zin(tt)), reads=[zres, "sdz"], writes=["sdz"])
                            for fi in range(NT):
                                if not (cfg.get("hy_nodma") and fi > 0):
                                    ct, cr = slc.get()
                                    st_, sr = sls.get()
                                    S.op("sp", DMA(ct[:], Din[pfx + "dc"][fi]), writes=[cr], dma=True)
                                    S.op("act", DMA(st_[:], Din[pfx + "ds"][fi]), writes=[sr], dma=True)
                                bC, bS = nb(), nb()
                                for tt in range(NT):
                                    f1, lst = tt == 0, tt == NT - 1
                                    S.op("pe", MM(ps[bC][:, 0:384], ct[:, tt * 128:(tt + 1) * 128], sd[:, tt, :, :], f1, lst), reads=[cr, "sd", "sdz"], writes=[PS(bC)])
                                    S.op("pe", MM(ps[bS][:, 0:384], st_[:, tt * 128:(tt + 1) * 128], sd[:, tt, :, :], f1, lst), reads=[sr, "sd", "sdz"], writes=[PS(bS)])
                                ca_, sa_ = cat[:, fi:fi + 1], sat[:, fi:fi + 1]
                                t1, r1 = kk.get()
                                kre, krr = kk.get()
                                t2, r2 = kk.get()
                                kim, kir = kk.get()
                                S.op("dve", TS(t1[:], ps[bC][:, 0:128], ca_, None, ALU.mult), reads=[PS(bC), "cat"], writes=[r1])
                                S.op("dve", STT(kre[:], ps[bS][:, 0:128], sa_, t1[:], ALU.mult, ALU.add), reads=[PS(bS), r1, "sat"], writes=[krr])
                                S.op("dve", TS(t2[:], ps[bS][:, 128:256], ca_, None, ALU.mult), reads=[PS(bS), "cat"], writes=[r2])
                                S.op("dve", STT(kim[:], ps[bC][:, 128:256], sa_, t2[:], ALU.mult, ALU.subtract), reads=[PS(bC), r2, "sat"], writes=[kir])
                                S.op("dve", TT(kre[:], kre[:], rl1[:], ALU.mult), reads=[krr, "rl1"], writes=[krr])
                                S.op("dve", TT(kim[:], kim[:], rl1[:], ALU.mult), reads=[kir, "rl1"], writes=[kir])
                                t3, r3 = kk.get()
                                t4, r4 = kk.get()
                                S.op("dve", TT(t3[:], ps[bC][:, 256:384], kre[:], ALU.mult), reads=[PS(bC), krr], writes=[r3])
                                S.op("dve", TT(t4[:], ps[bS][:, 256:384], kim[:], ALU.mult), reads=[PS(bS), kir], writes=[r4])
                                S.op("dve", TT(Y[:, fi, 0, :], t3[:], t4[:], ALU.add), reads=[r3, r4, "Y"], writes=["Y"])
                                S.op("dve", TT(t3[:], ps[bS][:, 256:384], kre[:], ALU.mult), reads=[PS(bS), krr, r3], writes=[r3])
                                S.op("dve", TT(t4[:], ps[bC][:, 256:384], kim[:], ALU.mult), reads=[PS(bC), kir, r4], writes=[r4])
                                S.op("dve", TT(Y[:, fi, 1, :], t3[:], t4[:], ALU.subtract), reads=[r3, r4, "Y"], writes=["Y"])
                            nb4 = (Ls + 511) // 512
                            acc = []
                            for _ in range(nb4):
                                acc.append(nb(tuple(acc)))
                            for fi in range(NT):
                                if not (cfg.get("hy_nodma") and fi > 0):
                                    ct, cr = slc.get()
                                    st_, sr = sls.get()
                                    S.op("sp", DMA(ct[:], Din[pfx + "rc"][fi]), writes=[cr], dma=True)
                                    S.op("act", DMA(st_[:], Din[pfx + "rs"][fi]), writes=[sr], dma=True)
                                for t4 in range(nb4):
                                    n4 = min(512, Ls - t4 * 512)
                                    S.op("pe", MM(ps[acc[t4]][:, :n4], Y[:, fi, 0, :], ct[:, t4 * 512:t4 * 512 + n4], fi == 0, False),
                                         reads=[cr, "Y"], writes=[PS(acc[t4])])
                                    S.op("pe", MM(ps[acc[t4]][:, :n4], Y[:, fi, 1, :], st_[:, t4 * 512:t4 * 512 + n4], False, fi == NT - 1),
                                         reads=[sr, "Y"], writes=[PS(acc[t4])])
                            for t4 in range(nb4):
                                n4 = min(512, Ls - t4 * 512)
                                tg = s0 + t4 * 512
                                zinT = vfm[:, 0, tg:tg + n4] if o == 0 else z1T[:, tg:tg + n4]
                                zinr = "vfm" if o == 0 else "z1T"
                                g_, ggr = gtp.get()
                                S.op("dve", STT(g_[:, :n4], zinT, hbias[:, o:o + 1], ps[acc[t4]][:, :n4], ALU.mult, ALU.add),
                                     reads=[zinr, "hbias", PS(acc[t4])], writes=[ggr])
                                if o == 0:
                                    S.op("dve", TT(z1T[:, tg:tg + n4], g_[:, :n4], vfm[:, 1, tg:tg + n4], ALU.mult), reads=[ggr, "vfm", "z1T"], writes=["z1T"])
                                else:
                                    S.op("dve", TT(oT[:, q4, tg:tg + n4], g_[:, :n4], vfm[:, 2, tg:tg + n4], ALU.mult), reads=[ggr, "vfm", "oT"], writes=["oT"])
                            if o == 0:
                                for tt in range(NT):
                                    b = nb()
                                    S.op("pe", MM(ps[b][:, 0:128], z1T[:, s0 + tt * 128:s0 + (tt + 1) * 128], identb[:], True, True),
                                         reads=["z1T"], writes=[PS(b)])
                                    S.op("act", ACT(zA[:, tb + tt, :], ps[b][:, 0:128], AF.Copy), reads=[PS(b), "zA"], writes=["zA"])
                S.barrier()

        def ffn(path, l):
            with ExitStack() as sc:
                wup = Pool(sc, "wup", 6, [128, 8, 256], BF16)
                wdn = Pool(sc, "wdn", 4, [128, 22, 128], BF16)
                hid = sbt(sc, "hid", [128, 22, 416], BF16)
                ta = Pool(sc, "ta", 2, [128, 416], F32)
                tg = Pool(sc, "tg", 2, [128, 416], F32)
                sg = Pool(sc, "sgf", 2, [128, 416], F32)
                for si, (s0, Ls) in enumerate(path.seqs):
                    nblk = (Ls + 409) // 410
                    for bi in range(nblk):
                        o0 = bi * 410
                        on = min(410, Ls - o0)
                        hc = path.hcol(s0 + o0, si) - 1
                        for j in range(22):
                            if not (cfg.get("ffn_nodma") and (bi > 0 or j > 1)):
                                wt, wr = wup.get()
                                fr = [("fupb", l, kc) for kc in range(8)]
                                S.op("sp", DMA(wt[:, :, 0:128], fupb[l][:, :, j * 128:(j + 1) * 128]), reads=fr, writes=[wr], dma=True)
                                S.op("act", DMA(wt[:, :, 128:256], fupb[l][:, :, 2816 + j * 128:2816 + (j + 1) * 128]), reads=fr, writes=[wr], dma=True)
                            ba, bg = nb(), nb()
                            for kc in range(8):
                                S.op("pe", MM(ps[ba][:, :on + 2], wt[:, kc, 0:128], hT[:, kc, hc:hc + on + 2], kc == 0, kc == 7),
                                     reads=[wr, "hT"], writes=[PS(ba)])
                            for kc in range(8):
                                S.op("pe", MM(ps[bg][:, :on + 2], wt[:, kc, 128:256], hT[:, kc, hc:hc + on + 2], kc == 0, kc == 7),
                                     reads=[wr, "hT"], writes=[PS(bg)])
                            a_, ar = ta.get()
                            g_, gr = tg.get()
                            s_, srr = sg.get()
                            ja, jg = j, 22 + j
                            S.op("dve", TS(a_[:, :on], ps[ba][:, 0:on], V(l, "fcw", ja * 3), None, ALU.mult), reads=[PS(ba)], writes=[ar])
                            S.op("dve", STT(a_[:, :on], ps[ba][:, 1:on + 1], V(l, "fcw", ja * 3 + 1), a_[:, :on], ALU.mult, ALU.add),
                                 reads=[PS(ba), ar], writes=[ar])
                            S.op("dve", STT(a_[:, :on], ps[ba][:, 2:on + 2], V(l, "fcw", ja * 3 + 2), a_[:, :on], ALU.mult, ALU.add),
                                 reads=[PS(ba), ar], writes=[ar])
                            S.op("dve", TS(g_[:, :on], ps[bg][:, 0:on], V(l, "fcw", jg * 3), None, ALU.mult), reads=[PS(bg)], writes=[gr])
                            S.op("dve", STT(g_[:, :on], ps[bg][:, 1:on + 1], V(l, "fcw", jg * 3 + 1), g_[:, :on], ALU.mult, ALU.add),
                                 reads=[PS(bg), gr], writes=[gr])
                            S.op("dve", STT(g_[:, :on], ps[bg][:, 2:on + 2], V(l, "fcw", jg * 3 + 2), g_[:, :on], ALU.mult, ALU.add),
                                 reads=[PS(bg), gr], writes=[gr])
                            S.op("act", ACT(s_[:, :on], g_[:, :on], AF.Silu, bias=V(l, "fcb", jg), scale=1.0), reads=[gr], writes=[srr])
                            S.op("dve", STT(hid[:, j, :on], a_[:, :on], V(l, "fcb", ja), s_[:, :on], ALU.add, ALU.mult),
                                 reads=[ar, srr, "hid"], writes=["hid"])
                        t0 = s0 + o0
                        for mo in range(8):
                            if not (cfg.get("ffn_nodma") and (bi > 0 or mo > 1)):
                                wtd, wrd = wdn.get()
                                S.op("sp" if mo % 2 else "act", DMA(wtd[:], fdnb[l][:, :, mo * 128:(mo + 1) * 128]),
                                     reads=[("fdnb", l, kc) for kc in range(22)], writes=[wrd], dma=True)
                            b = nb()
                            for kc in range(22):
                                S.op("pe", MM(ps[b][:, :on], wtd[:, kc, :], hid[:, kc, :on], kc == 0, kc == 21), reads=[wrd, "hid"], writes=[PS(b)])
                            S.op("dve", STT(xT[:, mo, t0:t0 + on], ps[b][:, :on], MOD(l, 5, mo, path.ccol), xT[:, mo, t0:t0 + on],
                                            ALU.mult, ALU.add), reads=[PS(b), "xT"], writes=["xT"])
                S.barrier()

        cast_done = set()

        def ensure_ffn_cast(l):
            if l in cast_done:
                return
            cast_done.add(l)
            for kc in range(8):
                S.op("pool", DMA(fupb[l][:, kc, :], Din["fup"][l][:, kc, :]), writes=[("fupb", l, kc)], dma=True)
            for kc in range(22):
                S.op("pool", DMA(fdnb[l][:, kc, :], Din["fdn"][l][:, kc, :]), writes=[("fdnb", l, kc)], dma=True)

        def derive_gains(l, path):
            for i, (nm, wch) in enumerate((("norm1", 1), ("norm2", 4))):
                for kc in range(8):
                    S.op("dve", STT(gsh[:, i, kc:kc + 1], MOD(l, wch, kc, path.ccol), 1.0, V(l, nm, kc), ALU.add, ALU.mult),
                         reads=["modT", "gsh"], writes=["gsh"])

        def store_y(path):
            with ExitStack() as sc:
                yl = Pool(sc, "yl", 2, [128, 1024], F32)
                for tt in range(path.T // 128):
                    y_, yr = yl.get()
                    for kc2 in range(2):
                        b = nb()
                        for j in range(4):
                            kc = kc2 * 4 + j
                            S.op("pe", MM(ps[b][:, j * 128:(j + 1) * 128], xT[:, kc, tt * 128:(tt + 1) * 128], identf[:], j == 0, True),
                                 reads=["xT"], writes=[PS(b)])
                        S.op("act" if kc2 else "dve",
                             ACT(y_[:, kc2 * 512:(kc2 + 1) * 512], ps[b][:, :], AF.Copy) if kc2 else CP(y_[:, kc2 * 512:(kc2 + 1) * 512], ps[b][:, :]),
                             reads=[PS(b), yr], writes=[yr])
                    S.op("sp", DMA(path.yout[tt * 128:(tt + 1) * 128, :], y_[:]), reads=[yr], dma=True)
                S.barrier()

        paths = []
        if cfg.get("prompt", True):
            paths.append(Path("p", 512, [(0, 256), (256, 256)], False, Din["xp"], O["yp"], 0))
        if cfg.get("sample", True):
            paths.append(Path("s", 2048, [(0, 2048)], True, Din["xs"], O["ys"], 1))
        for path in paths:
            load_x(path)
            for l in range(cfg.get("layers", 2) if not path.sample else cfg.get("slayers", cfg.get("layers", 2))):
                derive_gains(l, path)
                with ExitStack() as scn:
                    mkpools(scn)
                    norm_mod(path, gsh[:, 0, :], lambda kc: MOD(l, 0, kc, path.ccol))
                    S.barrier()
                if cfg.get("gqa", True):
                    gqa(path, l)
                    merge(path, l, 0)
                if cfg.get("mla", True):
                    mla(path, l)
                    merge(path, l, 1)
                if cfg.get("ffn", True):
                    ensure_ffn_cast(l)
                if cfg.get("hyena", True):
                    hyena(path, l)
                    merge(path, l, 2)
                if cfg.get("ffn", True):
                    for si, (s0, Ls) in enumerate(path.seqs):
                        c0 = path.hcol(s0, si) - 1
                        c1 = path.hcol(s0 + Ls, si)
                        S.op("dve", MS(hT[:, :, c0:c0 + 1], 0.0), writes=["hT"])
                        S.op("dve", MS(hT[:, :, c1:c1 + 1], 0.0), writes=["hT"])
                    with ExitStack() as scn:
                        mkpools(scn)
                        norm_mod(path, gsh[:, 1, :], lambda kc: MOD(l, 3, kc, path.ccol))
                        S.barrier()
                    ffn(path, l)
            with ExitStack() as scn:
                mkpools(scn)
                norm_mod(path, V(0, "fin", 0, 8), None, out_dram=True)
                S.barrier()
            store_y(path)
        S.finish()
        S.emit()
        print("kernel: recorded ops", S.nops, S.cnt, S.dcnt, {e: len(v) for e, v in S.streams.items()})
    return nc


_CFG = {}


def kernel(**inputs):
    I = {k: np.asarray(v) for k, v in inputs.items()}
    sh = prep_shared(I)
    ncores = _CFG.get("ncores", NCORES)
    cores = [prep_core(I, c) for c in range(ncores)]

    def sig(d):
        return {k: (v.shape, "bf16" if v.dtype == ml_dtypes.bfloat16 else "f32") for k, v in d.items()}

    nc = build(sig(sh), sig(cores[0]), _CFG)
    in_maps = []
    for c in range(ncores):
        m = dict(sh)
        m.update(cores[c])
        in_maps.append(m)
    if _CFG.get("trace"):
        res = run_bass_kernel_spmd(nc, in_maps, core_ids=list(range(ncores)), trace=True)
        print("EXEC_TIME_NS", res.exec_time_ns)
    else:
        res = run_bass_kernel_spmd(nc, in_maps, core_ids=list(range(ncores)))
    R = list(res.results)
    while len(R) < NCORES:
        R.append(R[0])
    y_prompt = np.concatenate([R[c]["yp"].reshape(2, 256, 1024) for c in range(NCORES)], 0)
    y_sample = np.stack([R[c]["ys"] for c in range(4)], 0)
    nk = np.concatenate([R[c]["nk"].reshape(2, 2, 256, 2, 64) for c in range(NCORES)], 0)
    nv = np.concatenate([R[c]["nv"].reshape(2, 2, 256, 2, 64) for c in range(NCORES)], 0)
    nckv = np.concatenate([R[c]["nckv"] for c in range(NCORES)], 0)
    nkpe = np.concatenate([R[c]["nkpe"] for c in range(NCORES)], 0)
    f = np.float32
    return (y_prompt.astype(f), y_sample.astype(f), nk.astype(f), nv.astype(f), nckv.astype(f), nkpe.astype(f))
```

```python
import math
from contextlib import ExitStack
import numpy as np
import ml_dtypes
import concourse.bass as bass
import concourse.mybir as mybir
from concourse.bass_utils import run_bass_kernel_spmd

F32 = mybir.dt.float32
BF16 = mybir.dt.bfloat16
AF = mybir.ActivationFunctionType
ALU = mybir.AluOpType

D = 1024
EPS = 1e-6
NCORES = 8


class Sched:
    ENG = ("pe", "act", "dve", "pool", "sp")
    NDS = 8
    NOSELF = ("pe",)

    def __init__(self, nc, stack):
        self.nc = nc
        self.stack = stack
        self.streams = {e: [] for e in self.ENG}
        self.cnt = {e: 0 for e in self.ENG}
        self.sem = {e: stack.enter_context(nc.semaphore("s_" + e)) for e in self.ENG}
        self.skey = {e: "s_" + e for e in self.ENG}
        self.epoch = 0
        self.dq = ("sp", "pool", "act")
        self.dsem = {q: [stack.enter_context(nc.semaphore("d_%s%d" % (q, i))) for i in range(self.NDS)]
                     for q in self.dq}
        self.dcnt = {q: [0] * self.NDS for q in self.dq}
        self.dnext = {q: 0 for q in self.dq}
        self.seen = {e: {} for e in self.ENG}
        self.lastw = {}
        self.readers = {}
        self.nops = 0

    def _wait(self, eng, tok):
        key, sem, val, src = tok
        if src == eng and eng in self.NOSELF:
            return
        if self.seen[eng].get(key, 0) >= val:
            return
        self.seen[eng][key] = val
        self.streams[eng].append(("wait", sem, val))

    def op(self, eng, fn, reads=(), writes=(), dma=False):
        toks = []
        for r in reads:
            t = self.lastw.get(r)
            if t is not None:
                toks.append(t)
            if isinstance(r, tuple) and r and r[0] == "ps":
                for t2 in self.readers.get(r, {}).values():
                    if t2[3] != eng:
                        toks.append(t2)
        for w in writes:
            t = self.lastw.get(w)
            if t is not None:
                toks.append(t)
            toks.extend(self.readers.get(w, {}).values())
        for t in toks:
            self._wait(eng, t)
        if dma:
            q = eng
            i = self.dnext[q]
            self.dnext[q] = (i + 1) % self.NDS
            key = "d_%s%d" % (q, i)
            if self.dcnt[q][i] > 0:
                self._wait(eng, (key, self.dsem[q][i], self.dcnt[q][i], None))
            self.dcnt[q][i] += 16
            tok = (key, self.dsem[q][i], self.dcnt[q][i], None)
            self.streams[eng].append(("op", fn, self.dsem[q][i], 16))
        else:
            self.cnt[eng] += 1
            tok = (self.skey[eng], self.sem[eng], self.cnt[eng], eng)
            self.streams[eng].append(("op", fn, self.sem[eng], 1))
        self.nops += 1
        for w in writes:
            self.lastw[w] = tok
            self.readers[w] = {}
        for r in reads:
            d = self.readers.setdefault(r, {})
            old = d.get(tok[0])
            if old is None or old[2] < tok[2]:
                d[tok[0]] = tok
        return tok

    def barrier(self):
        for e in self.ENG:
            for e2 in self.ENG:
                if self.cnt[e2] > 0 and e2 != e:
                    self._wait(e, (self.skey[e2], self.sem[e2], self.cnt[e2], e2))
            for q in self.dq:
                for i in range(self.NDS):
                    if self.dcnt[q][i] > 0:
                        self._wait(e, ("d_%s%d" % (q, i), self.dsem[q][i], self.dcnt[q][i], None))
        self.lastw = {}
        self.readers = {}
        for e in self.ENG:
            if self.cnt[e] > 8000:
                self.epoch += 1
                self.skey[e] = "s_%s_%d" % (e, self.epoch)
                self.sem[e] = self.stack.enter_context(self.nc.semaphore(self.skey[e]))
                self.cnt[e] = 0

    def finish(self):
        for e2 in self.ENG:
            if e2 != "sp" and self.cnt[e2] > 0:
                self._wait("sp", (self.skey[e2], self.sem[e2], self.cnt[e2], e2))
        for q in self.dq:
            for i in range(self.NDS):
                if self.dcnt[q][i] > 0:
                    self._wait("sp", ("d_%s%d" % (q, i), self.dsem[q][i], self.dcnt[q][i], None))

    def emit(self):
        nc = self.nc

        def run(e, eng):
            for it in self.streams[e]:
                if it[0] == "wait":
                    eng.wait_ge(it[1], it[2])
                else:
                    ins = it[1](eng)
                    ins.then_inc(it[2], it[3])

        with nc.Block() as block:
            @block.tensor
            def _(eng):
                run("pe", eng)

            @block.scalar
            def _(eng):
                run("act", eng)

            @block.vector
            def _(eng):
                run("dve", eng)

            @block.gpsimd
            def _(eng):
                run("pool", eng)

            @block.sync
            def _(eng):
                run("sp", eng)


VOFF = {}
_o = 0
for _n, _w in [("norm1", 8), ("norm2", 8), ("qg", 1), ("qgs", 1), ("kg", 1), ("kgs", 1), ("mqn", 3), ("mkvn", 2),
               ("hsw", 36), ("hsb", 12), ("fcw", 132), ("fcb", 44), ("fin", 8), ("bmod", 48)]:
    VOFF[_n] = _o
    _o += _w
NV = _o

WIN_Q, WIN_K, WIN_V, WIN_CQ, WIN_CKV, WIN_KPE, WIN_HY, WIN_G = 0, 512, 640, 768, 1152, 1408, 1440, 2976


def _fm(w, kc):
    return np.ascontiguousarray(w.reshape(kc, 128, w.shape[1]).transpose(1, 0, 2))


def _swap_pairs(w):
    o = np.empty_like(w)
    o[..., 0::2] = w[..., 1::2]
    o[..., 1::2] = w[..., 0::2]
    return o


def _rope_tables(L, rot_dim, grid_w=64, theta=10000.0):
    rows = L // grid_w
    row = np.repeat(np.arange(rows, dtype=np.float32), grid_w)
    col = np.tile(np.arange(grid_w, dtype=np.float32), rows)
    axis_dim = rot_dim // 2
    inv = (theta ** (-np.arange(0, axis_dim, 2, dtype=np.float32) / axis_dim)).astype(np.float32)
    ang = np.concatenate([row[:, None] * inv, col[:, None] * inv], axis=-1).astype(np.float32)
    c = np.cos(ang).astype(np.float32)
    s = np.sin(ang).astype(np.float32)
    cf = np.repeat(c, 2, axis=1).T
    sf = np.repeat(s, 2, axis=1).T.copy()
    sf[0::2] *= -1.0
    return np.ascontiguousarray(cf), np.ascontiguousarray(sf)


def _hy_consts(L):
    t = np.arange(L, dtype=np.float32)
    tn = t / max(L - 1, 1)
    bands = np.linspace(1e-4, 7, 8, dtype=np.float32)
    ang = (np.float32(2.0 * math.pi / L) * t[:, None] * bands[None, :]).astype(np.float32)
    z = np.concatenate([tn[:, None], np.cos(ang), -np.sin(ang)], axis=-1).astype(np.float32)
    min_decay = math.log(1e-2) / 0.3
    max_decay = math.log(1e-2) / 1.5
    deltas = np.abs(np.linspace(min_decay, max_decay, 512, dtype=np.float32))
    window = np.exp(-tn[:, None] * deltas[None, :]).astype(np.float32)
    idx = np.arange(L, dtype=np.float64) + 0.5
    phi = np.pi * np.outer(idx, idx) / L
    C = np.cos(phi)
    Sn = np.sin(phi)
    nt = L // 128

    def slabs(M):
        a = M.reshape(nt, 128, nt, 128).transpose(2, 1, 0, 3)
        return np.ascontiguousarray(a).astype(ml_dtypes.bfloat16)

    alpha = np.pi * idx / (2 * L)
    ca = np.cos(alpha).reshape(nt, 128).T.astype(np.float32)
    sa = np.sin(alpha).reshape(nt, 128).T.astype(np.float32)
    def rslabs(M):
        return np.ascontiguousarray(M.reshape(nt, 128, L)).astype(ml_dtypes.bfloat16)

    return dict(zT=np.ascontiguousarray(z.T), win=window, dc=slabs(C), ds=slabs(Sn), rc=rslabs(C), rs=rslabs(Sn),
                ca=np.ascontiguousarray(ca), sa=np.ascontiguousarray(sa))


def prep_shared(I):
    sh = {}
    sh["wmod"] = np.stack([_fm(I["w_mod"][l], 8) for l in range(2)])
    sh["win"] = np.stack([_fm(I["w_in"][l], 8) for l in range(2)])
    wx = []
    for l in range(2):
        w = I["w_in"][l]
        q = w[:, WIN_Q:WIN_Q + 512]
        k = w[:, WIN_K:WIN_K + 128]
        kd = np.concatenate([k[:, 0:64], k[:, 0:64], k[:, 64:128], k[:, 64:128]], axis=1)
        kpe = w[:, WIN_KPE:WIN_KPE + 32]
        wx.append(_fm(np.concatenate([_swap_pairs(q), kd, _swap_pairs(kd), _swap_pairs(kpe)], axis=1), 8))
    sh["winx"] = np.stack(wx)
    sh["wuq"] = np.stack([_fm(I["mla_w_uq"][l], 3) for l in range(2)])
    ux = []
    for l in range(2):
        w = I["mla_w_uq"][l].reshape(384, 8, 96)[:, :, 64:96].reshape(384, 256)
        ux.append(_fm(_swap_pairs(w), 3))
    sh["wuqx"] = np.stack(ux)
    sh["wukv"] = np.stack([_fm(I["mla_w_ukv"][l], 2) for l in range(2)])
    sh["wbr"] = np.stack([np.stack([_fm(I["w_branch"][l, n], 4) for n in range(3)]) for l in range(2)])
    sh["wout"] = np.stack([_fm(I["w_out"][l], 8) for l in range(2)])
    sh["fup"] = np.stack([_fm(I["ffn_up"][l], 8) for l in range(2)])
    sh["fdn"] = np.stack([_fm(I["ffn_down"][l], 22) for l in range(2)])
    vec = np.zeros((2, 128, NV), np.float32)
    for l in range(2):
        def put(name, arr):
            arr = np.asarray(arr, np.float32)
            vec[l, :, VOFF[name]:VOFF[name] + arr.shape[1]] = arr
        put("norm1", I["norm1"][l].reshape(8, 128).T)
        put("norm2", I["norm2"][l].reshape(8, 128).T)
        qg = I["gqa_q_norm"][l]
        kg = I["gqa_k_norm"][l]
        put("qg", np.tile(qg, 2)[:, None])
        put("qgs", np.tile(_swap_pairs(qg), 2)[:, None])
        put("kg", np.tile(kg, 2)[:, None])
        put("kgs", np.tile(_swap_pairs(kg), 2)[:, None])
        put("mqn", I["mla_q_norm"][l].reshape(3, 128).T)
        put("mkvn", I["mla_kv_norm"][l].reshape(2, 128).T)
        put("hsw", I["hy_short_w"][l].reshape(3, 12, 128).transpose(2, 1, 0).reshape(128, 36))
        put("hsb", I["hy_short_b"][l].reshape(12, 128).T)
        put("fcw", I["ffn_conv_w"][l].reshape(3, 44, 128).transpose(2, 1, 0).reshape(128, 132))
        put("fcb", I["ffn_conv_b"][l].reshape(44, 128).T)
        put("fin", I["final_norm"].reshape(8, 128).T)
        put("bmod", I["b_mod"][l].reshape(48, 128).T)
    sh["vec"] = vec
    sh["hyw1"] = np.ascontiguousarray(I["hy_w1"])
    sh["hyw2"] = np.ascontiguousarray(I["hy_w2"])
    sh["hyw3"] = np.ascontiguousarray(I["hy_w3"])
    hv = np.zeros((2, 64, 4), np.float32)
    for l in range(2):
        hv[l, :, 0] = I["hy_b1"][l]
        hv[l, :, 1] = I["hy_b2"][l]
        hv[l, :, 2] = I["hy_freq"][l, 0]
        hv[l, :, 3] = I["hy_freq"][l, 1]
    sh["hyv"] = hv
    sh["hybias"] = np.ascontiguousarray(I["hy_bias"].reshape(2, 2, 512))
    sh["hybiasT"] = np.ascontiguousarray(I["hy_bias"].reshape(2, 2, 4, 128).transpose(0, 2, 3, 1))
    ident = np.eye(128, dtype=np.float32)
    sh["ident"] = ident
    bd = np.zeros((128, 128), np.float32)
    bd[:64, :64] = 1.0
    bd[64:, 64:] = 1.0
    sh["bd64"] = bd
    ca, sa = _rope_tables(2048, 64)
    sh["ropeAc"] = np.concatenate([ca, ca], 0)
    sh["ropeAs"] = np.concatenate([sa, sa], 0)
    cb, sb_ = _rope_tables(2048, 32)
    rb = np.zeros((128, 2048), np.float32)
    rb[64:96] = cb
    rb[0:32] = cb
    rb[32:64] = cb
    sh["ropeBc"] = rb
    rb2 = np.zeros((128, 2048), np.float32)
    rb2[64:96] = sb_
    rb2[0:32] = sb_
    rb2[32:64] = sb_
    sh["ropeBs"] = rb2
    for L in (256, 2048):
        hc = _hy_consts(L)
        for k, v in hc.items():
            sh["hy%d_%s" % (L, k)] = v
    return sh


def prep_core(I, c):
    b = c % 4
    m = {}
    m["xp"] = np.ascontiguousarray(I["x_prompt"][2 * c:2 * c + 2].reshape(512, 1024))
    m["xs"] = np.ascontiguousarray(I["x_sample"][b])
    cond = np.stack([I["c_ctx"], I["c"][b]], axis=-1)
    m["cond"] = np.ascontiguousarray(cond.reshape(8, 128, 2).transpose(1, 0, 2))
    ck = I["cache_gqa_k"][b]
    m["ckd"] = np.ascontiguousarray(np.stack([ck, ck], axis=3).reshape(2, 256, 256))
    m["cv"] = np.ascontiguousarray(I["cache_gqa_v"][b].reshape(2, 256, 128))
    m["cckv"] = np.ascontiguousarray(I["cache_mla_ckv"][b])
    m["ckpe"] = np.ascontiguousarray(I["cache_mla_kpe"][b])
    return m


class Path:
    def __init__(self, name, T, seqs, sample, xin, yout, ccol):
        self.name, self.T, self.seqs, self.sample = name, T, seqs, sample
        self.xin, self.yout, self.ccol = xin, yout, ccol
        self.koff = 256 if sample else 0
        self.L = seqs[0][1]
        self.blocks = []
        for si, (t0, L) in enumerate(seqs):
            for s in range(0, L, 512):
                self.blocks.append((t0 + s, min(512, L - s), si))

    def hcol(self, t, si):
        return t + 1 + 2 * si


def build(shared_shapes, core_shapes, cfg):
    nc = bass.Bass("TRN2", target_bir_lowering=False)
    Din = {}
    for k, (shp, dt) in list(shared_shapes.items()) + list(core_shapes.items()):
        Din[k] = nc.dram_tensor(k, list(shp), BF16 if dt == "bf16" else F32, kind="ExternalInput").ap()

    def dout(name, shape):
        return nc.dram_tensor(name, list(shape), F32, kind="ExternalOutput").ap()

    O = dict(yp=dout("yp", [512, 1024]), ys=dout("ys", [2048, 1024]),
             nk=dout("nk", [2, 2, 256, 128]), nv=dout("nv", [2, 2, 256, 128]),
             nckv=dout("nckv", [2, 2, 256, 256]), nkpe=dout("nkpe", [2, 2, 256, 32]))

    fupb = nc.dram_tensor("fupb", [2, 128, 8, 5632], BF16, kind="Internal").ap()
    fdnb = nc.dram_tensor("fdnb", [2, 128, 22, 1024], BF16, kind="Internal").ap()

    with ExitStack() as st:
        S = Sched(nc, st)

        _un = [0]

        def sbt(stack, name, shape, dt):
            _un[0] += 1
            return stack.enter_context(nc.sbuf_tensor("%s_%d" % (name, _un[0]), list(shape), dt))

        ps = [st.enter_context(nc.psum_tensor("ps%d" % i, [128, 512], F32)) for i in range(8)]
        pctr = [0]
        BG = []

        def nb(excl=()):
            while True:
                i = pctr[0]
                pctr[0] = (i + 1) % 8
                if i not in excl:
                    return i

        def PS(i):
            return ("ps", i)

        class Pool:
            def __init__(self, stack, name, n, shape, dt):
                self.t = [sbt(stack, "%s%d" % (name, i), shape, dt) for i in range(n)]
                self.name, self.n, self.i = name, n, 0

            def get(self):
                i = self.i
                self.i = (i + 1) % self.n
                return self.t[i], (self.name, i)

        def MM(out, lhsT, rhs, start, stop):
            return lambda e: e.matmul(out, lhsT=lhsT, rhs=rhs, start=start, stop=stop)

        def ACT(out, in_, func, **kw):
            return lambda e: e.activation(out=out, in_=in_, func=func, **kw)

        def TT(out, in0, in1, op):
            return lambda e: e.tensor_tensor(out=out, in0=in0, in1=in1, op=op)

        def STT(out, in0, scalar, in1, op0, op1):
            return lambda e: e.scalar_tensor_tensor(out=out, in0=in0, scalar=scalar, in1=in1, op0=op0, op1=op1)

        def TS(out, in0, s1, s2, op0, op1=None):
            if op1 is None:
                return lambda e: e.tensor_scalar(out=out, in0=in0, scalar1=s1, scalar2=None, op0=op0)
            return lambda e: e.tensor_scalar(out=out, in0=in0, scalar1=s1, scalar2=s2, op0=op0, op1=op1)

        def CP(out, in_):
            return lambda e: e.tensor_copy(out=out, in_=in_)

        def DMA(out, in_):
            return lambda e: e.dma_start(out=out, in_=in_)

        def MS(ap, v):
            return lambda e: e.memset(ap, v)

        xT = sbt(st, "xT", [128, 8, 2048], F32)
        hT = sbt(st, "hT", [128, 8, 2052], BF16)
        oT = sbt(st, "oT", [128, 4, 2048], BF16)
        identf = sbt(st, "identf", [128, 128], F32)
        identb = sbt(st, "identb", [128, 128], BF16)
        bd64 = sbt(st, "bd64", [128, 128], F32)
        onesf = sbt(st, "onesf", [128, 128], BF16)
        bd64b = sbt(st, "bd64b", [128, 128], BF16)
        epsb = sbt(st, "epsb", [128, 1], F32)
        vec = sbt(st, "vec", [128, 2, NV], F32)
        modT = sbt(st, "modT", [128, 2, 48, 2], F32)
        gsh = sbt(st, "gsh", [128, 4, 8], F32)
        class _PP:
            pass
        PP = _PP()
        _pn = [0]

        def mkpools(sc):
            _pn[0] += 1
            k = _pn[0]
            PP.sq = Pool(sc, "sq%d_" % k, 2, [128, 512], BF16)
            PP.ln = Pool(sc, "ln%d_" % k, 1, [128, 512], F32)
            PP.rs = Pool(sc, "rs%d_" % k, 2, [128, 512], F32)
            PP.tm = Pool(sc, "tm%d_" % k, 4, [128, 512], F32)

        S.op("sp", DMA(identf[:], Din["ident"]), writes=["c0"], dma=True)
        S.op("pool", DMA(identb[:], Din["ident"]), writes=["c1"], dma=True)
        S.op("sp", DMA(bd64[:], Din["bd64"]), writes=["c2"], dma=True)
        S.op("pool", DMA(bd64b[:], Din["bd64"]), writes=["c2b"], dma=True)
        S.op("dve", MS(onesf[:], 1.0), writes=["c3"])
        S.op("dve", MS(epsb[:], EPS), writes=["c4"])
        S.op("dve", MS(hT[:], 0.0), writes=["c5"])
        for l in range(2):
            S.op("sp", DMA(vec[:, l, :], Din["vec"][l]), writes=["c6%d" % l], dma=True)

        def V(l, name, j=0, n=1):
            o = VOFF[name] + j
            return vec[:, l, o:o + n]

        with ExitStack() as sc:
            condt = sbt(sc, "condt", [128, 8, 2], F32)
            scond = sbt(sc, "scond", [128, 8, 64], F32)
            modrow = sbt(sc, "modrow", [64, 6144], F32)
            wmp = Pool(sc, "wm", 3, [128, 8, 512], F32)
            S.op("sp", DMA(condt[:], Din["cond"]), writes=["condt"], dma=True)
            S.op("dve", MS(scond[:], 0.0), writes=["scond"])
            S.op("act", ACT(scond[:, :, 0:2], condt[:], AF.Silu), reads=["condt", "scond"], writes=["scond"])
            for l in range(2):
                for sc12 in range(12):
                    wt, wr = wmp.get()
                    S.op("sp" if sc12 % 2 == 0 else "act", DMA(wt[:], Din["wmod"][l][:, :, sc12 * 512:(sc12 + 1) * 512]), writes=[wr], dma=True)
                    b = nb()
                    for kc in range(8):
                        S.op("pe", MM(ps[b][0:64, :], scond[:, kc, :], wt[:, kc, :], kc == 0, kc == 7),
                             reads=[wr, "scond"], writes=[PS(b)])
                    S.op("dve", CP(modrow[:, sc12 * 512:(sc12 + 1) * 512], ps[b][0:64, :]), reads=[PS(b), "modrow"], writes=["modrow"])
                b = nb()
                for ch in range(48):
                    S.op("pe", MM(ps[b][:, 2 * ch:2 * ch + 2], modrow[:, ch * 128:(ch + 1) * 128], identf[0:64, 0:2], True, True),
                         reads=["modrow", "c0"], writes=[PS(b)])
                pv_ = ps[b][:, 0:96].rearrange("p (ch c) -> p ch c", c=2)
                for c in range(2):
                    S.op("dve", TT(modT[:, l, :, c], pv_[:, :, c], V(l, "bmod", 0, 48), ALU.add),
                         reads=[PS(b), "c6%d" % l, "modT"], writes=["modT"])
            S.barrier()

        def MOD(l, which, kc, ccol):
            return modT[:, l, which * 8 + kc, ccol:ccol + 1]

        def load_x(path):
            with ExitStack() as sc:
                xl = Pool(sc, "xl", 2, [128, 1024], F32)
                for tt in range(path.T // 128):
                    t_, r_ = xl.get()
                    S.op("sp", DMA(t_[:], path.xin[tt * 128:(tt + 1) * 128, :]), writes=[r_], dma=True)
                    for kc2 in range(2):
                        b = nb()
                        for j in range(4):
                            kc = kc2 * 4 + j
                            S.op("pe", MM(ps[b][:, j * 128:(j + 1) * 128], t_[:, kc * 128:(kc + 1) * 128], identf[:],
                                          j == 0, True), reads=[r_], writes=[PS(b)])
                        for j in range(4):
                            kc = kc2 * 4 + j
                            S.op("act" if j % 2 else "dve",
                                 (ACT(xT[:, kc, tt * 128:(tt + 1) * 128], ps[b][:, j * 128:(j + 1) * 128], AF.Copy) if j % 2
                                  else CP(xT[:, kc, tt * 128:(tt + 1) * 128], ps[b][:, j * 128:(j + 1) * 128])),
                                 reads=[PS(b)], writes=["xT"])
                S.barrier()

        def norm_mod(path, gcols, shfn, out_dram=None):
            for (t0, n, si) in path.blocks:
                b = nb()
                for kc in range(8):
                    sq, sqr = PP.sq.get()
                    S.op("dve", TT(sq[:, :n], xT[:, kc, t0:t0 + n], xT[:, kc, t0:t0 + n], ALU.mult), reads=["xT"], writes=[sqr])
                    S.op("pe", MM(ps[b][:, :n], onesf[:], sq[:, :n], kc == 0, kc == 7), reads=[sqr], writes=[PS(b)])
                ln_, lr = PP.ln.get()
                S.op("act", ACT(ln_[:, :n], ps[b][:, :n], AF.Ln, bias=epsb[:, 0:1], scale=1.0 / D), reads=[PS(b)], writes=[lr])
                rs_, rr = PP.rs.get()
                S.op("act", ACT(rs_[:, :n], ln_[:, :n], AF.Exp, scale=-0.5), reads=[lr], writes=[rr])
                hc = path.hcol(t0, si)
                for kc in range(8):
                    if out_dram is None:
                        tm_, tr = PP.tm.get()
                        S.op("dve", STT(tm_[:, :n], xT[:, kc, t0:t0 + n], gcols[:, kc:kc + 1], rs_[:, :n], ALU.mult, ALU.mult),
                             reads=["xT", rr, "gsh"], writes=[tr])
                        S.op("dve", TS(hT[:, kc, hc:hc + n], tm_[:, :n], shfn(kc), None, ALU.add),
                             reads=[tr, "modT"], writes=["hT"])
                    else:
                        S.op("dve", STT(xT[:, kc, t0:t0 + n], xT[:, kc, t0:t0 + n], gcols[:, kc:kc + 1], rs_[:, :n],
                                        ALU.mult, ALU.mult), reads=["xT", rr], writes=["xT"])

        def headnorm(psr, pss, n, g, gs, ones_mat, nfeat, ropeC, ropeS, outs, roperes=None, prow=slice(0, 128)):
            sq, sqr = PP.sq.get()
            S.op("act", ACT(sq[prow, :n], ps[psr][prow, :n], AF.Square), reads=[PS(psr)], writes=[sqr])
            b3 = nb()
            S.op("pe", MM(ps[b3][prow, :n], ones_mat, sq[prow, :n], True, True), reads=[sqr], writes=[PS(b3)])
            ln_, lr = PP.ln.get()
            S.op("act", ACT(ln_[prow, :n], ps[b3][prow, :n], AF.Ln, bias=epsb[prow, 0:1], scale=1.0 / nfeat),
                 reads=[PS(b3)], writes=[lr])
            rs_, rr = PP.rs.get()
            S.op("act", ACT(rs_[prow, :n], ln_[prow, :n], AF.Exp, scale=-0.5), reads=[lr], writes=[rr])
            t1, r1 = PP.tm.get()
            S.op("dve", STT(t1[prow, :n], ps[psr][prow, :n], g, rs_[prow, :n], ALU.mult, ALU.mult),
                 reads=[PS(psr), rr], writes=[r1])
            if pss is not None:
                t2, r2 = PP.tm.get()
                S.op("dve", STT(t2[prow, :n], ps[pss][prow, :n], gs, rs_[prow, :n], ALU.mult, ALU.mult),
                     reads=[PS(pss), rr], writes=[r2])
                S.op("dve", TT(t1[prow, :n], t1[prow, :n], ropeC, ALU.mult), reads=[r1, roperes], writes=[r1])
                S.op("dve", TT(t2[prow, :n], t2[prow, :n], ropeS, ALU.mult), reads=[r2, roperes], writes=[r2])
                for (ap, res) in outs:
                    S.op("dve", TT(ap, t1[prow, :n], t2[prow, :n], ALU.add), reads=[r1, r2], writes=[res])
            else:
                for (ap, res) in outs:
                    S.op("act", ACT(ap, t1[prow, :n], AF.Copy), reads=[r1], writes=[res])
            return t1, r1

        def attend(sc_pools, qap_fn, kap_fn, vap_fn, nkt, kt0, scale, par, chunk, qs, qn, qres, kres, vres):
            if cfg.get("noattn"):
                return
            if BG:
                BG.pop(0)()
            PTp, rsm, rs0, otm = sc_pools
            bo = nb()
            pend = []

            def pv(item):
                kt, pt, pr = item
                S.op("pe", MM(ps[bo][:, :qn], vap_fn(kt0 + kt), pt[:, :qn], kt == 0, kt == nkt - 1),
                     reads=[pr] + vres, writes=[PS(bo)])
            for kt in range(nkt):
                bs = nb((bo,))
                S.op("pe", MM(ps[bs][:, :qn], kap_fn(kt0 + kt), qap_fn(), True, True), reads=qres + kres, writes=[PS(bs)])
                pt, pr = PTp.get()
                S.op("act", ACT(pt[:, :qn], ps[bs][:, :qn], AF.Exp, scale=scale), reads=[PS(bs)], writes=[pr])
                pend.append((kt, pt, pr))
                if len(pend) > 3:
                    pv(pend.pop(0))
            while pend:
                pv(pend.pop(0))
            r_, rr = rsm.get()
            S.op("dve", lambda e: e.reciprocal(out=r_[64:128, :qn], in_=ps[bo][64:128, :qn]), reads=[PS(bo)], writes=[rr])
            r0, r0r = rs0.get()
            S.op("act", ACT(r0[0:64, :qn], r_[64:128, :qn], AF.Copy), reads=[rr], writes=[r0r])
            if par == 0:
                S.op("dve", TT(oT[0:64, chunk, qs:qs + qn], ps[bo][0:64, :qn], r0[0:64, :qn], ALU.mult),
                     reads=[PS(bo), r0r], writes=["oT"])
            else:
                ot, otr = otm.get()
                S.op("dve", TT(ot[0:64, :qn], ps[bo][0:64, :qn], r0[0:64, :qn], ALU.mult), reads=[PS(bo), r0r], writes=[otr])
                S.op("act", ACT(oT[64:128, chunk, qs:qs + qn], ot[0:64, :qn], AF.Copy), reads=[otr], writes=["oT"])

        def rope_load(sc_pool, tabc, tabs, t0, n, prow=slice(0, 128)):
            rc, rcr = sc_pool.get()
            S.op("sp", DMA(rc[prow, 0, :n], Din[tabc][prow, t0:t0 + n]), writes=[rcr], dma=True)
            S.op("sp", DMA(rc[prow, 1, :n], Din[tabs][prow, t0:t0 + n]), writes=[rcr], dma=True)
            return rc, rcr

        def out_T(src_ap_fn, nrow, prow0, t0, n, dst_fn, res, pool32):
            for j in range(n // 128):
                b = nb()
                S.op("pe", MM(ps[b][:, 0:nrow], src_ap_fn(j), identf[prow0:prow0 + nrow, prow0:prow0 + nrow], True, True),
                     reads=res, writes=[PS(b)])
                o_, orr = pool32.get()
                S.op("dve", CP(o_[:, 0:nrow], ps[b][:, 0:nrow]), reads=[PS(b)], writes=[orr])
                S.op("sp", DMA(dst_fn(j), o_[:, 0:nrow]), reads=[orr], dma=True)

        def gqa(path, l):
            T, koff, smp = path.T, path.koff, path.sample
            nkt_all = (koff + T) // 128
            with ExitStack() as sc:
                mkpools(sc)
                qT = sbt(sc, "qT", [128, 4, T], BF16)
                kT = sbt(sc, "kT", [128, 2, koff + T], BF16)
                Va = sbt(sc, "Va", [128, nkt_all, 2, 128], BF16)
                wch = Pool(sc, "wch", 4, [128, 8, 128], BF16)
                wv = sbt(sc, "wv", [128, 8, 128], BF16)
                ropep = Pool(sc, "rp", 2, [128, 2, 512], F32)
                PTp = Pool(sc, "PT", 6, [128, 512], BF16)
                rsm = Pool(sc, "rsm", 1, [128, 512], F32)
                rs0 = Pool(sc, "rs0", 1, [128, 512], F32)
                otm = Pool(sc, "otm", 1, [128, 512], BF16)
                o32 = Pool(sc, "o32", 2, [128, 128], F32)
                k32 = Pool(sc, "k32", 2, [128, 512], F32)
                S.op("dve", MS(Va[:], 1.0), writes=["Va"])
                S.op("pool", DMA(wv[:], Din["win"][l][:, :, WIN_V:WIN_V + 128]), writes=["wv"], dma=True)
                if smp:
                    ckt = sbt(sc, "ckt", [128, 2, 256], BF16)
                    for tl in range(2):
                        S.op("pool", DMA(ckt[:, tl, :], Din["ckd"][l][tl * 128:(tl + 1) * 128, :]), writes=["ckt"], dma=True)
                        S.op("pool", DMA(Va[:, tl, :, 0:64],
                                         Din["cv"][l][tl * 128:(tl + 1) * 128, :].rearrange("p (g d) -> p g d", g=2)),
                             reads=["Va"], writes=["Va"], dma=True)
                    for tl in range(2):
                        for g in range(2):
                            b = nb()
                            S.op("pe", MM(ps[b][:, 0:128], ckt[:, tl, g * 128:(g + 1) * 128], identb[:], True, True),
                                 reads=["ckt"], writes=[PS(b)])
                            S.op("act", ACT(kT[:, g, tl * 128:(tl + 1) * 128], ps[b][:, 0:128], AF.Copy), reads=[PS(b)],
                                 writes=["kT"])

                def proj_chunk(src, c0, srcs, c0s, gname, gsname, t0, n, si, outs):
                    hc = path.hcol(t0, si)
                    w1_, w1r = src
                    b1 = nb()
                    for kc in range(8):
                        S.op("pe", MM(ps[b1][:, :n], w1_[:, kc, :], hT[:, kc, hc:hc + n], kc == 0, kc == 7),
                             reads=[w1r, "hT"], writes=[PS(b1)])
                    b2 = None
                    rc = rcr = None
                    if smp:
                        w2_, w2r = srcs
                        b2 = nb()
                        for kc in range(8):
                            S.op("pe", MM(ps[b2][:, :n], w2_[:, kc, :], hT[:, kc, hc:hc + n], kc == 0, kc == 7),
                                 reads=[w2r, "hT"], writes=[PS(b2)])
                        rc, rcr = rope_load(ropep, "ropeAc", "ropeAs", t0, n)
                    headnorm(b1, b2, n, V(l, gname), V(l, gsname), bd64b[:], 64,
                             rc[:, 0, :n] if smp else None, rc[:, 1, :n] if smp else None, outs, roperes=rcr)

                def wload(name, c0):
                    w_, wr_ = wch.get()
                    S.op("pool", DMA(w_[:], Din[name][l][:, :, c0:c0 + 128]), writes=[wr_], dma=True)
                    return (w_, wr_)

                for mi in range(4):
                    w1 = wload("win", WIN_Q + mi * 128)
                    w2 = wload("winx", mi * 128) if smp else None
                    for (t0, n, si) in path.blocks:
                        proj_chunk(w1, 0, w2, 0, "qg", "qgs", t0, n, si, [(qT[:, mi, t0:t0 + n], "qT")])
                kfs = {}
                for g in range(2):
                    w1 = wload("winx", 512 + g * 128)
                    w2 = wload("winx", 768 + g * 128) if smp else None
                    for bi_, (t0, n, si) in enumerate(path.blocks):
                        outs = [(kT[:, g, koff + t0:koff + t0 + n], "kT")]
                        if not smp:
                            k3, k3r = k32.get()
                            outs.append((k3[:, :n], k3r))
                        proj_chunk(w1, 0, w2, 0, "kg", "kgs", t0, n, si, outs)
                        if not smp:
                            for j in range(n // 128):
                                b = nb()
                                S.op("pe", MM(ps[b][:, 0:64], k3[0:64, j * 128:(j + 1) * 128], identf[0:64, 0:64], True, True),
                                     reads=[k3r], writes=[PS(b)])
                                o_, orr = o32.get()
                                S.op("dve", CP(o_[:, 0:64], ps[b][:, 0:64]), reads=[PS(b)], writes=[orr])
                                tl0 = t0 - path.seqs[si][0] + j * 128
                                S.op("sp", DMA(O["nk"][si, l, tl0:tl0 + 128, g * 64:(g + 1) * 64], o_[:, 0:64]), reads=[orr], dma=True)
                for (t0, n, si) in path.blocks:
                    hc = path.hcol(t0, si)
                    for j in range(n // 128):
                        b = nb()
                        for kc in range(8):
                            S.op("pe", MM(ps[b][:, 0:128], hT[:, kc, hc + j * 128:hc + (j + 1) * 128], wv[:, kc, :], kc == 0, kc == 7),
                                 reads=["wv", "hT"], writes=[PS(b)])
                        kt = (koff + t0) // 128 + j
                        for g in range(2):
                            S.op("act" if g else "dve",
                                 ACT(Va[:, kt, g, 0:64], ps[b][:, g * 64:(g + 1) * 64], AF.Copy) if g
                                 else CP(Va[:, kt, g, 0:64], ps[b][:, g * 64:(g + 1) * 64]),
                                 reads=[PS(b), "Va"], writes=["Va"])
                        if not smp:
                            o_, orr = o32.get()
                            S.op("dve", CP(o_[:], ps[b][:, 0:128]), reads=[PS(b)], writes=[orr])
                            tl0 = t0 - path.seqs[si][0] + j * 128
                            S.op("sp", DMA(O["nv"][si, l, tl0:tl0 + 128, :], o_[:]), reads=[orr], dma=True)
                for si, (s0, L) in enumerate(path.seqs):
                    if smp:
                        kt0, nkt = 0, (koff + L) // 128
                    else:
                        kt0, nkt = s0 // 128, L // 128
                    for h in range(8):
                        g, par, chunk = h // 4, h % 2, h // 2
                        pr = slice(par * 64, par * 64 + 64)
                        for qs in range(s0, s0 + L, 512):
                            qn = min(512, s0 + L - qs)
                            attend((PTp, rsm, rs0, otm),
                                   lambda: qT[pr, chunk, qs:qs + qn],
                                   lambda kt: kT[pr, g, kt * 128:(kt + 1) * 128],
                                   lambda kt: Va[:, kt, g, :],
                                   nkt, kt0, 64 ** -0.5, par, chunk, qs, qn, ["qT"], ["kT"], ["Va"])
                S.barrier()

        def mla(path, l):
            T, koff, smp = path.T, path.koff, path.sample
            nkt_all = (koff + T) // 128
            with ExitStack() as sc:
                mkpools(sc)
                cqT = sbt(sc, "cqT", [128, 3, T], BF16)
                ckvT = sbt(sc, "ckvT", [128, 2, koff + T], BF16)
                KhT = sbt(sc, "KhT", [128, koff + T], BF16)
                Vh = sbt(sc, "Vh", [128, nkt_all, 128], BF16)
                QhT = Pool(sc, "QhT", 2, [128, 512], BF16)
                wcq = sbt(sc, "wcq", [128, 8, 384], BF16)
                wckv = sbt(sc, "wckv", [128, 8, 256], BF16)
                wkpe = sbt(sc, "wkpe", [128, 8, 64], BF16)
                wuq = sbt(sc, "wuq", [128, 3, 768], BF16)
                wuqx = sbt(sc, "wuqx", [128, 3, 288], BF16)
                S.op("dve", MS(wuqx[:], 0.0), writes=["wuqx"])
                wukv = sbt(sc, "wukv", [128, 2, 1024], BF16)
                ropep = Pool(sc, "rpb", 1, [128, 2, 512], F32)
                PTp = Pool(sc, "PTb", 6, [128, 512], BF16)
                rsm = Pool(sc, "rsmb", 1, [128, 512], F32)
                rs0 = Pool(sc, "rs0b", 1, [128, 512], F32)
                otm = Pool(sc, "otmb", 1, [128, 512], BF16)
                o32 = Pool(sc, "o32b", 2, [128, 256], F32)
                c32 = Pool(sc, "c32", 3, [128, 512], F32) if not smp else None
                S.op("dve", MS(Vh[:], 1.0), writes=["Vh"])
                S.op("pool", DMA(wcq[:], Din["win"][l][:, :, WIN_CQ:WIN_CQ + 384]), writes=["wcq"], dma=True)
                S.op("pool", DMA(wckv[:], Din["win"][l][:, :, WIN_CKV:WIN_CKV + 256]), writes=["wckv"], dma=True)
                S.op("pool", DMA(wkpe[:, :, 0:32], Din["win"][l][:, :, WIN_KPE:WIN_KPE + 32]), writes=["wkpe"], dma=True)
                S.op("pool", DMA(wkpe[:, :, 32:64], Din["winx"][l][:, :, 1024:1056]), writes=["wkpe"], dma=True)
                if cfg.get("mla_stage", 3) >= 3:
                    for kc in range(3):
                        S.op("pool", DMA(wuq[:, kc, :], Din["wuq"][l][:, kc, :]), writes=["wuq"], dma=True)
                        S.op("pool", DMA(wuqx[:, kc, 0:256], Din["wuqx"][l][:, kc, :]), reads=["wuqx"], writes=["wuqx"], dma=True)
                    for kc in range(2):
                        S.op("pool", DMA(wukv[:, kc, :], Din["wukv"][l][:, kc, :]), writes=["wukv"], dma=True)
                if smp:
                    cct = sbt(sc, "cct", [128, 2, 256], BF16)
                    cpt = sbt(sc, "cpt", [128, 2, 64], BF16)
                    S.op("dve", MS(cpt[:], 0.0), writes=["cpt"])
                    for tl in range(2):
                        S.op("pool", DMA(cct[:, tl, :], Din["cckv"][l][tl * 128:(tl + 1) * 128, :]), writes=["cct"], dma=True)
                        S.op("pool", DMA(cpt[:, tl, 0:32], Din["ckpe"][l][tl * 128:(tl + 1) * 128, :]), reads=["cpt"], writes=["cpt"], dma=True)
                    for tl in range(2):
                        for j in range(2):
                            b = nb()
                            S.op("pe", MM(ps[b][:, 0:128], cct[:, tl, j * 128:(j + 1) * 128], identb[:], True, True),
                                 reads=["cct"], writes=[PS(b)])
                            S.op("act", ACT(ckvT[:, j, tl * 128:(tl + 1) * 128], ps[b][:, 0:128], AF.Copy), reads=[PS(b)],
                                 writes=["ckvT"])
                        b = nb()
                        S.op("pe", MM(ps[b][0:64, 0:128], cpt[:, tl, :], identb[:], True, True), reads=["cpt"], writes=[PS(b)])
                        tq, tqr = PP.tm.get()
                        S.op("dve", CP(tq[0:32, 0:128], ps[b][0:32, 0:128]), reads=[PS(b)], writes=[tqr])
                        S.op("act", ACT(KhT[64:96, tl * 128:(tl + 1) * 128], tq[0:32, 0:128], AF.Copy), reads=[tqr],
                             writes=["KhTpe"])
                for (t0, n, si) in path.blocks:
                    hc = path.hcol(t0, si)
                    bs = []
                    for j in range(3):
                        b = nb()
                        bs.append(b)
                        for kc in range(8):
                            S.op("pe", MM(ps[b][:, :n], wcq[:, kc, j * 128:(j + 1) * 128], hT[:, kc, hc:hc + n], kc == 0, kc == 7),
                                 reads=["wcq", "hT"], writes=[PS(b)])
                    b3 = nb()
                    for j in range(3):
                        sq, sqr = PP.sq.get()
                        S.op("act", ACT(sq[:, :n], ps[bs[j]][:, :n], AF.Square), reads=[PS(bs[j])], writes=[sqr])
                        S.op("pe", MM(ps[b3][:, :n], onesf[:], sq[:, :n], j == 0, j == 2), reads=[sqr], writes=[PS(b3)])
                    ln_, lr = PP.ln.get()
                    S.op("act", ACT(ln_[:, :n], ps[b3][:, :n], AF.Ln, bias=epsb[:, 0:1], scale=1.0 / 384), reads=[PS(b3)], writes=[lr])
                    rs_, rr = PP.rs.get()
                    S.op("act", ACT(rs_[:, :n], ln_[:, :n], AF.Exp, scale=-0.5), reads=[lr], writes=[rr])
                    for j in range(3):
                        S.op("dve", STT(cqT[:, j, t0:t0 + n], ps[bs[j]][:, :n], V(l, "mqn", j), rs_[:, :n], ALU.mult, ALU.mult),
                             reads=[PS(bs[j]), rr], writes=["cqT"])
                    bs = []
                    for j in range(2):
                        b = nb()
                        bs.append(b)
                        for kc in range(8):
                            S.op("pe", MM(ps[b][:, :n], wckv[:, kc, j * 128:(j + 1) * 128], hT[:, kc, hc:hc + n], kc == 0, kc == 7),
                                 reads=["wckv", "hT"], writes=[PS(b)])
                    b3 = nb()
                    for j in range(2):
                        sq, sqr = PP.sq.get()
                        S.op("act", ACT(sq[:, :n], ps[bs[j]][:, :n], AF.Square), reads=[PS(bs[j])], writes=[sqr])
                        S.op("pe", MM(ps[b3][:, :n], onesf[:], sq[:, :n], j == 0, j == 1), reads=[sqr], writes=[PS(b3)])
                    ln_, lr = PP.ln.get()
                    S.op("act", ACT(ln_[:, :n], ps[b3][:, :n], AF.Ln, bias=epsb[:, 0:1], scale=1.0 / 256), reads=[PS(b3)], writes=[lr])
                    rs_, rr = PP.rs.get()
                    S.op("act", ACT(rs_[:, :n], ln_[:, :n], AF.Exp, scale=-0.5), reads=[lr], writes=[rr])
                    cf = []
                    for j in range(2):
                        if smp:
                            S.op("dve", STT(ckvT[:, j, koff + t0:koff + t0 + n], ps[bs[j]][:, :n], V(l, "mkvn", j), rs_[:, :n],
                                            ALU.mult, ALU.mult), reads=[PS(bs[j]), rr], writes=["ckvT"])
                        else:
                            c3, c3r = c32.get()
                            S.op("dve", STT(c3[:, :n], ps[bs[j]][:, :n], V(l, "mkvn", j), rs_[:, :n], ALU.mult, ALU.mult),
                                 reads=[PS(bs[j]), rr], writes=[c3r])
                            S.op("act", ACT(ckvT[:, j, t0:t0 + n], c3[:, :n], AF.Copy), reads=[c3r], writes=["ckvT"])
                            cf.append((c3, c3r))
                    if not smp:
                        for jj in range(n // 128):
                            b = nb()
                            for j in range(2):
                                c3, c3r = cf[j]
                                S.op("pe", MM(ps[b][:, j * 128:(j + 1) * 128], c3[:, jj * 128:(jj + 1) * 128], identf[:], j == 0, True),
                                     reads=[c3r], writes=[PS(b)])
                            o_, orr = o32.get()
                            S.op("dve", CP(o_[:], ps[b][:, 0:256]), reads=[PS(b)], writes=[orr])
                            tl0 = t0 - path.seqs[si][0] + jj * 128
                            S.op("sp", DMA(O["nckv"][si, l, tl0:tl0 + 128, :], o_[:]), reads=[orr], dma=True)
                    if cfg.get("mla_stage", 3) < 2:
                        continue
                    b = nb()
                    for kc in range(8):
                        S.op("pe", MM(ps[b][0:64, :n], wkpe[:, kc, 0:64], hT[:, kc, hc:hc + n], kc == 0, kc == 7),
                             reads=["wkpe", "hT"], writes=[PS(b)])
                    if smp:
                        rc, rcr = rope_load(ropep, "ropeBc", "ropeBs", t0, n, slice(0, 64))
                        t1, r1 = PP.tm.get()
                        t2, r2 = PP.tm.get()
                        t3, r3 = PP.tm.get()
                        S.op("dve", TT(t1[0:32, :n], ps[b][0:32, :n], rc[0:32, 0, :n], ALU.mult), reads=[PS(b), rcr], writes=[r1])
                        S.op("dve", TT(t2[32:64, :n], ps[b][32:64, :n], rc[32:64, 1, :n], ALU.mult), reads=[PS(b), rcr], writes=[r2])
                        S.op("act", ACT(t3[0:32, :n], t2[32:64, :n], AF.Copy), reads=[r2], writes=[r3])
                        S.op("dve", TT(t1[0:32, :n], t1[0:32, :n], t3[0:32, :n], ALU.add), reads=[r1, r3], writes=[r1])
                        S.op("act", ACT(KhT[64:96, koff + t0:koff + t0 + n], t1[0:32, :n], AF.Copy), reads=[r1], writes=["KhTpe"])
                    else:
                        c3, c3r = c32.get()
                        S.op("dve", CP(c3[0:64, :n], ps[b][0:64, :n]), reads=[PS(b)], writes=[c3r])
                        S.op("act", ACT(KhT[64:96, t0:t0 + n], c3[0:32, :n], AF.Copy), reads=[c3r], writes=["KhTpe"])
                        for jj in range(n // 128 if cfg.get("kpe_out", True) else 0):
                            b4 = nb()
                            S.op("pe", MM(ps[b4][:, 0:32], c3[0:64, jj * 128:(jj + 1) * 128], identf[0:64, 0:32], True, True),
                                 reads=[c3r], writes=[PS(b4)])
                            o_, orr = o32.get()
                            S.op("dve", CP(o_[:, 0:32], ps[b4][:, 0:32]), reads=[PS(b4)], writes=[orr])
                            tl0 = t0 - path.seqs[si][0] + jj * 128
                            S.op("sp", DMA(O["nkpe"][si, l, tl0:tl0 + 128, :], o_[:, 0:32]), reads=[orr], dma=True)
                ktot = koff + T
                for h in range(8 if cfg.get("mla_stage", 3) >= 3 else 0):
                    par, chunk = h % 2, h // 2
                    for k0 in range(0, ktot, 512):
                        kn = min(512, ktot - k0)
                        b = nb()
                        for kc in range(2):
                            S.op("pe", MM(ps[b][0:64, :kn], wukv[:, kc, h * 128:h * 128 + 64], ckvT[:, kc, k0:k0 + kn], kc == 0, kc == 1),
                                 reads=["wukv", "ckvT"], writes=[PS(b)])
                        S.op("act", ACT(KhT[0:64, k0:k0 + kn], ps[b][0:64, :kn], AF.Copy), reads=[PS(b)], writes=["KhTn"])
                    for kt in range(ktot // 128):
                        b = nb()
                        for kc in range(2):
                            S.op("pe", MM(ps[b][:, 0:64], ckvT[:, kc, kt * 128:(kt + 1) * 128], wukv[:, kc, h * 128 + 64:h * 128 + 128],
                                          kc == 0, kc == 1), reads=["wukv", "ckvT"], writes=[PS(b)])
                        S.op("dve", CP(Vh[:, kt, 0:64], ps[b][:, 0:64]), reads=[PS(b), "Vh"], writes=["Vh"])
                    for si, (s0, L) in enumerate(path.seqs):
                        if smp:
                            kt0, nkt = 0, (koff + L) // 128
                        else:
                            kt0, nkt = s0 // 128, L // 128
                        for qs in range(s0, s0 + L, 512):
                            qn = min(512, s0 + L - qs)
                            b = nb()
                            for kc in range(3):
                                S.op("pe", MM(ps[b][0:96, :qn], wuq[:, kc, h * 96:(h + 1) * 96], cqT[:, kc, qs:qs + qn], kc == 0, kc == 2),
                                     reads=["wuq", "cqT"], writes=[PS(b)])
                            qh, qhr = QhT.get()
                            S.op("act", ACT(qh[0:64, :qn], ps[b][0:64, :qn], AF.Copy), reads=[PS(b)], writes=[(qhr, 0)])
                            if smp:
                                b2 = nb()
                                for kc in range(3):
                                    S.op("pe", MM(ps[b2][0:64, :qn], wuqx[:, kc, h * 32:h * 32 + 64], cqT[:, kc, qs:qs + qn],
                                                  kc == 0, kc == 2), reads=["wuqx", "cqT"], writes=[PS(b2)])
                                rc, rcr = rope_load(ropep, "ropeBc", "ropeBs", qs, qn, slice(0, 96))
                                t1, r1 = PP.tm.get()
                                t2, r2 = PP.tm.get()
                                t3, r3 = PP.tm.get()
                                S.op("dve", TT(t1[64:96, :qn], ps[b][64:96, :qn], rc[64:96, 0, :qn], ALU.mult), reads=[PS(b), rcr], writes=[r1])
                                S.op("dve", TT(t2[0:32, :qn], ps[b2][0:32, :qn], rc[0:32, 1, :qn], ALU.mult), reads=[PS(b2), rcr], writes=[r2])
                                S.op("act", ACT(t3[64:96, :qn], t2[0:32, :qn], AF.Copy), reads=[r2], writes=[r3])
                                S.op("dve", TT(qh[64:96, :qn], t1[64:96, :qn], t3[64:96, :qn], ALU.add), reads=[r1, r3], writes=[(qhr, 1)])
                            else:
                                S.op("dve", CP(qh[64:96, :qn], ps[b][64:96, :qn]), reads=[PS(b)], writes=[(qhr, 1)])
                            attend((PTp, rsm, rs0, otm),
                                   lambda: qh[0:96, :qn],
                                   lambda kt: KhT[0:96, kt * 128:(kt + 1) * 128],
                                   lambda kt: Vh[:, kt, :],
                                   nkt, kt0, 96 ** -0.5, par, chunk, qs, qn, [(qhr, 0), (qhr, 1)], ["KhTn", "KhTpe"], ["Vh"])
                S.barrier()

        def merge(path, l, n_br):
            if cfg.get("nomerge"):
                return
            with ExitStack() as sc:
                wb = sbt(sc, "wb", [128, 4, 1024], BF16)
                wg = sbt(sc, "wg", [128, 8, 1024], BF16)
                wo = sbt(sc, "wo", [128, 8, 1024], BF16)
                mgp = Pool(sc, "mg", 2, [128, 8, 512], BF16)
                sgp = Pool(sc, "sg", 2, [128, 512], F32)
                for mc in range(8):
                    cs_ = slice(mc * 128, (mc + 1) * 128)
                    S.op("pool", DMA(wb[:, :, cs_], Din["wbr"][l, n_br][:, :, cs_]), writes=[("wb", mc)], dma=True)
                    g0 = WIN_G + n_br * 1024 + mc * 128
                    S.op("pool", DMA(wg[:, :, cs_], Din["win"][l][:, :, g0:g0 + 128]), writes=[("wg", mc)], dma=True)
                for mc in range(8):
                    cs_ = slice(mc * 128, (mc + 1) * 128)
                    S.op("pool", DMA(wo[:, :, cs_], Din["wout"][l][:, :, cs_]), writes=[("wo", mc)], dma=True)
                for (t0, n, si) in path.blocks:
                    hc = path.hcol(t0, si)
                    mg, mgr = mgp.get()
                    for mc in range(8):
                        bB = nb()
                        for kc in range(4):
                            S.op("pe", MM(ps[bB][:, :n], wb[:, kc, mc * 128:(mc + 1) * 128], oT[:, kc, t0:t0 + n], kc == 0, kc == 3),
                                 reads=[("wb", mc), "oT"], writes=[PS(bB)])
                        bG = nb()
                        for kc in range(8):
                            S.op("pe", MM(ps[bG][:, :n], wg[:, kc, mc * 128:(mc + 1) * 128], hT[:, kc, hc:hc + n], kc == 0, kc == 7),
                                 reads=[("wg", mc), "hT"], writes=[PS(bG)])
                        sg, sgr = sgp.get()
                        S.op("act", ACT(sg[:, :n], ps[bG][:, :n], AF.Sigmoid), reads=[PS(bG)], writes=[sgr])
                        S.op("dve", TT(mg[:, mc, :n], ps[bB][:, :n], sg[:, :n], ALU.mult), reads=[PS(bB), sgr], writes=[mgr])
                    for mo in range(8):
                        b = nb()
                        for kc in range(8):
                            S.op("pe", MM(ps[b][:, :n], wo[:, kc, mo * 128:(mo + 1) * 128], mg[:, kc, :n], kc == 0, kc == 7),
                                 reads=[("wo", mo), mgr], writes=[PS(b)])
                        S.op("dve", STT(xT[:, mo, t0:t0 + n], ps[b][:, :n], MOD(l, 2, mo, path.ccol), xT[:, mo, t0:t0 + n],
                                        ALU.mult, ALU.add), reads=[PS(b), "xT"], writes=["xT"])
                S.barrier()

        def sin_quarter(pool4, psb, n, sc_ap, b_ap, bc_ap, out_ap, out_res):
            s4, s4r = pool4.get()
            c4, c4r = pool4.get()
            S.op("act", ACT(s4[0:64, :n], ps[psb][0:64, :n], AF.Sin, bias=b_ap, scale=sc_ap), reads=[PS(psb), "hyd"], writes=[s4r])
            S.op("act", ACT(c4[0:64, :n], ps[psb][0:64, :n], AF.Sin, bias=bc_ap, scale=sc_ap), reads=[PS(psb), "hyd"], writes=[c4r])
            S.op("dve", TT(c4[0:64, :n], s4[0:64, :n], c4[0:64, :n], ALU.mult), reads=[s4r, c4r], writes=[c4r])
            S.op("dve", TT(s4[0:64, :n], s4[0:64, :n], s4[0:64, :n], ALU.mult), reads=[s4r], writes=[s4r])
            S.op("dve", TS(s4[0:64, :n], s4[0:64, :n], -2.0, 1.0, ALU.mult, ALU.add), reads=[s4r], writes=[s4r])
            S.op("dve", STT(out_ap, c4[0:64, :n], 4.0, s4[0:64, :n], ALU.mult, ALU.mult), reads=[s4r, c4r], writes=[out_res])

        def hyena(path, l):
            T, L = path.T, path.L
            NT = L // 128
            pfx = "hy%d_" % L
            with ExitStack() as sc:
                h2T = sbt(sc, "h2T", [64, L], BF16)
                with ExitStack() as sc2:
                    h1p = Pool(sc2, "h1p", 2, [64, 512], F32)
                    p4 = Pool(sc2, "p4", 4, [64, 512], F32)
                    zTt = sbt(sc2, "zTt", [64, L], F32)
                    w1t = sbt(sc2, "w1t", [64, 64], F32)
                    w2t = sbt(sc2, "w2t", [64, 64], F32)
                    hyv = sbt(sc2, "hyv", [64, 4], F32)
                    hyd = sbt(sc2, "hyd", [64, 6], F32)
                    S.op("dve", MS(zTt[:], 0.0), writes=["zTt"])
                    S.op("dve", MS(w1t[:], 0.0), writes=["w1t"])
                    S.op("sp", DMA(zTt[0:17, :], Din[pfx + "zT"]), reads=["zTt"], writes=["zTt"], dma=True)
                    S.op("sp", DMA(w1t[0:17, :], Din["hyw1"][l]), reads=["w1t"], writes=["w1t"], dma=True)
                    S.op("sp", DMA(w2t[:], Din["hyw2"][l]), writes=["w2t"], dma=True)
                    S.op("sp", DMA(hyv[:], Din["hyv"][l]), writes=["hyv"], dma=True)
                    for i in range(2):
                        S.op("dve", TS(hyd[:, 3 * i:3 * i + 1], hyv[:, 2 + i:3 + i], 0.25, None, ALU.mult), reads=["hyv"], writes=["hyd"])
                        S.op("dve", TT(hyd[:, 3 * i + 1:3 * i + 2], hyd[:, 3 * i:3 * i + 1], hyv[:, i:i + 1], ALU.mult), reads=["hyd", "hyv"],
                             writes=["hyd"])
                        S.op("dve", TS(hyd[:, 3 * i + 2:3 * i + 3], hyd[:, 3 * i + 1:3 * i + 2], math.pi / 2, None, ALU.add), reads=["hyd"],
                             writes=["hyd"])
                    for c0 in range(0, L, 512):
                        n = min(512, L - c0)
                        b = nb()
                        S.op("pe", MM(ps[b][0:64, :n], w1t[:, :], zTt[:, c0:c0 + n], True, True), reads=["w1t", "zTt"], writes=[PS(b)])
                        h1, h1r = h1p.get()
                        sin_quarter(p4, b, n, hyd[:, 0:1], hyd[:, 1:2], hyd[:, 2:3], h1[:, :n], h1r)
                        b = nb()
                        S.op("pe", MM(ps[b][0:64, :n], w2t[:, :], h1[:, :n], True, True), reads=["w2t", h1r], writes=[PS(b)])
                        sin_quarter(p4, b, n, hyd[:, 3:4], hyd[:, 4:5], hyd[:, 5:6], h2T[:, c0:c0 + n], "h2T")
                    S.barrier()
                wh = sbt(sc, "wh", [128, 8, 384], BF16)
                vfm = sbt(sc, "vfm", [128, 3, T], BF16)
                vtm = sbt(sc, "vtm", [128, T // 128, 128], BF16)
                zA = sbt(sc, "zA", [128, T // 128, 128], BF16)
                z1T = sbt(sc, "z1T", [128, T], BF16)
                sd = sbt(sc, "sd", [128, NT, 3, 128], BF16)
                Y = sbt(sc, "Y", [128, NT, 2, 128], BF16)
                slc = Pool(sc, "slc", 2, [128, NT * 128], BF16)
                sls = Pool(sc, "sls", 2, [128, NT * 128], BF16)
                hbias = sbt(sc, "hbias", [128, 2], F32)
                gtp = Pool(sc, "gtp", 1, [128, 512], F32)
                w3t = sbt(sc, "w3t", [64, 256], BF16)
                cat = sbt(sc, "cat", [128, NT], F32)
                sat = sbt(sc, "sat", [128, NT], F32)
                winp = Pool(sc, "winp", 2, [128, 128], F32)
                hwp = Pool(sc, "hwp", 2, [128, 256], F32)
                abp = Pool(sc, "abp", 2, [128, 256], BF16)
                rl1 = sbt(sc, "rl1", [128, 128], F32)
                l1t = sbt(sc, "l1t", [128, 128], F32)
                ut = Pool(sc, "ut", 1, [128, 512], F32)
                kk = Pool(sc, "kk", 8, [128, 128], F32)
                S.op("sp", DMA(cat[:], Din[pfx + "ca"]), writes=["cat"], dma=True)
                S.op("sp", DMA(sat[:], Din[pfx + "sa"]), writes=["sat"], dma=True)
                for q4 in range(4):
                    for w in range(3):
                        c0 = WIN_HY + w * 512 + q4 * 128
                        S.op("pool", DMA(wh[:, :, w * 128:(w + 1) * 128], Din["win"][l][:, :, c0:c0 + 128]), reads=["wh"], writes=["wh"], dma=True)
                    for si, (s0, Ls) in enumerate(path.seqs):
                        for o0 in range(0, Ls, 384):
                            on = min(384, Ls - o0)
                            hc = path.hcol(s0 + o0, si) - 1
                            for w in range(3):
                                ch = w * 4 + q4
                                b = nb()
                                for kc in range(8):
                                    S.op("pe", MM(ps[b][:, :on + 2], wh[:, kc, w * 128:(w + 1) * 128], hT[:, kc, hc:hc + on + 2], kc == 0, kc == 7),
                                         reads=["wh", "hT"], writes=[PS(b)])
                                u, ur = ut.get()
                                S.op("dve", TS(u[:, :on], ps[b][:, 0:on], V(l, "hsw", ch * 3 + 0), V(l, "hsb", ch), ALU.mult, ALU.add),
                                     reads=[PS(b)], writes=[ur])
                                S.op("dve", STT(u[:, :on], ps[b][:, 1:on + 1], V(l, "hsw", ch * 3 + 1), u[:, :on], ALU.mult, ALU.add),
                                     reads=[PS(b), ur], writes=[ur])
                                S.op("dve", STT(u[:, :on], ps[b][:, 2:on + 2], V(l, "hsw", ch * 3 + 2), u[:, :on], ALU.mult, ALU.add),
                                     reads=[PS(b), ur], writes=[ur])
                                S.op("act", ACT(vfm[:, w, s0 + o0:s0 + o0 + on], u[:, :on], AF.Copy), reads=[ur, "vfm"], writes=["vfm"])
                                for j in range(on // 128 if w == 0 else 0):
                                    b2 = nb()
                                    S.op("pe", MM(ps[b2][:, 0:128], u[:, j * 128:(j + 1) * 128], identf[:], True, True), reads=[ur], writes=[PS(b2)])
                                    tt = (s0 + o0) // 128 + j
                                    S.op("dve", CP(vtm[:, tt, :], ps[b2][:, 0:128]), reads=[PS(b2), "vtm"], writes=["vtm"])
                    for o in range(2):
                        cf = o * 512 + q4 * 128
                        S.op("pool", DMA(w3t[:, 0:128], Din["hyw3"][l][:, cf:cf + 128]), reads=["w3t"], writes=["w3t"], dma=True)
                        S.op("pool", DMA(w3t[:, 128:256], Din["hyw3"][l][:, 1024 + cf:1024 + cf + 128]), reads=["w3t"], writes=["w3t"], dma=True)
                        if o == 0:
                            S.op("sp", DMA(hbias[:], Din["hybiasT"][l, q4]), reads=["hbias"], writes=["hbias"], dma=True)
                        bl = nb()
                        for tt in range(NT):
                            b = nb((bl,))
                            S.op("pe", MM(ps[b][:, 0:256], h2T[:, tt * 128:(tt + 1) * 128], w3t[:, :], True, True), reads=["h2T", "w3t"], writes=[PS(b)])
                            wt_, wr_ = winp.get()
                            S.op("pool", DMA(wt_[:], Din[pfx + "win"][tt * 128:(tt + 1) * 128, q4 * 128:(q4 + 1) * 128]), writes=[wr_], dma=True)
                            hw, hwr = hwp.get()
                            S.op("dve", TT(hw[:, 0:128], ps[b][:, 0:128], wt_[:], ALU.mult), reads=[PS(b), wr_], writes=[hwr])
                            S.op("dve", TT(hw[:, 128:256], ps[b][:, 128:256], wt_[:], ALU.mult), reads=[PS(b), wr_, hwr], writes=[hwr])
                            if tt == 0:
                                S.op("dve", MS(hw[0:1, 128:256], 0.0), reads=[hwr], writes=[hwr])
                            ab, abr = abp.get()
                            S.op("act", ACT(ab[:], hw[:], AF.Abs), reads=[hwr], writes=[abr])
                            S.op("pe", MM(ps[bl][:, 0:256], onesf[:], ab[:], tt == 0, tt == NT - 1), reads=[abr], writes=[PS(bl)])
                            S.op("dve", TT(sd[:, tt, 0, :], hw[:, 0:128], hw[:, 128:256], ALU.add), reads=[hwr, "sd"], writes=["sd"])
                            S.op("dve", TT(sd[:, tt, 1, :], hw[:, 0:128], hw[:, 128:256], ALU.subtract), reads=[hwr, "sd"], writes=["sd"])
                        S.op("act", ACT(l1t[:], ps[bl][:, 128:256], AF.Copy), reads=[PS(bl)], writes=["l1t"])
                        S.op("dve", STT(l1t[:], ps[bl][:, 0:128], EPS, l1t[:], ALU.add, ALU.add), reads=[PS(bl), "l1t"], writes=["l1t"])
                        S.op("dve", lambda e: e.reciprocal(out=rl1[:], in_=l1t[:]), reads=["l1t"], writes=["rl1"])
                        S.op("dve", TS(rl1[:], rl1[:], 1.0 / L, None, ALU.mult), reads=["rl1"], writes=["rl1"])
                        for si, (s0, Ls) in enumerate(path.seqs):
                            tb = s0 // 128

                            def zin(tt):
                                return vtm[:, tb + tt, :] if o == 0 else zA[:, tb + tt, :]
                            zres = "vtm" if o == 0 else "zA"
                            for tt in range(NT):
                                S.op("pool", CP(sd[:, tt, 2, :], zin(tt)), reads=[zres, "sdz"], writes=["sdz"])
                            for fi in range(NT):
                                if not (cfg.get("hy_nodma") and fi > 0):
                                    ct, cr = slc.get()
                                    st_, sr = sls.get()
                                    S.op("sp", DMA(ct[:], Din[pfx + "dc"][fi]), writes=[cr], dma=True)
                                    S.op("act", DMA(st_[:], Din[pfx + "ds"][fi]), writes=[sr], dma=True)
                                bC, bS = nb(), nb()
                                for tt in range(NT):
                                    f1, lst = tt == 0, tt == NT - 1
                                    S.op("pe", MM(ps[bC][:, 0:384], ct[:, tt * 128:(tt + 1) * 128], sd[:, tt, :, :], f1, lst), reads=[cr, "sd", "sdz"], writes=[PS(bC)])
                                    S.op("pe", MM(ps[bS][:, 0:384], st_[:, tt * 128:(tt + 1) * 128], sd[:, tt, :, :], f1, lst), reads=[sr, "sd", "sdz"], writes=[PS(bS)])
                                ca_, sa_ = cat[:, fi:fi + 1], sat[:, fi:fi + 1]
                                t1, r1 = kk.get()
                                kre, krr = kk.get()
                                t2, r2 = kk.get()
                                kim, kir = kk.get()
                                S.op("dve", TS(t1[:], ps[bC][:, 0:128], ca_, None, ALU.mult), reads=[PS(bC), "cat"], writes=[r1])
                                S.op("dve", STT(kre[:], ps[bS][:, 0:128], sa_, t1[:], ALU.mult, ALU.add), reads=[PS(bS), r1, "sat"], writes=[krr])
                                S.op("dve", TS(t2[:], ps[bS][:, 128:256], ca_, None, ALU.mult), reads=[PS(bS), "cat"], writes=[r2])
                                S.op("dve", STT(kim[:], ps[bC][:, 128:256], sa_, t2[:], ALU.mult, ALU.subtract), reads=[PS(bC), r2, "sat"], writes=[kir])
                                S.op("dve", TT(kre[:], kre[:], rl1[:], ALU.mult), reads=[krr, "rl1"], writes=[krr])
                                S.op("dve", TT(kim[:], kim[:], rl1[:], ALU.mult), reads=[kir, "rl1"], writes=[kir])
                                t3, r3 = kk.get()
                                t4, r4 = kk.get()
                                S.op("dve", TT(t3[:], ps[bC][:, 256:384], kre[:], ALU.mult), reads=[PS(bC), krr], writes=[r3])
                                S.op("dve", TT(t4[:], ps[bS][:, 256:384], kim[:], ALU.mult), reads=[PS(bS), kir], writes=[r4])
                                S.op("dve", TT(Y[:, fi, 0, :], t3[:], t4[:], ALU.add), reads=[r3, r4, "Y"], writes=["Y"])
                                S.op("dve", TT(t3[:], ps[bS][:, 256:384], kre[:], ALU.mult), reads=[PS(bS), krr, r3], writes=[r3])
                                S.op("dve", TT(t4[:], ps[bC][:, 256:384], kim[:], ALU.mult), reads=[PS(bC), kir, r4], writes=[r4])
                                S.op("dve", TT(Y[:, fi, 1, :], t3[:], t4[:], ALU.subtract), reads=[r3, r4, "Y"], writes=["Y"])
                            nb4 = (Ls + 511) // 512
                            acc = []
                            for _ in range(nb4):
                                acc.append(nb(tuple(acc)))
                            for fi in range(NT):
                                if not (cfg.get("hy_nodma") and fi > 0):
                                    ct, cr = slc.get()
                                    st_, sr = sls.get()
                                    S.op("sp", DMA(ct[:], Din[pfx + "rc"][fi]), writes=[cr], dma=True)
                                    S.op("act", DMA(st_[:], Din[pfx + "rs"][fi]), writes=[sr], dma=True)
                                for t4 in range(nb4):
                                    n4 = min(512, Ls - t4 * 512)
                                    S.op("pe", MM(ps[acc[t4]][:, :n4], Y[:, fi, 0, :], ct[:, t4 * 512:t4 * 512 + n4], fi == 0, False),
                                         reads=[cr, "Y"], writes=[PS(acc[t4])])
                                    S.op("pe", MM(ps[acc[t4]][:, :n4], Y[:, fi, 1, :], st_[:, t4 * 512:t4 * 512 + n4], False, fi == NT - 1),
                                         reads=[sr, "Y"], writes=[PS(acc[t4])])
                            for t4 in range(nb4):
                                n4 = min(512, Ls - t4 * 512)
                                tg = s0 + t4 * 512
                                zinT = vfm[:, 0, tg:tg + n4] if o == 0 else z1T[:, tg:tg + n4]
                                zinr = "vfm" if o == 0 else "z1T"
                                g_, ggr = gtp.get()
                                S.op("dve", STT(g_[:, :n4], zinT, hbias[:, o:o + 1], ps[acc[t4]][:, :n4], ALU.mult, ALU.add),
                                     reads=[zinr, "hbias", PS(acc[t4])], writes=[ggr])
                                if o == 0:
                                    S.op("dve", TT(z1T[:, tg:tg + n4], g_[:, :n4], vfm[:, 1, tg:tg + n4], ALU.mult), reads=[ggr, "vfm", "z1T"], writes=["z1T"])
                                else:
                                    S.op("dve", TT(oT[:, q4, tg:tg + n4], g_[:, :n4], vfm[:, 2, tg:tg + n4], ALU.mult), reads=[ggr, "vfm", "oT"], writes=["oT"])
                            if o == 0:
                                for tt in range(NT):
                                    b = nb()
                                    S.op("pe", MM(ps[b][:, 0:128], z1T[:, s0 + tt * 128:s0 + (tt + 1) * 128], identb[:], True, True),
                                         reads=["z1T"], writes=[PS(b)])
                                    S.op("act", ACT(zA[:, tb + tt, :], ps[b][:, 0:128], AF.Copy), reads=[PS(b), "zA"], writes=["zA"])
                S.barrier()

        def ffn(path, l):
            with ExitStack() as sc:
                wup = Pool(sc, "wup", 6, [128, 8, 256], BF16)
                wdn = Pool(sc, "wdn", 4, [128, 22, 128], BF16)
                hid = sbt(sc, "hid", [128, 22, 416], BF16)
                ta = Pool(sc, "ta", 2, [128, 416], F32)
                tg = Pool(sc, "tg", 2, [128, 416], F32)
                sg = Pool(sc, "sgf", 2, [128, 416], F32)
                for si, (s0, Ls) in enumerate(path.seqs):
                    nblk = (Ls + 409) // 410
                    for bi in range(nblk):
                        o0 = bi * 410
                        on = min(410, Ls - o0)
                        hc = path.hcol(s0 + o0, si) - 1
                        for j in range(22):
                            if not (cfg.get("ffn_nodma") and (bi > 0 or j > 1)):
                                wt, wr = wup.get()
                                fr = [("fupb", l, kc) for kc in range(8)]
                                S.op("sp", DMA(wt[:, :, 0:128], fupb[l][:, :, j * 128:(j + 1) * 128]), reads=fr, writes=[wr], dma=True)
                                S.op("act", DMA(wt[:, :, 128:256], fupb[l][:, :, 2816 + j * 128:2816 + (j + 1) * 128]), reads=fr, writes=[wr], dma=True)
                            ba, bg = nb(), nb()
                            for kc in range(8):
                                S.op("pe", MM(ps[ba][:, :on + 2], wt[:, kc, 0:128], hT[:, kc, hc:hc + on + 2], kc == 0, kc == 7),
                                     reads=[wr, "hT"], writes=[PS(ba)])
                            for kc in range(8):
                                S.op("pe", MM(ps[bg][:, :on + 2], wt[:, kc, 128:256], hT[:, kc, hc:hc + on + 2], kc == 0, kc == 7),
                                     reads=[wr, "hT"], writes=[PS(bg)])
                            a_, ar = ta.get()
                            g_, gr = tg.get()
                            s_, srr = sg.get()
                            ja, jg = j, 22 + j
                            S.op("dve", TS(a_[:, :on], ps[ba][:, 0:on], V(l, "fcw", ja * 3), None, ALU.mult), reads=[PS(ba)], writes=[ar])
                            S.op("dve", STT(a_[:, :on], ps[ba][:, 1:on + 1], V(l, "fcw", ja * 3 + 1), a_[:, :on], ALU.mult, ALU.add),
                                 reads=[PS(ba), ar], writes=[ar])
                            S.op("dve", STT(a_[:, :on], ps[ba][:, 2:on + 2], V(l, "fcw", ja * 3 + 2), a_[:, :on], ALU.mult, ALU.add),
                                 reads=[PS(ba), ar], writes=[ar])
                            S.op("dve", TS(g_[:, :on], ps[bg][:, 0:on], V(l, "fcw", jg * 3), None, ALU.mult), reads=[PS(bg)], writes=[gr])
                            S.op("dve", STT(g_[:, :on], ps[bg][:, 1:on + 1], V(l, "fcw", jg * 3 + 1), g_[:, :on], ALU.mult, ALU.add),
                                 reads=[PS(bg), gr], writes=[gr])
                            S.op("dve", STT(g_[:, :on], ps[bg][:, 2:on + 2], V(l, "fcw", jg * 3 + 2), g_[:, :on], ALU.mult, ALU.add),
                                 reads=[PS(bg), gr], writes=[gr])
                            S.op("act", ACT(s_[:, :on], g_[:, :on], AF.Silu, bias=V(l, "fcb", jg), scale=1.0), reads=[gr], writes=[srr])
                            S.op("dve", STT(hid[:, j, :on], a_[:, :on], V(l, "fcb", ja), s_[:, :on], ALU.add, ALU.mult),
                                 reads=[ar, srr, "hid"], writes=["hid"])
                        t0 = s0 + o0
                        for mo in range(8):
                            if not (cfg.get("ffn_nodma") and (bi > 0 or mo > 1)):
                                wtd, wrd = wdn.get()
                                S.op("sp" if mo % 2 else "act", DMA(wtd[:], fdnb[l][:, :, mo * 128:(mo + 1) * 128]),
                                     reads=[("fdnb", l, kc) for kc in range(22)], writes=[wrd], dma=True)
                            b = nb()
                            for kc in range(22):
                                S.op("pe", MM(ps[b][:, :on], wtd[:, kc, :], hid[:, kc, :on], kc == 0, kc == 21), reads=[wrd, "hid"], writes=[PS(b)])
                            S.op("dve", STT(xT[:, mo, t0:t0 + on], ps[b][:, :on], MOD(l, 5, mo, path.ccol), xT[:, mo, t0:t0 + on],
                                            ALU.mult, ALU.add), reads=[PS(b), "xT"], writes=["xT"])
                S.barrier()

        cast_done = set()
        bgq = BG

        def queue_ffn_cast(l):
            if l in cast_done:
                return
            cast_done.add(l)
            for kc in range(8):
                bgq.append(lambda l=l, kc=kc: S.op("pool", DMA(fupb[l][:, kc, :], Din["fup"][l][:, kc, :]),
                                                    writes=[("fupb", l, kc)], dma=True))
            for kc in range(22):
                bgq.append(lambda l=l, kc=kc: S.op("pool", DMA(fdnb[l][:, kc, :], Din["fdn"][l][:, kc, :]),
                                                    writes=[("fdnb", l, kc)], dma=True))

        def ensure_ffn_cast(l):
            queue_ffn_cast(l)
            while bgq:
                bgq.pop(0)()

        def derive_gains(l, path):
            for i, (nm, wch) in enumerate((("norm1", 1), ("norm2", 4))):
                for kc in range(8):
                    S.op("dve", STT(gsh[:, i, kc:kc + 1], MOD(l, wch, kc, path.ccol), 1.0, V(l, nm, kc), ALU.add, ALU.mult),
                         reads=["modT", "gsh"], writes=["gsh"])

        def store_y(path):
            with ExitStack() as sc:
                yl = Pool(sc, "yl", 2, [128, 1024], F32)
                for tt in range(path.T // 128):
                    y_, yr = yl.get()
                    for kc2 in range(2):
                        b = nb()
                        for j in range(4):
                            kc = kc2 * 4 + j
                            S.op("pe", MM(ps[b][:, j * 128:(j + 1) * 128], xT[:, kc, tt * 128:(tt + 1) * 128], identf[:], j == 0, True),
                                 reads=["xT"], writes=[PS(b)])
                        S.op("act" if kc2 else "dve",
                             ACT(y_[:, kc2 * 512:(kc2 + 1) * 512], ps[b][:, :], AF.Copy) if kc2 else CP(y_[:, kc2 * 512:(kc2 + 1) * 512], ps[b][:, :]),
                             reads=[PS(b), yr], writes=[yr])
                    S.op("sp", DMA(path.yout[tt * 128:(tt + 1) * 128, :], y_[:]), reads=[yr], dma=True)
                S.barrier()

        paths = []
        if cfg.get("prompt", True):
            paths.append(Path("p", 512, [(0, 256), (256, 256)], False, Din["xp"], O["yp"], 0))
        if cfg.get("sample", True):
            paths.append(Path("s", 2048, [(0, 2048)], True, Din["xs"], O["ys"], 1))
        for path in paths:
            load_x(path)
            for l in range(cfg.get("layers", 2) if not path.sample else cfg.get("slayers", cfg.get("layers", 2))):
                derive_gains(l, path)
                if cfg.get("ffn", True):
                    queue_ffn_cast(l)
                with ExitStack() as scn:
                    mkpools(scn)
                    norm_mod(path, gsh[:, 0, :], lambda kc: MOD(l, 0, kc, path.ccol))
                    S.barrier()
                if cfg.get("gqa", True):
                    gqa(path, l)
                    merge(path, l, 0)
                if cfg.get("mla", True):
                    mla(path, l)
                    merge(path, l, 1)
                if cfg.get("hyena", True):
                    hyena(path, l)
                    merge(path, l, 2)
                if cfg.get("ffn", True):
                    for si, (s0, Ls) in enumerate(path.seqs):
                        c0 = path.hcol(s0, si) - 1
                        c1 = path.hcol(s0 + Ls, si)
                        S.op("dve", MS(hT[:, :, c0:c0 + 1], 0.0), writes=["hT"])
                        S.op("dve", MS(hT[:, :, c1:c1 + 1], 0.0), writes=["hT"])
                    with ExitStack() as scn:
                        mkpools(scn)
                        norm_mod(path, gsh[:, 1, :], lambda kc: MOD(l, 3, kc, path.ccol))
                        S.barrier()
                    ensure_ffn_cast(l)
                    ffn(path, l)
            with ExitStack() as scn:
                mkpools(scn)
                norm_mod(path, V(0, "fin", 0, 8), None, out_dram=True)
                S.barrier()
            store_y(path)
        S.finish()
        S.emit()
        print("kernel: recorded ops", S.nops, S.cnt, S.dcnt, {e: len(v) for e, v in S.streams.items()})
    return nc


_CFG = {}


def kernel(**inputs):
    I = {k: np.asarray(v) for k, v in inputs.items()}
    sh = prep_shared(I)
    ncores = _CFG.get("ncores", NCORES)
    cores = [prep_core(I, c) for c in range(ncores)]

    def sig(d):
        return {k: (v.shape, "bf16" if v.dtype == ml_dtypes.bfloat16 else "f32") for k, v in d.items()}

    nc = build(sig(sh), sig(cores[0]), _CFG)
    in_maps = []
    for c in range(ncores):
        m = dict(sh)
        m.update(cores[c])
        in_maps.append(m)
    if _CFG.get("trace"):
        res = run_bass_kernel_spmd(nc, in_maps, core_ids=list(range(ncores)), trace=True)
        print("EXEC_TIME_NS", res.exec_time_ns)
    else:
        res = run_bass_kernel_spmd(nc, in_maps, core_ids=list(range(ncores)))
    R = list(res.results)
    while len(R) < NCORES:
        R.append(R[0])
    y_prompt = np.concatenate([R[c]["yp"].reshape(2, 256, 1024) for c in range(NCORES)], 0)
    y_sample = np.stack([R[c]["ys"] for c in range(4)], 0)
    nk = np.concatenate([R[c]["nk"].reshape(2, 2, 256, 2, 64) for c in range(NCORES)], 0)
    nv = np.concatenate([R[c]["nv"].reshape(2, 2, 256, 2, 64) for c in range(NCORES)], 0)
    nckv = np.concatenate([R[c]["nckv"] for c in range(NCORES)], 0)
    nkpe = np.concatenate([R[c]["nkpe"] for c in range(NCORES)], 0)
    f = np.float32
    return (y_prompt.astype(f), y_sample.astype(f), nk.astype(f), nv.astype(f), nckv.astype(f), nkpe.astype(f))
```

```python
import math
from contextlib import ExitStack
import numpy as np
import ml_dtypes
import concourse.bass as bass
import concourse.mybir as mybir
from concourse.bass_utils import run_bass_kernel_spmd

F32 = mybir.dt.float32
BF16 = mybir.dt.bfloat16
AF = mybir.ActivationFunctionType
ALU = mybir.AluOpType

D = 1024
EPS = 1e-6
NCORES = 8


class Sched:
    ENG = ("pe", "act", "dve", "pool", "sp")
    NDS = 8
    NOSELF = ("pe",)

    def __init__(self, nc, stack):
        self.nc = nc
        self.stack = stack
        self.streams = {e: [] for e in self.ENG}
        self.cnt = {e: 0 for e in self.ENG}
        self.sem = {e: stack.enter_context(nc.semaphore("s_" + e)) for e in self.ENG}
        self.skey = {e: "s_" + e for e in self.ENG}
        self.epoch = 0
        self.dq = ("sp", "pool", "act")
        self.dsem = {q: [stack.enter_context(nc.semaphore("d_%s%d" % (q, i))) for i in range(self.NDS)]
                     for q in self.dq}
        self.dcnt = {q: [0] * self.NDS for q in self.dq}
        self.dnext = {q: 0 for q in self.dq}
        self.seen = {e: {} for e in self.ENG}
        self.lastw = {}
        self.readers = {}
        self.nops = 0

    def _wait(self, eng, tok):
        key, sem, val, src = tok
        if src == eng and eng in self.NOSELF:
            return
        if self.seen[eng].get(key, 0) >= val:
            return
        self.seen[eng][key] = val
        self.streams[eng].append(("wait", sem, val))

    def op(self, eng, fn, reads=(), writes=(), dma=False):
        toks = []
        for r in reads:
            t = self.lastw.get(r)
            if t is not None:
                toks.append(t)
            if isinstance(r, tuple) and r and r[0] == "ps":
                for t2 in self.readers.get(r, {}).values():
                    if t2[3] != eng:
                        toks.append(t2)
        for w in writes:
            t = self.lastw.get(w)
            if t is not None:
                toks.append(t)
            toks.extend(self.readers.get(w, {}).values())
        for t in toks:
            self._wait(eng, t)
        if dma:
            q = eng
            i = self.dnext[q]
            self.dnext[q] = (i + 1) % self.NDS
            key = "d_%s%d" % (q, i)
            if self.dcnt[q][i] > 0:
                self._wait(eng, (key, self.dsem[q][i], self.dcnt[q][i], None))
            self.dcnt[q][i] += 16
            tok = (key, self.dsem[q][i], self.dcnt[q][i], None)
            self.streams[eng].append(("op", fn, self.dsem[q][i], 16))
        else:
            self.cnt[eng] += 1
            tok = (self.skey[eng], self.sem[eng], self.cnt[eng], eng)
            self.streams[eng].append(("op", fn, self.sem[eng], 1))
        self.nops += 1
        for w in writes:
            self.lastw[w] = tok
            self.readers[w] = {}
        for r in reads:
            d = self.readers.setdefault(r, {})
            old = d.get(tok[0])
            if old is None or old[2] < tok[2]:
                d[tok[0]] = tok
        return tok

    def barrier(self):
        for e in self.ENG:
            for e2 in self.ENG:
                if self.cnt[e2] > 0 and e2 != e:
                    self._wait(e, (self.skey[e2], self.sem[e2], self.cnt[e2], e2))
            for q in self.dq:
                for i in range(self.NDS):
                    if self.dcnt[q][i] > 0:
                        self._wait(e, ("d_%s%d" % (q, i), self.dsem[q][i], self.dcnt[q][i], None))
        self.lastw = {}
        self.readers = {}
        for e in self.ENG:
            if self.cnt[e] > 8000:
                self.epoch += 1
                self.skey[e] = "s_%s_%d" % (e, self.epoch)
                self.sem[e] = self.stack.enter_context(self.nc.semaphore(self.skey[e]))
                self.cnt[e] = 0

    def finish(self):
        for e2 in self.ENG:
            if e2 != "sp" and self.cnt[e2] > 0:
                self._wait("sp", (self.skey[e2], self.sem[e2], self.cnt[e2], e2))
        for q in self.dq:
            for i in range(self.NDS):
                if self.dcnt[q][i] > 0:
                    self._wait("sp", ("d_%s%d" % (q, i), self.dsem[q][i], self.dcnt[q][i], None))

    def emit(self):
        nc = self.nc

        def run(e, eng):
            for it in self.streams[e]:
                if it[0] == "wait":
                    eng.wait_ge(it[1], it[2])
                else:
                    ins = it[1](eng)
                    ins.then_inc(it[2], it[3])

        with nc.Block() as block:
            @block.tensor
            def _(eng):
                run("pe", eng)

            @block.scalar
            def _(eng):
                run("act", eng)

            @block.vector
            def _(eng):
                run("dve", eng)

            @block.gpsimd
            def _(eng):
                run("pool", eng)

            @block.sync
            def _(eng):
                run("sp", eng)


VOFF = {}
_o = 0
for _n, _w in [("norm1", 8), ("norm2", 8), ("qg", 1), ("qgs", 1), ("kg", 1), ("kgs", 1), ("mqn", 3), ("mkvn", 2),
               ("hsw", 36), ("hsb", 12), ("fcw", 132), ("fcb", 44), ("fin", 8), ("bmod", 48)]:
    VOFF[_n] = _o
    _o += _w
NV = _o

WIN_Q, WIN_K, WIN_V, WIN_CQ, WIN_CKV, WIN_KPE, WIN_HY, WIN_G = 0, 512, 640, 768, 1152, 1408, 1440, 2976


def _fm(w, kc):
    return np.ascontiguousarray(w.reshape(kc, 128, w.shape[1]).transpose(1, 0, 2))


def _swap_pairs(w):
    o = np.empty_like(w)
    o[..., 0::2] = w[..., 1::2]
    o[..., 1::2] = w[..., 0::2]
    return o


def _rope_tables(L, rot_dim, grid_w=64, theta=10000.0):
    rows = L // grid_w
    row = np.repeat(np.arange(rows, dtype=np.float32), grid_w)
    col = np.tile(np.arange(grid_w, dtype=np.float32), rows)
    axis_dim = rot_dim // 2
    inv = (theta ** (-np.arange(0, axis_dim, 2, dtype=np.float32) / axis_dim)).astype(np.float32)
    ang = np.concatenate([row[:, None] * inv, col[:, None] * inv], axis=-1).astype(np.float32)
    c = np.cos(ang).astype(np.float32)
    s = np.sin(ang).astype(np.float32)
    cf = np.repeat(c, 2, axis=1).T
    sf = np.repeat(s, 2, axis=1).T.copy()
    sf[0::2] *= -1.0
    return np.ascontiguousarray(cf), np.ascontiguousarray(sf)


def _hy_consts(L):
    t = np.arange(L, dtype=np.float32)
    tn = t / max(L - 1, 1)
    bands = np.linspace(1e-4, 7, 8, dtype=np.float32)
    ang = (np.float32(2.0 * math.pi / L) * t[:, None] * bands[None, :]).astype(np.float32)
    z = np.concatenate([tn[:, None], np.cos(ang), -np.sin(ang)], axis=-1).astype(np.float32)
    min_decay = math.log(1e-2) / 0.3
    max_decay = math.log(1e-2) / 1.5
    deltas = np.abs(np.linspace(min_decay, max_decay, 512, dtype=np.float32))
    window = np.exp(-tn[:, None] * deltas[None, :]).astype(np.float32)
    idx = np.arange(L, dtype=np.float64) + 0.5
    phi = np.pi * np.outer(idx, idx) / L
    C = np.cos(phi)
    Sn = np.sin(phi)
    nt = L // 128

    def slabs(M):
        a = M.reshape(nt, 128, nt, 128).transpose(2, 1, 0, 3)
        return np.ascontiguousarray(a).astype(ml_dtypes.bfloat16)

    alpha = np.pi * idx / (2 * L)
    ca = np.cos(alpha).reshape(nt, 128).T.astype(np.float32)
    sa = np.sin(alpha).reshape(nt, 128).T.astype(np.float32)
    def rslabs(M):
        return np.ascontiguousarray(M.reshape(nt, 128, L)).astype(ml_dtypes.bfloat16)

    return dict(zT=np.ascontiguousarray(z.T), win=window, dc=slabs(C), ds=slabs(Sn), rc=rslabs(C), rs=rslabs(Sn),
                ca=np.ascontiguousarray(ca), sa=np.ascontiguousarray(sa))


def prep_shared(I):
    sh = {}
    sh["wmod"] = np.stack([_fm(I["w_mod"][l], 8) for l in range(2)])
    sh["win"] = np.stack([_fm(I["w_in"][l], 8) for l in range(2)])
    wx = []
    for l in range(2):
        w = I["w_in"][l]
        q = w[:, WIN_Q:WIN_Q + 512]
        k = w[:, WIN_K:WIN_K + 128]
        kd = np.concatenate([k[:, 0:64], k[:, 0:64], k[:, 64:128], k[:, 64:128]], axis=1)
        kpe = w[:, WIN_KPE:WIN_KPE + 32]
        wx.append(_fm(np.concatenate([_swap_pairs(q), kd, _swap_pairs(kd), _swap_pairs(kpe)], axis=1), 8))
    sh["winx"] = np.stack(wx)
    sh["wuq"] = np.stack([_fm(I["mla_w_uq"][l], 3) for l in range(2)])
    ux = []
    for l in range(2):
        w = I["mla_w_uq"][l].reshape(384, 8, 96)[:, :, 64:96].reshape(384, 256)
        ux.append(_fm(_swap_pairs(w), 3))
    sh["wuqx"] = np.stack(ux)
    sh["wukv"] = np.stack([_fm(I["mla_w_ukv"][l], 2) for l in range(2)])
    sh["wbr"] = np.stack([np.stack([_fm(I["w_branch"][l, n], 4) for n in range(3)]) for l in range(2)])
    sh["wout"] = np.stack([_fm(I["w_out"][l], 8) for l in range(2)])
    sh["fup"] = np.stack([_fm(I["ffn_up"][l], 8) for l in range(2)])
    sh["fdn"] = np.stack([_fm(I["ffn_down"][l], 22) for l in range(2)])
    vec = np.zeros((2, 128, NV), np.float32)
    for l in range(2):
        def put(name, arr):
            arr = np.asarray(arr, np.float32)
            vec[l, :, VOFF[name]:VOFF[name] + arr.shape[1]] = arr
        put("norm1", I["norm1"][l].reshape(8, 128).T)
        put("norm2", I["norm2"][l].reshape(8, 128).T)
        qg = I["gqa_q_norm"][l]
        kg = I["gqa_k_norm"][l]
        put("qg", np.tile(qg, 2)[:, None])
        put("qgs", np.tile(_swap_pairs(qg), 2)[:, None])
        put("kg", np.tile(kg, 2)[:, None])
        put("kgs", np.tile(_swap_pairs(kg), 2)[:, None])
        put("mqn", I["mla_q_norm"][l].reshape(3, 128).T)
        put("mkvn", I["mla_kv_norm"][l].reshape(2, 128).T)
        put("hsw", I["hy_short_w"][l].reshape(3, 12, 128).transpose(2, 1, 0).reshape(128, 36))
        put("hsb", I["hy_short_b"][l].reshape(12, 128).T)
        put("fcw", I["ffn_conv_w"][l].reshape(3, 44, 128).transpose(2, 1, 0).reshape(128, 132))
        put("fcb", I["ffn_conv_b"][l].reshape(44, 128).T)
        put("fin", I["final_norm"].reshape(8, 128).T)
        put("bmod", I["b_mod"][l].reshape(48, 128).T)
    sh["vec"] = vec
    sh["hyw1"] = np.ascontiguousarray(I["hy_w1"])
    sh["hyw2"] = np.ascontiguousarray(I["hy_w2"])
    sh["hyw3"] = np.ascontiguousarray(I["hy_w3"])
    hv = np.zeros((2, 64, 4), np.float32)
    for l in range(2):
        hv[l, :, 0] = I["hy_b1"][l]
        hv[l, :, 1] = I["hy_b2"][l]
        hv[l, :, 2] = I["hy_freq"][l, 0]
        hv[l, :, 3] = I["hy_freq"][l, 1]
    sh["hyv"] = hv
    sh["hybias"] = np.ascontiguousarray(I["hy_bias"].reshape(2, 2, 512))
    sh["hybiasT"] = np.ascontiguousarray(I["hy_bias"].reshape(2, 2, 4, 128).transpose(0, 2, 3, 1))
    ident = np.eye(128, dtype=np.float32)
    sh["ident"] = ident
    bd = np.zeros((128, 128), np.float32)
    bd[:64, :64] = 1.0
    bd[64:, 64:] = 1.0
    sh["bd64"] = bd
    ca, sa = _rope_tables(2048, 64)
    sh["ropeAc"] = np.concatenate([ca, ca], 0)
    sh["ropeAs"] = np.concatenate([sa, sa], 0)
    cb, sb_ = _rope_tables(2048, 32)
    rb = np.zeros((128, 2048), np.float32)
    rb[64:96] = cb
    rb[0:32] = cb
    rb[32:64] = cb
    sh["ropeBc"] = rb
    rb2 = np.zeros((128, 2048), np.float32)
    rb2[64:96] = sb_
    rb2[0:32] = sb_
    rb2[32:64] = sb_
    sh["ropeBs"] = rb2
    for L in (256, 2048):
        hc = _hy_consts(L)
        for k, v in hc.items():
            sh["hy%d_%s" % (L, k)] = v
    return sh


def prep_core(I, c):
    b = c % 4
    m = {}
    m["xp"] = np.ascontiguousarray(I["x_prompt"][2 * c:2 * c + 2].reshape(512, 1024))
    m["xs"] = np.ascontiguousarray(I["x_sample"][b])
    cond = np.stack([I["c_ctx"], I["c"][b]], axis=-1)
    m["cond"] = np.ascontiguousarray(cond.reshape(8, 128, 2).transpose(1, 0, 2))
    ck = I["cache_gqa_k"][b]
    m["ckd"] = np.ascontiguousarray(np.stack([ck, ck], axis=3).reshape(2, 256, 256))
    m["cv"] = np.ascontiguousarray(I["cache_gqa_v"][b].reshape(2, 256, 128))
    m["cckv"] = np.ascontiguousarray(I["cache_mla_ckv"][b])
    m["ckpe"] = np.ascontiguousarray(I["cache_mla_kpe"][b])
    return m


class Path:
    def __init__(self, name, T, seqs, sample, xin, yout, ccol):
        self.name, self.T, self.seqs, self.sample = name, T, seqs, sample
        self.xin, self.yout, self.ccol = xin, yout, ccol
        self.koff = 256 if sample else 0
        self.L = seqs[0][1]
        self.blocks = []
        for t0 in range(0, T, 512):
            self.blocks.append((t0, min(512, T - t0), t0 // self.L))

    def hcol(self, t, si):
        return t + 1 + 2 * si


def build(shared_shapes, core_shapes, cfg):
    nc = bass.Bass("TRN2", target_bir_lowering=False)
    Din = {}
    for k, (shp, dt) in list(shared_shapes.items()) + list(core_shapes.items()):
        Din[k] = nc.dram_tensor(k, list(shp), BF16 if dt == "bf16" else F32, kind="ExternalInput").ap()

    def dout(name, shape):
        return nc.dram_tensor(name, list(shape), F32, kind="ExternalOutput").ap()

    O = dict(yp=dout("yp", [512, 1024]), ys=dout("ys", [2048, 1024]),
             nk=dout("nk", [2, 2, 256, 128]), nv=dout("nv", [2, 2, 256, 128]),
             nckv=dout("nckv", [2, 2, 256, 256]), nkpe=dout("nkpe", [2, 2, 256, 32]))

    fupb = nc.dram_tensor("fupb", [2, 128, 8, 5632], BF16, kind="Internal").ap()
    fdnb = nc.dram_tensor("fdnb", [2, 128, 22, 1024], BF16, kind="Internal").ap()

    with ExitStack() as st:
        S = Sched(nc, st)

        _un = [0]

        def sbt(stack, name, shape, dt):
            _un[0] += 1
            return stack.enter_context(nc.sbuf_tensor("%s_%d" % (name, _un[0]), list(shape), dt))

        ps = [st.enter_context(nc.psum_tensor("ps%d" % i, [128, 512], F32)) for i in range(8)]
        pctr = [0]
        BG = []

        def nb(excl=()):
            while True:
                i = pctr[0]
                pctr[0] = (i + 1) % 8
                if i not in excl:
                    return i

        def PS(i):
            return ("ps", i)

        class Pool:
            def __init__(self, stack, name, n, shape, dt):
                self.t = [sbt(stack, "%s%d" % (name, i), shape, dt) for i in range(n)]
                self.name, self.n, self.i = name, n, 0

            def get(self):
                i = self.i
                self.i = (i + 1) % self.n
                return self.t[i], (self.name, i)

        def MM(out, lhsT, rhs, start, stop):
            return lambda e: e.matmul(out, lhsT=lhsT, rhs=rhs, start=start, stop=stop)

        def ACT(out, in_, func, **kw):
            return lambda e: e.activation(out=out, in_=in_, func=func, **kw)

        def TT(out, in0, in1, op):
            return lambda e: e.tensor_tensor(out=out, in0=in0, in1=in1, op=op)

        def STT(out, in0, scalar, in1, op0, op1):
            return lambda e: e.scalar_tensor_tensor(out=out, in0=in0, scalar=scalar, in1=in1, op0=op0, op1=op1)

        def TS(out, in0, s1, s2, op0, op1=None):
            if op1 is None:
                return lambda e: e.tensor_scalar(out=out, in0=in0, scalar1=s1, scalar2=None, op0=op0)
            return lambda e: e.tensor_scalar(out=out, in0=in0, scalar1=s1, scalar2=s2, op0=op0, op1=op1)

        def CP(out, in_):
            return lambda e: e.tensor_copy(out=out, in_=in_)

        def DMA(out, in_):
            return lambda e: e.dma_start(out=out, in_=in_)

        def MS(ap, v):
            return lambda e: e.memset(ap, v)

        xT = sbt(st, "xT", [128, 8, 2048], F32)
        hT = sbt(st, "hT", [128, 8, 2052], BF16)
        oT = sbt(st, "oT", [128, 4, 2048], BF16)
        identf = sbt(st, "identf", [128, 128], F32)
        identb = sbt(st, "identb", [128, 128], BF16)
        bd64 = sbt(st, "bd64", [128, 128], F32)
        onesf = sbt(st, "onesf", [128, 128], BF16)
        bd64b = sbt(st, "bd64b", [128, 128], BF16)
        epsb = sbt(st, "epsb", [128, 1], F32)
        vec = sbt(st, "vec", [128, 2, NV], F32)
        modT = sbt(st, "modT", [128, 2, 48, 2], F32)
        gsh = sbt(st, "gsh", [128, 4, 8], F32)
        class _PP:
            pass
        PP = _PP()
        _pn = [0]

        def mkpools(sc):
            _pn[0] += 1
            k = _pn[0]
            PP.sq = Pool(sc, "sq%d_" % k, 2, [128, 512], BF16)
            PP.ln = Pool(sc, "ln%d_" % k, 1, [128, 512], F32)
            PP.rs = Pool(sc, "rs%d_" % k, 2, [128, 512], F32)
            PP.tm = Pool(sc, "tm%d_" % k, 4, [128, 512], F32)

        S.op("sp", DMA(identf[:], Din["ident"]), writes=["c0"], dma=True)
        S.op("pool", DMA(identb[:], Din["ident"]), writes=["c1"], dma=True)
        S.op("sp", DMA(bd64[:], Din["bd64"]), writes=["c2"], dma=True)
        S.op("pool", DMA(bd64b[:], Din["bd64"]), writes=["c2b"], dma=True)
        S.op("dve", MS(onesf[:], 1.0), writes=["c3"])
        S.op("dve", MS(epsb[:], EPS), writes=["c4"])
        S.op("dve", MS(hT[:], 0.0), writes=["c5"])
        for l in range(2):
            S.op("sp", DMA(vec[:, l, :], Din["vec"][l]), writes=["c6%d" % l], dma=True)

        def V(l, name, j=0, n=1):
            o = VOFF[name] + j
            return vec[:, l, o:o + n]

        with ExitStack() as sc:
            condt = sbt(sc, "condt", [128, 8, 2], F32)
            scond = sbt(sc, "scond", [128, 8, 64], F32)
            modrow = sbt(sc, "modrow", [64, 6144], F32)
            wmp = Pool(sc, "wm", 3, [128, 8, 512], F32)
            S.op("sp", DMA(condt[:], Din["cond"]), writes=["condt"], dma=True)
            S.op("dve", MS(scond[:], 0.0), writes=["scond"])
            S.op("act", ACT(scond[:, :, 0:2], condt[:], AF.Silu), reads=["condt", "scond"], writes=["scond"])
            for l in range(2):
                for sc12 in range(12):
                    wt, wr = wmp.get()
                    S.op("sp" if sc12 % 2 == 0 else "act", DMA(wt[:], Din["wmod"][l][:, :, sc12 * 512:(sc12 + 1) * 512]), writes=[wr], dma=True)
                    b = nb()
                    for kc in range(8):
                        S.op("pe", MM(ps[b][0:64, :], scond[:, kc, :], wt[:, kc, :], kc == 0, kc == 7),
                             reads=[wr, "scond"], writes=[PS(b)])
                    S.op("dve", CP(modrow[:, sc12 * 512:(sc12 + 1) * 512], ps[b][0:64, :]), reads=[PS(b), "modrow"], writes=["modrow"])
                b = nb()
                for ch in range(48):
                    S.op("pe", MM(ps[b][:, 2 * ch:2 * ch + 2], modrow[:, ch * 128:(ch + 1) * 128], identf[0:64, 0:2], True, True),
                         reads=["modrow", "c0"], writes=[PS(b)])
                pv_ = ps[b][:, 0:96].rearrange("p (ch c) -> p ch c", c=2)
                for c in range(2):
                    S.op("dve", TT(modT[:, l, :, c], pv_[:, :, c], V(l, "bmod", 0, 48), ALU.add),
                         reads=[PS(b), "c6%d" % l, "modT"], writes=["modT"])
            S.barrier()

        def HV(path, kc, t0, n):
            L = path.L
            if t0 // L == (t0 + n - 1) // L:
                hc = t0 + 1 + 2 * (t0 // L)
                return hT[:, kc, hc:hc + n]
            ns = n // L
            c0 = t0 + 1 + 2 * (t0 // L)
            return hT[:, kc, c0:c0 + ns * (L + 2)].rearrange("p (s c) -> p s c", c=L + 2)[:, :, 0:L]

        def SEG(ap, path, t0, n):
            L = path.L
            if t0 // L == (t0 + n - 1) // L:
                return ap
            return ap.rearrange("p (s c) -> p s c", c=L)

        def HTOK(path, t):
            return t + 1 + 2 * (t // path.L)

        def MOD(l, which, kc, ccol):
            return modT[:, l, which * 8 + kc, ccol:ccol + 1]

        def load_x(path):
            with ExitStack() as sc:
                xl = Pool(sc, "xl", 2, [128, 1024], F32)
                for tt in range(path.T // 128):
                    t_, r_ = xl.get()
                    S.op("sp", DMA(t_[:], path.xin[tt * 128:(tt + 1) * 128, :]), writes=[r_], dma=True)
                    for kc2 in range(2):
                        b = nb()
                        for j in range(4):
                            kc = kc2 * 4 + j
                            S.op("pe", MM(ps[b][:, j * 128:(j + 1) * 128], t_[:, kc * 128:(kc + 1) * 128], identf[:],
                                          j == 0, True), reads=[r_], writes=[PS(b)])
                        for j in range(4):
                            kc = kc2 * 4 + j
                            S.op("act" if j % 2 else "dve",
                                 (ACT(xT[:, kc, tt * 128:(tt + 1) * 128], ps[b][:, j * 128:(j + 1) * 128], AF.Copy) if j % 2
                                  else CP(xT[:, kc, tt * 128:(tt + 1) * 128], ps[b][:, j * 128:(j + 1) * 128])),
                                 reads=[PS(b)], writes=["xT"])
                S.barrier()

        def norm_mod(path, gcols, shfn, out_dram=None):
            for (t0, n, si) in path.blocks:
                b = nb()
                for kc in range(8):
                    sq, sqr = PP.sq.get()
                    S.op("dve", TT(sq[:, :n], xT[:, kc, t0:t0 + n], xT[:, kc, t0:t0 + n], ALU.mult), reads=["xT"], writes=[sqr])
                    S.op("pe", MM(ps[b][:, :n], onesf[:], sq[:, :n], kc == 0, kc == 7), reads=[sqr], writes=[PS(b)])
                ln_, lr = PP.ln.get()
                S.op("act", ACT(ln_[:, :n], ps[b][:, :n], AF.Ln, bias=epsb[:, 0:1], scale=1.0 / D), reads=[PS(b)], writes=[lr])
                rs_, rr = PP.rs.get()
                S.op("act", ACT(rs_[:, :n], ln_[:, :n], AF.Exp, scale=-0.5), reads=[lr], writes=[rr])
                hc = path.hcol(t0, si)
                for kc in range(8):
                    if out_dram is None:
                        tm_, tr = PP.tm.get()
                        S.op("dve", STT(tm_[:, :n], xT[:, kc, t0:t0 + n], gcols[:, kc:kc + 1], rs_[:, :n], ALU.mult, ALU.mult),
                             reads=["xT", rr, "gsh"], writes=[tr])
                        S.op("dve", TS(HV(path, kc, t0, n), SEG(tm_[:, :n], path, t0, n), shfn(kc), None, ALU.add),
                             reads=[tr, "modT"], writes=["hT"])
                    else:
                        S.op("dve", STT(xT[:, kc, t0:t0 + n], xT[:, kc, t0:t0 + n], gcols[:, kc:kc + 1], rs_[:, :n],
                                        ALU.mult, ALU.mult), reads=["xT", rr], writes=["xT"])

        def headnorm(psr, pss, n, g, gs, ones_mat, nfeat, ropeC, ropeS, outs, roperes=None, prow=slice(0, 128)):
            sq, sqr = PP.sq.get()
            S.op("act", ACT(sq[prow, :n], ps[psr][prow, :n], AF.Square), reads=[PS(psr)], writes=[sqr])
            b3 = nb()
            S.op("pe", MM(ps[b3][prow, :n], ones_mat, sq[prow, :n], True, True), reads=[sqr], writes=[PS(b3)])
            ln_, lr = PP.ln.get()
            S.op("act", ACT(ln_[prow, :n], ps[b3][prow, :n], AF.Ln, bias=epsb[prow, 0:1], scale=1.0 / nfeat),
                 reads=[PS(b3)], writes=[lr])
            rs_, rr = PP.rs.get()
            S.op("act", ACT(rs_[prow, :n], ln_[prow, :n], AF.Exp, scale=-0.5), reads=[lr], writes=[rr])
            t1, r1 = PP.tm.get()
            S.op("dve", STT(t1[prow, :n], ps[psr][prow, :n], g, rs_[prow, :n], ALU.mult, ALU.mult),
                 reads=[PS(psr), rr], writes=[r1])
            if pss is not None:
                t2, r2 = PP.tm.get()
                S.op("dve", STT(t2[prow, :n], ps[pss][prow, :n], gs, rs_[prow, :n], ALU.mult, ALU.mult),
                     reads=[PS(pss), rr], writes=[r2])
                S.op("dve", TT(t1[prow, :n], t1[prow, :n], ropeC, ALU.mult), reads=[r1, roperes], writes=[r1])
                S.op("dve", TT(t2[prow, :n], t2[prow, :n], ropeS, ALU.mult), reads=[r2, roperes], writes=[r2])
                for (ap, res) in outs:
                    S.op("dve", TT(ap, t1[prow, :n], t2[prow, :n], ALU.add), reads=[r1, r2], writes=[res])
            else:
                for (ap, res) in outs:
                    S.op("act", ACT(ap, t1[prow, :n], AF.Copy), reads=[r1], writes=[res])
            return t1, r1

        def attend(sc_pools, qap_fn, kap_fn, vap_fn, nkt, kt0, scale, par, chunk, qs, qn, qres, kres, vres):
            if cfg.get("noattn"):
                return
            if BG:
                BG.pop(0)()
            PTp, rsm, rs0, otm = sc_pools
            bo = nb()
            pend = []

            def pv(item):
                kt, pt, pr = item
                S.op("pe", MM(ps[bo][:, :qn], vap_fn(kt0 + kt), pt[:, :qn], kt == 0, kt == nkt - 1),
                     reads=[pr] + vres, writes=[PS(bo)])
            for kt in range(nkt):
                bs = nb((bo,))
                S.op("pe", MM(ps[bs][:, :qn], kap_fn(kt0 + kt), qap_fn(), True, True), reads=qres + kres, writes=[PS(bs)])
                pt, pr = PTp.get()
                S.op("act", ACT(pt[:, :qn], ps[bs][:, :qn], AF.Exp, scale=scale), reads=[PS(bs)], writes=[pr])
                pend.append((kt, pt, pr))
                if len(pend) > 3:
                    pv(pend.pop(0))
            while pend:
                pv(pend.pop(0))
            r_, rr = rsm.get()
            S.op("dve", lambda e: e.reciprocal(out=r_[64:128, :qn], in_=ps[bo][64:128, :qn]), reads=[PS(bo)], writes=[rr])
            r0, r0r = rs0.get()
            S.op("act", ACT(r0[0:64, :qn], r_[64:128, :qn], AF.Copy), reads=[rr], writes=[r0r])
            if par == 0:
                S.op("dve", TT(oT[0:64, chunk, qs:qs + qn], ps[bo][0:64, :qn], r0[0:64, :qn], ALU.mult),
                     reads=[PS(bo), r0r], writes=["oT"])
            else:
                ot, otr = otm.get()
                S.op("dve", TT(ot[0:64, :qn], ps[bo][0:64, :qn], r0[0:64, :qn], ALU.mult), reads=[PS(bo), r0r], writes=[otr])
                S.op("act", ACT(oT[64:128, chunk, qs:qs + qn], ot[0:64, :qn], AF.Copy), reads=[otr], writes=["oT"])

        def rope_load(sc_pool, tabc, tabs, t0, n, prow=slice(0, 128)):
            rc, rcr = sc_pool.get()
            S.op("sp", DMA(rc[prow, 0, :n], Din[tabc][prow, t0:t0 + n]), writes=[rcr], dma=True)
            S.op("sp", DMA(rc[prow, 1, :n], Din[tabs][prow, t0:t0 + n]), writes=[rcr], dma=True)
            return rc, rcr

        def out_T(src_ap_fn, nrow, prow0, t0, n, dst_fn, res, pool32):
            for j in range(n // 128):
                b = nb()
                S.op("pe", MM(ps[b][:, 0:nrow], src_ap_fn(j), identf[prow0:prow0 + nrow, prow0:prow0 + nrow], True, True),
                     reads=res, writes=[PS(b)])
                o_, orr = pool32.get()
                S.op("dve", CP(o_[:, 0:nrow], ps[b][:, 0:nrow]), reads=[PS(b)], writes=[orr])
                S.op("sp", DMA(dst_fn(j), o_[:, 0:nrow]), reads=[orr], dma=True)

        def gqa(path, l):
            T, koff, smp = path.T, path.koff, path.sample
            nkt_all = (koff + T) // 128
            with ExitStack() as sc:
                mkpools(sc)
                qT = sbt(sc, "qT", [128, 4, T], BF16)
                kT = sbt(sc, "kT", [128, 2, koff + T], BF16)
                Va = sbt(sc, "Va", [128, nkt_all, 2, 128], BF16)
                wch = Pool(sc, "wch", 4, [128, 8, 128], BF16)
                wv = sbt(sc, "wv", [128, 8, 128], BF16)
                ropep = Pool(sc, "rp", 2, [128, 2, 512], F32)
                PTp = Pool(sc, "PT", 6, [128, 512], BF16)
                rsm = Pool(sc, "rsm", 1, [128, 512], F32)
                rs0 = Pool(sc, "rs0", 1, [128, 512], F32)
                otm = Pool(sc, "otm", 1, [128, 512], BF16)
                o32 = Pool(sc, "o32", 2, [128, 128], F32)
                k32 = Pool(sc, "k32", 2, [128, 512], F32)
                S.op("dve", MS(Va[:], 1.0), writes=["Va"])
                S.op("pool", DMA(wv[:], Din["win"][l][:, :, WIN_V:WIN_V + 128]), writes=["wv"], dma=True)
                if smp:
                    ckt = sbt(sc, "ckt", [128, 2, 256], BF16)
                    for tl in range(2):
                        S.op("pool", DMA(ckt[:, tl, :], Din["ckd"][l][tl * 128:(tl + 1) * 128, :]), writes=["ckt"], dma=True)
                        S.op("pool", DMA(Va[:, tl, :, 0:64],
                                         Din["cv"][l][tl * 128:(tl + 1) * 128, :].rearrange("p (g d) -> p g d", g=2)),
                             reads=["Va"], writes=["Va"], dma=True)
                    for tl in range(2):
                        for g in range(2):
                            b = nb()
                            S.op("pe", MM(ps[b][:, 0:128], ckt[:, tl, g * 128:(g + 1) * 128], identb[:], True, True),
                                 reads=["ckt"], writes=[PS(b)])
                            S.op("act", ACT(kT[:, g, tl * 128:(tl + 1) * 128], ps[b][:, 0:128], AF.Copy), reads=[PS(b)],
                                 writes=["kT"])

                def proj_chunk(src, c0, srcs, c0s, gname, gsname, t0, n, si, outs):
                    hc = path.hcol(t0, si)
                    w1_, w1r = src
                    b1 = nb()
                    for kc in range(8):
                        S.op("pe", MM(ps[b1][:, :n], w1_[:, kc, :], HV(path, kc, t0, n), kc == 0, kc == 7),
                             reads=[w1r, "hT"], writes=[PS(b1)])
                    b2 = None
                    rc = rcr = None
                    if smp:
                        w2_, w2r = srcs
                        b2 = nb()
                        for kc in range(8):
                            S.op("pe", MM(ps[b2][:, :n], w2_[:, kc, :], HV(path, kc, t0, n), kc == 0, kc == 7),
                                 reads=[w2r, "hT"], writes=[PS(b2)])
                        rc, rcr = rope_load(ropep, "ropeAc", "ropeAs", t0, n)
                    headnorm(b1, b2, n, V(l, gname), V(l, gsname), bd64b[:], 64,
                             rc[:, 0, :n] if smp else None, rc[:, 1, :n] if smp else None, outs, roperes=rcr)

                def wload(name, c0):
                    w_, wr_ = wch.get()
                    S.op("pool", DMA(w_[:], Din[name][l][:, :, c0:c0 + 128]), writes=[wr_], dma=True)
                    return (w_, wr_)

                for mi in range(4):
                    w1 = wload("win", WIN_Q + mi * 128)
                    w2 = wload("winx", mi * 128) if smp else None
                    for (t0, n, si) in path.blocks:
                        proj_chunk(w1, 0, w2, 0, "qg", "qgs", t0, n, si, [(qT[:, mi, t0:t0 + n], "qT")])
                kfs = {}
                for g in range(2):
                    w1 = wload("winx", 512 + g * 128)
                    w2 = wload("winx", 768 + g * 128) if smp else None
                    for bi_, (t0, n, si) in enumerate(path.blocks):
                        outs = [(kT[:, g, koff + t0:koff + t0 + n], "kT")]
                        if not smp:
                            k3, k3r = k32.get()
                            outs.append((k3[:, :n], k3r))
                        proj_chunk(w1, 0, w2, 0, "kg", "kgs", t0, n, si, outs)
                        if not smp:
                            for j in range(n // 128):
                                b = nb()
                                S.op("pe", MM(ps[b][:, 0:64], k3[0:64, j * 128:(j + 1) * 128], identf[0:64, 0:64], True, True),
                                     reads=[k3r], writes=[PS(b)])
                                o_, orr = o32.get()
                                S.op("dve", CP(o_[:, 0:64], ps[b][:, 0:64]), reads=[PS(b)], writes=[orr])
                                tl0 = (t0 + j * 128) % path.L
                                S.op("sp", DMA(O["nk"][(t0 + j * 128) // path.L, l, tl0:tl0 + 128, g * 64:(g + 1) * 64], o_[:, 0:64]), reads=[orr], dma=True)
                for (t0, n, si) in path.blocks:
                    hc = path.hcol(t0, si)
                    for j in range(n // 128):
                        b = nb()
                        for kc in range(8):
                            S.op("pe", MM(ps[b][:, 0:128], hT[:, kc, HTOK(path, t0 + j * 128):HTOK(path, t0 + j * 128) + 128], wv[:, kc, :], kc == 0, kc == 7),
                                 reads=["wv", "hT"], writes=[PS(b)])
                        kt = (koff + t0) // 128 + j
                        for g in range(2):
                            S.op("act" if g else "dve",
                                 ACT(Va[:, kt, g, 0:64], ps[b][:, g * 64:(g + 1) * 64], AF.Copy) if g
                                 else CP(Va[:, kt, g, 0:64], ps[b][:, g * 64:(g + 1) * 64]),
                                 reads=[PS(b), "Va"], writes=["Va"])
                        if not smp:
                            o_, orr = o32.get()
                            S.op("dve", CP(o_[:], ps[b][:, 0:128]), reads=[PS(b)], writes=[orr])
                            tl0 = (t0 + j * 128) % path.L
                            S.op("sp", DMA(O["nv"][(t0 + j * 128) // path.L, l, tl0:tl0 + 128, :], o_[:]), reads=[orr], dma=True)
                for si, (s0, L) in enumerate(path.seqs):
                    if smp:
                        kt0, nkt = 0, (koff + L) // 128
                    else:
                        kt0, nkt = s0 // 128, L // 128
                    for h in range(8):
                        g, par, chunk = h // 4, h % 2, h // 2
                        pr = slice(par * 64, par * 64 + 64)
                        for qs in range(s0, s0 + L, 512):
                            qn = min(512, s0 + L - qs)
                            attend((PTp, rsm, rs0, otm),
                                   lambda: qT[pr, chunk, qs:qs + qn],
                                   lambda kt: kT[pr, g, kt * 128:(kt + 1) * 128],
                                   lambda kt: Va[:, kt, g, :],
                                   nkt, kt0, 64 ** -0.5, par, chunk, qs, qn, ["qT"], ["kT"], ["Va"])
                S.barrier()

        def mla(path, l):
            T, koff, smp = path.T, path.koff, path.sample
            nkt_all = (koff + T) // 128
            with ExitStack() as sc:
                mkpools(sc)
                cqT = sbt(sc, "cqT", [128, 3, T], BF16)
                ckvT = sbt(sc, "ckvT", [128, 2, koff + T], BF16)
                KhT = sbt(sc, "KhT", [128, koff + T], BF16)
                Vh = sbt(sc, "Vh", [128, nkt_all, 128], BF16)
                QhT = Pool(sc, "QhT", 2, [128, 512], BF16)
                wcq = sbt(sc, "wcq", [128, 8, 384], BF16)
                wckv = sbt(sc, "wckv", [128, 8, 256], BF16)
                wkpe = sbt(sc, "wkpe", [128, 8, 64], BF16)
                wuq = sbt(sc, "wuq", [128, 3, 768], BF16)
                wuqx = sbt(sc, "wuqx", [128, 3, 288], BF16)
                S.op("dve", MS(wuqx[:], 0.0), writes=["wuqx"])
                wukv = sbt(sc, "wukv", [128, 2, 1024], BF16)
                ropep = Pool(sc, "rpb", 1, [128, 2, 512], F32)
                PTp = Pool(sc, "PTb", 6, [128, 512], BF16)
                rsm = Pool(sc, "rsmb", 1, [128, 512], F32)
                rs0 = Pool(sc, "rs0b", 1, [128, 512], F32)
                otm = Pool(sc, "otmb", 1, [128, 512], BF16)
                o32 = Pool(sc, "o32b", 2, [128, 256], F32)
                c32 = Pool(sc, "c32", 3, [128, 512], F32) if not smp else None
                S.op("dve", MS(Vh[:], 1.0), writes=["Vh"])
                S.op("pool", DMA(wcq[:], Din["win"][l][:, :, WIN_CQ:WIN_CQ + 384]), writes=["wcq"], dma=True)
                S.op("pool", DMA(wckv[:], Din["win"][l][:, :, WIN_CKV:WIN_CKV + 256]), writes=["wckv"], dma=True)
                S.op("pool", DMA(wkpe[:, :, 0:32], Din["win"][l][:, :, WIN_KPE:WIN_KPE + 32]), writes=["wkpe"], dma=True)
                S.op("pool", DMA(wkpe[:, :, 32:64], Din["winx"][l][:, :, 1024:1056]), writes=["wkpe"], dma=True)
                if cfg.get("mla_stage", 3) >= 3:
                    for kc in range(3):
                        S.op("pool", DMA(wuq[:, kc, :], Din["wuq"][l][:, kc, :]), writes=["wuq"], dma=True)
                        S.op("pool", DMA(wuqx[:, kc, 0:256], Din["wuqx"][l][:, kc, :]), reads=["wuqx"], writes=["wuqx"], dma=True)
                    for kc in range(2):
                        S.op("pool", DMA(wukv[:, kc, :], Din["wukv"][l][:, kc, :]), writes=["wukv"], dma=True)
                if smp:
                    cct = sbt(sc, "cct", [128, 2, 256], BF16)
                    cpt = sbt(sc, "cpt", [128, 2, 64], BF16)
                    S.op("dve", MS(cpt[:], 0.0), writes=["cpt"])
                    for tl in range(2):
                        S.op("pool", DMA(cct[:, tl, :], Din["cckv"][l][tl * 128:(tl + 1) * 128, :]), writes=["cct"], dma=True)
                        S.op("pool", DMA(cpt[:, tl, 0:32], Din["ckpe"][l][tl * 128:(tl + 1) * 128, :]), reads=["cpt"], writes=["cpt"], dma=True)
                    for tl in range(2):
                        for j in range(2):
                            b = nb()
                            S.op("pe", MM(ps[b][:, 0:128], cct[:, tl, j * 128:(j + 1) * 128], identb[:], True, True),
                                 reads=["cct"], writes=[PS(b)])
                            S.op("act", ACT(ckvT[:, j, tl * 128:(tl + 1) * 128], ps[b][:, 0:128], AF.Copy), reads=[PS(b)],
                                 writes=["ckvT"])
                        b = nb()
                        S.op("pe", MM(ps[b][0:64, 0:128], cpt[:, tl, :], identb[:], True, True), reads=["cpt"], writes=[PS(b)])
                        tq, tqr = PP.tm.get()
                        S.op("dve", CP(tq[0:32, 0:128], ps[b][0:32, 0:128]), reads=[PS(b)], writes=[tqr])
                        S.op("act", ACT(KhT[64:96, tl * 128:(tl + 1) * 128], tq[0:32, 0:128], AF.Copy), reads=[tqr],
                             writes=["KhTpe"])
                for (t0, n, si) in path.blocks:
                    hc = path.hcol(t0, si)
                    bs = []
                    for j in range(3):
                        b = nb()
                        bs.append(b)
                        for kc in range(8):
                            S.op("pe", MM(ps[b][:, :n], wcq[:, kc, j * 128:(j + 1) * 128], HV(path, kc, t0, n), kc == 0, kc == 7),
                                 reads=["wcq", "hT"], writes=[PS(b)])
                    b3 = nb()
                    for j in range(3):
                        sq, sqr = PP.sq.get()
                        S.op("act", ACT(sq[:, :n], ps[bs[j]][:, :n], AF.Square), reads=[PS(bs[j])], writes=[sqr])
                        S.op("pe", MM(ps[b3][:, :n], onesf[:], sq[:, :n], j == 0, j == 2), reads=[sqr], writes=[PS(b3)])
                    ln_, lr = PP.ln.get()
                    S.op("act", ACT(ln_[:, :n], ps[b3][:, :n], AF.Ln, bias=epsb[:, 0:1], scale=1.0 / 384), reads=[PS(b3)], writes=[lr])
                    rs_, rr = PP.rs.get()
                    S.op("act", ACT(rs_[:, :n], ln_[:, :n], AF.Exp, scale=-0.5), reads=[lr], writes=[rr])
                    for j in range(3):
                        S.op("dve", STT(cqT[:, j, t0:t0 + n], ps[bs[j]][:, :n], V(l, "mqn", j), rs_[:, :n], ALU.mult, ALU.mult),
                             reads=[PS(bs[j]), rr], writes=["cqT"])
                    bs = []
                    for j in range(2):
                        b = nb()
                        bs.append(b)
                        for kc in range(8):
                            S.op("pe", MM(ps[b][:, :n], wckv[:, kc, j * 128:(j + 1) * 128], HV(path, kc, t0, n), kc == 0, kc == 7),
                                 reads=["wckv", "hT"], writes=[PS(b)])
                    b3 = nb()
                    for j in range(2):
                        sq, sqr = PP.sq.get()
                        S.op("act", ACT(sq[:, :n], ps[bs[j]][:, :n], AF.Square), reads=[PS(bs[j])], writes=[sqr])
                        S.op("pe", MM(ps[b3][:, :n], onesf[:], sq[:, :n], j == 0, j == 1), reads=[sqr], writes=[PS(b3)])
                    ln_, lr = PP.ln.get()
                    S.op("act", ACT(ln_[:, :n], ps[b3][:, :n], AF.Ln, bias=epsb[:, 0:1], scale=1.0 / 256), reads=[PS(b3)], writes=[lr])
                    rs_, rr = PP.rs.get()
                    S.op("act", ACT(rs_[:, :n], ln_[:, :n], AF.Exp, scale=-0.5), reads=[lr], writes=[rr])
                    cf = []
                    for j in range(2):
                        if smp:
                            S.op("dve", STT(ckvT[:, j, koff + t0:koff + t0 + n], ps[bs[j]][:, :n], V(l, "mkvn", j), rs_[:, :n],
                                            ALU.mult, ALU.mult), reads=[PS(bs[j]), rr], writes=["ckvT"])
                        else:
                            c3, c3r = c32.get()
                            S.op("dve", STT(c3[:, :n], ps[bs[j]][:, :n], V(l, "mkvn", j), rs_[:, :n], ALU.mult, ALU.mult),
                                 reads=[PS(bs[j]), rr], writes=[c3r])
                            S.op("act", ACT(ckvT[:, j, t0:t0 + n], c3[:, :n], AF.Copy), reads=[c3r], writes=["ckvT"])
                            cf.append((c3, c3r))
                    if not smp:
                        for jj in range(n // 128):
                            b = nb()
                            for j in range(2):
                                c3, c3r = cf[j]
                                S.op("pe", MM(ps[b][:, j * 128:(j + 1) * 128], c3[:, jj * 128:(jj + 1) * 128], identf[:], j == 0, True),
                                     reads=[c3r], writes=[PS(b)])
                            o_, orr = o32.get()
                            S.op("dve", CP(o_[:], ps[b][:, 0:256]), reads=[PS(b)], writes=[orr])
                            tl0 = (t0 + jj * 128) % path.L
                            S.op("sp", DMA(O["nckv"][(t0 + jj * 128) // path.L, l, tl0:tl0 + 128, :], o_[:]), reads=[orr], dma=True)
                    if cfg.get("mla_stage", 3) < 2:
                        continue
                    b = nb()
                    for kc in range(8):
                        S.op("pe", MM(ps[b][0:64, :n], wkpe[:, kc, 0:64], HV(path, kc, t0, n), kc == 0, kc == 7),
                             reads=["wkpe", "hT"], writes=[PS(b)])
                    if smp:
                        rc, rcr = rope_load(ropep, "ropeBc", "ropeBs", t0, n, slice(0, 64))
                        t1, r1 = PP.tm.get()
                        t2, r2 = PP.tm.get()
                        t3, r3 = PP.tm.get()
                        S.op("dve", TT(t1[0:32, :n], ps[b][0:32, :n], rc[0:32, 0, :n], ALU.mult), reads=[PS(b), rcr], writes=[r1])
                        S.op("dve", TT(t2[32:64, :n], ps[b][32:64, :n], rc[32:64, 1, :n], ALU.mult), reads=[PS(b), rcr], writes=[r2])
                        S.op("act", ACT(t3[0:32, :n], t2[32:64, :n], AF.Copy), reads=[r2], writes=[r3])
                        S.op("dve", TT(t1[0:32, :n], t1[0:32, :n], t3[0:32, :n], ALU.add), reads=[r1, r3], writes=[r1])
                        S.op("act", ACT(KhT[64:96, koff + t0:koff + t0 + n], t1[0:32, :n], AF.Copy), reads=[r1], writes=["KhTpe"])
                    else:
                        c3, c3r = c32.get()
                        S.op("dve", CP(c3[0:64, :n], ps[b][0:64, :n]), reads=[PS(b)], writes=[c3r])
                        S.op("act", ACT(KhT[64:96, t0:t0 + n], c3[0:32, :n], AF.Copy), reads=[c3r], writes=["KhTpe"])
                        for jj in range(n // 128 if cfg.get("kpe_out", True) else 0):
                            b4 = nb()
                            S.op("pe", MM(ps[b4][:, 0:32], c3[0:64, jj * 128:(jj + 1) * 128], identf[0:64, 0:32], True, True),
                                 reads=[c3r], writes=[PS(b4)])
                            o_, orr = o32.get()
                            S.op("dve", CP(o_[:, 0:32], ps[b4][:, 0:32]), reads=[PS(b4)], writes=[orr])
                            tl0 = (t0 + jj * 128) % path.L
                            S.op("sp", DMA(O["nkpe"][(t0 + jj * 128) // path.L, l, tl0:tl0 + 128, :], o_[:, 0:32]), reads=[orr], dma=True)
                ktot = koff + T
                for h in range(8 if cfg.get("mla_stage", 3) >= 3 else 0):
                    par, chunk = h % 2, h // 2
                    for k0 in range(0, ktot, 512):
                        kn = min(512, ktot - k0)
                        b = nb()
                        for kc in range(2):
                            S.op("pe", MM(ps[b][0:64, :kn], wukv[:, kc, h * 128:h * 128 + 64], ckvT[:, kc, k0:k0 + kn], kc == 0, kc == 1),
                                 reads=["wukv", "ckvT"], writes=[PS(b)])
                        S.op("act", ACT(KhT[0:64, k0:k0 + kn], ps[b][0:64, :kn], AF.Copy), reads=[PS(b)], writes=["KhTn"])
                    for kt in range(ktot // 128):
                        b = nb()
                        for kc in range(2):
                            S.op("pe", MM(ps[b][:, 0:64], ckvT[:, kc, kt * 128:(kt + 1) * 128], wukv[:, kc, h * 128 + 64:h * 128 + 128],
                                          kc == 0, kc == 1), reads=["wukv", "ckvT"], writes=[PS(b)])
                        S.op("dve", CP(Vh[:, kt, 0:64], ps[b][:, 0:64]), reads=[PS(b), "Vh"], writes=["Vh"])
                    for si, (s0, L) in enumerate(path.seqs):
                        if smp:
                            kt0, nkt = 0, (koff + L) // 128
                        else:
                            kt0, nkt = s0 // 128, L // 128
                        for qs in range(s0, s0 + L, 512):
                            qn = min(512, s0 + L - qs)
                            b = nb()
                            for kc in range(3):
                                S.op("pe", MM(ps[b][0:96, :qn], wuq[:, kc, h * 96:(h + 1) * 96], cqT[:, kc, qs:qs + qn], kc == 0, kc == 2),
                                     reads=["wuq", "cqT"], writes=[PS(b)])
                            qh, qhr = QhT.get()
                            S.op("act", ACT(qh[0:64, :qn], ps[b][0:64, :qn], AF.Copy), reads=[PS(b)], writes=[(qhr, 0)])
                            if smp:
                                b2 = nb()
                                for kc in range(3):
                                    S.op("pe", MM(ps[b2][0:64, :qn], wuqx[:, kc, h * 32:h * 32 + 64], cqT[:, kc, qs:qs + qn],
                                                  kc == 0, kc == 2), reads=["wuqx", "cqT"], writes=[PS(b2)])
                                rc, rcr = rope_load(ropep, "ropeBc", "ropeBs", qs, qn, slice(0, 96))
                                t1, r1 = PP.tm.get()
                                t2, r2 = PP.tm.get()
                                t3, r3 = PP.tm.get()
                                S.op("dve", TT(t1[64:96, :qn], ps[b][64:96, :qn], rc[64:96, 0, :qn], ALU.mult), reads=[PS(b), rcr], writes=[r1])
                                S.op("dve", TT(t2[0:32, :qn], ps[b2][0:32, :qn], rc[0:32, 1, :qn], ALU.mult), reads=[PS(b2), rcr], writes=[r2])
                                S.op("act", ACT(t3[64:96, :qn], t2[0:32, :qn], AF.Copy), reads=[r2], writes=[r3])
                                S.op("dve", TT(qh[64:96, :qn], t1[64:96, :qn], t3[64:96, :qn], ALU.add), reads=[r1, r3], writes=[(qhr, 1)])
                            else:
                                S.op("dve", CP(qh[64:96, :qn], ps[b][64:96, :qn]), reads=[PS(b)], writes=[(qhr, 1)])
                            attend((PTp, rsm, rs0, otm),
                                   lambda: qh[0:96, :qn],
                                   lambda kt: KhT[0:96, kt * 128:(kt + 1) * 128],
                                   lambda kt: Vh[:, kt, :],
                                   nkt, kt0, 96 ** -0.5, par, chunk, qs, qn, [(qhr, 0), (qhr, 1)], ["KhTn", "KhTpe"], ["Vh"])
                S.barrier()

        def merge(path, l, n_br):
            if cfg.get("nomerge"):
                return
            with ExitStack() as sc:
                wb = sbt(sc, "wb", [128, 4, 1024], BF16)
                wg = sbt(sc, "wg", [128, 8, 1024], BF16)
                wo = sbt(sc, "wo", [128, 8, 1024], BF16)
                mgp = Pool(sc, "mg", 2, [128, 8, 512], BF16)
                sgp = Pool(sc, "sg", 2, [128, 512], F32)
                for mc in range(8):
                    cs_ = slice(mc * 128, (mc + 1) * 128)
                    S.op("pool", DMA(wb[:, :, cs_], Din["wbr"][l, n_br][:, :, cs_]), writes=[("wb", mc)], dma=True)
                    g0 = WIN_G + n_br * 1024 + mc * 128
                    S.op("pool", DMA(wg[:, :, cs_], Din["win"][l][:, :, g0:g0 + 128]), writes=[("wg", mc)], dma=True)
                for mc in range(8):
                    cs_ = slice(mc * 128, (mc + 1) * 128)
                    S.op("pool", DMA(wo[:, :, cs_], Din["wout"][l][:, :, cs_]), writes=[("wo", mc)], dma=True)
                for (t0, n, si) in path.blocks:
                    hc = path.hcol(t0, si)
                    mg, mgr = mgp.get()
                    for mc in range(8):
                        bB = nb()
                        for kc in range(4):
                            S.op("pe", MM(ps[bB][:, :n], wb[:, kc, mc * 128:(mc + 1) * 128], oT[:, kc, t0:t0 + n], kc == 0, kc == 3),
                                 reads=[("wb", mc), "oT"], writes=[PS(bB)])
                        bG = nb()
                        for kc in range(8):
                            S.op("pe", MM(ps[bG][:, :n], wg[:, kc, mc * 128:(mc + 1) * 128], HV(path, kc, t0, n), kc == 0, kc == 7),
                                 reads=[("wg", mc), "hT"], writes=[PS(bG)])
                        sg, sgr = sgp.get()
                        S.op("act", ACT(sg[:, :n], ps[bG][:, :n], AF.Sigmoid), reads=[PS(bG)], writes=[sgr])
                        S.op("dve", TT(mg[:, mc, :n], ps[bB][:, :n], sg[:, :n], ALU.mult), reads=[PS(bB), sgr], writes=[mgr])
                    for mo in range(8):
                        b = nb()
                        for kc in range(8):
                            S.op("pe", MM(ps[b][:, :n], wo[:, kc, mo * 128:(mo + 1) * 128], mg[:, kc, :n], kc == 0, kc == 7),
                                 reads=[("wo", mo), mgr], writes=[PS(b)])
                        S.op("dve", STT(xT[:, mo, t0:t0 + n], ps[b][:, :n], MOD(l, 2, mo, path.ccol), xT[:, mo, t0:t0 + n],
                                        ALU.mult, ALU.add), reads=[PS(b), "xT"], writes=["xT"])
                S.barrier()

        def sin_quarter(pool4, psb, n, sc_ap, b_ap, bc_ap, out_ap, out_res):
            s4, s4r = pool4.get()
            c4, c4r = pool4.get()
            S.op("act", ACT(s4[0:64, :n], ps[psb][0:64, :n], AF.Sin, bias=b_ap, scale=sc_ap), reads=[PS(psb), "hyd"], writes=[s4r])
            S.op("act", ACT(c4[0:64, :n], ps[psb][0:64, :n], AF.Sin, bias=bc_ap, scale=sc_ap), reads=[PS(psb), "hyd"], writes=[c4r])
            S.op("dve", TT(c4[0:64, :n], s4[0:64, :n], c4[0:64, :n], ALU.mult), reads=[s4r, c4r], writes=[c4r])
            S.op("dve", TT(s4[0:64, :n], s4[0:64, :n], s4[0:64, :n], ALU.mult), reads=[s4r], writes=[s4r])
            S.op("dve", TS(s4[0:64, :n], s4[0:64, :n], -2.0, 1.0, ALU.mult, ALU.add), reads=[s4r], writes=[s4r])
            S.op("dve", STT(out_ap, c4[0:64, :n], 4.0, s4[0:64, :n], ALU.mult, ALU.mult), reads=[s4r, c4r], writes=[out_res])

        def hyena(path, l):
            T, L = path.T, path.L
            NT = L // 128
            pfx = "hy%d_" % L
            with ExitStack() as sc:
                h2T = sbt(sc, "h2T", [64, L], BF16)
                with ExitStack() as sc2:
                    h1p = Pool(sc2, "h1p", 2, [64, 512], F32)
                    p4 = Pool(sc2, "p4", 4, [64, 512], F32)
                    zTt = sbt(sc2, "zTt", [64, L], F32)
                    w1t = sbt(sc2, "w1t", [64, 64], F32)
                    w2t = sbt(sc2, "w2t", [64, 64], F32)
                    hyv = sbt(sc2, "hyv", [64, 4], F32)
                    hyd = sbt(sc2, "hyd", [64, 6], F32)
                    S.op("dve", MS(zTt[:], 0.0), writes=["zTt"])
                    S.op("dve", MS(w1t[:], 0.0), writes=["w1t"])
                    S.op("sp", DMA(zTt[0:17, :], Din[pfx + "zT"]), reads=["zTt"], writes=["zTt"], dma=True)
                    S.op("sp", DMA(w1t[0:17, :], Din["hyw1"][l]), reads=["w1t"], writes=["w1t"], dma=True)
                    S.op("sp", DMA(w2t[:], Din["hyw2"][l]), writes=["w2t"], dma=True)
                    S.op("sp", DMA(hyv[:], Din["hyv"][l]), writes=["hyv"], dma=True)
                    for i in range(2):
                        S.op("dve", TS(hyd[:, 3 * i:3 * i + 1], hyv[:, 2 + i:3 + i], 0.25, None, ALU.mult), reads=["hyv"], writes=["hyd"])
                        S.op("dve", TT(hyd[:, 3 * i + 1:3 * i + 2], hyd[:, 3 * i:3 * i + 1], hyv[:, i:i + 1], ALU.mult), reads=["hyd", "hyv"],
                             writes=["hyd"])
                        S.op("dve", TS(hyd[:, 3 * i + 2:3 * i + 3], hyd[:, 3 * i + 1:3 * i + 2], math.pi / 2, None, ALU.add), reads=["hyd"],
                             writes=["hyd"])
                    for c0 in range(0, L, 512):
                        n = min(512, L - c0)
                        b = nb()
                        S.op("pe", MM(ps[b][0:64, :n], w1t[:, :], zTt[:, c0:c0 + n], True, True), reads=["w1t", "zTt"], writes=[PS(b)])
                        h1, h1r = h1p.get()
                        sin_quarter(p4, b, n, hyd[:, 0:1], hyd[:, 1:2], hyd[:, 2:3], h1[:, :n], h1r)
                        b = nb()
                        S.op("pe", MM(ps[b][0:64, :n], w2t[:, :], h1[:, :n], True, True), reads=["w2t", h1r], writes=[PS(b)])
                        sin_quarter(p4, b, n, hyd[:, 3:4], hyd[:, 4:5], hyd[:, 5:6], h2T[:, c0:c0 + n], "h2T")
                    S.barrier()
                wh = sbt(sc, "wh", [128, 8, 384], BF16)
                vfm = sbt(sc, "vfm", [128, 3, T], BF16)
                vtm = sbt(sc, "vtm", [128, T // 128, 128], BF16)
                zA = sbt(sc, "zA", [128, T // 128, 128], BF16)
                z1T = sbt(sc, "z1T", [128, T], BF16)
                sd = sbt(sc, "sd", [128, NT, 3, 128], BF16)
                Y = sbt(sc, "Y", [128, NT, 2, 128], BF16)
                slc = Pool(sc, "slc", 2, [128, NT * 128], BF16)
                sls = Pool(sc, "sls", 2, [128, NT * 128], BF16)
                hbias = sbt(sc, "hbias", [128, 2], F32)
                gtp = Pool(sc, "gtp", 1, [128, 512], F32)
                w3t = sbt(sc, "w3t", [64, 256], BF16)
                cat = sbt(sc, "cat", [128, NT], F32)
                sat = sbt(sc, "sat", [128, NT], F32)
                winp = Pool(sc, "winp", 2, [128, 128], F32)
                hwp = Pool(sc, "hwp", 2, [128, 256], F32)
                abp = Pool(sc, "abp", 2, [128, 256], BF16)
                rl1 = sbt(sc, "rl1", [128, 128], F32)
                l1t = sbt(sc, "l1t", [128, 128], F32)
                ut = Pool(sc, "ut", 1, [128, 512], F32)
                kk = Pool(sc, "kk", 8, [128, 128], F32)
                S.op("sp", DMA(cat[:], Din[pfx + "ca"]), writes=["cat"], dma=True)
                S.op("sp", DMA(sat[:], Din[pfx + "sa"]), writes=["sat"], dma=True)
                for q4 in range(4):
                    for w in range(3):
                        c0 = WIN_HY + w * 512 + q4 * 128
                        S.op("pool", DMA(wh[:, :, w * 128:(w + 1) * 128], Din["win"][l][:, :, c0:c0 + 128]), reads=["wh"], writes=["wh"], dma=True)
                    for si, (s0, Ls) in enumerate(path.seqs):
                        for o0 in range(0, Ls, 384):
                            on = min(384, Ls - o0)
                            hc = path.hcol(s0 + o0, si) - 1
                            for w in range(3):
                                ch = w * 4 + q4
                                b = nb()
                                for kc in range(8):
                                    S.op("pe", MM(ps[b][:, :on + 2], wh[:, kc, w * 128:(w + 1) * 128], hT[:, kc, hc:hc + on + 2], kc == 0, kc == 7),
                                         reads=["wh", "hT"], writes=[PS(b)])
                                u, ur = ut.get()
                                S.op("dve", TS(u[:, :on], ps[b][:, 0:on], V(l, "hsw", ch * 3 + 0), V(l, "hsb", ch), ALU.mult, ALU.add),
                                     reads=[PS(b)], writes=[ur])
                                S.op("dve", STT(u[:, :on], ps[b][:, 1:on + 1], V(l, "hsw", ch * 3 + 1), u[:, :on], ALU.mult, ALU.add),
                                     reads=[PS(b), ur], writes=[ur])
                                S.op("dve", STT(u[:, :on], ps[b][:, 2:on + 2], V(l, "hsw", ch * 3 + 2), u[:, :on], ALU.mult, ALU.add),
                                     reads=[PS(b), ur], writes=[ur])
                                S.op("act", ACT(vfm[:, w, s0 + o0:s0 + o0 + on], u[:, :on], AF.Copy), reads=[ur, "vfm"], writes=["vfm"])
                                for j in range(on // 128 if w == 0 else 0):
                                    b2 = nb()
                                    S.op("pe", MM(ps[b2][:, 0:128], u[:, j * 128:(j + 1) * 128], identf[:], True, True), reads=[ur], writes=[PS(b2)])
                                    tt = (s0 + o0) // 128 + j
                                    S.op("dve", CP(vtm[:, tt, :], ps[b2][:, 0:128]), reads=[PS(b2), "vtm"], writes=["vtm"])
                    for o in range(2):
                        cf = o * 512 + q4 * 128
                        S.op("pool", DMA(w3t[:, 0:128], Din["hyw3"][l][:, cf:cf + 128]), reads=["w3t"], writes=["w3t"], dma=True)
                        S.op("pool", DMA(w3t[:, 128:256], Din["hyw3"][l][:, 1024 + cf:1024 + cf + 128]), reads=["w3t"], writes=["w3t"], dma=True)
                        if o == 0:
                            S.op("sp", DMA(hbias[:], Din["hybiasT"][l, q4]), reads=["hbias"], writes=["hbias"], dma=True)
                        bl = nb()
                        for tt in range(NT):
                            b = nb((bl,))
                            S.op("pe", MM(ps[b][:, 0:256], h2T[:, tt * 128:(tt + 1) * 128], w3t[:, :], True, True), reads=["h2T", "w3t"], writes=[PS(b)])
                            wt_, wr_ = winp.get()
                            S.op("pool", DMA(wt_[:], Din[pfx + "win"][tt * 128:(tt + 1) * 128, q4 * 128:(q4 + 1) * 128]), writes=[wr_], dma=True)
                            hw, hwr = hwp.get()
                            S.op("dve", TT(hw[:, 0:128], ps[b][:, 0:128], wt_[:], ALU.mult), reads=[PS(b), wr_], writes=[hwr])
                            S.op("dve", TT(hw[:, 128:256], ps[b][:, 128:256], wt_[:], ALU.mult), reads=[PS(b), wr_, hwr], writes=[hwr])
                            if tt == 0:
                                S.op("dve", MS(hw[0:1, 128:256], 0.0), reads=[hwr], writes=[hwr])
                            ab, abr = abp.get()
                            S.op("act", ACT(ab[:], hw[:], AF.Abs), reads=[hwr], writes=[abr])
                            S.op("pe", MM(ps[bl][:, 0:256], onesf[:], ab[:], tt == 0, tt == NT - 1), reads=[abr], writes=[PS(bl)])
                            S.op("dve", TT(sd[:, tt, 0, :], hw[:, 0:128], hw[:, 128:256], ALU.add), reads=[hwr, "sd"], writes=["sd"])
                            S.op("dve", TT(sd[:, tt, 1, :], hw[:, 0:128], hw[:, 128:256], ALU.subtract), reads=[hwr, "sd"], writes=["sd"])
                        S.op("act", ACT(l1t[:], ps[bl][:, 128:256], AF.Copy), reads=[PS(bl)], writes=["l1t"])
                        S.op("dve", STT(l1t[:], ps[bl][:, 0:128], EPS, l1t[:], ALU.add, ALU.add), reads=[PS(bl), "l1t"], writes=["l1t"])
                        S.op("dve", lambda e: e.reciprocal(out=rl1[:], in_=l1t[:]), reads=["l1t"], writes=["rl1"])
                        S.op("dve", TS(rl1[:], rl1[:], 1.0 / L, None, ALU.mult), reads=["rl1"], writes=["rl1"])
                        for si, (s0, Ls) in enumerate(path.seqs):
                            tb = s0 // 128

                            def zin(tt):
                                return vtm[:, tb + tt, :] if o == 0 else zA[:, tb + tt, :]
                            zres = "vtm" if o == 0 else "zA"
                            for tt in range(NT):
                                S.op("pool", CP(sd[:, tt, 2, :], zin(tt)), reads=[zres, "sdz"], writes=["sdz"])
                            for fi in range(NT):
                                if not (cfg.get("hy_nodma") and fi > 0):
                                    ct, cr = slc.get()
                                    st_, sr = sls.get()
                                    S.op("sp", DMA(ct[:], Din[pfx + "dc"][fi]), writes=[cr], dma=True)
                                    S.op("act", DMA(st_[:], Din[pfx + "ds"][fi]), writes=[sr], dma=True)
                                bC, bS = nb(), nb()
                                for tt in range(NT):
                                    f1, lst = tt == 0, tt == NT - 1
                                    S.op("pe", MM(ps[bC][:, 0:384], ct[:, tt * 128:(tt + 1) * 128], sd[:, tt, :, :], f1, lst), reads=[cr, "sd", "sdz"], writes=[PS(bC)])
                                    S.op("pe", MM(ps[bS][:, 0:384], st_[:, tt * 128:(tt + 1) * 128], sd[:, tt, :, :], f1, lst), reads=[sr, "sd", "sdz"], writes=[PS(bS)])
                                ca_, sa_ = cat[:, fi:fi + 1], sat[:, fi:fi + 1]
                                t1, r1 = kk.get()
                                kre, krr = kk.get()
                                t2, r2 = kk.get()
                                kim, kir = kk.get()
                                S.op("dve", TS(t1[:], ps[bC][:, 0:128], ca_, None, ALU.mult), reads=[PS(bC), "cat"], writes=[r1])
                                S.op("dve", STT(kre[:], ps[bS][:, 0:128], sa_, t1[:], ALU.mult, ALU.add), reads=[PS(bS), r1, "sat"], writes=[krr])
                                S.op("dve", TS(t2[:], ps[bS][:, 128:256], ca_, None, ALU.mult), reads=[PS(bS), "cat"], writes=[r2])
                                S.op("dve", STT(kim[:], ps[bC][:, 128:256], sa_, t2[:], ALU.mult, ALU.subtract), reads=[PS(bC), r2, "sat"], writes=[kir])
                                S.op("dve", TT(kre[:], kre[:], rl1[:], ALU.mult), reads=[krr, "rl1"], writes=[krr])
                                S.op("dve", TT(kim[:], kim[:], rl1[:], ALU.mult), reads=[kir, "rl1"], writes=[kir])
                                t3, r3 = kk.get()
                                t4, r4 = kk.get()
                                S.op("dve", TT(t3[:], ps[bC][:, 256:384], kre[:], ALU.mult), reads=[PS(bC), krr], writes=[r3])
                                S.op("dve", TT(t4[:], ps[bS][:, 256:384], kim[:], ALU.mult), reads=[PS(bS), kir], writes=[r4])
                                S.op("dve", TT(Y[:, fi, 0, :], t3[:], t4[:], ALU.add), reads=[r3, r4, "Y"], writes=["Y"])
                                S.op("dve", TT(t3[:], ps[bS][:, 256:384], kre[:], ALU.mult), reads=[PS(bS), krr, r3], writes=[r3])
                                S.op("dve", TT(t4[:], ps[bC][:, 256:384], kim[:], ALU.mult), reads=[PS(bC), kir, r4], writes=[r4])
                                S.op("dve", TT(Y[:, fi, 1, :], t3[:], t4[:], ALU.subtract), reads=[r3, r4, "Y"], writes=["Y"])
                            nb4 = (Ls + 511) // 512
                            acc = []
                            for _ in range(nb4):
                                acc.append(nb(tuple(acc)))
                            for fi in range(NT):
                                if not (cfg.get("hy_nodma") and fi > 0):
                                    ct, cr = slc.get()
                                    st_, sr = sls.get()
                                    S.op("sp", DMA(ct[:], Din[pfx + "rc"][fi]), writes=[cr], dma=True)
                                    S.op("act", DMA(st_[:], Din[pfx + "rs"][fi]), writes=[sr], dma=True)
                                for t4 in range(nb4):
                                    n4 = min(512, Ls - t4 * 512)
                                    S.op("pe", MM(ps[acc[t4]][:, :n4], Y[:, fi, 0, :], ct[:, t4 * 512:t4 * 512 + n4], fi == 0, False),
                                         reads=[cr, "Y"], writes=[PS(acc[t4])])
                                    S.op("pe", MM(ps[acc[t4]][:, :n4], Y[:, fi, 1, :], st_[:, t4 * 512:t4 * 512 + n4], False, fi == NT - 1),
                                         reads=[sr, "Y"], writes=[PS(acc[t4])])
                            for t4 in range(nb4):
                                n4 = min(512, Ls - t4 * 512)
                                tg = s0 + t4 * 512
                                zinT = vfm[:, 0, tg:tg + n4] if o == 0 else z1T[:, tg:tg + n4]
                                zinr = "vfm" if o == 0 else "z1T"
                                g_, ggr = gtp.get()
                                S.op("dve", STT(g_[:, :n4], zinT, hbias[:, o:o + 1], ps[acc[t4]][:, :n4], ALU.mult, ALU.add),
                                     reads=[zinr, "hbias", PS(acc[t4])], writes=[ggr])
                                if o == 0:
                                    S.op("dve", TT(z1T[:, tg:tg + n4], g_[:, :n4], vfm[:, 1, tg:tg + n4], ALU.mult), reads=[ggr, "vfm", "z1T"], writes=["z1T"])
                                else:
                                    S.op("dve", TT(oT[:, q4, tg:tg + n4], g_[:, :n4], vfm[:, 2, tg:tg + n4], ALU.mult), reads=[ggr, "vfm", "oT"], writes=["oT"])
                            if o == 0:
                                for tt in range(NT):
                                    b = nb()
                                    S.op("pe", MM(ps[b][:, 0:128], z1T[:, s0 + tt * 128:s0 + (tt + 1) * 128], identb[:], True, True),
                                         reads=["z1T"], writes=[PS(b)])
                                    S.op("act", ACT(zA[:, tb + tt, :], ps[b][:, 0:128], AF.Copy), reads=[PS(b), "zA"], writes=["zA"])
                S.barrier()

        def ffn(path, l):
            with ExitStack() as sc:
                wup = Pool(sc, "wup", 6, [128, 8, 256], BF16)
                wdn = Pool(sc, "wdn", 4, [128, 22, 128], BF16)
                hid = sbt(sc, "hid", [128, 22, 416], BF16)
                ta = Pool(sc, "ta", 2, [128, 416], F32)
                tg = Pool(sc, "tg", 2, [128, 416], F32)
                sg = Pool(sc, "sgf", 2, [128, 416], F32)
                for si, (s0, Ls) in enumerate(path.seqs):
                    nblk = (Ls + 409) // 410
                    for bi in range(nblk):
                        o0 = bi * 410
                        on = min(410, Ls - o0)
                        hc = path.hcol(s0 + o0, si) - 1
                        for j in range(22):
                            if not (cfg.get("ffn_nodma") and (bi > 0 or j > 1)):
                                wt, wr = wup.get()
                                fr = [("fupb", l, kc) for kc in range(8)]
                                S.op("sp", DMA(wt[:, :, 0:128], fupb[l][:, :, j * 128:(j + 1) * 128]), reads=fr, writes=[wr], dma=True)
                                S.op("act", DMA(wt[:, :, 128:256], fupb[l][:, :, 2816 + j * 128:2816 + (j + 1) * 128]), reads=fr, writes=[wr], dma=True)
                            ba, bg = nb(), nb()
                            for kc in range(8):
                                S.op("pe", MM(ps[ba][:, :on + 2], wt[:, kc, 0:128], hT[:, kc, hc:hc + on + 2], kc == 0, kc == 7),
                                     reads=[wr, "hT"], writes=[PS(ba)])
                            for kc in range(8):
                                S.op("pe", MM(ps[bg][:, :on + 2], wt[:, kc, 128:256], hT[:, kc, hc:hc + on + 2], kc == 0, kc == 7),
                                     reads=[wr, "hT"], writes=[PS(bg)])
                            a_, ar = ta.get()
                            g_, gr = tg.get()
                            s_, srr = sg.get()
                            ja, jg = j, 22 + j
                            S.op("dve", TS(a_[:, :on], ps[ba][:, 0:on], V(l, "fcw", ja * 3), None, ALU.mult), reads=[PS(ba)], writes=[ar])
                            S.op("dve", STT(a_[:, :on], ps[ba][:, 1:on + 1], V(l, "fcw", ja * 3 + 1), a_[:, :on], ALU.mult, ALU.add),
                                 reads=[PS(ba), ar], writes=[ar])
                            S.op("dve", STT(a_[:, :on], ps[ba][:, 2:on + 2], V(l, "fcw", ja * 3 + 2), a_[:, :on], ALU.mult, ALU.add),
                                 reads=[PS(ba), ar], writes=[ar])
                            S.op("dve", TS(g_[:, :on], ps[bg][:, 0:on], V(l, "fcw", jg * 3), None, ALU.mult), reads=[PS(bg)], writes=[gr])
                            S.op("dve", STT(g_[:, :on], ps[bg][:, 1:on + 1], V(l, "fcw", jg * 3 + 1), g_[:, :on], ALU.mult, ALU.add),
                                 reads=[PS(bg), gr], writes=[gr])
                            S.op("dve", STT(g_[:, :on], ps[bg][:, 2:on + 2], V(l, "fcw", jg * 3 + 2), g_[:, :on], ALU.mult, ALU.add),
                                 reads=[PS(bg), gr], writes=[gr])
                            S.op("act", ACT(s_[:, :on], g_[:, :on], AF.Silu, bias=V(l, "fcb", jg), scale=1.0), reads=[gr], writes=[srr])
                            S.op("dve", STT(hid[:, j, :on], a_[:, :on], V(l, "fcb", ja), s_[:, :on], ALU.add, ALU.mult),
                                 reads=[ar, srr, "hid"], writes=["hid"])
                        t0 = s0 + o0
                        for mo in range(8):
                            if not (cfg.get("ffn_nodma") and (bi > 0 or mo > 1)):
                                wtd, wrd = wdn.get()
                                S.op("sp" if mo % 2 else "act", DMA(wtd[:], fdnb[l][:, :, mo * 128:(mo + 1) * 128]),
                                     reads=[("fdnb", l, kc) for kc in range(22)], writes=[wrd], dma=True)
                            b = nb()
                            for kc in range(22):
                                S.op("pe", MM(ps[b][:, :on], wtd[:, kc, :], hid[:, kc, :on], kc == 0, kc == 21), reads=[wrd, "hid"], writes=[PS(b)])
                            S.op("dve", STT(xT[:, mo, t0:t0 + on], ps[b][:, :on], MOD(l, 5, mo, path.ccol), xT[:, mo, t0:t0 + on],
                                            ALU.mult, ALU.add), reads=[PS(b), "xT"], writes=["xT"])
                S.barrier()

        cast_done = set()
        bgq = BG

        def queue_ffn_cast(l):
            if l in cast_done:
                return
            cast_done.add(l)
            for kc in range(8):
                bgq.append(lambda l=l, kc=kc: S.op("pool", DMA(fupb[l][:, kc, :], Din["fup"][l][:, kc, :]),
                                                    writes=[("fupb", l, kc)], dma=True))
            for kc in range(22):
                bgq.append(lambda l=l, kc=kc: S.op("pool", DMA(fdnb[l][:, kc, :], Din["fdn"][l][:, kc, :]),
                                                    writes=[("fdnb", l, kc)], dma=True))

        def ensure_ffn_cast(l):
            queue_ffn_cast(l)
            while bgq:
                bgq.pop(0)()

        def derive_gains(l, path):
            for i, (nm, wch) in enumerate((("norm1", 1), ("norm2", 4))):
                for kc in range(8):
                    S.op("dve", STT(gsh[:, i, kc:kc + 1], MOD(l, wch, kc, path.ccol), 1.0, V(l, nm, kc), ALU.add, ALU.mult),
                         reads=["modT", "gsh"], writes=["gsh"])

        def store_y(path):
            with ExitStack() as sc:
                yl = Pool(sc, "yl", 2, [128, 1024], F32)
                for tt in range(path.T // 128):
                    y_, yr = yl.get()
                    for kc2 in range(2):
                        b = nb()
                        for j in range(4):
                            kc = kc2 * 4 + j
                            S.op("pe", MM(ps[b][:, j * 128:(j + 1) * 128], xT[:, kc, tt * 128:(tt + 1) * 128], identf[:], j == 0, True),
                                 reads=["xT"], writes=[PS(b)])
                        S.op("act" if kc2 else "dve",
                             ACT(y_[:, kc2 * 512:(kc2 + 1) * 512], ps[b][:, :], AF.Copy) if kc2 else CP(y_[:, kc2 * 512:(kc2 + 1) * 512], ps[b][:, :]),
                             reads=[PS(b), yr], writes=[yr])
                    S.op("sp", DMA(path.yout[tt * 128:(tt + 1) * 128, :], y_[:]), reads=[yr], dma=True)
                S.barrier()

        paths = []
        if cfg.get("prompt", True):
            paths.append(Path("p", 512, [(0, 256), (256, 256)], False, Din["xp"], O["yp"], 0))
        if cfg.get("sample", True):
            paths.append(Path("s", 2048, [(0, 2048)], True, Din["xs"], O["ys"], 1))
        for path in paths:
            load_x(path)
            for l in range(cfg.get("layers", 2) if not path.sample else cfg.get("slayers", cfg.get("layers", 2))):
                derive_gains(l, path)
                if cfg.get("ffn", True):
                    queue_ffn_cast(l)
                with ExitStack() as scn:
                    mkpools(scn)
                    norm_mod(path, gsh[:, 0, :], lambda kc: MOD(l, 0, kc, path.ccol))
                    S.barrier()
                if cfg.get("gqa", True):
                    gqa(path, l)
                    merge(path, l, 0)
                if cfg.get("mla", True):
                    mla(path, l)
                    merge(path, l, 1)
                if cfg.get("hyena", True):
                    hyena(path, l)
                    merge(path, l, 2)
                if cfg.get("ffn", True):
                    for si, (s0, Ls) in enumerate(path.seqs):
                        c0 = path.hcol(s0, si) - 1
                        c1 = path.hcol(s0 + Ls, si)
                        S.op("dve", MS(hT[:, :, c0:c0 + 1], 0.0), writes=["hT"])
                        S.op("dve", MS(hT[:, :, c1:c1 + 1], 0.0), writes=["hT"])
                    with ExitStack() as scn:
                        mkpools(scn)
                        norm_mod(path, gsh[:, 1, :], lambda kc: MOD(l, 3, kc, path.ccol))
                        S.barrier()
                    ensure_ffn_cast(l)
                    ffn(path, l)
            with ExitStack() as scn:
                mkpools(scn)
                norm_mod(path, V(0, "fin", 0, 8), None, out_dram=True)
                S.barrier()
            store_y(path)
        S.finish()
        S.emit()
        print("kernel: recorded ops", S.nops, S.cnt, S.dcnt, {e: len(v) for e, v in S.streams.items()})
    return nc


_CFG = {}


def kernel(**inputs):
    I = {k: np.asarray(v) for k, v in inputs.items()}
    sh = prep_shared(I)
    ncores = _CFG.get("ncores", NCORES)
    cores = [prep_core(I, c) for c in range(ncores)]

    def sig(d):
        return {k: (v.shape, "bf16" if v.dtype == ml_dtypes.bfloat16 else "f32") for k, v in d.items()}

    nc = build(sig(sh), sig(cores[0]), _CFG)
    in_maps = []
    for c in range(ncores):
        m = dict(sh)
        m.update(cores[c])
        in_maps.append(m)
    if _CFG.get("trace"):
        res = run_bass_kernel_spmd(nc, in_maps, core_ids=list(range(ncores)), trace=True)
        print("EXEC_TIME_NS", res.exec_time_ns)
    else:
        res = run_bass_kernel_spmd(nc, in_maps, core_ids=list(range(ncores)))
    R = list(res.results)
    while len(R) < NCORES:
        R.append(R[0])
    y_prompt = np.concatenate([R[c]["yp"].reshape(2, 256, 1024) for c in range(NCORES)], 0)
    y_sample = np.stack([R[c]["ys"] for c in range(4)], 0)
    nk = np.concatenate([R[c]["nk"].reshape(2, 2, 256, 2, 64) for c in range(NCORES)], 0)
    nv = np.concatenate([R[c]["nv"].reshape(2, 2, 256, 2, 64) for c in range(NCORES)], 0)
    nckv = np.concatenate([R[c]["nckv"] for c in range(NCORES)], 0)
    nkpe = np.concatenate([R[c]["nkpe"] for c in range(NCORES)], 0)
    f = np.float32
    return (y_prompt.astype(f), y_sample.astype(f), nk.astype(f), nv.astype(f), nckv.astype(f), nkpe.astype(f))
```

```python
import math
from contextlib import ExitStack
import numpy as np
import ml_dtypes
import concourse.bass as bass
import concourse.mybir as mybir
from concourse.bass_utils import run_bass_kernel_spmd

F32 = mybir.dt.float32
BF16 = mybir.dt.bfloat16
AF = mybir.ActivationFunctionType
ALU = mybir.AluOpType

D = 1024
EPS = 1e-6
NCORES = 8


class Sched:
    ENG = ("pe", "act", "dve", "pool", "sp")
    NDS = 8
    NOSELF = ("pe",)

    def __init__(self, nc, stack):
        self.nc = nc
        self.stack = stack
        self.streams = {e: [] for e in self.ENG}
        self.cnt = {e: 0 for e in self.ENG}
        self.sem = {e: stack.enter_context(nc.semaphore("s_" + e)) for e in self.ENG}
        self.skey = {e: "s_" + e for e in self.ENG}
        self.epoch = 0
        self.dq = ("sp", "pool", "act")
        self.dsem = {q: [stack.enter_context(nc.semaphore("d_%s%d" % (q, i))) for i in range(self.NDS)]
                     for q in self.dq}
        self.dcnt = {q: [0] * self.NDS for q in self.dq}
        self.dnext = {q: 0 for q in self.dq}
        self.seen = {e: {} for e in self.ENG}
        self.lastw = {}
        self.readers = {}
        self.nops = 0

    def _wait(self, eng, tok):
        key, sem, val, src = tok
        if src == eng and eng in self.NOSELF:
            return
        if self.seen[eng].get(key, 0) >= val:
            return
        self.seen[eng][key] = val
        self.streams[eng].append(("wait", sem, val))

    def op(self, eng, fn, reads=(), writes=(), dma=False):
        toks = []
        for r in reads:
            t = self.lastw.get(r)
            if t is not None:
                toks.append(t)
            if isinstance(r, tuple) and r and r[0] == "ps":
                for t2 in self.readers.get(r, {}).values():
                    if t2[3] != eng:
                        toks.append(t2)
        for w in writes:
            t = self.lastw.get(w)
            if t is not None:
                toks.append(t)
            toks.extend(self.readers.get(w, {}).values())
        for t in toks:
            self._wait(eng, t)
        if dma:
            q = eng
            i = self.dnext[q]
            self.dnext[q] = (i + 1) % self.NDS
            key = "d_%s%d" % (q, i)
            if self.dcnt[q][i] > 0:
                self._wait(eng, (key, self.dsem[q][i], self.dcnt[q][i], None))
            self.dcnt[q][i] += 16
            tok = (key, self.dsem[q][i], self.dcnt[q][i], None)
            self.streams[eng].append(("op", fn, self.dsem[q][i], 16))
        else:
            self.cnt[eng] += 1
            tok = (self.skey[eng], self.sem[eng], self.cnt[eng], eng)
            self.streams[eng].append(("op", fn, self.sem[eng], 1))
        self.nops += 1
        for w in writes:
            self.lastw[w] = tok
            self.readers[w] = {}
        for r in reads:
            d = self.readers.setdefault(r, {})
            old = d.get(tok[0])
            if old is None or old[2] < tok[2]:
                d[tok[0]] = tok
        return tok

    def barrier(self):
        for e in self.ENG:
            for e2 in self.ENG:
                if self.cnt[e2] > 0 and e2 != e:
                    self._wait(e, (self.skey[e2], self.sem[e2], self.cnt[e2], e2))
            for q in self.dq:
                for i in range(self.NDS):
                    if self.dcnt[q][i] > 0:
                        self._wait(e, ("d_%s%d" % (q, i), self.dsem[q][i], self.dcnt[q][i], None))
        self.lastw = {}
        self.readers = {}
        for e in self.ENG:
            if self.cnt[e] > 8000:
                self.epoch += 1
                self.skey[e] = "s_%s_%d" % (e, self.epoch)
                self.sem[e] = self.stack.enter_context(self.nc.semaphore(self.skey[e]))
                self.cnt[e] = 0

    def finish(self):
        for e2 in self.ENG:
            if e2 != "sp" and self.cnt[e2] > 0:
                self._wait("sp", (self.skey[e2], self.sem[e2], self.cnt[e2], e2))
        for q in self.dq:
            for i in range(self.NDS):
                if self.dcnt[q][i] > 0:
                    self._wait("sp", ("d_%s%d" % (q, i), self.dsem[q][i], self.dcnt[q][i], None))

    def emit(self):
        nc = self.nc

        def run(e, eng):
            for it in self.streams[e]:
                if it[0] == "wait":
                    eng.wait_ge(it[1], it[2])
                else:
                    ins = it[1](eng)
                    ins.then_inc(it[2], it[3])

        with nc.Block() as block:
            @block.tensor
            def _(eng):
                run("pe", eng)

            @block.scalar
            def _(eng):
                run("act", eng)

            @block.vector
            def _(eng):
                run("dve", eng)

            @block.gpsimd
            def _(eng):
                run("pool", eng)

            @block.sync
            def _(eng):
                run("sp", eng)


VOFF = {}
_o = 0
for _n, _w in [("norm1", 8), ("norm2", 8), ("qg", 1), ("qgs", 1), ("kg", 1), ("kgs", 1), ("mqn", 3), ("mkvn", 2),
               ("hsw", 36), ("hsb", 12), ("fcw", 132), ("fcb", 44), ("fin", 8), ("bmod", 48)]:
    VOFF[_n] = _o
    _o += _w
NV = _o

WIN_Q, WIN_K, WIN_V, WIN_CQ, WIN_CKV, WIN_KPE, WIN_HY, WIN_G = 0, 512, 640, 768, 1152, 1408, 1440, 2976


def _fm(w, kc):
    return np.ascontiguousarray(w.reshape(kc, 128, w.shape[1]).transpose(1, 0, 2))


def _swap_pairs(w):
    o = np.empty_like(w)
    o[..., 0::2] = w[..., 1::2]
    o[..., 1::2] = w[..., 0::2]
    return o


def _rope_tables(L, rot_dim, grid_w=64, theta=10000.0):
    rows = L // grid_w
    row = np.repeat(np.arange(rows, dtype=np.float32), grid_w)
    col = np.tile(np.arange(grid_w, dtype=np.float32), rows)
    axis_dim = rot_dim // 2
    inv = (theta ** (-np.arange(0, axis_dim, 2, dtype=np.float32) / axis_dim)).astype(np.float32)
    ang = np.concatenate([row[:, None] * inv, col[:, None] * inv], axis=-1).astype(np.float32)
    c = np.cos(ang).astype(np.float32)
    s = np.sin(ang).astype(np.float32)
    cf = np.repeat(c, 2, axis=1).T
    sf = np.repeat(s, 2, axis=1).T.copy()
    sf[0::2] *= -1.0
    return np.ascontiguousarray(cf), np.ascontiguousarray(sf)


def _hy_consts(L):
    t = np.arange(L, dtype=np.float32)
    tn = t / max(L - 1, 1)
    bands = np.linspace(1e-4, 7, 8, dtype=np.float32)
    ang = (np.float32(2.0 * math.pi / L) * t[:, None] * bands[None, :]).astype(np.float32)
    z = np.concatenate([tn[:, None], np.cos(ang), -np.sin(ang)], axis=-1).astype(np.float32)
    min_decay = math.log(1e-2) / 0.3
    max_decay = math.log(1e-2) / 1.5
    deltas = np.abs(np.linspace(min_decay, max_decay, 512, dtype=np.float32))
    window = np.exp(-tn[:, None] * deltas[None, :]).astype(np.float32)
    idx = np.arange(L, dtype=np.float64) + 0.5
    phi = np.pi * np.outer(idx, idx) / L
    C = np.cos(phi)
    Sn = np.sin(phi)
    nt = L // 128

    def slabs(M):
        a = M.reshape(nt, 128, nt, 128).transpose(2, 1, 0, 3)
        return np.ascontiguousarray(a).astype(ml_dtypes.bfloat16)

    alpha = np.pi * idx / (2 * L)
    ca = np.cos(alpha).reshape(nt, 128).T.astype(np.float32)
    sa = np.sin(alpha).reshape(nt, 128).T.astype(np.float32)
    def rslabs(M):
        return np.ascontiguousarray(M.reshape(nt, 128, L)).astype(ml_dtypes.bfloat16)

    return dict(zT=np.ascontiguousarray(z.T), win=window, dc=slabs(C), ds=slabs(Sn), rc=rslabs(C), rs=rslabs(Sn),
                ca=np.ascontiguousarray(ca), sa=np.ascontiguousarray(sa))


def prep_shared(I):
    sh = {}
    sh["wmod"] = np.stack([_fm(I["w_mod"][l], 8) for l in range(2)])
    sh["win"] = np.stack([_fm(I["w_in"][l], 8) for l in range(2)])
    wx = []
    for l in range(2):
        w = I["w_in"][l]
        q = w[:, WIN_Q:WIN_Q + 512]
        k = w[:, WIN_K:WIN_K + 128]
        kd = np.concatenate([k[:, 0:64], k[:, 0:64], k[:, 64:128], k[:, 64:128]], axis=1)
        kpe = w[:, WIN_KPE:WIN_KPE + 32]
        wx.append(_fm(np.concatenate([_swap_pairs(q), kd, _swap_pairs(kd), _swap_pairs(kpe)], axis=1), 8))
    sh["winx"] = np.stack(wx)
    sh["wuq"] = np.stack([_fm(I["mla_w_uq"][l], 3) for l in range(2)])
    ux = []
    for l in range(2):
        w = I["mla_w_uq"][l].reshape(384, 8, 96)[:, :, 64:96].reshape(384, 256)
        ux.append(_fm(_swap_pairs(w), 3))
    sh["wuqx"] = np.stack(ux)
    sh["wukv"] = np.stack([_fm(I["mla_w_ukv"][l], 2) for l in range(2)])
    sh["wbr"] = np.stack([np.stack([_fm(I["w_branch"][l, n], 4) for n in range(3)]) for l in range(2)])
    sh["wout"] = np.stack([_fm(I["w_out"][l], 8) for l in range(2)])
    sh["fup"] = np.stack([_fm(I["ffn_up"][l], 8) for l in range(2)])
    sh["fdn"] = np.stack([_fm(I["ffn_down"][l], 22) for l in range(2)])
    vec = np.zeros((2, 128, NV), np.float32)
    for l in range(2):
        def put(name, arr):
            arr = np.asarray(arr, np.float32)
            vec[l, :, VOFF[name]:VOFF[name] + arr.shape[1]] = arr
        put("norm1", I["norm1"][l].reshape(8, 128).T)
        put("norm2", I["norm2"][l].reshape(8, 128).T)
        qg = I["gqa_q_norm"][l]
        kg = I["gqa_k_norm"][l]
        put("qg", np.tile(qg, 2)[:, None])
        put("qgs", np.tile(_swap_pairs(qg), 2)[:, None])
        put("kg", np.tile(kg, 2)[:, None])
        put("kgs", np.tile(_swap_pairs(kg), 2)[:, None])
        put("mqn", I["mla_q_norm"][l].reshape(3, 128).T)
        put("mkvn", I["mla_kv_norm"][l].reshape(2, 128).T)
        put("hsw", I["hy_short_w"][l].reshape(3, 12, 128).transpose(2, 1, 0).reshape(128, 36))
        put("hsb", I["hy_short_b"][l].reshape(12, 128).T)
        put("fcw", I["ffn_conv_w"][l].reshape(3, 44, 128).transpose(2, 1, 0).reshape(128, 132))
        put("fcb", I["ffn_conv_b"][l].reshape(44, 128).T)
        put("fin", I["final_norm"].reshape(8, 128).T)
        put("bmod", I["b_mod"][l].reshape(48, 128).T)
    sh["vec"] = vec
    sh["hyw1"] = np.ascontiguousarray(I["hy_w1"])
    sh["hyw2"] = np.ascontiguousarray(I["hy_w2"])
    sh["hyw3"] = np.ascontiguousarray(I["hy_w3"])
    hv = np.zeros((2, 64, 4), np.float32)
    for l in range(2):
        hv[l, :, 0] = I["hy_b1"][l]
        hv[l, :, 1] = I["hy_b2"][l]
        hv[l, :, 2] = I["hy_freq"][l, 0]
        hv[l, :, 3] = I["hy_freq"][l, 1]
    sh["hyv"] = hv
    sh["hybias"] = np.ascontiguousarray(I["hy_bias"].reshape(2, 2, 512))
    sh["hybiasT"] = np.ascontiguousarray(I["hy_bias"].reshape(2, 2, 4, 128).transpose(0, 2, 3, 1))
    ident = np.eye(128, dtype=np.float32)
    sh["ident"] = ident
    bd = np.zeros((128, 128), np.float32)
    bd[:64, :64] = 1.0
    bd[64:, 64:] = 1.0
    sh["bd64"] = bd
    ca, sa = _rope_tables(2048, 64)
    sh["ropeAc"] = np.concatenate([ca, ca], 0)
    sh["ropeAs"] = np.concatenate([sa, sa], 0)
    cb, sb_ = _rope_tables(2048, 32)
    rb = np.zeros((128, 2048), np.float32)
    rb[64:96] = cb
    rb[0:32] = cb
    rb[32:64] = cb
    sh["ropeBc"] = rb
    rb2 = np.zeros((128, 2048), np.float32)
    rb2[64:96] = sb_
    rb2[0:32] = sb_
    rb2[32:64] = sb_
    sh["ropeBs"] = rb2
    for L in (256, 2048):
        hc = _hy_consts(L)
        for k, v in hc.items():
            sh["hy%d_%s" % (L, k)] = v
    return sh


def prep_core(I, c):
    b = c % 4
    m = {}
    m["xp"] = np.ascontiguousarray(I["x_prompt"][2 * c:2 * c + 2].reshape(512, 1024))
    m["xs"] = np.ascontiguousarray(I["x_sample"][b])
    cond = np.stack([I["c_ctx"], I["c"][b]], axis=-1)
    m["cond"] = np.ascontiguousarray(cond.reshape(8, 128, 2).transpose(1, 0, 2))
    ck = I["cache_gqa_k"][b]
    m["ckd"] = np.ascontiguousarray(np.stack([ck, ck], axis=3).reshape(2, 256, 256))
    m["cv"] = np.ascontiguousarray(I["cache_gqa_v"][b].reshape(2, 256, 128))
    m["cckv"] = np.ascontiguousarray(I["cache_mla_ckv"][b])
    m["ckpe"] = np.ascontiguousarray(I["cache_mla_kpe"][b])
    return m


class Path:
    def __init__(self, name, T, seqs, sample, xin, yout, ccol):
        self.name, self.T, self.seqs, self.sample = name, T, seqs, sample
        self.xin, self.yout, self.ccol = xin, yout, ccol
        self.koff = 256 if sample else 0
        self.L = seqs[0][1]
        self.blocks = []
        for t0 in range(0, T, 512):
            self.blocks.append((t0, min(512, T - t0), t0 // self.L))

    def hcol(self, t, si):
        return t + 1 + 2 * si


def build(shared_shapes, core_shapes, cfg):
    nc = bass.Bass("TRN2", target_bir_lowering=False)
    Din = {}
    for k, (shp, dt) in list(shared_shapes.items()) + list(core_shapes.items()):
        Din[k] = nc.dram_tensor(k, list(shp), BF16 if dt == "bf16" else F32, kind="ExternalInput").ap()

    def dout(name, shape):
        return nc.dram_tensor(name, list(shape), F32, kind="ExternalOutput").ap()

    O = dict(yp=dout("yp", [512, 1024]), ys=dout("ys", [2048, 1024]),
             nk=dout("nk", [2, 2, 256, 128]), nv=dout("nv", [2, 2, 256, 128]),
             nckv=dout("nckv", [2, 2, 256, 256]), nkpe=dout("nkpe", [2, 2, 256, 32]))

    fupb = nc.dram_tensor("fupb", [2, 128, 8, 5632], BF16, kind="Internal").ap()
    fdnb = nc.dram_tensor("fdnb", [2, 128, 22, 1024], BF16, kind="Internal").ap()

    with ExitStack() as st:
        S = Sched(nc, st)

        _un = [0]

        def sbt(stack, name, shape, dt):
            _un[0] += 1
            return stack.enter_context(nc.sbuf_tensor("%s_%d" % (name, _un[0]), list(shape), dt))

        ps = [st.enter_context(nc.psum_tensor("ps%d" % i, [128, 512], F32)) for i in range(8)]
        pctr = [0]
        BG = []

        def nb(excl=()):
            while True:
                i = pctr[0]
                pctr[0] = (i + 1) % 8
                if i not in excl:
                    return i

        def PS(i):
            return ("ps", i)

        class Pool:
            def __init__(self, stack, name, n, shape, dt):
                self.t = [sbt(stack, "%s%d" % (name, i), shape, dt) for i in range(n)]
                self.name, self.n, self.i = name, n, 0

            def get(self):
                i = self.i
                self.i = (i + 1) % self.n
                return self.t[i], (self.name, i)

        def MM(out, lhsT, rhs, start, stop):
            return lambda e: e.matmul(out, lhsT=lhsT, rhs=rhs, start=start, stop=stop)

        def ACT(out, in_, func, **kw):
            return lambda e: e.activation(out=out, in_=in_, func=func, **kw)

        def TT(out, in0, in1, op):
            return lambda e: e.tensor_tensor(out=out, in0=in0, in1=in1, op=op)

        def STT(out, in0, scalar, in1, op0, op1):
            return lambda e: e.scalar_tensor_tensor(out=out, in0=in0, scalar=scalar, in1=in1, op0=op0, op1=op1)

        def TS(out, in0, s1, s2, op0, op1=None):
            if op1 is None:
                return lambda e: e.tensor_scalar(out=out, in0=in0, scalar1=s1, scalar2=None, op0=op0)
            return lambda e: e.tensor_scalar(out=out, in0=in0, scalar1=s1, scalar2=s2, op0=op0, op1=op1)

        def CP(out, in_):
            return lambda e: e.tensor_copy(out=out, in_=in_)

        def DMA(out, in_):
            return lambda e: e.dma_start(out=out, in_=in_)

        def MS(ap, v):
            return lambda e: e.memset(ap, v)

        xT = sbt(st, "xT", [128, 8, 2048], F32)
        hT = sbt(st, "hT", [128, 8, 2052], BF16)
        oT = sbt(st, "oT", [128, 4, 2048], BF16)
        identf = sbt(st, "identf", [128, 128], F32)
        identb = sbt(st, "identb", [128, 128], BF16)
        bd64 = sbt(st, "bd64", [128, 128], F32)
        onesf = sbt(st, "onesf", [128, 128], BF16)
        bd64b = sbt(st, "bd64b", [128, 128], BF16)
        epsb = sbt(st, "epsb", [128, 1], F32)
        vec = sbt(st, "vec", [128, 2, NV], F32)
        modT = sbt(st, "modT", [128, 2, 48, 2], F32)
        gsh = sbt(st, "gsh", [128, 4, 8], F32)
        class _PP:
            pass
        PP = _PP()
        _pn = [0]

        def mkpools(sc):
            _pn[0] += 1
            k = _pn[0]
            PP.sq = Pool(sc, "sq%d_" % k, 2, [128, 512], BF16)
            PP.ln = Pool(sc, "ln%d_" % k, 1, [128, 512], F32)
            PP.rs = Pool(sc, "rs%d_" % k, 2, [128, 512], F32)
            PP.tm = Pool(sc, "tm%d_" % k, 4, [128, 512], F32)

        S.op("sp", DMA(identf[:], Din["ident"]), writes=["c0"], dma=True)
        S.op("pool", DMA(identb[:], Din["ident"]), writes=["c1"], dma=True)
        S.op("sp", DMA(bd64[:], Din["bd64"]), writes=["c2"], dma=True)
        S.op("pool", DMA(bd64b[:], Din["bd64"]), writes=["c2b"], dma=True)
        S.op("dve", MS(onesf[:], 1.0), writes=["c3"])
        S.op("dve", MS(epsb[:], EPS), writes=["c4"])
        S.op("dve", MS(hT[:], 0.0), writes=["c5"])
        for l in range(2):
            S.op("sp", DMA(vec[:, l, :], Din["vec"][l]), writes=["c6%d" % l], dma=True)

        def V(l, name, j=0, n=1):
            o = VOFF[name] + j
            return vec[:, l, o:o + n]

        with ExitStack() as sc:
            condt = sbt(sc, "condt", [128, 8, 2], F32)
            scond = sbt(sc, "scond", [128, 8, 64], F32)
            modrow = sbt(sc, "modrow", [64, 6144], F32)
            wmp = Pool(sc, "wm", 3, [128, 8, 512], F32)
            S.op("sp", DMA(condt[:], Din["cond"]), writes=["condt"], dma=True)
            S.op("dve", MS(scond[:], 0.0), writes=["scond"])
            S.op("act", ACT(scond[:, :, 0:2], condt[:], AF.Silu), reads=["condt", "scond"], writes=["scond"])
            for l in range(2):
                for sc12 in range(12):
                    wt, wr = wmp.get()
                    S.op("sp" if sc12 % 2 == 0 else "act", DMA(wt[:], Din["wmod"][l][:, :, sc12 * 512:(sc12 + 1) * 512]), writes=[wr], dma=True)
                    b = nb()
                    for kc in range(8):
                        S.op("pe", MM(ps[b][0:64, :], scond[:, kc, :], wt[:, kc, :], kc == 0, kc == 7),
                             reads=[wr, "scond"], writes=[PS(b)])
                    S.op("dve", CP(modrow[:, sc12 * 512:(sc12 + 1) * 512], ps[b][0:64, :]), reads=[PS(b), "modrow"], writes=["modrow"])
                b = nb()
                for ch in range(48):
                    S.op("pe", MM(ps[b][:, 2 * ch:2 * ch + 2], modrow[:, ch * 128:(ch + 1) * 128], identf[0:64, 0:2], True, True),
                         reads=["modrow", "c0"], writes=[PS(b)])
                pv_ = ps[b][:, 0:96].rearrange("p (ch c) -> p ch c", c=2)
                for c in range(2):
                    S.op("dve", TT(modT[:, l, :, c], pv_[:, :, c], V(l, "bmod", 0, 48), ALU.add),
                         reads=[PS(b), "c6%d" % l, "modT"], writes=["modT"])
            S.barrier()

        def HV(path, kc, t0, n):
            L = path.L
            if t0 // L == (t0 + n - 1) // L:
                hc = t0 + 1 + 2 * (t0 // L)
                return hT[:, kc, hc:hc + n]
            ns = n // L
            c0 = t0 + 1 + 2 * (t0 // L)
            return hT[:, kc, c0:c0 + ns * (L + 2)].rearrange("p (s c) -> p s c", c=L + 2)[:, :, 0:L]

        def SEG(ap, path, t0, n):
            L = path.L
            if t0 // L == (t0 + n - 1) // L:
                return ap
            return ap.rearrange("p (s c) -> p s c", c=L)

        def HTOK(path, t):
            return t + 1 + 2 * (t // path.L)

        def MOD(l, which, kc, ccol):
            return modT[:, l, which * 8 + kc, ccol:ccol + 1]

        def load_x(path):
            with ExitStack() as sc:
                xl = Pool(sc, "xl", 2, [128, 1024], F32)
                for tt in range(path.T // 128):
                    t_, r_ = xl.get()
                    S.op("sp", DMA(t_[:], path.xin[tt * 128:(tt + 1) * 128, :]), writes=[r_], dma=True)
                    for kc2 in range(2):
                        b = nb()
                        for j in range(4):
                            kc = kc2 * 4 + j
                            S.op("pe", MM(ps[b][:, j * 128:(j + 1) * 128], t_[:, kc * 128:(kc + 1) * 128], identf[:],
                                          j == 0, True), reads=[r_], writes=[PS(b)])
                        for j in range(4):
                            kc = kc2 * 4 + j
                            S.op("act" if j % 2 else "dve",
                                 (ACT(xT[:, kc, tt * 128:(tt + 1) * 128], ps[b][:, j * 128:(j + 1) * 128], AF.Copy) if j % 2
                                  else CP(xT[:, kc, tt * 128:(tt + 1) * 128], ps[b][:, j * 128:(j + 1) * 128])),
                                 reads=[PS(b)], writes=["xT"])
                S.barrier()

        def norm_mod(path, gcols, shfn, out_dram=None):
            for (t0, n, si) in path.blocks:
                b = nb()
                for kc in range(8):
                    sq, sqr = PP.sq.get()
                    S.op("dve", TT(sq[:, :n], xT[:, kc, t0:t0 + n], xT[:, kc, t0:t0 + n], ALU.mult), reads=["xT"], writes=[sqr])
                    S.op("pe", MM(ps[b][:, :n], onesf[:], sq[:, :n], kc == 0, kc == 7), reads=[sqr], writes=[PS(b)])
                ln_, lr = PP.ln.get()
                S.op("act", ACT(ln_[:, :n], ps[b][:, :n], AF.Ln, bias=epsb[:, 0:1], scale=1.0 / D), reads=[PS(b)], writes=[lr])
                rs_, rr = PP.rs.get()
                S.op("act", ACT(rs_[:, :n], ln_[:, :n], AF.Exp, scale=-0.5), reads=[lr], writes=[rr])
                hc = path.hcol(t0, si)
                for kc in range(8):
                    if out_dram is None:
                        tm_, tr = PP.tm.get()
                        S.op("dve", STT(tm_[:, :n], xT[:, kc, t0:t0 + n], gcols[:, kc:kc + 1], rs_[:, :n], ALU.mult, ALU.mult),
                             reads=["xT", rr, "gsh"], writes=[tr])
                        S.op("dve", TS(HV(path, kc, t0, n), SEG(tm_[:, :n], path, t0, n), shfn(kc), None, ALU.add),
                             reads=[tr, "modT"], writes=["hT"])
                    else:
                        S.op("dve", STT(xT[:, kc, t0:t0 + n], xT[:, kc, t0:t0 + n], gcols[:, kc:kc + 1], rs_[:, :n],
                                        ALU.mult, ALU.mult), reads=["xT", rr], writes=["xT"])

        def headnorm(psr, pss, n, g, gs, ones_mat, nfeat, ropeC, ropeS, outs, roperes=None, prow=slice(0, 128)):
            sq, sqr = PP.sq.get()
            S.op("act", ACT(sq[prow, :n], ps[psr][prow, :n], AF.Square), reads=[PS(psr)], writes=[sqr])
            b3 = nb()
            S.op("pe", MM(ps[b3][prow, :n], ones_mat, sq[prow, :n], True, True), reads=[sqr], writes=[PS(b3)])
            ln_, lr = PP.ln.get()
            S.op("act", ACT(ln_[prow, :n], ps[b3][prow, :n], AF.Ln, bias=epsb[prow, 0:1], scale=1.0 / nfeat),
                 reads=[PS(b3)], writes=[lr])
            rs_, rr = PP.rs.get()
            S.op("act", ACT(rs_[prow, :n], ln_[prow, :n], AF.Exp, scale=-0.5), reads=[lr], writes=[rr])
            t1, r1 = PP.tm.get()
            S.op("dve", STT(t1[prow, :n], ps[psr][prow, :n], g, rs_[prow, :n], ALU.mult, ALU.mult),
                 reads=[PS(psr), rr], writes=[r1])
            if pss is not None:
                t2, r2 = PP.tm.get()
                S.op("dve", STT(t2[prow, :n], ps[pss][prow, :n], gs, rs_[prow, :n], ALU.mult, ALU.mult),
                     reads=[PS(pss), rr], writes=[r2])
                S.op("dve", TT(t1[prow, :n], t1[prow, :n], ropeC, ALU.mult), reads=[r1, roperes], writes=[r1])
                S.op("dve", TT(t2[prow, :n], t2[prow, :n], ropeS, ALU.mult), reads=[r2, roperes], writes=[r2])
                for (ap, res) in outs:
                    S.op("dve", TT(ap, t1[prow, :n], t2[prow, :n], ALU.add), reads=[r1, r2], writes=[res])
            else:
                for (ap, res) in outs:
                    S.op("act", ACT(ap, t1[prow, :n], AF.Copy), reads=[r1], writes=[res])
            return t1, r1

        def attend(sc_pools, qap_fn, kap_fn, vap_fn, nkt, kt0, scale, par, chunk, qs, qn, qres, kres, vres):
            attend_multi(sc_pools, [dict(qap=qap_fn(), qs=qs, qn=qn, qres=qres)], kap_fn, vap_fn, nkt, kt0, scale,
                         par, chunk, kres, vres)

        def attend_multi(sc_pools, streams, kap_fn, vap_fn, nkt, kt0, scale, par, chunk, kres, vres):
            if cfg.get("noattn"):
                return
            if BG:
                BG.pop(0)()
            PTp, rsm, rs0, otm = sc_pools
            bos = []
            for _ in streams:
                bos.append(nb(tuple(bos)))
            excl = tuple(bos)
            ns = len(streams)
            depth = 3 if ns == 1 else 1
            pend = []

            def pv(item):
                i, kt, pt, pr = item
                qn = streams[i]["qn"]
                S.op("pe", MM(ps[bos[i]][:, :qn], vap_fn(kt0 + kt), pt[:, :qn], kt == 0, kt == nkt - 1),
                     reads=[pr] + vres, writes=[PS(bos[i])])
            for kt in range(nkt):
                for i, st_ in enumerate(streams):
                    qn = st_["qn"]
                    bs = nb(excl)
                    S.op("pe", MM(ps[bs][:, :qn], kap_fn(kt0 + kt), st_["qap"], True, True), reads=st_["qres"] + kres, writes=[PS(bs)])
                    pt, pr = PTp.get()
                    S.op("act", ACT(pt[:, :qn], ps[bs][:, :qn], AF.Exp, scale=scale), reads=[PS(bs)], writes=[pr])
                    pend.append((i, kt, pt, pr))
                while len(pend) > depth * ns:
                    pv(pend.pop(0))
            while pend:
                pv(pend.pop(0))
            for i, st_ in enumerate(streams):
                qs, qn, bo = st_["qs"], st_["qn"], bos[i]
                r_, rr = rsm.get()
                S.op("dve", lambda e, r_=r_, bo=bo, qn=qn: e.reciprocal(out=r_[64:128, :qn], in_=ps[bo][64:128, :qn]), reads=[PS(bo)], writes=[rr])
                r0, r0r = rs0.get()
                S.op("act", ACT(r0[0:64, :qn], r_[64:128, :qn], AF.Copy), reads=[rr], writes=[r0r])
                if par == 0:
                    S.op("dve", TT(oT[0:64, chunk, qs:qs + qn], ps[bo][0:64, :qn], r0[0:64, :qn], ALU.mult),
                         reads=[PS(bo), r0r], writes=["oT"])
                else:
                    ot, otr = otm.get()
                    S.op("dve", TT(ot[0:64, :qn], ps[bo][0:64, :qn], r0[0:64, :qn], ALU.mult), reads=[PS(bo), r0r], writes=[otr])
                    S.op("act", ACT(oT[64:128, chunk, qs:qs + qn], ot[0:64, :qn], AF.Copy), reads=[otr], writes=["oT"])

        def rope_load(sc_pool, tabc, tabs, t0, n, prow=slice(0, 128)):
            rc, rcr = sc_pool.get()
            S.op("sp", DMA(rc[prow, 0, :n], Din[tabc][prow, t0:t0 + n]), writes=[rcr], dma=True)
            S.op("sp", DMA(rc[prow, 1, :n], Din[tabs][prow, t0:t0 + n]), writes=[rcr], dma=True)
            return rc, rcr

        def out_T(src_ap_fn, nrow, prow0, t0, n, dst_fn, res, pool32):
            for j in range(n // 128):
                b = nb()
                S.op("pe", MM(ps[b][:, 0:nrow], src_ap_fn(j), identf[prow0:prow0 + nrow, prow0:prow0 + nrow], True, True),
                     reads=res, writes=[PS(b)])
                o_, orr = pool32.get()
                S.op("dve", CP(o_[:, 0:nrow], ps[b][:, 0:nrow]), reads=[PS(b)], writes=[orr])
                S.op("sp", DMA(dst_fn(j), o_[:, 0:nrow]), reads=[orr], dma=True)

        def gqa(path, l):
            T, koff, smp = path.T, path.koff, path.sample
            nkt_all = (koff + T) // 128
            with ExitStack() as sc:
                mkpools(sc)
                qT = sbt(sc, "qT", [128, 4, T], BF16)
                kT = sbt(sc, "kT", [128, 2, koff + T], BF16)
                Va = sbt(sc, "Va", [128, nkt_all, 2, 128], BF16)
                wch = Pool(sc, "wch", 4, [128, 8, 128], BF16)
                wv = sbt(sc, "wv", [128, 8, 128], BF16)
                ropep = Pool(sc, "rp", 2, [128, 2, 512], F32)
                PTp = Pool(sc, "PT", 6, [128, 512], BF16)
                rsm = Pool(sc, "rsm", 2, [128, 512], F32)
                rs0 = Pool(sc, "rs0", 1, [128, 512], F32)
                otm = Pool(sc, "otm", 1, [128, 512], BF16)
                o32 = Pool(sc, "o32", 2, [128, 128], F32)
                k32 = Pool(sc, "k32", 2, [128, 512], F32)
                S.op("dve", MS(Va[:], 1.0), writes=["Va"])
                S.op("pool", DMA(wv[:], Din["win"][l][:, :, WIN_V:WIN_V + 128]), writes=["wv"], dma=True)
                if smp:
                    ckt = sbt(sc, "ckt", [128, 2, 256], BF16)
                    for tl in range(2):
                        S.op("pool", DMA(ckt[:, tl, :], Din["ckd"][l][tl * 128:(tl + 1) * 128, :]), writes=["ckt"], dma=True)
                        S.op("pool", DMA(Va[:, tl, :, 0:64],
                                         Din["cv"][l][tl * 128:(tl + 1) * 128, :].rearrange("p (g d) -> p g d", g=2)),
                             reads=["Va"], writes=["Va"], dma=True)
                    for tl in range(2):
                        for g in range(2):
                            b = nb()
                            S.op("pe", MM(ps[b][:, 0:128], ckt[:, tl, g * 128:(g + 1) * 128], identb[:], True, True),
                                 reads=["ckt"], writes=[PS(b)])
                            S.op("act", ACT(kT[:, g, tl * 128:(tl + 1) * 128], ps[b][:, 0:128], AF.Copy), reads=[PS(b)],
                                 writes=["kT"])

                def proj_chunk(src, c0, srcs, c0s, gname, gsname, t0, n, si, outs):
                    hc = path.hcol(t0, si)
                    w1_, w1r = src
                    b1 = nb()
                    for kc in range(8):
                        S.op("pe", MM(ps[b1][:, :n], w1_[:, kc, :], HV(path, kc, t0, n), kc == 0, kc == 7),
                             reads=[w1r, "hT"], writes=[PS(b1)])
                    b2 = None
                    rc = rcr = None
                    if smp:
                        w2_, w2r = srcs
                        b2 = nb()
                        for kc in range(8):
                            S.op("pe", MM(ps[b2][:, :n], w2_[:, kc, :], HV(path, kc, t0, n), kc == 0, kc == 7),
                                 reads=[w2r, "hT"], writes=[PS(b2)])
                        rc, rcr = rope_load(ropep, "ropeAc", "ropeAs", t0, n)
                    headnorm(b1, b2, n, V(l, gname), V(l, gsname), bd64b[:], 64,
                             rc[:, 0, :n] if smp else None, rc[:, 1, :n] if smp else None, outs, roperes=rcr)

                def wload(name, c0):
                    w_, wr_ = wch.get()
                    S.op("pool", DMA(w_[:], Din[name][l][:, :, c0:c0 + 128]), writes=[wr_], dma=True)
                    return (w_, wr_)

                for mi in range(4):
                    w1 = wload("win", WIN_Q + mi * 128)
                    w2 = wload("winx", mi * 128) if smp else None
                    for (t0, n, si) in path.blocks:
                        proj_chunk(w1, 0, w2, 0, "qg", "qgs", t0, n, si, [(qT[:, mi, t0:t0 + n], "qT")])
                kfs = {}
                for g in range(2):
                    w1 = wload("winx", 512 + g * 128)
                    w2 = wload("winx", 768 + g * 128) if smp else None
                    for bi_, (t0, n, si) in enumerate(path.blocks):
                        outs = [(kT[:, g, koff + t0:koff + t0 + n], "kT")]
                        if not smp:
                            k3, k3r = k32.get()
                            outs.append((k3[:, :n], k3r))
                        proj_chunk(w1, 0, w2, 0, "kg", "kgs", t0, n, si, outs)
                        if not smp:
                            for j in range(n // 128):
                                b = nb()
                                S.op("pe", MM(ps[b][:, 0:64], k3[0:64, j * 128:(j + 1) * 128], identf[0:64, 0:64], True, True),
                                     reads=[k3r], writes=[PS(b)])
                                o_, orr = o32.get()
                                S.op("dve", CP(o_[:, 0:64], ps[b][:, 0:64]), reads=[PS(b)], writes=[orr])
                                tl0 = (t0 + j * 128) % path.L
                                S.op("sp", DMA(O["nk"][(t0 + j * 128) // path.L, l, tl0:tl0 + 128, g * 64:(g + 1) * 64], o_[:, 0:64]), reads=[orr], dma=True)
                for (t0, n, si) in path.blocks:
                    hc = path.hcol(t0, si)
                    for j in range(n // 128):
                        b = nb()
                        for kc in range(8):
                            S.op("pe", MM(ps[b][:, 0:128], hT[:, kc, HTOK(path, t0 + j * 128):HTOK(path, t0 + j * 128) + 128], wv[:, kc, :], kc == 0, kc == 7),
                                 reads=["wv", "hT"], writes=[PS(b)])
                        kt = (koff + t0) // 128 + j
                        for g in range(2):
                            S.op("act" if g else "dve",
                                 ACT(Va[:, kt, g, 0:64], ps[b][:, g * 64:(g + 1) * 64], AF.Copy) if g
                                 else CP(Va[:, kt, g, 0:64], ps[b][:, g * 64:(g + 1) * 64]),
                                 reads=[PS(b), "Va"], writes=["Va"])
                        if not smp:
                            o_, orr = o32.get()
                            S.op("dve", CP(o_[:], ps[b][:, 0:128]), reads=[PS(b)], writes=[orr])
                            tl0 = (t0 + j * 128) % path.L
                            S.op("sp", DMA(O["nv"][(t0 + j * 128) // path.L, l, tl0:tl0 + 128, :], o_[:]), reads=[orr], dma=True)
                for si, (s0, L) in enumerate(path.seqs):
                    if smp:
                        kt0, nkt = 0, (koff + L) // 128
                    else:
                        kt0, nkt = s0 // 128, L // 128
                    for h in range(8):
                        g, par, chunk = h // 4, h % 2, h // 2
                        pr = slice(par * 64, par * 64 + 64)
                        qlist = [(qs, min(512, s0 + L - qs)) for qs in range(s0, s0 + L, 512)]
                        npair = 2 if cfg.get("attn_pair", True) else 1
                        for i0 in range(0, len(qlist), npair):
                            streams = [dict(qap=qT[pr, chunk, qs:qs + qn], qs=qs, qn=qn, qres=["qT"]) for (qs, qn) in qlist[i0:i0 + npair]]
                            attend_multi((PTp, rsm, rs0, otm), streams,
                                         lambda kt: kT[pr, g, kt * 128:(kt + 1) * 128],
                                         lambda kt: Va[:, kt, g, :],
                                         nkt, kt0, 64 ** -0.5, par, chunk, ["kT"], ["Va"])
                S.barrier()

        def mla(path, l):
            T, koff, smp = path.T, path.koff, path.sample
            nkt_all = (koff + T) // 128
            with ExitStack() as sc:
                mkpools(sc)
                cqT = sbt(sc, "cqT", [128, 3, T], BF16)
                ckvT = sbt(sc, "ckvT", [128, 2, koff + T], BF16)
                KhT = sbt(sc, "KhT", [128, koff + T], BF16)
                Vh = sbt(sc, "Vh", [128, nkt_all, 128], BF16)
                QhT = Pool(sc, "QhT", 2, [128, 512], BF16)
                wcq = sbt(sc, "wcq", [128, 8, 384], BF16)
                wckv = sbt(sc, "wckv", [128, 8, 256], BF16)
                wkpe = sbt(sc, "wkpe", [128, 8, 64], BF16)
                wuq = sbt(sc, "wuq", [128, 3, 768], BF16)
                wuqx = sbt(sc, "wuqx", [128, 3, 288], BF16)
                S.op("dve", MS(wuqx[:], 0.0), writes=["wuqx"])
                wukv = sbt(sc, "wukv", [128, 2, 1024], BF16)
                ropep = Pool(sc, "rpb", 1, [128, 2, 512], F32)
                PTp = Pool(sc, "PTb", 6, [128, 512], BF16)
                rsm = Pool(sc, "rsmb", 1, [128, 512], F32)
                rs0 = Pool(sc, "rs0b", 1, [128, 512], F32)
                otm = Pool(sc, "otmb", 1, [128, 512], BF16)
                o32 = Pool(sc, "o32b", 2, [128, 256], F32)
                c32 = Pool(sc, "c32", 3, [128, 512], F32) if not smp else None
                S.op("dve", MS(Vh[:], 1.0), writes=["Vh"])
                S.op("pool", DMA(wcq[:], Din["win"][l][:, :, WIN_CQ:WIN_CQ + 384]), writes=["wcq"], dma=True)
                S.op("pool", DMA(wckv[:], Din["win"][l][:, :, WIN_CKV:WIN_CKV + 256]), writes=["wckv"], dma=True)
                S.op("pool", DMA(wkpe[:, :, 0:32], Din["win"][l][:, :, WIN_KPE:WIN_KPE + 32]), writes=["wkpe"], dma=True)
                S.op("pool", DMA(wkpe[:, :, 32:64], Din["winx"][l][:, :, 1024:1056]), writes=["wkpe"], dma=True)
                if cfg.get("mla_stage", 3) >= 3:
                    for kc in range(3):
                        S.op("pool", DMA(wuq[:, kc, :], Din["wuq"][l][:, kc, :]), writes=["wuq"], dma=True)
                        S.op("pool", DMA(wuqx[:, kc, 0:256], Din["wuqx"][l][:, kc, :]), reads=["wuqx"], writes=["wuqx"], dma=True)
                    for kc in range(2):
                        S.op("pool", DMA(wukv[:, kc, :], Din["wukv"][l][:, kc, :]), writes=["wukv"], dma=True)
                if smp:
                    cct = sbt(sc, "cct", [128, 2, 256], BF16)
                    cpt = sbt(sc, "cpt", [128, 2, 64], BF16)
                    S.op("dve", MS(cpt[:], 0.0), writes=["cpt"])
                    for tl in range(2):
                        S.op("pool", DMA(cct[:, tl, :], Din["cckv"][l][tl * 128:(tl + 1) * 128, :]), writes=["cct"], dma=True)
                        S.op("pool", DMA(cpt[:, tl, 0:32], Din["ckpe"][l][tl * 128:(tl + 1) * 128, :]), reads=["cpt"], writes=["cpt"], dma=True)
                    for tl in range(2):
                        for j in range(2):
                            b = nb()
                            S.op("pe", MM(ps[b][:, 0:128], cct[:, tl, j * 128:(j + 1) * 128], identb[:], True, True),
                                 reads=["cct"], writes=[PS(b)])
                            S.op("act", ACT(ckvT[:, j, tl * 128:(tl + 1) * 128], ps[b][:, 0:128], AF.Copy), reads=[PS(b)],
                                 writes=["ckvT"])
                        b = nb()
                        S.op("pe", MM(ps[b][0:64, 0:128], cpt[:, tl, :], identb[:], True, True), reads=["cpt"], writes=[PS(b)])
                        tq, tqr = PP.tm.get()
                        S.op("dve", CP(tq[0:32, 0:128], ps[b][0:32, 0:128]), reads=[PS(b)], writes=[tqr])
                        S.op("act", ACT(KhT[64:96, tl * 128:(tl + 1) * 128], tq[0:32, 0:128], AF.Copy), reads=[tqr],
                             writes=["KhTpe"])
                for (t0, n, si) in path.blocks:
                    hc = path.hcol(t0, si)
                    bs = []
                    for j in range(3):
                        b = nb()
                        bs.append(b)
                        for kc in range(8):
                            S.op("pe", MM(ps[b][:, :n], wcq[:, kc, j * 128:(j + 1) * 128], HV(path, kc, t0, n), kc == 0, kc == 7),
                                 reads=["wcq", "hT"], writes=[PS(b)])
                    b3 = nb()
                    for j in range(3):
                        sq, sqr = PP.sq.get()
                        S.op("act", ACT(sq[:, :n], ps[bs[j]][:, :n], AF.Square), reads=[PS(bs[j])], writes=[sqr])
                        S.op("pe", MM(ps[b3][:, :n], onesf[:], sq[:, :n], j == 0, j == 2), reads=[sqr], writes=[PS(b3)])
                    ln_, lr = PP.ln.get()
                    S.op("act", ACT(ln_[:, :n], ps[b3][:, :n], AF.Ln, bias=epsb[:, 0:1], scale=1.0 / 384), reads=[PS(b3)], writes=[lr])
                    rs_, rr = PP.rs.get()
                    S.op("act", ACT(rs_[:, :n], ln_[:, :n], AF.Exp, scale=-0.5), reads=[lr], writes=[rr])
                    for j in range(3):
                        S.op("dve", STT(cqT[:, j, t0:t0 + n], ps[bs[j]][:, :n], V(l, "mqn", j), rs_[:, :n], ALU.mult, ALU.mult),
                             reads=[PS(bs[j]), rr], writes=["cqT"])
                    bs = []
                    for j in range(2):
                        b = nb()
                        bs.append(b)
                        for kc in range(8):
                            S.op("pe", MM(ps[b][:, :n], wckv[:, kc, j * 128:(j + 1) * 128], HV(path, kc, t0, n), kc == 0, kc == 7),
                                 reads=["wckv", "hT"], writes=[PS(b)])
                    b3 = nb()
                    for j in range(2):
                        sq, sqr = PP.sq.get()
                        S.op("act", ACT(sq[:, :n], ps[bs[j]][:, :n], AF.Square), reads=[PS(bs[j])], writes=[sqr])
                        S.op("pe", MM(ps[b3][:, :n], onesf[:], sq[:, :n], j == 0, j == 1), reads=[sqr], writes=[PS(b3)])
                    ln_, lr = PP.ln.get()
                    S.op("act", ACT(ln_[:, :n], ps[b3][:, :n], AF.Ln, bias=epsb[:, 0:1], scale=1.0 / 256), reads=[PS(b3)], writes=[lr])
                    rs_, rr = PP.rs.get()
                    S.op("act", ACT(rs_[:, :n], ln_[:, :n], AF.Exp, scale=-0.5), reads=[lr], writes=[rr])
                    cf = []
                    for j in range(2):
                        if smp:
                            S.op("dve", STT(ckvT[:, j, koff + t0:koff + t0 + n], ps[bs[j]][:, :n], V(l, "mkvn", j), rs_[:, :n],
                                            ALU.mult, ALU.mult), reads=[PS(bs[j]), rr], writes=["ckvT"])
                        else:
                            c3, c3r = c32.get()
                            S.op("dve", STT(c3[:, :n], ps[bs[j]][:, :n], V(l, "mkvn", j), rs_[:, :n], ALU.mult, ALU.mult),
                                 reads=[PS(bs[j]), rr], writes=[c3r])
                            S.op("act", ACT(ckvT[:, j, t0:t0 + n], c3[:, :n], AF.Copy), reads=[c3r], writes=["ckvT"])
                            cf.append((c3, c3r))
                    if not smp:
                        for jj in range(n // 128):
                            b = nb()
                            for j in range(2):
                                c3, c3r = cf[j]
                                S.op("pe", MM(ps[b][:, j * 128:(j + 1) * 128], c3[:, jj * 128:(jj + 1) * 128], identf[:], j == 0, True),
                                     reads=[c3r], writes=[PS(b)])
                            o_, orr = o32.get()
                            S.op("dve", CP(o_[:], ps[b][:, 0:256]), reads=[PS(b)], writes=[orr])
                            tl0 = (t0 + jj * 128) % path.L
                            S.op("sp", DMA(O["nckv"][(t0 + jj * 128) // path.L, l, tl0:tl0 + 128, :], o_[:]), reads=[orr], dma=True)
                    if cfg.get("mla_stage", 3) < 2:
                        continue
                    b = nb()
                    for kc in range(8):
                        S.op("pe", MM(ps[b][0:64, :n], wkpe[:, kc, 0:64], HV(path, kc, t0, n), kc == 0, kc == 7),
                             reads=["wkpe", "hT"], writes=[PS(b)])
                    if smp:
                        rc, rcr = rope_load(ropep, "ropeBc", "ropeBs", t0, n, slice(0, 64))
                        t1, r1 = PP.tm.get()
                        t2, r2 = PP.tm.get()
                        t3, r3 = PP.tm.get()
                        S.op("dve", TT(t1[0:32, :n], ps[b][0:32, :n], rc[0:32, 0, :n], ALU.mult), reads=[PS(b), rcr], writes=[r1])
                        S.op("dve", TT(t2[32:64, :n], ps[b][32:64, :n], rc[32:64, 1, :n], ALU.mult), reads=[PS(b), rcr], writes=[r2])
                        S.op("act", ACT(t3[0:32, :n], t2[32:64, :n], AF.Copy), reads=[r2], writes=[r3])
                        S.op("dve", TT(t1[0:32, :n], t1[0:32, :n], t3[0:32, :n], ALU.add), reads=[r1, r3], writes=[r1])
                        S.op("act", ACT(KhT[64:96, koff + t0:koff + t0 + n], t1[0:32, :n], AF.Copy), reads=[r1], writes=["KhTpe"])
                    else:
                        c3, c3r = c32.get()
                        S.op("dve", CP(c3[0:64, :n], ps[b][0:64, :n]), reads=[PS(b)], writes=[c3r])
                        S.op("act", ACT(KhT[64:96, t0:t0 + n], c3[0:32, :n], AF.Copy), reads=[c3r], writes=["KhTpe"])
                        for jj in range(n // 128 if cfg.get("kpe_out", True) else 0):
                            b4 = nb()
                            S.op("pe", MM(ps[b4][:, 0:32], c3[0:64, jj * 128:(jj + 1) * 128], identf[0:64, 0:32], True, True),
                                 reads=[c3r], writes=[PS(b4)])
                            o_, orr = o32.get()
                            S.op("dve", CP(o_[:, 0:32], ps[b4][:, 0:32]), reads=[PS(b4)], writes=[orr])
                            tl0 = (t0 + jj * 128) % path.L
                            S.op("sp", DMA(O["nkpe"][(t0 + jj * 128) // path.L, l, tl0:tl0 + 128, :], o_[:, 0:32]), reads=[orr], dma=True)
                ktot = koff + T
                for h in range(8 if cfg.get("mla_stage", 3) >= 3 else 0):
                    par, chunk = h % 2, h // 2
                    for k0 in range(0, ktot, 512):
                        kn = min(512, ktot - k0)
                        b = nb()
                        for kc in range(2):
                            S.op("pe", MM(ps[b][0:64, :kn], wukv[:, kc, h * 128:h * 128 + 64], ckvT[:, kc, k0:k0 + kn], kc == 0, kc == 1),
                                 reads=["wukv", "ckvT"], writes=[PS(b)])
                        S.op("act", ACT(KhT[0:64, k0:k0 + kn], ps[b][0:64, :kn], AF.Copy), reads=[PS(b)], writes=["KhTn"])
                    for kt in range(ktot // 128):
                        b = nb()
                        for kc in range(2):
                            S.op("pe", MM(ps[b][:, 0:64], ckvT[:, kc, kt * 128:(kt + 1) * 128], wukv[:, kc, h * 128 + 64:h * 128 + 128],
                                          kc == 0, kc == 1), reads=["wukv", "ckvT"], writes=[PS(b)])
                        S.op("dve", CP(Vh[:, kt, 0:64], ps[b][:, 0:64]), reads=[PS(b), "Vh"], writes=["Vh"])
                    for si, (s0, L) in enumerate(path.seqs):
                        if smp:
                            kt0, nkt = 0, (koff + L) // 128
                        else:
                            kt0, nkt = s0 // 128, L // 128
                        for qs in range(s0, s0 + L, 512):
                            qn = min(512, s0 + L - qs)
                            b = nb()
                            for kc in range(3):
                                S.op("pe", MM(ps[b][0:96, :qn], wuq[:, kc, h * 96:(h + 1) * 96], cqT[:, kc, qs:qs + qn], kc == 0, kc == 2),
                                     reads=["wuq", "cqT"], writes=[PS(b)])
                            qh, qhr = QhT.get()
                            S.op("act", ACT(qh[0:64, :qn], ps[b][0:64, :qn], AF.Copy), reads=[PS(b)], writes=[(qhr, 0)])
                            if smp:
                                b2 = nb()
                                for kc in range(3):
                                    S.op("pe", MM(ps[b2][0:64, :qn], wuqx[:, kc, h * 32:h * 32 + 64], cqT[:, kc, qs:qs + qn],
                                                  kc == 0, kc == 2), reads=["wuqx", "cqT"], writes=[PS(b2)])
                                rc, rcr = rope_load(ropep, "ropeBc", "ropeBs", qs, qn, slice(0, 96))
                                t1, r1 = PP.tm.get()
                                t2, r2 = PP.tm.get()
                                t3, r3 = PP.tm.get()
                                S.op("dve", TT(t1[64:96, :qn], ps[b][64:96, :qn], rc[64:96, 0, :qn], ALU.mult), reads=[PS(b), rcr], writes=[r1])
                                S.op("dve", TT(t2[0:32, :qn], ps[b2][0:32, :qn], rc[0:32, 1, :qn], ALU.mult), reads=[PS(b2), rcr], writes=[r2])
                                S.op("act", ACT(t3[64:96, :qn], t2[0:32, :qn], AF.Copy), reads=[r2], writes=[r3])
                                S.op("dve", TT(qh[64:96, :qn], t1[64:96, :qn], t3[64:96, :qn], ALU.add), reads=[r1, r3], writes=[(qhr, 1)])
                            else:
                                S.op("dve", CP(qh[64:96, :qn], ps[b][64:96, :qn]), reads=[PS(b)], writes=[(qhr, 1)])
                            attend((PTp, rsm, rs0, otm),
                                   lambda: qh[0:96, :qn],
                                   lambda kt: KhT[0:96, kt * 128:(kt + 1) * 128],
                                   lambda kt: Vh[:, kt, :],
                                   nkt, kt0, 96 ** -0.5, par, chunk, qs, qn, [(qhr, 0), (qhr, 1)], ["KhTn", "KhTpe"], ["Vh"])
                S.barrier()

        def merge(path, l, n_br):
            if cfg.get("nomerge"):
                return
            with ExitStack() as sc:
                wb = sbt(sc, "wb", [128, 4, 1024], BF16)
                wg = sbt(sc, "wg", [128, 8, 1024], BF16)
                wo = sbt(sc, "wo", [128, 8, 1024], BF16)
                mgp = Pool(sc, "mg", 2, [128, 8, 512], BF16)
                sgp = Pool(sc, "sg", 2, [128, 512], F32)
                for mc in range(8):
                    cs_ = slice(mc * 128, (mc + 1) * 128)
                    S.op("pool", DMA(wb[:, :, cs_], Din["wbr"][l, n_br][:, :, cs_]), writes=[("wb", mc)], dma=True)
                    g0 = WIN_G + n_br * 1024 + mc * 128
                    S.op("pool", DMA(wg[:, :, cs_], Din["win"][l][:, :, g0:g0 + 128]), writes=[("wg", mc)], dma=True)
                for mc in range(8):
                    cs_ = slice(mc * 128, (mc + 1) * 128)
                    S.op("pool", DMA(wo[:, :, cs_], Din["wout"][l][:, :, cs_]), writes=[("wo", mc)], dma=True)
                for (t0, n, si) in path.blocks:
                    hc = path.hcol(t0, si)
                    mg, mgr = mgp.get()
                    for mc in range(8):
                        bB = nb()
                        for kc in range(4):
                            S.op("pe", MM(ps[bB][:, :n], wb[:, kc, mc * 128:(mc + 1) * 128], oT[:, kc, t0:t0 + n], kc == 0, kc == 3),
                                 reads=[("wb", mc), "oT"], writes=[PS(bB)])
                        bG = nb()
                        for kc in range(8):
                            S.op("pe", MM(ps[bG][:, :n], wg[:, kc, mc * 128:(mc + 1) * 128], HV(path, kc, t0, n), kc == 0, kc == 7),
                                 reads=[("wg", mc), "hT"], writes=[PS(bG)])
                        sg, sgr = sgp.get()
                        S.op("act", ACT(sg[:, :n], ps[bG][:, :n], AF.Sigmoid), reads=[PS(bG)], writes=[sgr])
                        S.op("dve", TT(mg[:, mc, :n], ps[bB][:, :n], sg[:, :n], ALU.mult), reads=[PS(bB), sgr], writes=[mgr])
                    for mo in range(8):
                        b = nb()
                        for kc in range(8):
                            S.op("pe", MM(ps[b][:, :n], wo[:, kc, mo * 128:(mo + 1) * 128], mg[:, kc, :n], kc == 0, kc == 7),
                                 reads=[("wo", mo), mgr], writes=[PS(b)])
                        S.op("dve", STT(xT[:, mo, t0:t0 + n], ps[b][:, :n], MOD(l, 2, mo, path.ccol), xT[:, mo, t0:t0 + n],
                                        ALU.mult, ALU.add), reads=[PS(b), "xT"], writes=["xT"])
                S.barrier()

        def sin_quarter(pool4, psb, n, sc_ap, b_ap, bc_ap, out_ap, out_res):
            s4, s4r = pool4.get()
            c4, c4r = pool4.get()
            S.op("act", ACT(s4[0:64, :n], ps[psb][0:64, :n], AF.Sin, bias=b_ap, scale=sc_ap), reads=[PS(psb), "hyd"], writes=[s4r])
            S.op("act", ACT(c4[0:64, :n], ps[psb][0:64, :n], AF.Sin, bias=bc_ap, scale=sc_ap), reads=[PS(psb), "hyd"], writes=[c4r])
            S.op("dve", TT(c4[0:64, :n], s4[0:64, :n], c4[0:64, :n], ALU.mult), reads=[s4r, c4r], writes=[c4r])
            S.op("dve", TT(s4[0:64, :n], s4[0:64, :n], s4[0:64, :n], ALU.mult), reads=[s4r], writes=[s4r])
            S.op("dve", TS(s4[0:64, :n], s4[0:64, :n], -2.0, 1.0, ALU.mult, ALU.add), reads=[s4r], writes=[s4r])
            S.op("dve", STT(out_ap, c4[0:64, :n], 4.0, s4[0:64, :n], ALU.mult, ALU.mult), reads=[s4r, c4r], writes=[out_res])

        def hyena(path, l):
            T, L = path.T, path.L
            NT = L // 128
            pfx = "hy%d_" % L
            with ExitStack() as sc:
                h2T = sbt(sc, "h2T", [64, L], BF16)
                with ExitStack() as sc2:
                    h1p = Pool(sc2, "h1p", 2, [64, 512], F32)
                    p4 = Pool(sc2, "p4", 4, [64, 512], F32)
                    zTt = sbt(sc2, "zTt", [64, L], F32)
                    w1t = sbt(sc2, "w1t", [64, 64], F32)
                    w2t = sbt(sc2, "w2t", [64, 64], F32)
                    hyv = sbt(sc2, "hyv", [64, 4], F32)
                    hyd = sbt(sc2, "hyd", [64, 6], F32)
                    S.op("dve", MS(zTt[:], 0.0), writes=["zTt"])
                    S.op("dve", MS(w1t[:], 0.0), writes=["w1t"])
                    S.op("sp", DMA(zTt[0:17, :], Din[pfx + "zT"]), reads=["zTt"], writes=["zTt"], dma=True)
                    S.op("sp", DMA(w1t[0:17, :], Din["hyw1"][l]), reads=["w1t"], writes=["w1t"], dma=True)
                    S.op("sp", DMA(w2t[:], Din["hyw2"][l]), writes=["w2t"], dma=True)
                    S.op("sp", DMA(hyv[:], Din["hyv"][l]), writes=["hyv"], dma=True)
                    for i in range(2):
                        S.op("dve", TS(hyd[:, 3 * i:3 * i + 1], hyv[:, 2 + i:3 + i], 0.25, None, ALU.mult), reads=["hyv"], writes=["hyd"])
                        S.op("dve", TT(hyd[:, 3 * i + 1:3 * i + 2], hyd[:, 3 * i:3 * i + 1], hyv[:, i:i + 1], ALU.mult), reads=["hyd", "hyv"],
                             writes=["hyd"])
                        S.op("dve", TS(hyd[:, 3 * i + 2:3 * i + 3], hyd[:, 3 * i + 1:3 * i + 2], math.pi / 2, None, ALU.add), reads=["hyd"],
                             writes=["hyd"])
                    for c0 in range(0, L, 512):
                        n = min(512, L - c0)
                        b = nb()
                        S.op("pe", MM(ps[b][0:64, :n], w1t[:, :], zTt[:, c0:c0 + n], True, True), reads=["w1t", "zTt"], writes=[PS(b)])
                        h1, h1r = h1p.get()
                        sin_quarter(p4, b, n, hyd[:, 0:1], hyd[:, 1:2], hyd[:, 2:3], h1[:, :n], h1r)
                        b = nb()
                        S.op("pe", MM(ps[b][0:64, :n], w2t[:, :], h1[:, :n], True, True), reads=["w2t", h1r], writes=[PS(b)])
                        sin_quarter(p4, b, n, hyd[:, 3:4], hyd[:, 4:5], hyd[:, 5:6], h2T[:, c0:c0 + n], "h2T")
                    S.barrier()
                wh = sbt(sc, "wh", [128, 8, 384], BF16)
                vfm = sbt(sc, "vfm", [128, 3, T], BF16)
                vtm = sbt(sc, "vtm", [128, T // 128, 128], BF16)
                zA = sbt(sc, "zA", [128, T // 128, 128], BF16)
                z1T = sbt(sc, "z1T", [128, T], BF16)
                sd = sbt(sc, "sd", [128, NT, 3, 128], BF16)
                Y = sbt(sc, "Y", [128, NT, 2, 128], BF16)
                slc = Pool(sc, "slc", 2, [128, NT * 128], BF16)
                sls = Pool(sc, "sls", 2, [128, NT * 128], BF16)
                hbias = sbt(sc, "hbias", [128, 2], F32)
                gtp = Pool(sc, "gtp", 1, [128, 512], F32)
                w3t = sbt(sc, "w3t", [64, 256], BF16)
                cat = sbt(sc, "cat", [128, NT], F32)
                sat = sbt(sc, "sat", [128, NT], F32)
                winp = Pool(sc, "winp", 2, [128, 128], F32)
                hwp = Pool(sc, "hwp", 2, [128, 256], F32)
                abp = Pool(sc, "abp", 2, [128, 256], BF16)
                rl1 = sbt(sc, "rl1", [128, 128], F32)
                l1t = sbt(sc, "l1t", [128, 128], F32)
                ut = Pool(sc, "ut", 1, [128, 512], F32)
                kk = Pool(sc, "kk", 8, [128, 128], F32)
                S.op("sp", DMA(cat[:], Din[pfx + "ca"]), writes=["cat"], dma=True)
                S.op("sp", DMA(sat[:], Din[pfx + "sa"]), writes=["sat"], dma=True)
                for q4 in range(4):
                    for w in range(3):
                        c0 = WIN_HY + w * 512 + q4 * 128
                        S.op("pool", DMA(wh[:, :, w * 128:(w + 1) * 128], Din["win"][l][:, :, c0:c0 + 128]), reads=["wh"], writes=["wh"], dma=True)
                    for si, (s0, Ls) in enumerate(path.seqs):
                        for o0 in range(0, Ls, 384):
                            on = min(384, Ls - o0)
                            hc = path.hcol(s0 + o0, si) - 1
                            for w in range(3):
                                ch = w * 4 + q4
                                b = nb()
                                for kc in range(8):
                                    S.op("pe", MM(ps[b][:, :on + 2], wh[:, kc, w * 128:(w + 1) * 128], hT[:, kc, hc:hc + on + 2], kc == 0, kc == 7),
                                         reads=["wh", "hT"], writes=[PS(b)])
                                u, ur = ut.get()
                                S.op("dve", TS(u[:, :on], ps[b][:, 0:on], V(l, "hsw", ch * 3 + 0), V(l, "hsb", ch), ALU.mult, ALU.add),
                                     reads=[PS(b)], writes=[ur])
                                S.op("dve", STT(u[:, :on], ps[b][:, 1:on + 1], V(l, "hsw", ch * 3 + 1), u[:, :on], ALU.mult, ALU.add),
                                     reads=[PS(b), ur], writes=[ur])
                                S.op("dve", STT(u[:, :on], ps[b][:, 2:on + 2], V(l, "hsw", ch * 3 + 2), u[:, :on], ALU.mult, ALU.add),
                                     reads=[PS(b), ur], writes=[ur])
                                S.op("act", ACT(vfm[:, w, s0 + o0:s0 + o0 + on], u[:, :on], AF.Copy), reads=[ur, "vfm"], writes=["vfm"])
                                for j in range(on // 128 if w == 0 else 0):
                                    b2 = nb()
                                    S.op("pe", MM(ps[b2][:, 0:128], u[:, j * 128:(j + 1) * 128], identf[:], True, True), reads=[ur], writes=[PS(b2)])
                                    tt = (s0 + o0) // 128 + j
                                    S.op("dve", CP(vtm[:, tt, :], ps[b2][:, 0:128]), reads=[PS(b2), "vtm"], writes=["vtm"])
                    for o in range(2):
                        cf = o * 512 + q4 * 128
                        S.op("pool", DMA(w3t[:, 0:128], Din["hyw3"][l][:, cf:cf + 128]), reads=["w3t"], writes=["w3t"], dma=True)
                        S.op("pool", DMA(w3t[:, 128:256], Din["hyw3"][l][:, 1024 + cf:1024 + cf + 128]), reads=["w3t"], writes=["w3t"], dma=True)
                        if o == 0:
                            S.op("sp", DMA(hbias[:], Din["hybiasT"][l, q4]), reads=["hbias"], writes=["hbias"], dma=True)
                        bl = nb()
                        for tt in range(NT):
                            b = nb((bl,))
                            S.op("pe", MM(ps[b][:, 0:256], h2T[:, tt * 128:(tt + 1) * 128], w3t[:, :], True, True), reads=["h2T", "w3t"], writes=[PS(b)])
                            wt_, wr_ = winp.get()
                            S.op("pool", DMA(wt_[:], Din[pfx + "win"][tt * 128:(tt + 1) * 128, q4 * 128:(q4 + 1) * 128]), writes=[wr_], dma=True)
                            hw, hwr = hwp.get()
                            S.op("dve", TT(hw[:, 0:128], ps[b][:, 0:128], wt_[:], ALU.mult), reads=[PS(b), wr_], writes=[hwr])
                            S.op("dve", TT(hw[:, 128:256], ps[b][:, 128:256], wt_[:], ALU.mult), reads=[PS(b), wr_, hwr], writes=[hwr])
                            if tt == 0:
                                S.op("dve", MS(hw[0:1, 128:256], 0.0), reads=[hwr], writes=[hwr])
                            ab, abr = abp.get()
                            S.op("act", ACT(ab[:], hw[:], AF.Abs), reads=[hwr], writes=[abr])
                            S.op("pe", MM(ps[bl][:, 0:256], onesf[:], ab[:], tt == 0, tt == NT - 1), reads=[abr], writes=[PS(bl)])
                            S.op("dve", TT(sd[:, tt, 0, :], hw[:, 0:128], hw[:, 128:256], ALU.add), reads=[hwr, "sd"], writes=["sd"])
                            S.op("dve", TT(sd[:, tt, 1, :], hw[:, 0:128], hw[:, 128:256], ALU.subtract), reads=[hwr, "sd"], writes=["sd"])
                        S.op("act", ACT(l1t[:], ps[bl][:, 128:256], AF.Copy), reads=[PS(bl)], writes=["l1t"])
                        S.op("dve", STT(l1t[:], ps[bl][:, 0:128], EPS, l1t[:], ALU.add, ALU.add), reads=[PS(bl), "l1t"], writes=["l1t"])
                        S.op("dve", lambda e: e.reciprocal(out=rl1[:], in_=l1t[:]), reads=["l1t"], writes=["rl1"])
                        S.op("dve", TS(rl1[:], rl1[:], 1.0 / L, None, ALU.mult), reads=["rl1"], writes=["rl1"])
                        for si, (s0, Ls) in enumerate(path.seqs):
                            tb = s0 // 128

                            def zin(tt):
                                return vtm[:, tb + tt, :] if o == 0 else zA[:, tb + tt, :]
                            zres = "vtm" if o == 0 else "zA"
                            for tt in range(NT):
                                S.op("pool", CP(sd[:, tt, 2, :], zin(tt)), reads=[zres, "sdz"], writes=["sdz"])
                            for fi in range(NT):
                                if not (cfg.get("hy_nodma") and fi > 0):
                                    ct, cr = slc.get()
                                    st_, sr = sls.get()
                                    S.op("sp", DMA(ct[:], Din[pfx + "dc"][fi]), writes=[cr], dma=True)
                                    S.op("act", DMA(st_[:], Din[pfx + "ds"][fi]), writes=[sr], dma=True)
                                bC, bS = nb(), nb()
                                for tt in range(NT):
                                    f1, lst = tt == 0, tt == NT - 1
                                    S.op("pe", MM(ps[bC][:, 0:384], ct[:, tt * 128:(tt + 1) * 128], sd[:, tt, :, :], f1, lst), reads=[cr, "sd", "sdz"], writes=[PS(bC)])
                                    S.op("pe", MM(ps[bS][:, 0:384], st_[:, tt * 128:(tt + 1) * 128], sd[:, tt, :, :], f1, lst), reads=[sr, "sd", "sdz"], writes=[PS(bS)])
                                ca_, sa_ = cat[:, fi:fi + 1], sat[:, fi:fi + 1]
                                t1, r1 = kk.get()
                                kre, krr = kk.get()
                                t2, r2 = kk.get()
                                kim, kir = kk.get()
                                S.op("dve", TS(t1[:], ps[bC][:, 0:128], ca_, None, ALU.mult), reads=[PS(bC), "cat"], writes=[r1])
                                S.op("dve", STT(kre[:], ps[bS][:, 0:128], sa_, t1[:], ALU.mult, ALU.add), reads=[PS(bS), r1, "sat"], writes=[krr])
                                S.op("dve", TS(t2[:], ps[bS][:, 128:256], ca_, None, ALU.mult), reads=[PS(bS), "cat"], writes=[r2])
                                S.op("dve", STT(kim[:], ps[bC][:, 128:256], sa_, t2[:], ALU.mult, ALU.subtract), reads=[PS(bC), r2, "sat"], writes=[kir])
                                S.op("dve", TT(kre[:], kre[:], rl1[:], ALU.mult), reads=[krr, "rl1"], writes=[krr])
                                S.op("dve", TT(kim[:], kim[:], rl1[:], ALU.mult), reads=[kir, "rl1"], writes=[kir])
                                t3, r3 = kk.get()
                                t4, r4 = kk.get()
                                S.op("dve", TT(t3[:], ps[bC][:, 256:384], kre[:], ALU.mult), reads=[PS(bC), krr], writes=[r3])
                                S.op("dve", TT(t4[:], ps[bS][:, 256:384], kim[:], ALU.mult), reads=[PS(bS), kir], writes=[r4])
                                S.op("dve", TT(Y[:, fi, 0, :], t3[:], t4[:], ALU.add), reads=[r3, r4, "Y"], writes=["Y"])
                                S.op("dve", TT(t3[:], ps[bS][:, 256:384], kre[:], ALU.mult), reads=[PS(bS), krr, r3], writes=[r3])
                                S.op("dve", TT(t4[:], ps[bC][:, 256:384], kim[:], ALU.mult), reads=[PS(bC), kir, r4], writes=[r4])
                                S.op("dve", TT(Y[:, fi, 1, :], t3[:], t4[:], ALU.subtract), reads=[r3, r4, "Y"], writes=["Y"])
                            nb4 = (Ls + 511) // 512
                            acc = []
                            for _ in range(nb4):
                                acc.append(nb(tuple(acc)))
                            for fi in range(NT):
                                if not (cfg.get("hy_nodma") and fi > 0):
                                    ct, cr = slc.get()
                                    st_, sr = sls.get()
                                    S.op("sp", DMA(ct[:], Din[pfx + "rc"][fi]), writes=[cr], dma=True)
                                    S.op("act", DMA(st_[:], Din[pfx + "rs"][fi]), writes=[sr], dma=True)
                                for t4 in range(nb4):
                                    n4 = min(512, Ls - t4 * 512)
                                    S.op("pe", MM(ps[acc[t4]][:, :n4], Y[:, fi, 0, :], ct[:, t4 * 512:t4 * 512 + n4], fi == 0, False),
                                         reads=[cr, "Y"], writes=[PS(acc[t4])])
                                    S.op("pe", MM(ps[acc[t4]][:, :n4], Y[:, fi, 1, :], st_[:, t4 * 512:t4 * 512 + n4], False, fi == NT - 1),
                                         reads=[sr, "Y"], writes=[PS(acc[t4])])
                            for t4 in range(nb4):
                                n4 = min(512, Ls - t4 * 512)
                                tg = s0 + t4 * 512
                                zinT = vfm[:, 0, tg:tg + n4] if o == 0 else z1T[:, tg:tg + n4]
                                zinr = "vfm" if o == 0 else "z1T"
                                g_, ggr = gtp.get()
                                S.op("dve", STT(g_[:, :n4], zinT, hbias[:, o:o + 1], ps[acc[t4]][:, :n4], ALU.mult, ALU.add),
                                     reads=[zinr, "hbias", PS(acc[t4])], writes=[ggr])
                                if o == 0:
                                    S.op("dve", TT(z1T[:, tg:tg + n4], g_[:, :n4], vfm[:, 1, tg:tg + n4], ALU.mult), reads=[ggr, "vfm", "z1T"], writes=["z1T"])
                                else:
                                    S.op("dve", TT(oT[:, q4, tg:tg + n4], g_[:, :n4], vfm[:, 2, tg:tg + n4], ALU.mult), reads=[ggr, "vfm", "oT"], writes=["oT"])
                            if o == 0:
                                for tt in range(NT):
                                    b = nb()
                                    S.op("pe", MM(ps[b][:, 0:128], z1T[:, s0 + tt * 128:s0 + (tt + 1) * 128], identb[:], True, True),
                                         reads=["z1T"], writes=[PS(b)])
                                    S.op("act", ACT(zA[:, tb + tt, :], ps[b][:, 0:128], AF.Copy), reads=[PS(b), "zA"], writes=["zA"])
                S.barrier()

        def ffn(path, l):
            with ExitStack() as sc:
                wup = Pool(sc, "wup", 6, [128, 8, 256], BF16)
                wdn = Pool(sc, "wdn", 4, [128, 22, 128], BF16)
                hid = sbt(sc, "hid", [128, 22, 416], BF16)
                ta = Pool(sc, "ta", 2, [128, 416], F32)
                tg = Pool(sc, "tg", 2, [128, 416], F32)
                sg = Pool(sc, "sgf", 2, [128, 416], F32)
                for si, (s0, Ls) in enumerate(path.seqs):
                    nblk = (Ls + 409) // 410
                    for bi in range(nblk):
                        o0 = bi * 410
                        on = min(410, Ls - o0)
                        hc = path.hcol(s0 + o0, si) - 1
                        for j in range(22):
                            if not (cfg.get("ffn_nodma") and (bi > 0 or j > 1)):
                                wt, wr = wup.get()
                                fr = [("fupb", l, kc) for kc in range(8)]
                                S.op("sp", DMA(wt[:, :, 0:128], fupb[l][:, :, j * 128:(j + 1) * 128]), reads=fr, writes=[wr], dma=True)
                                S.op("act", DMA(wt[:, :, 128:256], fupb[l][:, :, 2816 + j * 128:2816 + (j + 1) * 128]), reads=fr, writes=[wr], dma=True)
                            ba, bg = nb(), nb()
                            for kc in range(8):
                                S.op("pe", MM(ps[ba][:, :on + 2], wt[:, kc, 0:128], hT[:, kc, hc:hc + on + 2], kc == 0, kc == 7),
                                     reads=[wr, "hT"], writes=[PS(ba)])
                            for kc in range(8):
                                S.op("pe", MM(ps[bg][:, :on + 2], wt[:, kc, 128:256], hT[:, kc, hc:hc + on + 2], kc == 0, kc == 7),
                                     reads=[wr, "hT"], writes=[PS(bg)])
                            a_, ar = ta.get()
                            g_, gr = tg.get()
                            s_, srr = sg.get()
                            ja, jg = j, 22 + j
                            S.op("dve", TS(a_[:, :on], ps[ba][:, 0:on], V(l, "fcw", ja * 3), None, ALU.mult), reads=[PS(ba)], writes=[ar])
                            S.op("dve", STT(a_[:, :on], ps[ba][:, 1:on + 1], V(l, "fcw", ja * 3 + 1), a_[:, :on], ALU.mult, ALU.add),
                                 reads=[PS(ba), ar], writes=[ar])
                            S.op("dve", STT(a_[:, :on], ps[ba][:, 2:on + 2], V(l, "fcw", ja * 3 + 2), a_[:, :on], ALU.mult, ALU.add),
                                 reads=[PS(ba), ar], writes=[ar])
                            S.op("dve", TS(g_[:, :on], ps[bg][:, 0:on], V(l, "fcw", jg * 3), None, ALU.mult), reads=[PS(bg)], writes=[gr])
                            S.op("dve", STT(g_[:, :on], ps[bg][:, 1:on + 1], V(l, "fcw", jg * 3 + 1), g_[:, :on], ALU.mult, ALU.add),
                                 reads=[PS(bg), gr], writes=[gr])
                            S.op("dve", STT(g_[:, :on], ps[bg][:, 2:on + 2], V(l, "fcw", jg * 3 + 2), g_[:, :on], ALU.mult, ALU.add),
                                 reads=[PS(bg), gr], writes=[gr])
                            S.op("act", ACT(s_[:, :on], g_[:, :on], AF.Silu, bias=V(l, "fcb", jg), scale=1.0), reads=[gr], writes=[srr])
                            S.op("dve", STT(hid[:, j, :on], a_[:, :on], V(l, "fcb", ja), s_[:, :on], ALU.add, ALU.mult),
                                 reads=[ar, srr, "hid"], writes=["hid"])
                        t0 = s0 + o0
                        for mo in range(8):
                            if not (cfg.get("ffn_nodma") and (bi > 0 or mo > 1)):
                                wtd, wrd = wdn.get()
                                S.op("sp" if mo % 2 else "act", DMA(wtd[:], fdnb[l][:, :, mo * 128:(mo + 1) * 128]),
                                     reads=[("fdnb", l, kc) for kc in range(22)], writes=[wrd], dma=True)
                            b = nb()
                            for kc in range(22):
                                S.op("pe", MM(ps[b][:, :on], wtd[:, kc, :], hid[:, kc, :on], kc == 0, kc == 21), reads=[wrd, "hid"], writes=[PS(b)])
                            S.op("dve", STT(xT[:, mo, t0:t0 + on], ps[b][:, :on], MOD(l, 5, mo, path.ccol), xT[:, mo, t0:t0 + on],
                                            ALU.mult, ALU.add), reads=[PS(b), "xT"], writes=["xT"])
                S.barrier()

        cast_done = set()
        bgq = BG

        def queue_ffn_cast(l):
            if l in cast_done:
                return
            cast_done.add(l)
            for kc in range(8):
                bgq.append(lambda l=l, kc=kc: S.op("pool", DMA(fupb[l][:, kc, :], Din["fup"][l][:, kc, :]),
                                                    writes=[("fupb", l, kc)], dma=True))
            for kc in range(22):
                bgq.append(lambda l=l, kc=kc: S.op("pool", DMA(fdnb[l][:, kc, :], Din["fdn"][l][:, kc, :]),
                                                    writes=[("fdnb", l, kc)], dma=True))

        def ensure_ffn_cast(l):
            queue_ffn_cast(l)
            while bgq:
                bgq.pop(0)()

        def derive_gains(l, path):
            for i, (nm, wch) in enumerate((("norm1", 1), ("norm2", 4))):
                for kc in range(8):
                    S.op("dve", STT(gsh[:, i, kc:kc + 1], MOD(l, wch, kc, path.ccol), 1.0, V(l, nm, kc), ALU.add, ALU.mult),
                         reads=["modT", "gsh"], writes=["gsh"])

        def store_y(path):
            with ExitStack() as sc:
                yl = Pool(sc, "yl", 2, [128, 1024], F32)
                for tt in range(path.T // 128):
                    y_, yr = yl.get()
                    for kc2 in range(2):
                        b = nb()
                        for j in range(4):
                            kc = kc2 * 4 + j
                            S.op("pe", MM(ps[b][:, j * 128:(j + 1) * 128], xT[:, kc, tt * 128:(tt + 1) * 128], identf[:], j == 0, True),
                                 reads=["xT"], writes=[PS(b)])
                        S.op("act" if kc2 else "dve",
                             ACT(y_[:, kc2 * 512:(kc2 + 1) * 512], ps[b][:, :], AF.Copy) if kc2 else CP(y_[:, kc2 * 512:(kc2 + 1) * 512], ps[b][:, :]),
                             reads=[PS(b), yr], writes=[yr])
                    S.op("sp", DMA(path.yout[tt * 128:(tt + 1) * 128, :], y_[:]), reads=[yr], dma=True)
                S.barrier()

        paths = []
        if cfg.get("prompt", True):
            paths.append(Path("p", 512, [(0, 256), (256, 256)], False, Din["xp"], O["yp"], 0))
        if cfg.get("sample", True):
            paths.append(Path("s", 2048, [(0, 2048)], True, Din["xs"], O["ys"], 1))
        for path in paths:
            load_x(path)
            for l in range(cfg.get("layers", 2) if not path.sample else cfg.get("slayers", cfg.get("layers", 2))):
                derive_gains(l, path)
                if cfg.get("ffn", True):
                    queue_ffn_cast(l)
                with ExitStack() as scn:
                    mkpools(scn)
                    norm_mod(path, gsh[:, 0, :], lambda kc: MOD(l, 0, kc, path.ccol))
                    S.barrier()
                if cfg.get("gqa", True):
                    gqa(path, l)
                    merge(path, l, 0)
                if cfg.get("mla", True):
                    mla(path, l)
                    merge(path, l, 1)
                if cfg.get("hyena", True):
                    hyena(path, l)
                    merge(path, l, 2)
                if cfg.get("ffn", True):
                    for si, (s0, Ls) in enumerate(path.seqs):
                        c0 = path.hcol(s0, si) - 1
                        c1 = path.hcol(s0 + Ls, si)
                        S.op("dve", MS(hT[:, :, c0:c0 + 1], 0.0), writes=["hT"])
                        S.op("dve", MS(hT[:, :, c1:c1 + 1], 0.0), writes=["hT"])
                    with ExitStack() as scn:
                        mkpools(scn)
                        norm_mod(path, gsh[:, 1, :], lambda kc: MOD(l, 3, kc, path.ccol))
                        S.barrier()
                    ensure_ffn_cast(l)
                    ffn(path, l)
            with ExitStack() as scn:
                mkpools(scn)
                norm_mod(path, V(0, "fin", 0, 8), None, out_dram=True)
                S.barrier()
            store_y(path)
        S.finish()
        S.emit()
        print("kernel: recorded ops", S.nops, S.cnt, S.dcnt, {e: len(v) for e, v in S.streams.items()})
    return nc


_CFG = {}


def kernel(**inputs):
    I = {k: np.asarray(v) for k, v in inputs.items()}
    sh = prep_shared(I)
    ncores = _CFG.get("ncores", NCORES)
    cores = [prep_core(I, c) for c in range(ncores)]

    def sig(d):
        return {k: (v.shape, "bf16" if v.dtype == ml_dtypes.bfloat16 else "f32") for k, v in d.items()}

    nc = build(sig(sh), sig(cores[0]), _CFG)
    in_maps = []
    for c in range(ncores):
        m = dict(sh)
        m.update(cores[c])
        in_maps.append(m)
    if _CFG.get("trace"):
        res = run_bass_kernel_spmd(nc, in_maps, core_ids=list(range(ncores)), trace=True)
        print("EXEC_TIME_NS", res.exec_time_ns)
    else:
        res = run_bass_kernel_spmd(nc, in_maps, core_ids=list(range(ncores)))
    R = list(res.results)
    while len(R) < NCORES:
        R.append(R[0])
    y_prompt = np.concatenate([R[c]["yp"].reshape(2, 256, 1024) for c in range(NCORES)], 0)
    y_sample = np.stack([R[c]["ys"] for c in range(4)], 0)
    nk = np.concatenate([R[c]["nk"].reshape(2, 2, 256, 2, 64) for c in range(NCORES)], 0)
    nv = np.concatenate([R[c]["nv"].reshape(2, 2, 256, 2, 64) for c in range(NCORES)], 0)
    nckv = np.concatenate([R[c]["nckv"] for c in range(NCORES)], 0)
    nkpe = np.concatenate([R[c]["nkpe"] for c in range(NCORES)], 0)
    f = np.float32
    return (y_prompt.astype(f), y_sample.astype(f), nk.astype(f), nv.astype(f), nckv.astype(f), nkpe.astype(f))
```

```python
import math
from contextlib import ExitStack
import numpy as np
import ml_dtypes
import concourse.bass as bass
import concourse.mybir as mybir
from concourse.bass_utils import run_bass_kernel_spmd

F32 = mybir.dt.float32
BF16 = mybir.dt.bfloat16
AF = mybir.ActivationFunctionType
ALU = mybir.AluOpType

D = 1024
EPS = 1e-6
NCORES = 8


class Sched:
    ENG = ("pe", "act", "dve", "pool", "sp")
    NDS = 8
    NOSELF = ("pe",)

    def __init__(self, nc, stack):
        self.nc = nc
        self.stack = stack
        self.streams = {e: [] for e in self.ENG}
        self.cnt = {e: 0 for e in self.ENG}
        self.sem = {e: stack.enter_context(nc.semaphore("s_" + e)) for e in self.ENG}
        self.skey = {e: "s_" + e for e in self.ENG}
        self.epoch = 0
        self.dq = ("sp", "pool", "act")
        self.dsem = {q: [stack.enter_context(nc.semaphore("d_%s%d" % (q, i))) for i in range(self.NDS)]
                     for q in self.dq}
        self.dcnt = {q: [0] * self.NDS for q in self.dq}
        self.dnext = {q: 0 for q in self.dq}
        self.seen = {e: {} for e in self.ENG}
        self.lastw = {}
        self.readers = {}
        self.nops = 0

    def _wait(self, eng, tok):
        key, sem, val, src = tok
        if src == eng and eng in self.NOSELF:
            return
        if self.seen[eng].get(key, 0) >= val:
            return
        self.seen[eng][key] = val
        self.streams[eng].append(("wait", sem, val))

    def op(self, eng, fn, reads=(), writes=(), dma=False):
        toks = []
        for r in reads:
            t = self.lastw.get(r)
            if t is not None:
                toks.append(t)
            if isinstance(r, tuple) and r and r[0] == "ps":
                for t2 in self.readers.get(r, {}).values():
                    if t2[3] != eng:
                        toks.append(t2)
        for w in writes:
            t = self.lastw.get(w)
            if t is not None:
                toks.append(t)
            toks.extend(self.readers.get(w, {}).values())
        for t in toks:
            self._wait(eng, t)
        if dma:
            q = eng
            i = self.dnext[q]
            self.dnext[q] = (i + 1) % self.NDS
            key = "d_%s%d" % (q, i)
            if self.dcnt[q][i] > 0:
                self._wait(eng, (key, self.dsem[q][i], self.dcnt[q][i], None))
            self.dcnt[q][i] += 16
            tok = (key, self.dsem[q][i], self.dcnt[q][i], None)
            self.streams[eng].append(("op", fn, self.dsem[q][i], 16))
        else:
            self.cnt[eng] += 1
            tok = (self.skey[eng], self.sem[eng], self.cnt[eng], eng)
            self.streams[eng].append(("op", fn, self.sem[eng], 1))
        self.nops += 1
        for w in writes:
            self.lastw[w] = tok
            self.readers[w] = {}
        for r in reads:
            d = self.readers.setdefault(r, {})
            old = d.get(tok[0])
            if old is None or old[2] < tok[2]:
                d[tok[0]] = tok
        return tok

    def barrier(self):
        for e in self.ENG:
            for e2 in self.ENG:
                if self.cnt[e2] > 0 and e2 != e:
                    self._wait(e, (self.skey[e2], self.sem[e2], self.cnt[e2], e2))
            for q in self.dq:
                for i in range(self.NDS):
                    if self.dcnt[q][i] > 0:
                        self._wait(e, ("d_%s%d" % (q, i), self.dsem[q][i], self.dcnt[q][i], None))
        self.lastw = {}
        self.readers = {}
        for e in self.ENG:
            if self.cnt[e] > 8000:
                self.epoch += 1
                self.skey[e] = "s_%s_%d" % (e, self.epoch)
                self.sem[e] = self.stack.enter_context(self.nc.semaphore(self.skey[e]))
                self.cnt[e] = 0

    def finish(self):
        for e2 in self.ENG:
            if e2 != "sp" and self.cnt[e2] > 0:
                self._wait("sp", (self.skey[e2], self.sem[e2], self.cnt[e2], e2))
        for q in self.dq:
            for i in range(self.NDS):
                if self.dcnt[q][i] > 0:
                    self._wait("sp", ("d_%s%d" % (q, i), self.dsem[q][i], self.dcnt[q][i], None))

    def emit(self):
        nc = self.nc

        def run(e, eng):
            for it in self.streams[e]:
                if it[0] == "wait":
                    eng.wait_ge(it[1], it[2])
                else:
                    ins = it[1](eng)
                    ins.then_inc(it[2], it[3])

        with nc.Block() as block:
            @block.tensor
            def _(eng):
                run("pe", eng)

            @block.scalar
            def _(eng):
                run("act", eng)

            @block.vector
            def _(eng):
                run("dve", eng)

            @block.gpsimd
            def _(eng):
                run("pool", eng)

            @block.sync
            def _(eng):
                run("sp", eng)


VOFF = {}
_o = 0
for _n, _w in [("norm1", 8), ("norm2", 8), ("qg", 1), ("qgs", 1), ("kg", 1), ("kgs", 1), ("mqn", 3), ("mkvn", 2),
               ("hsw", 36), ("hsb", 12), ("fcw", 132), ("fcb", 44), ("fin", 8), ("bmod", 48)]:
    VOFF[_n] = _o
    _o += _w
NV = _o

WIN_Q, WIN_K, WIN_V, WIN_CQ, WIN_CKV, WIN_KPE, WIN_HY, WIN_G = 0, 512, 640, 768, 1152, 1408, 1440, 2976


def _fm(w, kc):
    return np.ascontiguousarray(w.reshape(kc, 128, w.shape[1]).transpose(1, 0, 2))


def _swap_pairs(w):
    o = np.empty_like(w)
    o[..., 0::2] = w[..., 1::2]
    o[..., 1::2] = w[..., 0::2]
    return o


def _rope_tables(L, rot_dim, grid_w=64, theta=10000.0):
    rows = L // grid_w
    row = np.repeat(np.arange(rows, dtype=np.float32), grid_w)
    col = np.tile(np.arange(grid_w, dtype=np.float32), rows)
    axis_dim = rot_dim // 2
    inv = (theta ** (-np.arange(0, axis_dim, 2, dtype=np.float32) / axis_dim)).astype(np.float32)
    ang = np.concatenate([row[:, None] * inv, col[:, None] * inv], axis=-1).astype(np.float32)
    c = np.cos(ang).astype(np.float32)
    s = np.sin(ang).astype(np.float32)
    cf = np.repeat(c, 2, axis=1).T
    sf = np.repeat(s, 2, axis=1).T.copy()
    sf[0::2] *= -1.0
    return np.ascontiguousarray(cf), np.ascontiguousarray(sf)


def _hy_consts(L):
    t = np.arange(L, dtype=np.float32)
    tn = t / max(L - 1, 1)
    bands = np.linspace(1e-4, 7, 8, dtype=np.float32)
    ang = (np.float32(2.0 * math.pi / L) * t[:, None] * bands[None, :]).astype(np.float32)
    z = np.concatenate([tn[:, None], np.cos(ang), -np.sin(ang)], axis=-1).astype(np.float32)
    min_decay = math.log(1e-2) / 0.3
    max_decay = math.log(1e-2) / 1.5
    deltas = np.abs(np.linspace(min_decay, max_decay, 512, dtype=np.float32))
    window = np.exp(-tn[:, None] * deltas[None, :]).astype(np.float32)
    idx = np.arange(L, dtype=np.float64) + 0.5
    phi = np.pi * np.outer(idx, idx) / L
    C = np.cos(phi)
    Sn = np.sin(phi)
    nt = L // 128

    def slabs(M):
        a = M.reshape(nt, 128, nt, 128).transpose(2, 1, 0, 3)
        return np.ascontiguousarray(a).astype(ml_dtypes.bfloat16)

    alpha = np.pi * idx / (2 * L)
    ca = np.cos(alpha).reshape(nt, 128).T.astype(np.float32)
    sa = np.sin(alpha).reshape(nt, 128).T.astype(np.float32)
    def rslabs(M):
        return np.ascontiguousarray(M.reshape(nt, 128, L)).astype(ml_dtypes.bfloat16)

    return dict(zT=np.ascontiguousarray(z.T), win=window, dc=slabs(C), ds=slabs(Sn), rc=rslabs(C), rs=rslabs(Sn),
                ca=np.ascontiguousarray(ca), sa=np.ascontiguousarray(sa))


def prep_shared(I):
    sh = {}
    sh["wmod"] = np.stack([_fm(I["w_mod"][l], 8) for l in range(2)])
    sh["win"] = np.stack([_fm(I["w_in"][l], 8) for l in range(2)])
    wx = []
    for l in range(2):
        w = I["w_in"][l]
        q = w[:, WIN_Q:WIN_Q + 512]
        k = w[:, WIN_K:WIN_K + 128]
        kd = np.concatenate([k[:, 0:64], k[:, 0:64], k[:, 64:128], k[:, 64:128]], axis=1)
        kpe = w[:, WIN_KPE:WIN_KPE + 32]
        wx.append(_fm(np.concatenate([_swap_pairs(q), kd, _swap_pairs(kd), _swap_pairs(kpe)], axis=1), 8))
    sh["winx"] = np.stack(wx)
    sh["wuq"] = np.stack([_fm(I["mla_w_uq"][l], 3) for l in range(2)])
    ux = []
    for l in range(2):
        w = I["mla_w_uq"][l].reshape(384, 8, 96)[:, :, 64:96].reshape(384, 256)
        ux.append(_fm(_swap_pairs(w), 3))
    sh["wuqx"] = np.stack(ux)
    sh["wukv"] = np.stack([_fm(I["mla_w_ukv"][l], 2) for l in range(2)])
    sh["wbr"] = np.stack([np.stack([_fm(I["w_branch"][l, n], 4) for n in range(3)]) for l in range(2)])
    sh["wout"] = np.stack([_fm(I["w_out"][l], 8) for l in range(2)])
    sh["fup"] = np.stack([_fm(I["ffn_up"][l], 8) for l in range(2)])
    sh["fdn"] = np.stack([_fm(I["ffn_down"][l], 22) for l in range(2)])
    vec = np.zeros((2, 128, NV), np.float32)
    for l in range(2):
        def put(name, arr):
            arr = np.asarray(arr, np.float32)
            vec[l, :, VOFF[name]:VOFF[name] + arr.shape[1]] = arr
        put("norm1", I["norm1"][l].reshape(8, 128).T)
        put("norm2", I["norm2"][l].reshape(8, 128).T)
        qg = I["gqa_q_norm"][l]
        kg = I["gqa_k_norm"][l]
        put("qg", np.tile(qg, 2)[:, None])
        put("qgs", np.tile(_swap_pairs(qg), 2)[:, None])
        put("kg", np.tile(kg, 2)[:, None])
        put("kgs", np.tile(_swap_pairs(kg), 2)[:, None])
        put("mqn", I["mla_q_norm"][l].reshape(3, 128).T)
        put("mkvn", I["mla_kv_norm"][l].reshape(2, 128).T)
        put("hsw", I["hy_short_w"][l].reshape(3, 12, 128).transpose(2, 1, 0).reshape(128, 36))
        put("hsb", I["hy_short_b"][l].reshape(12, 128).T)
        put("fcw", I["ffn_conv_w"][l].reshape(3, 44, 128).transpose(2, 1, 0).reshape(128, 132))
        put("fcb", I["ffn_conv_b"][l].reshape(44, 128).T)
        put("fin", I["final_norm"].reshape(8, 128).T)
        put("bmod", I["b_mod"][l].reshape(48, 128).T)
    sh["vec"] = vec
    sh["hyw1"] = np.ascontiguousarray(I["hy_w1"])
    sh["hyw2"] = np.ascontiguousarray(I["hy_w2"])
    sh["hyw3"] = np.ascontiguousarray(I["hy_w3"])
    hv = np.zeros((2, 64, 4), np.float32)
    for l in range(2):
        hv[l, :, 0] = I["hy_b1"][l]
        hv[l, :, 1] = I["hy_b2"][l]
        hv[l, :, 2] = I["hy_freq"][l, 0]
        hv[l, :, 3] = I["hy_freq"][l, 1]
    sh["hyv"] = hv
    sh["hybias"] = np.ascontiguousarray(I["hy_bias"].reshape(2, 2, 512))
    sh["hybiasT"] = np.ascontiguousarray(I["hy_bias"].reshape(2, 2, 4, 128).transpose(0, 2, 3, 1))
    ident = np.eye(128, dtype=np.float32)
    sh["ident"] = ident
    bd = np.zeros((128, 128), np.float32)
    bd[:64, :64] = 1.0
    bd[64:, 64:] = 1.0
    sh["bd64"] = bd
    ca, sa = _rope_tables(2048, 64)
    sh["ropeAc"] = np.concatenate([ca, ca], 0)
    sh["ropeAs"] = np.concatenate([sa, sa], 0)
    cb, sb_ = _rope_tables(2048, 32)
    rb = np.zeros((128, 2048), np.float32)
    rb[64:96] = cb
    rb[0:32] = cb
    rb[32:64] = cb
    sh["ropeBc"] = rb
    rb2 = np.zeros((128, 2048), np.float32)
    rb2[64:96] = sb_
    rb2[0:32] = sb_
    rb2[32:64] = sb_
    sh["ropeBs"] = rb2
    for L in (256, 2048):
        hc = _hy_consts(L)
        for k, v in hc.items():
            sh["hy%d_%s" % (L, k)] = v
    return sh


def prep_core(I, c):
    b = c % 4
    m = {}
    m["xp"] = np.ascontiguousarray(I["x_prompt"][2 * c:2 * c + 2].reshape(512, 1024))
    m["xs"] = np.ascontiguousarray(I["x_sample"][b])
    cond = np.stack([I["c_ctx"], I["c"][b]], axis=-1)
    m["cond"] = np.ascontiguousarray(cond.reshape(8, 128, 2).transpose(1, 0, 2))
    ck = I["cache_gqa_k"][b]
    m["ckd"] = np.ascontiguousarray(np.stack([ck, ck], axis=3).reshape(2, 256, 256))
    m["cv"] = np.ascontiguousarray(I["cache_gqa_v"][b].reshape(2, 256, 128))
    m["cckv"] = np.ascontiguousarray(I["cache_mla_ckv"][b])
    m["ckpe"] = np.ascontiguousarray(I["cache_mla_kpe"][b])
    return m


class Path:
    def __init__(self, name, T, seqs, sample, xin, yout, ccol):
        self.name, self.T, self.seqs, self.sample = name, T, seqs, sample
        self.xin, self.yout, self.ccol = xin, yout, ccol
        self.koff = 256 if sample else 0
        self.L = seqs[0][1]
        self.blocks = []
        for t0 in range(0, T, 512):
            self.blocks.append((t0, min(512, T - t0), t0 // self.L))

    def hcol(self, t, si):
        return t + 1 + 2 * si


def build(shared_shapes, core_shapes, cfg):
    nc = bass.Bass("TRN2", target_bir_lowering=False)
    Din = {}
    for k, (shp, dt) in list(shared_shapes.items()) + list(core_shapes.items()):
        Din[k] = nc.dram_tensor(k, list(shp), BF16 if dt == "bf16" else F32, kind="ExternalInput").ap()

    def dout(name, shape):
        return nc.dram_tensor(name, list(shape), F32, kind="ExternalOutput").ap()

    O = dict(yp=dout("yp", [512, 1024]), ys=dout("ys", [2048, 1024]),
             nk=dout("nk", [2, 2, 256, 128]), nv=dout("nv", [2, 2, 256, 128]),
             nckv=dout("nckv", [2, 2, 256, 256]), nkpe=dout("nkpe", [2, 2, 256, 32]))

    fupb = nc.dram_tensor("fupb", [2, 128, 8, 5632], BF16, kind="Internal").ap()
    fdnb = nc.dram_tensor("fdnb", [2, 128, 22, 1024], BF16, kind="Internal").ap()

    with ExitStack() as st:
        S = Sched(nc, st)

        _un = [0]

        def sbt(stack, name, shape, dt):
            _un[0] += 1
            return stack.enter_context(nc.sbuf_tensor("%s_%d" % (name, _un[0]), list(shape), dt))

        ps = [st.enter_context(nc.psum_tensor("ps%d" % i, [128, 512], F32)) for i in range(8)]
        pctr = [0]
        BG = []

        def nb(excl=()):
            while True:
                i = pctr[0]
                pctr[0] = (i + 1) % 8
                if i not in excl:
                    return i

        def PS(i):
            return ("ps", i)

        class Pool:
            def __init__(self, stack, name, n, shape, dt):
                self.t = [sbt(stack, "%s%d" % (name, i), shape, dt) for i in range(n)]
                self.name, self.n, self.i = name, n, 0

            def get(self):
                i = self.i
                self.i = (i + 1) % self.n
                return self.t[i], (self.name, i)

        def MM(out, lhsT, rhs, start, stop):
            return lambda e: e.matmul(out, lhsT=lhsT, rhs=rhs, start=start, stop=stop)

        def ACT(out, in_, func, **kw):
            return lambda e: e.activation(out=out, in_=in_, func=func, **kw)

        def TT(out, in0, in1, op):
            return lambda e: e.tensor_tensor(out=out, in0=in0, in1=in1, op=op)

        def STT(out, in0, scalar, in1, op0, op1):
            return lambda e: e.scalar_tensor_tensor(out=out, in0=in0, scalar=scalar, in1=in1, op0=op0, op1=op1)

        def TS(out, in0, s1, s2, op0, op1=None):
            if op1 is None:
                return lambda e: e.tensor_scalar(out=out, in0=in0, scalar1=s1, scalar2=None, op0=op0)
            return lambda e: e.tensor_scalar(out=out, in0=in0, scalar1=s1, scalar2=s2, op0=op0, op1=op1)

        def CP(out, in_):
            return lambda e: e.tensor_copy(out=out, in_=in_)

        def DMA(out, in_):
            return lambda e: e.dma_start(out=out, in_=in_)

        def MS(ap, v):
            return lambda e: e.memset(ap, v)

        xT = sbt(st, "xT", [128, 8, 2048], F32)
        hT = sbt(st, "hT", [128, 8, 2052], BF16)
        oT = sbt(st, "oT", [128, 4, 2048], BF16)
        identf = sbt(st, "identf", [128, 128], F32)
        identb = sbt(st, "identb", [128, 128], BF16)
        bd64 = sbt(st, "bd64", [128, 128], F32)
        onesf = sbt(st, "onesf", [128, 128], BF16)
        bd64b = sbt(st, "bd64b", [128, 128], BF16)
        epsb = sbt(st, "epsb", [128, 1], F32)
        vec = sbt(st, "vec", [128, 2, NV], F32)
        modT = sbt(st, "modT", [128, 2, 48, 2], F32)
        gsh = sbt(st, "gsh", [128, 4, 8], F32)
        class _PP:
            pass
        PP = _PP()
        _pn = [0]

        def mkpools(sc):
            _pn[0] += 1
            k = _pn[0]
            PP.sq = Pool(sc, "sq%d_" % k, 2, [128, 512], BF16)
            PP.ln = Pool(sc, "ln%d_" % k, 1, [128, 512], F32)
            PP.rs = Pool(sc, "rs%d_" % k, 2, [128, 512], F32)
            PP.tm = Pool(sc, "tm%d_" % k, 4, [128, 512], F32)

        S.op("sp", DMA(identf[:], Din["ident"]), writes=["c0"], dma=True)
        S.op("pool", DMA(identb[:], Din["ident"]), writes=["c1"], dma=True)
        S.op("sp", DMA(bd64[:], Din["bd64"]), writes=["c2"], dma=True)
        S.op("pool", DMA(bd64b[:], Din["bd64"]), writes=["c2b"], dma=True)
        S.op("dve", MS(onesf[:], 1.0), writes=["c3"])
        S.op("dve", MS(epsb[:], EPS), writes=["c4"])
        S.op("dve", MS(hT[:], 0.0), writes=["c5"])
        for l in range(2):
            S.op("sp", DMA(vec[:, l, :], Din["vec"][l]), writes=["c6%d" % l], dma=True)

        def V(l, name, j=0, n=1):
            o = VOFF[name] + j
            return vec[:, l, o:o + n]

        with ExitStack() as sc:
            condt = sbt(sc, "condt", [128, 8, 2], F32)
            scond = sbt(sc, "scond", [128, 8, 64], F32)
            modrow = sbt(sc, "modrow", [64, 6144], F32)
            wmp = Pool(sc, "wm", 3, [128, 8, 512], F32)
            S.op("sp", DMA(condt[:], Din["cond"]), writes=["condt"], dma=True)
            S.op("dve", MS(scond[:], 0.0), writes=["scond"])
            S.op("act", ACT(scond[:, :, 0:2], condt[:], AF.Silu), reads=["condt", "scond"], writes=["scond"])
            for l in range(2):
                for sc12 in range(12):
                    wt, wr = wmp.get()
                    S.op("sp" if sc12 % 2 == 0 else "act", DMA(wt[:], Din["wmod"][l][:, :, sc12 * 512:(sc12 + 1) * 512]), writes=[wr], dma=True)
                    b = nb()
                    for kc in range(8):
                        S.op("pe", MM(ps[b][0:64, :], scond[:, kc, :], wt[:, kc, :], kc == 0, kc == 7),
                             reads=[wr, "scond"], writes=[PS(b)])
                    S.op("dve", CP(modrow[:, sc12 * 512:(sc12 + 1) * 512], ps[b][0:64, :]), reads=[PS(b), "modrow"], writes=["modrow"])
                b = nb()
                for ch in range(48):
                    S.op("pe", MM(ps[b][:, 2 * ch:2 * ch + 2], modrow[:, ch * 128:(ch + 1) * 128], identf[0:64, 0:2], True, True),
                         reads=["modrow", "c0"], writes=[PS(b)])
                pv_ = ps[b][:, 0:96].rearrange("p (ch c) -> p ch c", c=2)
                for c in range(2):
                    S.op("dve", TT(modT[:, l, :, c], pv_[:, :, c], V(l, "bmod", 0, 48), ALU.add),
                         reads=[PS(b), "c6%d" % l, "modT"], writes=["modT"])
            S.barrier()

        def HV(path, kc, t0, n):
            L = path.L
            if t0 // L == (t0 + n - 1) // L:
                hc = t0 + 1 + 2 * (t0 // L)
                return hT[:, kc, hc:hc + n]
            ns = n // L
            c0 = t0 + 1 + 2 * (t0 // L)
            return hT[:, kc, c0:c0 + ns * (L + 2)].rearrange("p (s c) -> p s c", c=L + 2)[:, :, 0:L]

        def SEG(ap, path, t0, n):
            L = path.L
            if t0 // L == (t0 + n - 1) // L:
                return ap
            return ap.rearrange("p (s c) -> p s c", c=L)

        def HTOK(path, t):
            return t + 1 + 2 * (t // path.L)

        def MOD(l, which, kc, ccol):
            return modT[:, l, which * 8 + kc, ccol:ccol + 1]

        def load_x(path):
            with ExitStack() as sc:
                xl = Pool(sc, "xl", 2, [128, 1024], F32)
                for tt in range(path.T // 128):
                    t_, r_ = xl.get()
                    S.op("sp", DMA(t_[:], path.xin[tt * 128:(tt + 1) * 128, :]), writes=[r_], dma=True)
                    for kc2 in range(2):
                        b = nb()
                        for j in range(4):
                            kc = kc2 * 4 + j
                            S.op("pe", MM(ps[b][:, j * 128:(j + 1) * 128], t_[:, kc * 128:(kc + 1) * 128], identf[:],
                                          True, True), reads=[r_], writes=[PS(b)])
                        for j in range(4):
                            kc = kc2 * 4 + j
                            S.op("act" if j % 2 else "dve",
                                 (ACT(xT[:, kc, tt * 128:(tt + 1) * 128], ps[b][:, j * 128:(j + 1) * 128], AF.Copy) if j % 2
                                  else CP(xT[:, kc, tt * 128:(tt + 1) * 128], ps[b][:, j * 128:(j + 1) * 128])),
                                 reads=[PS(b)], writes=["xT"])
                S.barrier()

        def norm_mod(path, gcols, shfn, out_dram=None):
            for (t0, n, si) in path.blocks:
                b = nb()
                for kc in range(8):
                    sq, sqr = PP.sq.get()
                    S.op("dve", TT(sq[:, :n], xT[:, kc, t0:t0 + n], xT[:, kc, t0:t0 + n], ALU.mult), reads=["xT"], writes=[sqr])
                    S.op("pe", MM(ps[b][:, :n], onesf[:], sq[:, :n], kc == 0, kc == 7), reads=[sqr], writes=[PS(b)])
                ln_, lr = PP.ln.get()
                S.op("act", ACT(ln_[:, :n], ps[b][:, :n], AF.Ln, bias=epsb[:, 0:1], scale=1.0 / D), reads=[PS(b)], writes=[lr])
                rs_, rr = PP.rs.get()
                S.op("act", ACT(rs_[:, :n], ln_[:, :n], AF.Exp, scale=-0.5), reads=[lr], writes=[rr])
                hc = path.hcol(t0, si)
                for kc in range(8):
                    if out_dram is None:
                        tm_, tr = PP.tm.get()
                        S.op("dve", STT(tm_[:, :n], xT[:, kc, t0:t0 + n], gcols[:, kc:kc + 1], rs_[:, :n], ALU.mult, ALU.mult),
                             reads=["xT", rr, "gsh"], writes=[tr])
                        S.op("dve", TS(HV(path, kc, t0, n), SEG(tm_[:, :n], path, t0, n), shfn(kc), None, ALU.add),
                             reads=[tr, "modT"], writes=["hT"])
                    else:
                        S.op("dve", STT(xT[:, kc, t0:t0 + n], xT[:, kc, t0:t0 + n], gcols[:, kc:kc + 1], rs_[:, :n],
                                        ALU.mult, ALU.mult), reads=["xT", rr], writes=["xT"])

        def headnorm(psr, pss, n, g, gs, ones_mat, nfeat, ropeC, ropeS, outs, roperes=None, prow=slice(0, 128)):
            sq, sqr = PP.sq.get()
            S.op("act", ACT(sq[prow, :n], ps[psr][prow, :n], AF.Square), reads=[PS(psr)], writes=[sqr])
            b3 = nb()
            S.op("pe", MM(ps[b3][prow, :n], ones_mat, sq[prow, :n], True, True), reads=[sqr], writes=[PS(b3)])
            ln_, lr = PP.ln.get()
            S.op("act", ACT(ln_[prow, :n], ps[b3][prow, :n], AF.Ln, bias=epsb[prow, 0:1], scale=1.0 / nfeat),
                 reads=[PS(b3)], writes=[lr])
            rs_, rr = PP.rs.get()
            S.op("act", ACT(rs_[prow, :n], ln_[prow, :n], AF.Exp, scale=-0.5), reads=[lr], writes=[rr])
            t1, r1 = PP.tm.get()
            S.op("dve", STT(t1[prow, :n], ps[psr][prow, :n], g, rs_[prow, :n], ALU.mult, ALU.mult),
                 reads=[PS(psr), rr], writes=[r1])
            if pss is not None:
                t2, r2 = PP.tm.get()
                S.op("dve", STT(t2[prow, :n], ps[pss][prow, :n], gs, rs_[prow, :n], ALU.mult, ALU.mult),
                     reads=[PS(pss), rr], writes=[r2])
                S.op("dve", TT(t1[prow, :n], t1[prow, :n], ropeC, ALU.mult), reads=[r1, roperes], writes=[r1])
                S.op("dve", TT(t2[prow, :n], t2[prow, :n], ropeS, ALU.mult), reads=[r2, roperes], writes=[r2])
                for (ap, res) in outs:
                    S.op("dve", TT(ap, t1[prow, :n], t2[prow, :n], ALU.add), reads=[r1, r2], writes=[res])
            else:
                for (ap, res) in outs:
                    S.op("act", ACT(ap, t1[prow, :n], AF.Copy), reads=[r1], writes=[res])
            return t1, r1

        def attend(sc_pools, qap_fn, kap_fn, vap_fn, nkt, kt0, scale, par, chunk, qs, qn, qres, kres, vres):
            if cfg.get("noattn"):
                return
            if BG:
                BG.pop(0)()
            PTp, rsm, rs0, otm = sc_pools
            bo = nb()
            pend = []

            def pv(item):
                kt, pt, pr = item
                S.op("pe", MM(ps[bo][:, :qn], vap_fn(kt0 + kt), pt[:, :qn], kt == 0, kt == nkt - 1),
                     reads=[pr] + vres, writes=[PS(bo)])
            for kt in range(nkt):
                bs = nb((bo,))
                S.op("pe", MM(ps[bs][:, :qn], kap_fn(kt0 + kt), qap_fn(), True, True), reads=qres + kres, writes=[PS(bs)])
                pt, pr = PTp.get()
                S.op("act", ACT(pt[:, :qn], ps[bs][:, :qn], AF.Exp, scale=scale), reads=[PS(bs)], writes=[pr])
                pend.append((kt, pt, pr))
                if len(pend) > 3:
                    pv(pend.pop(0))
            while pend:
                pv(pend.pop(0))
            r_, rr = rsm.get()
            S.op("dve", lambda e: e.reciprocal(out=r_[64:128, :qn], in_=ps[bo][64:128, :qn]), reads=[PS(bo)], writes=[rr])
            r0, r0r = rs0.get()
            S.op("act", ACT(r0[0:64, :qn], r_[64:128, :qn], AF.Copy), reads=[rr], writes=[r0r])
            if par == 0:
                S.op("dve", TT(oT[0:64, chunk, qs:qs + qn], ps[bo][0:64, :qn], r0[0:64, :qn], ALU.mult),
                     reads=[PS(bo), r0r], writes=["oT"])
            else:
                ot, otr = otm.get()
                S.op("dve", TT(ot[0:64, :qn], ps[bo][0:64, :qn], r0[0:64, :qn], ALU.mult), reads=[PS(bo), r0r], writes=[otr])
                S.op("act", ACT(oT[64:128, chunk, qs:qs + qn], ot[0:64, :qn], AF.Copy), reads=[otr], writes=["oT"])

        def rope_load(sc_pool, tabc, tabs, t0, n, prow=slice(0, 128)):
            rc, rcr = sc_pool.get()
            S.op("sp", DMA(rc[prow, 0, :n], Din[tabc][prow, t0:t0 + n]), writes=[rcr], dma=True)
            S.op("sp", DMA(rc[prow, 1, :n], Din[tabs][prow, t0:t0 + n]), writes=[rcr], dma=True)
            return rc, rcr

        def out_T(src_ap_fn, nrow, prow0, t0, n, dst_fn, res, pool32):
            for j in range(n // 128):
                b = nb()
                S.op("pe", MM(ps[b][:, 0:nrow], src_ap_fn(j), identf[prow0:prow0 + nrow, prow0:prow0 + nrow], True, True),
                     reads=res, writes=[PS(b)])
                o_, orr = pool32.get()
                S.op("dve", CP(o_[:, 0:nrow], ps[b][:, 0:nrow]), reads=[PS(b)], writes=[orr])
                S.op("sp", DMA(dst_fn(j), o_[:, 0:nrow]), reads=[orr], dma=True)

        def gqa(path, l):
            T, koff, smp = path.T, path.koff, path.sample
            nkt_all = (koff + T) // 128
            with ExitStack() as sc:
                mkpools(sc)
                qT = sbt(sc, "qT", [128, 4, T], BF16)
                kT = sbt(sc, "kT", [128, 2, koff + T], BF16)
                Va = sbt(sc, "Va", [128, nkt_all, 2, 128], BF16)
                wch = Pool(sc, "wch", 4, [128, 8, 128], BF16)
                wv = sbt(sc, "wv", [128, 8, 128], BF16)
                ropep = Pool(sc, "rp", 2, [128, 2, 512], F32)
                PTp = Pool(sc, "PT", 6, [128, 512], BF16)
                rsm = Pool(sc, "rsm", 1, [128, 512], F32)
                rs0 = Pool(sc, "rs0", 1, [128, 512], F32)
                otm = Pool(sc, "otm", 1, [128, 512], BF16)
                o32 = Pool(sc, "o32", 2, [128, 128], F32)
                k32 = Pool(sc, "k32", 2, [128, 512], F32)
                S.op("dve", MS(Va[:], 1.0), writes=["Va"])
                S.op("pool", DMA(wv[:], Din["win"][l][:, :, WIN_V:WIN_V + 128]), writes=["wv"], dma=True)
                if smp:
                    ckt = sbt(sc, "ckt", [128, 2, 256], BF16)
                    for tl in range(2):
                        S.op("pool", DMA(ckt[:, tl, :], Din["ckd"][l][tl * 128:(tl + 1) * 128, :]), writes=["ckt"], dma=True)
                        S.op("pool", DMA(Va[:, tl, :, 0:64],
                                         Din["cv"][l][tl * 128:(tl + 1) * 128, :].rearrange("p (g d) -> p g d", g=2)),
                             reads=["Va"], writes=["Va"], dma=True)
                    for tl in range(2):
                        for g in range(2):
                            b = nb()
                            S.op("pe", MM(ps[b][:, 0:128], ckt[:, tl, g * 128:(g + 1) * 128], identb[:], True, True),
                                 reads=["ckt"], writes=[PS(b)])
                            S.op("act", ACT(kT[:, g, tl * 128:(tl + 1) * 128], ps[b][:, 0:128], AF.Copy), reads=[PS(b)],
                                 writes=["kT"])

                def proj_chunk(src, c0, srcs, c0s, gname, gsname, t0, n, si, outs):
                    hc = path.hcol(t0, si)
                    w1_, w1r = src
                    b1 = nb()
                    for kc in range(8):
                        S.op("pe", MM(ps[b1][:, :n], w1_[:, kc, :], HV(path, kc, t0, n), kc == 0, kc == 7),
                             reads=[w1r, "hT"], writes=[PS(b1)])
                    b2 = None
                    rc = rcr = None
                    if smp:
                        w2_, w2r = srcs
                        b2 = nb()
                        for kc in range(8):
                            S.op("pe", MM(ps[b2][:, :n], w2_[:, kc, :], HV(path, kc, t0, n), kc == 0, kc == 7),
                                 reads=[w2r, "hT"], writes=[PS(b2)])
                        rc, rcr = rope_load(ropep, "ropeAc", "ropeAs", t0, n)
                    headnorm(b1, b2, n, V(l, gname), V(l, gsname), bd64b[:], 64,
                             rc[:, 0, :n] if smp else None, rc[:, 1, :n] if smp else None, outs, roperes=rcr)

                def wload(name, c0):
                    w_, wr_ = wch.get()
                    S.op("pool", DMA(w_[:], Din[name][l][:, :, c0:c0 + 128]), writes=[wr_], dma=True)
                    return (w_, wr_)

                for mi in range(4):
                    w1 = wload("win", WIN_Q + mi * 128)
                    w2 = wload("winx", mi * 128) if smp else None
                    for (t0, n, si) in path.blocks:
                        proj_chunk(w1, 0, w2, 0, "qg", "qgs", t0, n, si, [(qT[:, mi, t0:t0 + n], "qT")])
                kfs = {}
                for g in range(2):
                    w1 = wload("winx", 512 + g * 128)
                    w2 = wload("winx", 768 + g * 128) if smp else None
                    for bi_, (t0, n, si) in enumerate(path.blocks):
                        outs = [(kT[:, g, koff + t0:koff + t0 + n], "kT")]
                        if not smp:
                            k3, k3r = k32.get()
                            outs.append((k3[:, :n], k3r))
                        proj_chunk(w1, 0, w2, 0, "kg", "kgs", t0, n, si, outs)
                        if not smp:
                            for j in range(n // 128):
                                b = nb()
                                S.op("pe", MM(ps[b][:, 0:64], k3[0:64, j * 128:(j + 1) * 128], identf[0:64, 0:64], True, True),
                                     reads=[k3r], writes=[PS(b)])
                                o_, orr = o32.get()
                                S.op("dve", CP(o_[:, 0:64], ps[b][:, 0:64]), reads=[PS(b)], writes=[orr])
                                tl0 = (t0 + j * 128) % path.L
                                S.op("sp", DMA(O["nk"][(t0 + j * 128) // path.L, l, tl0:tl0 + 128, g * 64:(g + 1) * 64], o_[:, 0:64]), reads=[orr], dma=True)
                for (t0, n, si) in path.blocks:
                    hc = path.hcol(t0, si)
                    for j in range(n // 128):
                        b = nb()
                        for kc in range(8):
                            S.op("pe", MM(ps[b][:, 0:128], hT[:, kc, HTOK(path, t0 + j * 128):HTOK(path, t0 + j * 128) + 128], wv[:, kc, :], kc == 0, kc == 7),
                                 reads=["wv", "hT"], writes=[PS(b)])
                        kt = (koff + t0) // 128 + j
                        for g in range(2):
                            S.op("act" if g else "dve",
                                 ACT(Va[:, kt, g, 0:64], ps[b][:, g * 64:(g + 1) * 64], AF.Copy) if g
                                 else CP(Va[:, kt, g, 0:64], ps[b][:, g * 64:(g + 1) * 64]),
                                 reads=[PS(b), "Va"], writes=["Va"])
                        if not smp:
                            o_, orr = o32.get()
                            S.op("dve", CP(o_[:], ps[b][:, 0:128]), reads=[PS(b)], writes=[orr])
                            tl0 = (t0 + j * 128) % path.L
                            S.op("sp", DMA(O["nv"][(t0 + j * 128) // path.L, l, tl0:tl0 + 128, :], o_[:]), reads=[orr], dma=True)
                for si, (s0, L) in enumerate(path.seqs):
                    if smp:
                        kt0, nkt = 0, (koff + L) // 128
                    else:
                        kt0, nkt = s0 // 128, L // 128
                    for h in range(8):
                        g, par, chunk = h // 4, h % 2, h // 2
                        pr = slice(par * 64, par * 64 + 64)
                        for qs in range(s0, s0 + L, 512):
                            qn = min(512, s0 + L - qs)
                            attend((PTp, rsm, rs0, otm),
                                   lambda: qT[pr, chunk, qs:qs + qn],
                                   lambda kt: kT[pr, g, kt * 128:(kt + 1) * 128],
                                   lambda kt: Va[:, kt, g, :],
                                   nkt, kt0, 64 ** -0.5, par, chunk, qs, qn, ["qT"], ["kT"], ["Va"])
                S.barrier()

        def mla(path, l):
            T, koff, smp = path.T, path.koff, path.sample
            nkt_all = (koff + T) // 128
            with ExitStack() as sc:
                mkpools(sc)
                cqT = sbt(sc, "cqT", [128, 3, T], BF16)
                ckvT = sbt(sc, "ckvT", [128, 2, koff + T], BF16)
                KhT = sbt(sc, "KhT", [128, koff + T], BF16)
                Vh = sbt(sc, "Vh", [128, nkt_all, 128], BF16)
                QhT = Pool(sc, "QhT", 2, [128, 512], BF16)
                wcq = sbt(sc, "wcq", [128, 8, 384], BF16)
                wckv = sbt(sc, "wckv", [128, 8, 256], BF16)
                wkpe = sbt(sc, "wkpe", [128, 8, 64], BF16)
                wuq = sbt(sc, "wuq", [128, 3, 768], BF16)
                wuqx = sbt(sc, "wuqx", [128, 3, 288], BF16)
                S.op("dve", MS(wuqx[:], 0.0), writes=["wuqx"])
                wukv = sbt(sc, "wukv", [128, 2, 1024], BF16)
                ropep = Pool(sc, "rpb", 1, [128, 2, 512], F32)
                PTp = Pool(sc, "PTb", 6, [128, 512], BF16)
                rsm = Pool(sc, "rsmb", 1, [128, 512], F32)
                rs0 = Pool(sc, "rs0b", 1, [128, 512], F32)
                otm = Pool(sc, "otmb", 1, [128, 512], BF16)
                o32 = Pool(sc, "o32b", 2, [128, 256], F32)
                c32 = Pool(sc, "c32", 3, [128, 512], F32) if not smp else None
                S.op("dve", MS(Vh[:], 1.0), writes=["Vh"])
                S.op("pool", DMA(wcq[:], Din["win"][l][:, :, WIN_CQ:WIN_CQ + 384]), writes=["wcq"], dma=True)
                S.op("pool", DMA(wckv[:], Din["win"][l][:, :, WIN_CKV:WIN_CKV + 256]), writes=["wckv"], dma=True)
                S.op("pool", DMA(wkpe[:, :, 0:32], Din["win"][l][:, :, WIN_KPE:WIN_KPE + 32]), writes=["wkpe"], dma=True)
                S.op("pool", DMA(wkpe[:, :, 32:64], Din["winx"][l][:, :, 1024:1056]), writes=["wkpe"], dma=True)
                if cfg.get("mla_stage", 3) >= 3:
                    for kc in range(3):
                        S.op("pool", DMA(wuq[:, kc, :], Din["wuq"][l][:, kc, :]), writes=["wuq"], dma=True)
                        S.op("pool", DMA(wuqx[:, kc, 0:256], Din["wuqx"][l][:, kc, :]), reads=["wuqx"], writes=["wuqx"], dma=True)
                    for kc in range(2):
                        S.op("pool", DMA(wukv[:, kc, :], Din["wukv"][l][:, kc, :]), writes=["wukv"], dma=True)
                if smp:
                    cct = sbt(sc, "cct", [128, 2, 256], BF16)
                    cpt = sbt(sc, "cpt", [128, 2, 64], BF16)
                    S.op("dve", MS(cpt[:], 0.0), writes=["cpt"])
                    for tl in range(2):
                        S.op("pool", DMA(cct[:, tl, :], Din["cckv"][l][tl * 128:(tl + 1) * 128, :]), writes=["cct"], dma=True)
                        S.op("pool", DMA(cpt[:, tl, 0:32], Din["ckpe"][l][tl * 128:(tl + 1) * 128, :]), reads=["cpt"], writes=["cpt"], dma=True)
                    for tl in range(2):
                        for j in range(2):
                            b = nb()
                            S.op("pe", MM(ps[b][:, 0:128], cct[:, tl, j * 128:(j + 1) * 128], identb[:], True, True),
                                 reads=["cct"], writes=[PS(b)])
                            S.op("act", ACT(ckvT[:, j, tl * 128:(tl + 1) * 128], ps[b][:, 0:128], AF.Copy), reads=[PS(b)],
                                 writes=["ckvT"])
                        b = nb()
                        S.op("pe", MM(ps[b][0:64, 0:128], cpt[:, tl, :], identb[:], True, True), reads=["cpt"], writes=[PS(b)])
                        tq, tqr = PP.tm.get()
                        S.op("dve", CP(tq[0:32, 0:128], ps[b][0:32, 0:128]), reads=[PS(b)], writes=[tqr])
                        S.op("act", ACT(KhT[64:96, tl * 128:(tl + 1) * 128], tq[0:32, 0:128], AF.Copy), reads=[tqr],
                             writes=["KhTpe"])
                for (t0, n, si) in path.blocks:
                    hc = path.hcol(t0, si)
                    bs = []
                    for j in range(3):
                        b = nb()
                        bs.append(b)
                        for kc in range(8):
                            S.op("pe", MM(ps[b][:, :n], wcq[:, kc, j * 128:(j + 1) * 128], HV(path, kc, t0, n), kc == 0, kc == 7),
                                 reads=["wcq", "hT"], writes=[PS(b)])
                    b3 = nb()
                    for j in range(3):
                        sq, sqr = PP.sq.get()
                        S.op("act", ACT(sq[:, :n], ps[bs[j]][:, :n], AF.Square), reads=[PS(bs[j])], writes=[sqr])
                        S.op("pe", MM(ps[b3][:, :n], onesf[:], sq[:, :n], j == 0, j == 2), reads=[sqr], writes=[PS(b3)])
                    ln_, lr = PP.ln.get()
                    S.op("act", ACT(ln_[:, :n], ps[b3][:, :n], AF.Ln, bias=epsb[:, 0:1], scale=1.0 / 384), reads=[PS(b3)], writes=[lr])
                    rs_, rr = PP.rs.get()
                    S.op("act", ACT(rs_[:, :n], ln_[:, :n], AF.Exp, scale=-0.5), reads=[lr], writes=[rr])
                    for j in range(3):
                        S.op("dve", STT(cqT[:, j, t0:t0 + n], ps[bs[j]][:, :n], V(l, "mqn", j), rs_[:, :n], ALU.mult, ALU.mult),
                             reads=[PS(bs[j]), rr], writes=["cqT"])
                    bs = []
                    for j in range(2):
                        b = nb()
                        bs.append(b)
                        for kc in range(8):
                            S.op("pe", MM(ps[b][:, :n], wckv[:, kc, j * 128:(j + 1) * 128], HV(path, kc, t0, n), kc == 0, kc == 7),
                                 reads=["wckv", "hT"], writes=[PS(b)])
                    b3 = nb()
                    for j in range(2):
                        sq, sqr = PP.sq.get()
                        S.op("act", ACT(sq[:, :n], ps[bs[j]][:, :n], AF.Square), reads=[PS(bs[j])], writes=[sqr])
                        S.op("pe", MM(ps[b3][:, :n], onesf[:], sq[:, :n], j == 0, j == 1), reads=[sqr], writes=[PS(b3)])
                    ln_, lr = PP.ln.get()
                    S.op("act", ACT(ln_[:, :n], ps[b3][:, :n], AF.Ln, bias=epsb[:, 0:1], scale=1.0 / 256), reads=[PS(b3)], writes=[lr])
                    rs_, rr = PP.rs.get()
                    S.op("act", ACT(rs_[:, :n], ln_[:, :n], AF.Exp, scale=-0.5), reads=[lr], writes=[rr])
                    cf = []
                    for j in range(2):
                        if smp:
                            S.op("dve", STT(ckvT[:, j, koff + t0:koff + t0 + n], ps[bs[j]][:, :n], V(l, "mkvn", j), rs_[:, :n],
                                            ALU.mult, ALU.mult), reads=[PS(bs[j]), rr], writes=["ckvT"])
                        else:
                            c3, c3r = c32.get()
                            S.op("dve", STT(c3[:, :n], ps[bs[j]][:, :n], V(l, "mkvn", j), rs_[:, :n], ALU.mult, ALU.mult),
                                 reads=[PS(bs[j]), rr], writes=[c3r])
                            S.op("act", ACT(ckvT[:, j, t0:t0 + n], c3[:, :n], AF.Copy), reads=[c3r], writes=["ckvT"])
                            cf.append((c3, c3r))
                    if not smp:
                        for jj in range(n // 128):
                            b = nb()
                            for j in range(2):
                                c3, c3r = cf[j]
                                S.op("pe", MM(ps[b][:, j * 128:(j + 1) * 128], c3[:, jj * 128:(jj + 1) * 128], identf[:], True, True),
                                     reads=[c3r], writes=[PS(b)])
                            o_, orr = o32.get()
                            S.op("dve", CP(o_[:], ps[b][:, 0:256]), reads=[PS(b)], writes=[orr])
                            tl0 = (t0 + jj * 128) % path.L
                            S.op("sp", DMA(O["nckv"][(t0 + jj * 128) // path.L, l, tl0:tl0 + 128, :], o_[:]), reads=[orr], dma=True)
                    if cfg.get("mla_stage", 3) < 2:
                        continue
                    b = nb()
                    for kc in range(8):
                        S.op("pe", MM(ps[b][0:64, :n], wkpe[:, kc, 0:64], HV(path, kc, t0, n), kc == 0, kc == 7),
                             reads=["wkpe", "hT"], writes=[PS(b)])
                    if smp:
                        rc, rcr = rope_load(ropep, "ropeBc", "ropeBs", t0, n, slice(0, 64))
                        t1, r1 = PP.tm.get()
                        t2, r2 = PP.tm.get()
                        t3, r3 = PP.tm.get()
                        S.op("dve", TT(t1[0:32, :n], ps[b][0:32, :n], rc[0:32, 0, :n], ALU.mult), reads=[PS(b), rcr], writes=[r1])
                        S.op("dve", TT(t2[32:64, :n], ps[b][32:64, :n], rc[32:64, 1, :n], ALU.mult), reads=[PS(b), rcr], writes=[r2])
                        S.op("act", ACT(t3[0:32, :n], t2[32:64, :n], AF.Copy), reads=[r2], writes=[r3])
                        S.op("dve", TT(t1[0:32, :n], t1[0:32, :n], t3[0:32, :n], ALU.add), reads=[r1, r3], writes=[r1])
                        S.op("act", ACT(KhT[64:96, koff + t0:koff + t0 + n], t1[0:32, :n], AF.Copy), reads=[r1], writes=["KhTpe"])
                    else:
                        c3, c3r = c32.get()
                        S.op("dve", CP(c3[0:64, :n], ps[b][0:64, :n]), reads=[PS(b)], writes=[c3r])
                        S.op("act", ACT(KhT[64:96, t0:t0 + n], c3[0:32, :n], AF.Copy), reads=[c3r], writes=["KhTpe"])
                        for jj in range(n // 128 if cfg.get("kpe_out", True) else 0):
                            b4 = nb()
                            S.op("pe", MM(ps[b4][:, 0:32], c3[0:64, jj * 128:(jj + 1) * 128], identf[0:64, 0:32], True, True),
                                 reads=[c3r], writes=[PS(b4)])
                            o_, orr = o32.get()
                            S.op("dve", CP(o_[:, 0:32], ps[b4][:, 0:32]), reads=[PS(b4)], writes=[orr])
                            tl0 = (t0 + jj * 128) % path.L
                            S.op("sp", DMA(O["nkpe"][(t0 + jj * 128) // path.L, l, tl0:tl0 + 128, :], o_[:, 0:32]), reads=[orr], dma=True)
                ktot = koff + T
                for h in range(8 if cfg.get("mla_stage", 3) >= 3 else 0):
                    par, chunk = h % 2, h // 2
                    for k0 in range(0, ktot, 512):
                        kn = min(512, ktot - k0)
                        b = nb()
                        for kc in range(2):
                            S.op("pe", MM(ps[b][0:64, :kn], wukv[:, kc, h * 128:h * 128 + 64], ckvT[:, kc, k0:k0 + kn], kc == 0, kc == 1),
                                 reads=["wukv", "ckvT"], writes=[PS(b)])
                        S.op("act", ACT(KhT[0:64, k0:k0 + kn], ps[b][0:64, :kn], AF.Copy), reads=[PS(b)], writes=["KhTn"])
                    for kt in range(ktot // 128):
                        b = nb()
                        for kc in range(2):
                            S.op("pe", MM(ps[b][:, 0:64], ckvT[:, kc, kt * 128:(kt + 1) * 128], wukv[:, kc, h * 128 + 64:h * 128 + 128],
                                          kc == 0, kc == 1), reads=["wukv", "ckvT"], writes=[PS(b)])
                        S.op("dve", CP(Vh[:, kt, 0:64], ps[b][:, 0:64]), reads=[PS(b), "Vh"], writes=["Vh"])
                    for si, (s0, L) in enumerate(path.seqs):
                        if smp:
                            kt0, nkt = 0, (koff + L) // 128
                        else:
                            kt0, nkt = s0 // 128, L // 128
                        for qs in range(s0, s0 + L, 512):
                            qn = min(512, s0 + L - qs)
                            b = nb()
                            for kc in range(3):
                                S.op("pe", MM(ps[b][0:96, :qn], wuq[:, kc, h * 96:(h + 1) * 96], cqT[:, kc, qs:qs + qn], kc == 0, kc == 2),
                                     reads=["wuq", "cqT"], writes=[PS(b)])
                            qh, qhr = QhT.get()
                            S.op("act", ACT(qh[0:64, :qn], ps[b][0:64, :qn], AF.Copy), reads=[PS(b)], writes=[(qhr, 0)])
                            if smp:
                                b2 = nb()
                                for kc in range(3):
                                    S.op("pe", MM(ps[b2][0:64, :qn], wuqx[:, kc, h * 32:h * 32 + 64], cqT[:, kc, qs:qs + qn],
                                                  kc == 0, kc == 2), reads=["wuqx", "cqT"], writes=[PS(b2)])
                                rc, rcr = rope_load(ropep, "ropeBc", "ropeBs", qs, qn, slice(0, 96))
                                t1, r1 = PP.tm.get()
                                t2, r2 = PP.tm.get()
                                t3, r3 = PP.tm.get()
                                S.op("dve", TT(t1[64:96, :qn], ps[b][64:96, :qn], rc[64:96, 0, :qn], ALU.mult), reads=[PS(b), rcr], writes=[r1])
                                S.op("dve", TT(t2[0:32, :qn], ps[b2][0:32, :qn], rc[0:32, 1, :qn], ALU.mult), reads=[PS(b2), rcr], writes=[r2])
                                S.op("act", ACT(t3[64:96, :qn], t2[0:32, :qn], AF.Copy), reads=[r2], writes=[r3])
                                S.op("dve", TT(qh[64:96, :qn], t1[64:96, :qn], t3[64:96, :qn], ALU.add), reads=[r1, r3], writes=[(qhr, 1)])
                            else:
                                S.op("dve", CP(qh[64:96, :qn], ps[b][64:96, :qn]), reads=[PS(b)], writes=[(qhr, 1)])
                            attend((PTp, rsm, rs0, otm),
                                   lambda: qh[0:96, :qn],
                                   lambda kt: KhT[0:96, kt * 128:(kt + 1) * 128],
                                   lambda kt: Vh[:, kt, :],
                                   nkt, kt0, 96 ** -0.5, par, chunk, qs, qn, [(qhr, 0), (qhr, 1)], ["KhTn", "KhTpe"], ["Vh"])
                S.barrier()

        def merge(path, l, n_br):
            if cfg.get("nomerge"):
                return
            with ExitStack() as sc:
                wb = sbt(sc, "wb", [128, 4, 1024], BF16)
                wg = sbt(sc, "wg", [128, 8, 1024], BF16)
                wo = sbt(sc, "wo", [128, 8, 1024], BF16)
                mgp = Pool(sc, "mg", 2, [128, 8, 512], BF16)
                sgp = Pool(sc, "sg", 2, [128, 512], F32)
                for mc in range(8):
                    cs_ = slice(mc * 128, (mc + 1) * 128)
                    S.op("pool", DMA(wb[:, :, cs_], Din["wbr"][l, n_br][:, :, cs_]), writes=[("wb", mc)], dma=True)
                    g0 = WIN_G + n_br * 1024 + mc * 128
                    S.op("pool", DMA(wg[:, :, cs_], Din["win"][l][:, :, g0:g0 + 128]), writes=[("wg", mc)], dma=True)
                for mc in range(8):
                    cs_ = slice(mc * 128, (mc + 1) * 128)
                    S.op("pool", DMA(wo[:, :, cs_], Din["wout"][l][:, :, cs_]), writes=[("wo", mc)], dma=True)
                for (t0, n, si) in path.blocks:
                    hc = path.hcol(t0, si)
                    mg, mgr = mgp.get()
                    for mc in range(8):
                        bB = nb()
                        for kc in range(4):
                            S.op("pe", MM(ps[bB][:, :n], wb[:, kc, mc * 128:(mc + 1) * 128], oT[:, kc, t0:t0 + n], kc == 0, kc == 3),
                                 reads=[("wb", mc), "oT"], writes=[PS(bB)])
                        bG = nb()
                        for kc in range(8):
                            S.op("pe", MM(ps[bG][:, :n], wg[:, kc, mc * 128:(mc + 1) * 128], HV(path, kc, t0, n), kc == 0, kc == 7),
                                 reads=[("wg", mc), "hT"], writes=[PS(bG)])
                        sg, sgr = sgp.get()
                        S.op("act", ACT(sg[:, :n], ps[bG][:, :n], AF.Sigmoid), reads=[PS(bG)], writes=[sgr])
                        S.op("dve", TT(mg[:, mc, :n], ps[bB][:, :n], sg[:, :n], ALU.mult), reads=[PS(bB), sgr], writes=[mgr])
                    for mo in range(8):
                        b = nb()
                        for kc in range(8):
                            S.op("pe", MM(ps[b][:, :n], wo[:, kc, mo * 128:(mo + 1) * 128], mg[:, kc, :n], kc == 0, kc == 7),
                                 reads=[("wo", mo), mgr], writes=[PS(b)])
                        S.op("dve", STT(xT[:, mo, t0:t0 + n], ps[b][:, :n], MOD(l, 2, mo, path.ccol), xT[:, mo, t0:t0 + n],
                                        ALU.mult, ALU.add), reads=[PS(b), "xT"], writes=["xT"])
                S.barrier()

        def sin_quarter(pool4, psb, n, sc_ap, b_ap, bc_ap, out_ap, out_res):
            s4, s4r = pool4.get()
            c4, c4r = pool4.get()
            S.op("act", ACT(s4[0:64, :n], ps[psb][0:64, :n], AF.Sin, bias=b_ap, scale=sc_ap), reads=[PS(psb), "hyd"], writes=[s4r])
            S.op("act", ACT(c4[0:64, :n], ps[psb][0:64, :n], AF.Sin, bias=bc_ap, scale=sc_ap), reads=[PS(psb), "hyd"], writes=[c4r])
            S.op("dve", TT(c4[0:64, :n], s4[0:64, :n], c4[0:64, :n], ALU.mult), reads=[s4r, c4r], writes=[c4r])
            S.op("dve", TT(s4[0:64, :n], s4[0:64, :n], s4[0:64, :n], ALU.mult), reads=[s4r], writes=[s4r])
            S.op("dve", TS(s4[0:64, :n], s4[0:64, :n], -2.0, 1.0, ALU.mult, ALU.add), reads=[s4r], writes=[s4r])
            S.op("dve", STT(out_ap, c4[0:64, :n], 4.0, s4[0:64, :n], ALU.mult, ALU.mult), reads=[s4r, c4r], writes=[out_res])

        def hyena(path, l):
            T, L = path.T, path.L
            NT = L // 128
            pfx = "hy%d_" % L
            with ExitStack() as sc:
                h2T = sbt(sc, "h2T", [64, L], BF16)
                with ExitStack() as sc2:
                    h1p = Pool(sc2, "h1p", 2, [64, 512], F32)
                    p4 = Pool(sc2, "p4", 4, [64, 512], F32)
                    zTt = sbt(sc2, "zTt", [64, L], F32)
                    w1t = sbt(sc2, "w1t", [64, 64], F32)
                    w2t = sbt(sc2, "w2t", [64, 64], F32)
                    hyv = sbt(sc2, "hyv", [64, 4], F32)
                    hyd = sbt(sc2, "hyd", [64, 6], F32)
                    S.op("dve", MS(zTt[:], 0.0), writes=["zTt"])
                    S.op("dve", MS(w1t[:], 0.0), writes=["w1t"])
                    S.op("sp", DMA(zTt[0:17, :], Din[pfx + "zT"]), reads=["zTt"], writes=["zTt"], dma=True)
                    S.op("sp", DMA(w1t[0:17, :], Din["hyw1"][l]), reads=["w1t"], writes=["w1t"], dma=True)
                    S.op("sp", DMA(w2t[:], Din["hyw2"][l]), writes=["w2t"], dma=True)
                    S.op("sp", DMA(hyv[:], Din["hyv"][l]), writes=["hyv"], dma=True)
                    for i in range(2):
                        S.op("dve", TS(hyd[:, 3 * i:3 * i + 1], hyv[:, 2 + i:3 + i], 0.25, None, ALU.mult), reads=["hyv"], writes=["hyd"])
                        S.op("dve", TT(hyd[:, 3 * i + 1:3 * i + 2], hyd[:, 3 * i:3 * i + 1], hyv[:, i:i + 1], ALU.mult), reads=["hyd", "hyv"],
                             writes=["hyd"])
                        S.op("dve", TS(hyd[:, 3 * i + 2:3 * i + 3], hyd[:, 3 * i + 1:3 * i + 2], math.pi / 2, None, ALU.add), reads=["hyd"],
                             writes=["hyd"])
                    for c0 in range(0, L, 512):
                        n = min(512, L - c0)
                        b = nb()
                        S.op("pe", MM(ps[b][0:64, :n], w1t[:, :], zTt[:, c0:c0 + n], True, True), reads=["w1t", "zTt"], writes=[PS(b)])
                        h1, h1r = h1p.get()
                        sin_quarter(p4, b, n, hyd[:, 0:1], hyd[:, 1:2], hyd[:, 2:3], h1[:, :n], h1r)
                        b = nb()
                        S.op("pe", MM(ps[b][0:64, :n], w2t[:, :], h1[:, :n], True, True), reads=["w2t", h1r], writes=[PS(b)])
                        sin_quarter(p4, b, n, hyd[:, 3:4], hyd[:, 4:5], hyd[:, 5:6], h2T[:, c0:c0 + n], "h2T")
                    S.barrier()
                wh = sbt(sc, "wh", [128, 8, 384], BF16)
                vfm = sbt(sc, "vfm", [128, 3, T], BF16)
                vtm = sbt(sc, "vtm", [128, T // 128, 128], BF16)
                zA = sbt(sc, "zA", [128, T // 128, 128], BF16)
                z1T = sbt(sc, "z1T", [128, T], BF16)
                sd = sbt(sc, "sd", [128, NT, 3, 128], BF16)
                Y = sbt(sc, "Y", [128, NT, 2, 128], BF16)
                slc = Pool(sc, "slc", 2, [128, NT * 128], BF16)
                sls = Pool(sc, "sls", 2, [128, NT * 128], BF16)
                hbias = sbt(sc, "hbias", [128, 2], F32)
                gtp = Pool(sc, "gtp", 1, [128, 512], F32)
                w3t = sbt(sc, "w3t", [64, 256], BF16)
                cat = sbt(sc, "cat", [128, NT], F32)
                sat = sbt(sc, "sat", [128, NT], F32)
                winp = Pool(sc, "winp", 2, [128, 128], F32)
                hwp = Pool(sc, "hwp", 2, [128, 256], F32)
                abp = Pool(sc, "abp", 2, [128, 256], BF16)
                rl1 = sbt(sc, "rl1", [128, 128], F32)
                l1t = sbt(sc, "l1t", [128, 128], F32)
                ut = Pool(sc, "ut", 1, [128, 512], F32)
                kk = Pool(sc, "kk", 8, [128, 128], F32)
                S.op("sp", DMA(cat[:], Din[pfx + "ca"]), writes=["cat"], dma=True)
                S.op("sp", DMA(sat[:], Din[pfx + "sa"]), writes=["sat"], dma=True)
                for q4 in range(4):
                    for w in range(3):
                        c0 = WIN_HY + w * 512 + q4 * 128
                        S.op("pool", DMA(wh[:, :, w * 128:(w + 1) * 128], Din["win"][l][:, :, c0:c0 + 128]), reads=["wh"], writes=["wh"], dma=True)
                    for si, (s0, Ls) in enumerate(path.seqs):
                        for o0 in range(0, Ls, 384):
                            on = min(384, Ls - o0)
                            hc = path.hcol(s0 + o0, si) - 1
                            for w in range(3):
                                ch = w * 4 + q4
                                b = nb()
                                for kc in range(8):
                                    S.op("pe", MM(ps[b][:, :on + 2], wh[:, kc, w * 128:(w + 1) * 128], hT[:, kc, hc:hc + on + 2], kc == 0, kc == 7),
                                         reads=["wh", "hT"], writes=[PS(b)])
                                u, ur = ut.get()
                                S.op("dve", TS(u[:, :on], ps[b][:, 0:on], V(l, "hsw", ch * 3 + 0), V(l, "hsb", ch), ALU.mult, ALU.add),
                                     reads=[PS(b)], writes=[ur])
                                S.op("dve", STT(u[:, :on], ps[b][:, 1:on + 1], V(l, "hsw", ch * 3 + 1), u[:, :on], ALU.mult, ALU.add),
                                     reads=[PS(b), ur], writes=[ur])
                                S.op("dve", STT(u[:, :on], ps[b][:, 2:on + 2], V(l, "hsw", ch * 3 + 2), u[:, :on], ALU.mult, ALU.add),
                                     reads=[PS(b), ur], writes=[ur])
                                S.op("act", ACT(vfm[:, w, s0 + o0:s0 + o0 + on], u[:, :on], AF.Copy), reads=[ur, "vfm"], writes=["vfm"])
                                for j in range(on // 128 if w == 0 else 0):
                                    b2 = nb()
                                    S.op("pe", MM(ps[b2][:, 0:128], u[:, j * 128:(j + 1) * 128], identf[:], True, True), reads=[ur], writes=[PS(b2)])
                                    tt = (s0 + o0) // 128 + j
                                    S.op("dve", CP(vtm[:, tt, :], ps[b2][:, 0:128]), reads=[PS(b2), "vtm"], writes=["vtm"])
                    for o in range(2):
                        cf = o * 512 + q4 * 128
                        S.op("pool", DMA(w3t[:, 0:128], Din["hyw3"][l][:, cf:cf + 128]), reads=["w3t"], writes=["w3t"], dma=True)
                        S.op("pool", DMA(w3t[:, 128:256], Din["hyw3"][l][:, 1024 + cf:1024 + cf + 128]), reads=["w3t"], writes=["w3t"], dma=True)
                        if o == 0:
                            S.op("sp", DMA(hbias[:], Din["hybiasT"][l, q4]), reads=["hbias"], writes=["hbias"], dma=True)
                        bl = nb()
                        for tt in range(NT):
                            b = nb((bl,))
                            S.op("pe", MM(ps[b][:, 0:256], h2T[:, tt * 128:(tt + 1) * 128], w3t[:, :], True, True), reads=["h2T", "w3t"], writes=[PS(b)])
                            wt_, wr_ = winp.get()
                            S.op("pool", DMA(wt_[:], Din[pfx + "win"][tt * 128:(tt + 1) * 128, q4 * 128:(q4 + 1) * 128]), writes=[wr_], dma=True)
                            hw, hwr = hwp.get()
                            S.op("dve", TT(hw[:, 0:128], ps[b][:, 0:128], wt_[:], ALU.mult), reads=[PS(b), wr_], writes=[hwr])
                            S.op("dve", TT(hw[:, 128:256], ps[b][:, 128:256], wt_[:], ALU.mult), reads=[PS(b), wr_, hwr], writes=[hwr])
                            if tt == 0:
                                S.op("dve", MS(hw[0:1, 128:256], 0.0), reads=[hwr], writes=[hwr])
                            ab, abr = abp.get()
                            S.op("act", ACT(ab[:], hw[:], AF.Abs), reads=[hwr], writes=[abr])
                            S.op("pe", MM(ps[bl][:, 0:256], onesf[:], ab[:], tt == 0, tt == NT - 1), reads=[abr], writes=[PS(bl)])
                            S.op("dve", TT(sd[:, tt, 0, :], hw[:, 0:128], hw[:, 128:256], ALU.add), reads=[hwr, "sd"], writes=["sd"])
                            S.op("dve", TT(sd[:, tt, 1, :], hw[:, 0:128], hw[:, 128:256], ALU.subtract), reads=[hwr, "sd"], writes=["sd"])
                        S.op("act", ACT(l1t[:], ps[bl][:, 128:256], AF.Copy), reads=[PS(bl)], writes=["l1t"])
                        S.op("dve", STT(l1t[:], ps[bl][:, 0:128], EPS, l1t[:], ALU.add, ALU.add), reads=[PS(bl), "l1t"], writes=["l1t"])
                        S.op("dve", lambda e: e.reciprocal(out=rl1[:], in_=l1t[:]), reads=["l1t"], writes=["rl1"])
                        S.op("dve", TS(rl1[:], rl1[:], 1.0 / L, None, ALU.mult), reads=["rl1"], writes=["rl1"])
                        for si, (s0, Ls) in enumerate(path.seqs):
                            tb = s0 // 128

                            def zin(tt):
                                return vtm[:, tb + tt, :] if o == 0 else zA[:, tb + tt, :]
                            zres = "vtm" if o == 0 else "zA"
                            for tt in range(NT):
                                S.op("pool", CP(sd[:, tt, 2, :], zin(tt)), reads=[zres, "sdz"], writes=["sdz"])
                            for fi in range(NT):
                                if not (cfg.get("hy_nodma") and fi > 0):
                                    ct, cr = slc.get()
                                    st_, sr = sls.get()
                                    S.op("sp", DMA(ct[:], Din[pfx + "dc"][fi]), writes=[cr], dma=True)
                                    S.op("act", DMA(st_[:], Din[pfx + "ds"][fi]), writes=[sr], dma=True)
                                bC, bS = nb(), nb()
                                for tt in range(NT):
                                    f1, lst = tt == 0, tt == NT - 1
                                    S.op("pe", MM(ps[bC][:, 0:384], ct[:, tt * 128:(tt + 1) * 128], sd[:, tt, :, :], f1, lst), reads=[cr, "sd", "sdz"], writes=[PS(bC)])
                                    S.op("pe", MM(ps[bS][:, 0:384], st_[:, tt * 128:(tt + 1) * 128], sd[:, tt, :, :], f1, lst), reads=[sr, "sd", "sdz"], writes=[PS(bS)])
                                ca_, sa_ = cat[:, fi:fi + 1], sat[:, fi:fi + 1]
                                t1, r1 = kk.get()
                                kre, krr = kk.get()
                                t2, r2 = kk.get()
                                kim, kir = kk.get()
                                S.op("dve", TS(t1[:], ps[bC][:, 0:128], ca_, None, ALU.mult), reads=[PS(bC), "cat"], writes=[r1])
                                S.op("dve", STT(kre[:], ps[bS][:, 0:128], sa_, t1[:], ALU.mult, ALU.add), reads=[PS(bS), r1, "sat"], writes=[krr])
                                S.op("dve", TS(t2[:], ps[bS][:, 128:256], ca_, None, ALU.mult), reads=[PS(bS), "cat"], writes=[r2])
                                S.op("dve", STT(kim[:], ps[bC][:, 128:256], sa_, t2[:], ALU.mult, ALU.subtract), reads=[PS(bC), r2, "sat"], writes=[kir])
                                S.op("dve", TT(kre[:], kre[:], rl1[:], ALU.mult), reads=[krr, "rl1"], writes=[krr])
                                S.op("dve", TT(kim[:], kim[:], rl1[:], ALU.mult), reads=[kir, "rl1"], writes=[kir])
                                t3, r3 = kk.get()
                                t4, r4 = kk.get()
                                S.op("dve", TT(t3[:], ps[bC][:, 256:384], kre[:], ALU.mult), reads=[PS(bC), krr], writes=[r3])
                                S.op("dve", TT(t4[:], ps[bS][:, 256:384], kim[:], ALU.mult), reads=[PS(bS), kir], writes=[r4])
                                S.op("dve", TT(Y[:, fi, 0, :], t3[:], t4[:], ALU.add), reads=[r3, r4, "Y"], writes=["Y"])
                                S.op("dve", TT(t3[:], ps[bS][:, 256:384], kre[:], ALU.mult), reads=[PS(bS), krr, r3], writes=[r3])
                                S.op("dve", TT(t4[:], ps[bC][:, 256:384], kim[:], ALU.mult), reads=[PS(bC), kir, r4], writes=[r4])
                                S.op("dve", TT(Y[:, fi, 1, :], t3[:], t4[:], ALU.subtract), reads=[r3, r4, "Y"], writes=["Y"])
                            nb4 = (Ls + 511) // 512
                            acc = []
                            for _ in range(nb4):
                                acc.append(nb(tuple(acc)))
                            for fi in range(NT):
                                if not (cfg.get("hy_nodma") and fi > 0):
                                    ct, cr = slc.get()
                                    st_, sr = sls.get()
                                    S.op("sp", DMA(ct[:], Din[pfx + "rc"][fi]), writes=[cr], dma=True)
                                    S.op("act", DMA(st_[:], Din[pfx + "rs"][fi]), writes=[sr], dma=True)
                                for t4 in range(nb4):
                                    n4 = min(512, Ls - t4 * 512)
                                    S.op("pe", MM(ps[acc[t4]][:, :n4], Y[:, fi, 0, :], ct[:, t4 * 512:t4 * 512 + n4], fi == 0, False),
                                         reads=[cr, "Y"], writes=[PS(acc[t4])])
                                    S.op("pe", MM(ps[acc[t4]][:, :n4], Y[:, fi, 1, :], st_[:, t4 * 512:t4 * 512 + n4], False, fi == NT - 1),
                                         reads=[sr, "Y"], writes=[PS(acc[t4])])
                            for t4 in range(nb4):
                                n4 = min(512, Ls - t4 * 512)
                                tg = s0 + t4 * 512
                                zinT = vfm[:, 0, tg:tg + n4] if o == 0 else z1T[:, tg:tg + n4]
                                zinr = "vfm" if o == 0 else "z1T"
                                g_, ggr = gtp.get()
                                S.op("dve", STT(g_[:, :n4], zinT, hbias[:, o:o + 1], ps[acc[t4]][:, :n4], ALU.mult, ALU.add),
                                     reads=[zinr, "hbias", PS(acc[t4])], writes=[ggr])
                                if o == 0:
                                    S.op("dve", TT(z1T[:, tg:tg + n4], g_[:, :n4], vfm[:, 1, tg:tg + n4], ALU.mult), reads=[ggr, "vfm", "z1T"], writes=["z1T"])
                                else:
                                    S.op("dve", TT(oT[:, q4, tg:tg + n4], g_[:, :n4], vfm[:, 2, tg:tg + n4], ALU.mult), reads=[ggr, "vfm", "oT"], writes=["oT"])
                            if o == 0:
                                for tt in range(NT):
                                    b = nb()
                                    S.op("pe", MM(ps[b][:, 0:128], z1T[:, s0 + tt * 128:s0 + (tt + 1) * 128], identb[:], True, True),
                                         reads=["z1T"], writes=[PS(b)])
                                    S.op("act", ACT(zA[:, tb + tt, :], ps[b][:, 0:128], AF.Copy), reads=[PS(b), "zA"], writes=["zA"])
                S.barrier()

        def ffn(path, l):
            with ExitStack() as sc:
                wup = Pool(sc, "wup", 6, [128, 8, 256], BF16)
                wdn = Pool(sc, "wdn", 4, [128, 22, 128], BF16)
                hid = sbt(sc, "hid", [128, 22, 416], BF16)
                ta = Pool(sc, "ta", 2, [128, 416], F32)
                tg = Pool(sc, "tg", 2, [128, 416], F32)
                sg = Pool(sc, "sgf", 2, [128, 416], F32)
                for si, (s0, Ls) in enumerate(path.seqs):
                    nblk = (Ls + 409) // 410
                    for bi in range(nblk):
                        o0 = bi * 410
                        on = min(410, Ls - o0)
                        hc = path.hcol(s0 + o0, si) - 1
                        for j in range(22):
                            if not (cfg.get("ffn_nodma") and (bi > 0 or j > 1)):
                                wt, wr = wup.get()
                                fr = [("fupb", l, kc) for kc in range(8)]
                                S.op("sp", DMA(wt[:, :, 0:128], fupb[l][:, :, j * 128:(j + 1) * 128]), reads=fr, writes=[wr], dma=True)
                                S.op("act", DMA(wt[:, :, 128:256], fupb[l][:, :, 2816 + j * 128:2816 + (j + 1) * 128]), reads=fr, writes=[wr], dma=True)
                            ba, bg = nb(), nb()
                            for kc in range(8):
                                S.op("pe", MM(ps[ba][:, :on + 2], wt[:, kc, 0:128], hT[:, kc, hc:hc + on + 2], kc == 0, kc == 7),
                                     reads=[wr, "hT"], writes=[PS(ba)])
                            for kc in range(8):
                                S.op("pe", MM(ps[bg][:, :on + 2], wt[:, kc, 128:256], hT[:, kc, hc:hc + on + 2], kc == 0, kc == 7),
                                     reads=[wr, "hT"], writes=[PS(bg)])
                            a_, ar = ta.get()
                            g_, gr = tg.get()
                            s_, srr = sg.get()
                            ja, jg = j, 22 + j
                            S.op("dve", TS(a_[:, :on], ps[ba][:, 0:on], V(l, "fcw", ja * 3), None, ALU.mult), reads=[PS(ba)], writes=[ar])
                            S.op("dve", STT(a_[:, :on], ps[ba][:, 1:on + 1], V(l, "fcw", ja * 3 + 1), a_[:, :on], ALU.mult, ALU.add),
                                 reads=[PS(ba), ar], writes=[ar])
                            S.op("dve", STT(a_[:, :on], ps[ba][:, 2:on + 2], V(l, "fcw", ja * 3 + 2), a_[:, :on], ALU.mult, ALU.add),
                                 reads=[PS(ba), ar], writes=[ar])
                            S.op("dve", TS(g_[:, :on], ps[bg][:, 0:on], V(l, "fcw", jg * 3), None, ALU.mult), reads=[PS(bg)], writes=[gr])
                            S.op("dve", STT(g_[:, :on], ps[bg][:, 1:on + 1], V(l, "fcw", jg * 3 + 1), g_[:, :on], ALU.mult, ALU.add),
                                 reads=[PS(bg), gr], writes=[gr])
                            S.op("dve", STT(g_[:, :on], ps[bg][:, 2:on + 2], V(l, "fcw", jg * 3 + 2), g_[:, :on], ALU.mult, ALU.add),
                                 reads=[PS(bg), gr], writes=[gr])
                            S.op("act", ACT(s_[:, :on], g_[:, :on], AF.Silu, bias=V(l, "fcb", jg), scale=1.0), reads=[gr], writes=[srr])
                            S.op("dve", STT(hid[:, j, :on], a_[:, :on], V(l, "fcb", ja), s_[:, :on], ALU.add, ALU.mult),
                                 reads=[ar, srr, "hid"], writes=["hid"])
                        t0 = s0 + o0
                        for mo in range(8):
                            if not (cfg.get("ffn_nodma") and (bi > 0 or mo > 1)):
                                wtd, wrd = wdn.get()
                                S.op("sp" if mo % 2 else "act", DMA(wtd[:], fdnb[l][:, :, mo * 128:(mo + 1) * 128]),
                                     reads=[("fdnb", l, kc) for kc in range(22)], writes=[wrd], dma=True)
                            b = nb()
                            for kc in range(22):
                                S.op("pe", MM(ps[b][:, :on], wtd[:, kc, :], hid[:, kc, :on], kc == 0, kc == 21), reads=[wrd, "hid"], writes=[PS(b)])
                            S.op("dve", STT(xT[:, mo, t0:t0 + on], ps[b][:, :on], MOD(l, 5, mo, path.ccol), xT[:, mo, t0:t0 + on],
                                            ALU.mult, ALU.add), reads=[PS(b), "xT"], writes=["xT"])
                S.barrier()

        cast_done = set()
        bgq = BG

        def queue_ffn_cast(l):
            if l in cast_done:
                return
            cast_done.add(l)
            for kc in range(8):
                bgq.append(lambda l=l, kc=kc: S.op("pool", DMA(fupb[l][:, kc, :], Din["fup"][l][:, kc, :]),
                                                    writes=[("fupb", l, kc)], dma=True))
            for kc in range(22):
                bgq.append(lambda l=l, kc=kc: S.op("pool", DMA(fdnb[l][:, kc, :], Din["fdn"][l][:, kc, :]),
                                                    writes=[("fdnb", l, kc)], dma=True))

        def ensure_ffn_cast(l):
            queue_ffn_cast(l)
            while bgq:
                bgq.pop(0)()

        def derive_gains(l, path):
            for i, (nm, wch) in enumerate((("norm1", 1), ("norm2", 4))):
                for kc in range(8):
                    S.op("dve", STT(gsh[:, i, kc:kc + 1], MOD(l, wch, kc, path.ccol), 1.0, V(l, nm, kc), ALU.add, ALU.mult),
                         reads=["modT", "gsh"], writes=["gsh"])

        def store_y(path):
            with ExitStack() as sc:
                yl = Pool(sc, "yl", 2, [128, 1024], F32)
                for tt in range(path.T // 128):
                    y_, yr = yl.get()
                    for kc2 in range(2):
                        b = nb()
                        for j in range(4):
                            kc = kc2 * 4 + j
                            S.op("pe", MM(ps[b][:, j * 128:(j + 1) * 128], xT[:, kc, tt * 128:(tt + 1) * 128], identf[:], True, True),
                                 reads=["xT"], writes=[PS(b)])
                        S.op("act" if kc2 else "dve",
                             ACT(y_[:, kc2 * 512:(kc2 + 1) * 512], ps[b][:, :], AF.Copy) if kc2 else CP(y_[:, kc2 * 512:(kc2 + 1) * 512], ps[b][:, :]),
                             reads=[PS(b), yr], writes=[yr])
                    S.op("sp", DMA(path.yout[tt * 128:(tt + 1) * 128, :], y_[:]), reads=[yr], dma=True)
                S.barrier()

        paths = []
        if cfg.get("prompt", True):
            paths.append(Path("p", 512, [(0, 256), (256, 256)], False, Din["xp"], O["yp"], 0))
        if cfg.get("sample", True):
            paths.append(Path("s", 2048, [(0, 2048)], True, Din["xs"], O["ys"], 1))
        for path in paths:
            load_x(path)
            for l in range(cfg.get("layers", 2) if not path.sample else cfg.get("slayers", cfg.get("layers", 2))):
                derive_gains(l, path)
                if cfg.get("ffn", True):
                    queue_ffn_cast(l)
                with ExitStack() as scn:
                    mkpools(scn)
                    norm_mod(path, gsh[:, 0, :], lambda kc: MOD(l, 0, kc, path.ccol))
                    S.barrier()
                if cfg.get("gqa", True):
                    gqa(path, l)
                    merge(path, l, 0)
                if cfg.get("mla", True):
                    mla(path, l)
                    merge(path, l, 1)
                if cfg.get("hyena", True):
                    hyena(path, l)
                    merge(path, l, 2)
                if cfg.get("ffn", True):
                    for si, (s0, Ls) in enumerate(path.seqs):
                        c0 = path.hcol(s0, si) - 1
                        c1 = path.hcol(s0 + Ls, si)
                        S.op("dve", MS(hT[:, :, c0:c0 + 1], 0.0), writes=["hT"])
                        S.op("dve", MS(hT[:, :, c1:c1 + 1], 0.0), writes=["hT"])
                    with ExitStack() as scn:
                        mkpools(scn)
                        norm_mod(path, gsh[:, 1, :], lambda kc: MOD(l, 3, kc, path.ccol))
                        S.barrier()
                    ensure_ffn_cast(l)
                    ffn(path, l)
            with ExitStack() as scn:
                mkpools(scn)
                norm_mod(path, V(0, "fin", 0, 8), None, out_dram=True)
                S.barrier()
            store_y(path)
        S.finish()
        S.emit()
        print("kernel: recorded ops", S.nops, S.cnt, S.dcnt, {e: len(v) for e, v in S.streams.items()})
    return nc


_CFG = {}


def kernel(**inputs):
    I = {k: np.asarray(v) for k, v in inputs.items()}
    sh = prep_shared(I)
    ncores = _CFG.get("ncores", NCORES)
    cores = [prep_core(I, c) for c in range(ncores)]

    def sig(d):
        return {k: (v.shape, "bf16" if v.dtype == ml_dtypes.bfloat16 else "f32") for k, v in d.items()}

    nc = build(sig(sh), sig(cores[0]), _CFG)
    in_maps = []
    for c in range(ncores):
        m = dict(sh)
        m.update(cores[c])
        in_maps.append(m)
    if _CFG.get("trace"):
        res = run_bass_kernel_spmd(nc, in_maps, core_ids=list(range(ncores)), trace=True)
        print("EXEC_TIME_NS", res.exec_time_ns)
    else:
        res = run_bass_kernel_spmd(nc, in_maps, core_ids=list(range(ncores)))
    R = list(res.results)
    while len(R) < NCORES:
        R.append(R[0])
    y_prompt = np.concatenate([R[c]["yp"].reshape(2, 256, 1024) for c in range(NCORES)], 0)
    y_sample = np.stack([R[c]["ys"] for c in range(4)], 0)
    nk = np.concatenate([R[c]["nk"].reshape(2, 2, 256, 2, 64) for c in range(NCORES)], 0)
    nv = np.concatenate([R[c]["nv"].reshape(2, 2, 256, 2, 64) for c in range(NCORES)], 0)
    nckv = np.concatenate([R[c]["nckv"] for c in range(NCORES)], 0)
    nkpe = np.concatenate([R[c]["nkpe"] for c in range(NCORES)], 0)
    f = np.float32
    return (y_prompt.astype(f), y_sample.astype(f), nk.astype(f), nv.astype(f), nckv.astype(f), nkpe.astype(f))
```
